# Optimizing a Trainium2 kernel written in Bass

```python
import math
import jax, jax.numpy as jnp
from jax import lax
import numpy as np

D_MODEL = 2048
BATCH = 1
SEQ = 16384
DEPTH = 2

N_EVEN = (DEPTH + 1) // 2
N_ODD = DEPTH // 2
EPS = 1e-6
MEM_LEN = 256

RET_HEADS = 4
RET_QK_DIM = 256
RET_V_DIM = 512
RET_CHUNK = 128
ROPE_BASE = 10000.0

DIL_HEADS = 8
DIL_DIM = 128
DIL_PATTERNS = ((128, 1), (512, 4), (2048, 16))
REL_BUCKETS = 32
REL_MAX_DIST = 2048

AR_QK = RET_HEADS * RET_QK_DIM
AR_V = RET_HEADS * RET_V_DIM
DIL_W = DIL_HEADS * DIL_DIM
AR_IN = 2 * AR_QK + 2 * AR_V + 3 * DIL_W
AR_OUT = AR_V + DIL_W

GDN_K_HEADS = 16
GDN_V_HEADS = 32
GDN_K_DIM = 128
GDN_V_DIM = 128
GDN_CONV = 4
GDN_CHUNK = 64
GDN_QK_W = GDN_K_HEADS * GDN_K_DIM
GDN_V_W = GDN_V_HEADS * GDN_V_DIM
GDN_QKV = 2 * GDN_QK_W + GDN_V_W
GDN_IN = GDN_QKV + GDN_V_W + 2 * GDN_V_HEADS

XA_HEADS = 4
XA_DIM = 128

FFN_HIDDEN = -(-8 * D_MODEL // (3 * 256)) * 256

kernel_name = "hybrid_retention_dilated_gdn_trunk"


def _rms(x):
    xf = x.astype(jnp.float32)
    return xf * lax.rsqrt(jnp.mean(xf * xf, axis=-1, keepdims=True) + EPS)


def rms_norm(x, gain):
    return (_rms(x) * gain.astype(jnp.float32)).astype(x.dtype)


def l2_norm(x):
    xf = x.astype(jnp.float32)
    return xf * lax.rsqrt(jnp.sum(xf * xf, axis=-1, keepdims=True) + EPS)


def rotary(x, pos):
    d = x.shape[-1]
    inv = ROPE_BASE ** (-jnp.arange(0, d, 2, dtype=jnp.float32) / d)
    ang = pos.astype(jnp.float32)[:, None] * inv[None, :]
    cos = jnp.cos(ang)[None, :, None, :]
    sin = jnp.sin(ang)[None, :, None, :]
    xf = x.astype(jnp.float32)
    x1, x2 = xf[..., : d // 2], xf[..., d // 2:]
    return jnp.concatenate([x1 * cos - x2 * sin, x1 * sin + x2 * cos], axis=-1)


def retention(q, k, v):
    B, S, H, Dk = q.shape
    Dv = v.shape[-1]
    C = RET_CHUNK
    N = S // C
    log_gamma = jnp.log(1.0 - 2.0 ** (-5.0 - jnp.arange(H, dtype=jnp.float32)))
    idx = jnp.arange(C, dtype=jnp.float32)
    rel = idx[:, None] - idx[None, :]
    inner_decay = jnp.where(rel >= 0, jnp.exp(log_gamma[:, None, None] * jnp.maximum(rel, 0.0)), 0.0)
    q_decay = jnp.exp(log_gamma[:, None] * (idx + 1.0))[..., None]
    k_decay = jnp.exp(log_gamma[:, None] * (C - 1.0 - idx))[..., None]
    chunk_decay = jnp.exp(log_gamma * C)[:, None, None]

    def chunks(t):
        return t.astype(jnp.float32).reshape(B, N, C, H, t.shape[-1]).transpose(1, 0, 3, 2, 4)

    qc, kc, vc = chunks(q), chunks(k) * (Dk ** -0.5), chunks(v)

    def step(state, inp):
        qi, ki, vi = inp
        scores = jnp.einsum('bhqd,bhkd->bhqk', qi, ki) * inner_decay
        o = (jnp.einsum('bhqk,bhkv->bhqv', scores, vi)
             + jnp.einsum('bhqd,bhdv->bhqv', qi * q_decay, state))
        state = state * chunk_decay + jnp.einsum('bhkd,bhkv->bhdv', ki * k_decay, vi)
        return state, o

    state0 = jnp.zeros((B, H, Dk, Dv), jnp.float32)
    _, o = lax.scan(step, state0, (qc, kc, vc))
    return o.transpose(1, 0, 3, 2, 4).reshape(B, S, H, Dv)


def t5_bucket(dist):
    exact = REL_BUCKETS // 2
    large = exact + (jnp.log(jnp.maximum(dist, exact).astype(jnp.float32) / exact)
                     / math.log(REL_MAX_DIST / exact) * (REL_BUCKETS - exact)).astype(jnp.int32)
    large = jnp.minimum(large, REL_BUCKETS - 1)
    return jnp.where(dist < exact, dist, large)


def dilated_branch(q, k, v, rel_bias, window, dilation):
    B, S, H, D = q.shape
    n = window // dilation
    L = S // dilation
    nb = -(-L // n)
    Lp = nb * n

    def to_blocks(t):
        t = t.astype(jnp.float32).reshape(B, L, dilation, H, D).transpose(0, 2, 3, 1, 4)
        t = jnp.pad(t, ((0, 0), (0, 0), (0, 0), (0, Lp - L), (0, 0)))
        return t.reshape(B, dilation, H, nb, n, D)

    def with_prev(t):
        prev = jnp.pad(t[:, :, :, :-1], ((0, 0), (0, 0), (0, 0), (1, 0), (0, 0), (0, 0)))
        return jnp.concatenate([prev, t], axis=4)

    def from_blocks(t):
        rest = t.shape[5:]
        t = t.reshape((B, dilation, H, Lp) + rest)[:, :, :, :L]
        t = t.transpose((0, 3, 1, 2) + tuple(range(4, 4 + len(rest))))
        return t.reshape((B, S, H) + rest)

    qb = to_blocks(q)
    kb = with_prev(to_blocks(k))
    vb = with_prev(to_blocks(v))

    qi = jnp.arange(n)[:, None] + n
    ki = jnp.arange(2 * n)[None, :]
    sub_dist = qi - ki
    band = (sub_dist >= 0) & (sub_dist <= n)
    blk = jnp.arange(nb)[:, None, None]
    valid = band[None] & ((blk > 0) | (ki[None] >= n))
    bias = rel_bias.astype(jnp.float32)[t5_bucket(jnp.maximum(sub_dist, 0) * dilation)]
    bias = bias.transpose(2, 0, 1)[:, None]

    s = jnp.einsum('bdhnqc,bdhnkc->bdhnqk', qb, kb) + bias
    s = jnp.where(valid, s, -jnp.inf)
    m = jnp.max(s, axis=-1)
    p = jnp.exp(s - m[..., None])
    l = jnp.sum(p, axis=-1)
    o = jnp.einsum('bdhnqk,bdhnkc->bdhnqc', p, vb) / l[..., None]
    return from_blocks(o), from_blocks(m), from_blocks(l)


def dilated_attention(q, k, v, rel_bias):
    outs = [dilated_branch(q, k, v, rel_bias, w, d) for (w, d) in DIL_PATTERNS]
    m_all = jnp.stack([m for (_, m, _) in outs])
    m_max = jnp.max(m_all, axis=0)
    wts = jnp.stack([l * jnp.exp(m - m_max) for (_, m, l) in outs])
    o_all = jnp.stack([o for (o, _, _) in outs])
    return jnp.sum(wts[..., None] * o_all, axis=0) / jnp.sum(wts, axis=0)[..., None]


def retention_dilated_mixer(h, w_in, w_out, q_gain, k_gain, rel_bias):
    B, S, _ = h.shape
    proj = h @ w_in
    qa = proj[..., :AR_QK].reshape(B, S, RET_HEADS, RET_QK_DIM)
    ka = proj[..., AR_QK:2 * AR_QK].reshape(B, S, RET_HEADS, RET_QK_DIM)
    va = proj[..., 2 * AR_QK:2 * AR_QK + AR_V].reshape(B, S, RET_HEADS, RET_V_DIM)
    ga = proj[..., 2 * AR_QK + AR_V:2 * AR_QK + 2 * AR_V]
    qkv_b = proj[..., 2 * AR_QK + 2 * AR_V:]

    pos = jnp.arange(S)
    ya = retention(rotary(qa, pos), rotary(ka, pos), va)
    ya = _rms(ya).reshape(B, S, AR_V) * jax.nn.silu(ga.astype(jnp.float32))

    qb = qkv_b[..., :DIL_W].reshape(B, S, DIL_HEADS, DIL_DIM)
    kb = qkv_b[..., DIL_W:2 * DIL_W].reshape(B, S, DIL_HEADS, DIL_DIM)
    vb = qkv_b[..., 2 * DIL_W:].reshape(B, S, DIL_HEADS, DIL_DIM)
    qb = rms_norm(qb, q_gain) * (DIL_DIM ** -0.5)
    kb = rms_norm(kb, k_gain)
    yb = dilated_attention(qb, kb, vb, rel_bias).reshape(B, S, DIL_W)

    y = jnp.concatenate([ya, yb], axis=-1).astype(h.dtype)
    return y @ w_out


def causal_depthwise_conv(x, w):
    K, ch = w.shape
    return lax.conv_general_dilated(
        x, w[:, None, :].astype(x.dtype), window_strides=(1,), padding=[(K - 1, 0)],
        dimension_numbers=('NWC', 'WIO', 'NWC'), feature_group_count=ch)


def chunk_gated_delta_rule(q, k, v, g, beta):
    B, S, H, Dk = q.shape
    Dv = v.shape[-1]
    C = GDN_CHUNK
    N = S // C

    def chunks(t):
        return t.astype(jnp.float32).reshape((B, N, C) + t.shape[2:]).swapaxes(2, 3)

    q, k, v, g, beta = chunks(q), chunks(k), chunks(v), chunks(g), chunks(beta)
    g_cum = jnp.cumsum(g, axis=-1)
    idx = jnp.arange(C)
    causal = idx[:, None] >= idx[None, :]
    strict = idx[:, None] > idx[None, :]
    decay = jnp.exp(jnp.where(causal, g_cum[..., :, None] - g_cum[..., None, :], -jnp.inf))
    k_beta = k * beta[..., None]
    v_beta = v * beta[..., None]
    a_mat = jnp.where(strict, jnp.einsum('bnhid,bnhjd->bnhij', k_beta, k) * decay, 0.0)
    eye = jnp.eye(C, dtype=jnp.float32)
    t_mat = lax.linalg.triangular_solve(eye + a_mat, jnp.broadcast_to(eye, a_mat.shape),
                                        left_side=True, lower=True, unit_diagonal=True)
    u = jnp.einsum('bnhij,bnhjv->bnhiv', t_mat, v_beta)
    w = jnp.einsum('bnhij,bnhjd->bnhid', t_mat, k_beta * jnp.exp(g_cum)[..., None])
    intra = jnp.where(causal, jnp.einsum('bnhid,bnhjd->bnhij', q, k) * decay, 0.0)

    def step(state, inp):
        q_c, k_c, u_c, w_c, intra_c, g_c = inp
        v_new = u_c - jnp.einsum('bhcd,bhdv->bhcv', w_c, state)
        o = (jnp.einsum('bhcd,bhdv->bhcv', q_c * jnp.exp(g_c)[..., None], state)
             + jnp.einsum('bhij,bhjv->bhiv', intra_c, v_new))
        g_last = g_c[..., -1]
        state = (state * jnp.exp(g_last)[..., None, None]
                 + jnp.einsum('bhcd,bhcv->bhdv', k_c * jnp.exp(g_last[..., None] - g_c)[..., None], v_new))
        return state, o

    xs = tuple(jnp.moveaxis(t, 1, 0) for t in (q, k, u, w, intra, g_cum))
    _, o = lax.scan(step, jnp.zeros((B, H, Dk, Dv), jnp.float32), xs)
    return jnp.moveaxis(o, 0, 1).swapaxes(2, 3).reshape(B, S, H, Dv)


def gated_deltanet_mixer(h, w_in, conv_w, a_log, dt_bias, norm_gain, w_out):
    B, S, _ = h.shape
    proj = h @ w_in
    qkv = jax.nn.silu(causal_depthwise_conv(proj[..., :GDN_QKV], conv_w))
    z = proj[..., GDN_QKV:GDN_QKV + GDN_V_W].reshape(B, S, GDN_V_HEADS, GDN_V_DIM)
    b = proj[..., GDN_QKV + GDN_V_W:GDN_QKV + GDN_V_W + GDN_V_HEADS]
    a = proj[..., GDN_QKV + GDN_V_W + GDN_V_HEADS:]
    rep = GDN_V_HEADS // GDN_K_HEADS
    q = l2_norm(qkv[..., :GDN_QK_W].reshape(B, S, GDN_K_HEADS, GDN_K_DIM)) * (GDN_K_DIM ** -0.5)
    k = l2_norm(qkv[..., GDN_QK_W:2 * GDN_QK_W].reshape(B, S, GDN_K_HEADS, GDN_K_DIM))
    v = qkv[..., 2 * GDN_QK_W:].reshape(B, S, GDN_V_HEADS, GDN_V_DIM)
    q = jnp.repeat(q, rep, axis=2)
    k = jnp.repeat(k, rep, axis=2)
    beta = jax.nn.sigmoid(b.astype(jnp.float32))
    g = -jnp.exp(a_log.astype(jnp.float32)) * jax.nn.softplus(a.astype(jnp.float32) + dt_bias.astype(jnp.float32))
    o = chunk_gated_delta_rule(q, k, v, g, beta)
    o = _rms(o) * norm_gain.astype(jnp.float32) * jax.nn.silu(z.astype(jnp.float32))
    return o.reshape(B, S, GDN_V_W).astype(h.dtype) @ w_out


def memory_cross_attention(h, mem_n, w_q, w_kv, w_o, q_gain, k_gain):
    B, S, _ = h.shape
    M = mem_n.shape[1]
    q = rms_norm((h @ w_q).reshape(B, S, XA_HEADS, XA_DIM), q_gain) * (XA_DIM ** -0.5)
    kv = (mem_n @ w_kv).reshape(B, M, 2, XA_HEADS, XA_DIM)
    k = rms_norm(kv[:, :, 0], k_gain)
    v = kv[:, :, 1]
    s = jnp.einsum('bqhd,bkhd->bhqk', q, k).astype(jnp.float32)
    p = jax.nn.softmax(s, axis=-1).astype(v.dtype)
    o = jnp.einsum('bhqk,bkhd->bqhd', p, v).reshape(B, S, XA_HEADS * XA_DIM)
    return o @ w_o


def swiglu(h, w1, w3, w2):
    return (jax.nn.silu(h @ w1) * (h @ w3)) @ w2


def setup_inputs(seed: int = 0) -> dict:
    key = jax.random.key(seed)
    ks = jax.random.split(key, 32)
    D = D_MODEL

    def w(k, shape, fan_in):
        return jax.random.normal(k, shape, jnp.float32) * fan_in ** -0.5

    def gain(k, shape):
        return 1.0 + 0.05 * jax.random.normal(k, shape, jnp.float32)

    return {
        "x": jax.random.normal(ks[0], (BATCH, SEQ, D), jnp.float32),
        "mem": jax.random.normal(ks[1], (BATCH, MEM_LEN, D), jnp.float32),
        "norm_mix": gain(ks[2], (DEPTH, D)),
        "norm_xa": gain(ks[3], (DEPTH, D)),
        "norm_ffn": gain(ks[4], (DEPTH, D)),
        "mem_norm": gain(ks[5], (D,)),
        "rel_bias": 0.5 * jax.random.normal(ks[6], (REL_BUCKETS, DIL_HEADS), jnp.float32),
        "ar_w_in": w(ks[7], (N_EVEN, D, AR_IN), D),
        "ar_w_out": w(ks[8], (N_EVEN, AR_OUT, D), AR_OUT),
        "dil_q_gain": gain(ks[9], (N_EVEN, DIL_DIM)),
        "dil_k_gain": gain(ks[10], (N_EVEN, DIL_DIM)),
        "gdn_w_in": w(ks[11], (N_ODD, D, GDN_IN), D),
        "gdn_conv": w(ks[12], (N_ODD, GDN_CONV, GDN_QKV), GDN_CONV),
        "gdn_a_log": jnp.log(jax.random.uniform(ks[13], (N_ODD, GDN_V_HEADS), jnp.float32, 1.0, 16.0)),
        "gdn_dt_bias": 0.1 * jax.random.normal(ks[14], (N_ODD, GDN_V_HEADS), jnp.float32),
        "gdn_norm": gain(ks[15], (N_ODD, GDN_V_DIM)),
        "gdn_w_out": w(ks[16], (N_ODD, GDN_V_W, D), GDN_V_W),
        "xa_w_q": w(ks[17], (DEPTH, D, XA_HEADS * XA_DIM), D),
        "xa_w_kv": w(ks[18], (DEPTH, D, 2 * XA_HEADS * XA_DIM), D),
        "xa_w_o": w(ks[19], (DEPTH, XA_HEADS * XA_DIM, D), XA_HEADS * XA_DIM),
        "xa_q_gain": gain(ks[20], (DEPTH, XA_DIM)),
        "xa_k_gain": gain(ks[21], (DEPTH, XA_DIM)),
        "ffn_w1": w(ks[22], (DEPTH, D, FFN_HIDDEN), D),
        "ffn_w3": w(ks[23], (DEPTH, D, FFN_HIDDEN), D),
        "ffn_w2": w(ks[24], (DEPTH, FFN_HIDDEN, D), FFN_HIDDEN),
    }


def reference(x, mem, norm_mix, norm_xa, norm_ffn, mem_norm, rel_bias,
              ar_w_in, ar_w_out, dil_q_gain, dil_k_gain,
              gdn_w_in, gdn_conv, gdn_a_log, gdn_dt_bias, gdn_norm, gdn_w_out,
              xa_w_q, xa_w_kv, xa_w_o, xa_q_gain, xa_k_gain,
              ffn_w1, ffn_w3, ffn_w2):
    mem_n = rms_norm(mem, mem_norm)
    for layer in range(DEPTH):
        i = layer // 2
        h = rms_norm(x, norm_mix[layer])
        if layer % 2 == 0:
            y = retention_dilated_mixer(h, ar_w_in[i], ar_w_out[i], dil_q_gain[i], dil_k_gain[i], rel_bias)
        else:
            y = gated_deltanet_mixer(h, gdn_w_in[i], gdn_conv[i], gdn_a_log[i], gdn_dt_bias[i],
                                     gdn_norm[i], gdn_w_out[i])
        x = x + y.astype(x.dtype)
        h = rms_norm(x, norm_xa[layer])
        x = x + memory_cross_attention(h, mem_n, xa_w_q[layer], xa_w_kv[layer], xa_w_o[layer],
                                       xa_q_gain[layer], xa_k_gain[layer]).astype(x.dtype)
        h = rms_norm(x, norm_ffn[layer])
        x = x + swiglu(h, ffn_w1[layer], ffn_w3[layer], ffn_w2[layer]).astype(x.dtype)
    return x
```

```python
import math
import numpy as np
from contextlib import ExitStack
from concourse.bass_utils import run_bass_kernel_spmd
import concourse.bass as bass
import concourse.mybir as mybir

F32 = mybir.dt.float32
BF16 = mybir.dt.bfloat16
AF = mybir.ActivationFunctionType
ALU = mybir.AluOpType
EPS = 1e-6


class Sem:
    def __init__(self, h):
        self.h = h
        self.n = 0

    def inc(self, ins, by=1):
        ins.then_inc(self.h, by)
        self.n += by
        return self.n


class KB:
    def __init__(self):
        self.nc = bass.Bass("TRN2", target_bir_lowering=False)
        self.es = ExitStack()
        self.sems = {}
        self.uid = 0

    def sb(self, name, shape, dt, es=None):
        return (es or self.es).enter_context(self.nc.sbuf_tensor(name, shape, dt))

    def psum(self, name, shape, dt, es=None):
        return (es or self.es).enter_context(self.nc.psum_tensor(name, shape, dt))

    def sem(self, name):
        if name not in self.sems:
            self.sems[name] = Sem(self.es.enter_context(self.nc.semaphore(name)))
        return self.sems[name]

    def din(self, name, shape, dt=F32):
        return self.nc.dram_tensor(name, list(shape), dt, kind="ExternalInput").ap()

    def dout(self, name, shape, dt=F32):
        return self.nc.dram_tensor(name, list(shape), dt, kind="ExternalOutput").ap()

    def dscr(self, name, shape, dt=F32):
        return self.nc.dram_tensor(name, list(shape), dt, kind="Internal").ap()


D = 2048
T = 2048
TT = 1024
NT = TT // 512
FF = 5632
WB = 8192


class Post:
    def __init__(self, layer):
        self.layer = layer
        self.FY = 3072 if layer == 0 else 4096
        self.G = 2048 if layer == 0 else 4096
        self.kb = kb = KB()
        nc = self.nc = kb.nc
        FY, G = self.FY, self.G
        self.xT = kb.din("xT", [D, T])
        self.oT = kb.din("oT", [FY, T])
        self.w_gate = kb.din("w_gate", [D, G])
        self.w_out = kb.din("w_out", [FY, D])
        self.gains = kb.din("gains", [128, 4, 16])
        self.hg = kb.din("hg", [128, 3])
        self.memT = kb.din("memT", [D, 256])
        self.w_q = kb.din("w_q", [D, 512])
        self.w_kv = kb.din("w_kv", [D, 1024])
        self.w_o = kb.din("w_o", [512, D])
        self.w1 = kb.din("w1", [D, FF])
        self.w3 = kb.din("w3", [D, FF])
        self.w2 = kb.din("w2", [FF, D])
        self.xo = kb.dout("xo", [D, T])
        self.x1 = kb.dout("x1s", [D, T])
        self.x2 = kb.dout("x2s", [D, T])
        self.wbuf = [kb.sb("wbuf0", [128, WB], BF16), kb.sb("wbuf1", [128, WB], BF16)]
        self.ones = kb.sb("ones", [128, 128], BF16)
        self.gains_sb = kb.sb("gains_sb", [128, 4, 16], F32)
        self.hg_sb = kb.sb("hg_sb", [128, 3], F32)
        self.eps_sb = kb.sb("eps_sb", [128, 1], F32)
        self.eps2_sb = kb.sb("eps2_sb", [128, 1], F32)
        self.hT = kb.sb("hT", [128, 16, TT], BF16)
        self.big = kb.sb("big", [128, 32 * TT], BF16)
        self.xst_f = kb.sb("xst", [128, 4096], F32)
        self.sq_f = kb.sb("sq", [128, 4096], BF16)
        self.xst_n = self.xst_f[:, :].rearrange("p (c t) -> p c t", t=256)
        self.sq_n = self.sq_f[:, :].rearrange("p (c t) -> p c t", t=256)
        self.xst = self.xst_f[:, 0:2048].rearrange("p (c t) -> p c t", t=512)
        self.sq = self.sq_f[:, 0:2048].rearrange("p (c t) -> p c t", t=512)
        self.qf = self.xst
        self.rstd = kb.sb("rstd", [128, 4, 512], F32)
        self.rbuf = kb.sb("rbuf", [128, 3, 512], F32)
        self.obuf = kb.sb("obuf", [128, 3, 512], F32)
        self.stmp = kb.sb("stmp", [128, 2, 512], F32)
        self.knT = kb.sb("knT", [128, 4, 256], BF16)
        self.vm = kb.sb("vm", [128, 2, 512], BF16)
        self.qn = self.big[:, 0:4 * TT].rearrange("p (c t) -> p c t", t=TT)
        self.pT = kb.sb("pT", [128, 2, 512], BF16)
        self.oxa = self.big[:, 4 * TT:8 * TT].rearrange("p (c t) -> p c t", t=TT)
        self.ps = kb.psum("ps", [128, 8, 512], F32)
        self.gidx = 0
        self.grp_end = []
        self.mm = kb.sem("g_mm")
        self.pf = kb.sem("g_pf")
        self.wl = [kb.sem("g_wl0"), kb.sem("g_wl1")]
        self.sts = [kb.sem("st%d" % i) for i in range(3)]
        self.bar_n = 0

    def wait_stores(self):
        for st in self.sts:
            self.nc.sync.wait_ge(st.h, st.n)

    def barrier(self):
        self.nc.all_engine_barrier()

    def y(self):
        return self.big[:, 0:(self.FY // 128) * TT].rearrange("p (c t) -> p c t", t=TT)

    def g(self):
        return self.big[:, 0:22 * TT].rearrange("p (c t) -> p c t", t=TT)

    def rstd_op(self, ps_ap, out_ap, inv_n, wait, post=1.0):
        nc = self.nc
        s = self.kb.sem("r_a")
        nc.scalar.wait_ge(wait[0], wait[1])
        s.inc(nc.scalar.activation(out=out_ap, in_=ps_ap, func=AF.Sqrt, scale=inv_n / post ** 2, bias=self.eps_sb[:, 0:1] if post == 1.0 else self.eps2_sb[:, 0:1]))
        nc.vector.wait_ge(s.h, s.n)
        return nc.vector.reciprocal(out=out_ap, in_=out_ap)

    def gemm(self, wsrc, KC, GW, ngroups, act, ntt, epi, pair=False, tw=512, pe_waits=()):
        nc = self.nc
        mpg = GW // 128
        if pair:
            mpg //= 2
        G0 = len(self.grp_end)

        def load(g):
            Gg = G0 + g
            b = Gg % 2
            if Gg >= 2:
                nc.gpsimd.wait_ge(self.mm.h, self.grp_end[Gg - 2])
            wv = self.wbuf[b][:, 0:KC * GW].rearrange("p (c n) -> p c n", n=GW)
            for (ap, off, w) in wsrc(g):
                src = ap.rearrange("(c p) n -> p c n", p=128)
                kstep = 8
                for k0 in range(0, KC, kstep):
                    k1 = min(KC, k0 + kstep)
                    self.wl[b].inc(nc.gpsimd.dma_start(out=wv[:, k0:k1, off:off + w], in_=src[:, k0:k1, :]), 16)
            return self.wl[b].n

        wl_need = {}
        wl_need[0] = load(0)
        cnt = 0
        for (sh, sv) in pe_waits:
            nc.tensor.wait_ge(sh, sv)
        for g in range(ngroups):
            if g + 1 < ngroups:
                wl_need[g + 1] = load(g + 1)
            b = (G0 + g) % 2
            nc.tensor.wait_ge(self.wl[b].h, wl_need[g])
            wv = self.wbuf[b][:, 0:KC * GW].rearrange("p (c n) -> p c n", n=GW)
            for j in range(mpg):
                for tt in range(ntt):
                    cols = [j] if not pair else [j, j + mpg]
                    ps_list = []
                    for cj in cols:
                        idx = self.gidx
                        bank = idx % 4
                        if idx >= 4:
                            nc.tensor.wait_ge(self.pf.h, idx - 3)
                        for k in range(KC):
                            ins = nc.tensor.matmul(self.ps[:, bank, 0:tw], lhsT=wv[:, k, cj * 128:(cj + 1) * 128],
                                                   rhs=act[:, k, tt * tw:(tt + 1) * tw], start=(k == 0), stop=(k == KC - 1))
                        self.mm.inc(ins)
                        self.gidx += 1
                        ps_list.append(self.ps[:, bank, 0:tw])
                    fin = epi(cnt, g * mpg + j, tt, ps_list, self.gidx)
                    self.pf.inc(fin, len(cols))
                    cnt += 1
            self.grp_end.append(self.mm.n)

    def norm(self, src, tok0, which, dst, ntok_tiles, tw=256, gains=None):
        nc = self.nc
        s_ld = self.kb.sem("n_ld"); s_sq = self.kb.sem("n_sq"); s_mm = self.kb.sem("n_mm"); s_dv = self.kb.sem("n_dv")
        for tt in range(ntok_tiles):
            t0 = tok0 + tt * tw
            nc.sync.wait_ge(s_dv.h, s_dv.n)
            srcv = src.rearrange("(c p) t -> p c t", p=128)
            for hh in range(2):
                s_ld.inc(nc.sync.dma_start(out=self.xst_n[:, hh * 8:(hh + 1) * 8, 0:tw], in_=srcv[:, hh * 8:(hh + 1) * 8, t0:t0 + tw]), 16)
            nc.scalar.wait_ge(s_ld.h, s_ld.n)
            nc.scalar.wait_ge(s_mm.h, s_mm.n)
            s_sq.inc(nc.scalar.activation(out=self.sq_n[:, :, 0:tw], in_=self.xst_n[:, :, 0:tw], func=AF.Square))
            nc.tensor.wait_ge(s_sq.h, s_sq.n)
            nc.tensor.wait_ge(s_dv.h, s_dv.n)
            for k in range(16):
                ins = nc.tensor.matmul(self.ps[:, 4, 0:tw], lhsT=self.ones[:, :], rhs=self.sq_n[:, k, 0:tw], start=(k == 0), stop=(k == 15))
            s_mm.inc(ins)
            self.rstd_op(self.ps[:, 4, 0:tw], self.rstd[:, 0, 0:tw], 1.0 / D, (s_mm.h, s_mm.n))
            for k in range(16):
                ins = nc.vector.scalar_tensor_tensor(out=dst[:, k, tt * tw:(tt + 1) * tw], in0=self.xst_n[:, k, 0:tw],
                                                     scalar=self.gains_sb[:, which, k:k + 1], in1=self.rstd[:, 0, 0:tw],
                                                     op0=ALU.mult, op1=ALU.mult)
            s_dv.inc(ins)
        return s_dv

    def onorm(self, tok0):
        nc = self.nc
        layer = self.layer
        y = self.y()
        s_ld = self.kb.sem("o_ld"); s_sq = self.kb.sem("o_sq"); s_mm = self.kb.sem("o_mm"); s_dv = self.kb.sem("o_dv")
        nblk = self.FY // 512
        nnorm = 4 if layer == 0 else 8
        ov = self.oT.rearrange("(c p) t -> p c t", p=128)
        for tt in range(NT):
            t0 = tok0 + tt * 512
            for blk in range(nblk):
                nc.sync.wait_ge(s_dv.h, s_dv.n)
                s_ld.inc(nc.sync.dma_start(out=self.xst[:, 0:4, :], in_=ov[:, blk * 4:(blk + 1) * 4, t0:t0 + 512]), 16)
                if blk >= nnorm:
                    nc.vector.wait_ge(s_ld.h, s_ld.n)
                    ins = nc.vector.tensor_copy(out=y[:, blk * 4:(blk + 1) * 4, tt * 512:(tt + 1) * 512], in_=self.xst[:, 0:4, :])
                    s_dv.inc(ins)
                    continue
                nc.scalar.wait_ge(s_ld.h, s_ld.n)
                nc.scalar.wait_ge(s_mm.h, s_mm.n)
                s_sq.inc(nc.scalar.activation(out=self.sq[:, 0:4, :], in_=self.xst[:, 0:4, :], func=AF.Square))
                nc.tensor.wait_ge(s_sq.h, s_sq.n)
                nc.tensor.wait_ge(s_dv.h, s_dv.n)
                if layer == 0:
                    for k in range(4):
                        ins = nc.tensor.matmul(self.ps[:, 4, :], lhsT=self.ones[:, :], rhs=self.sq[:, k, :], start=(k == 0), stop=(k == 3))
                else:
                    for k in range(4):
                        ins = nc.tensor.matmul(self.ps[:, 4 + k, :], lhsT=self.ones[:, :], rhs=self.sq[:, k, :], start=True, stop=True)
                s_mm.inc(ins)
                if layer == 0:
                    self.rstd_op(self.ps[:, 4, :], self.rstd[:, 0, :], 1.0 / 512, (s_mm.h, s_mm.n))
                    for k in range(4):
                        ins = nc.vector.tensor_tensor(out=y[:, blk * 4 + k, tt * 512:(tt + 1) * 512], in0=self.xst[:, k, :], in1=self.rstd[:, 0, :], op=ALU.mult)
                else:
                    for k in range(4):
                        self.rstd_op(self.ps[:, 4 + k, :], self.rstd[:, k, :], 1.0 / 128, (s_mm.h, s_mm.n))
                        ins = nc.vector.scalar_tensor_tensor(out=y[:, blk * 4 + k, tt * 512:(tt + 1) * 512], in0=self.xst[:, k, :],
                                                             scalar=self.hg_sb[:, 2:3], in1=self.rstd[:, k, :], op0=ALU.mult, op1=ALU.mult)
                s_dv.inc(ins)

    def epi_gate(self):
        nc = self.nc
        y = self.y()
        s_d = self.kb.sem("eg_d")
        base_d = s_d.n

        def epi(cnt, mt, tt, ps_list, idx_after):
            s = cnt % 2
            nc.scalar.wait_ge(self.mm.h, idx_after)
            if cnt >= 2:
                nc.scalar.wait_ge(s_d.h, base_d + cnt - 1)
            fin = nc.scalar.activation(out=self.stmp[:, s, :], in_=ps_list[0], func=AF.Silu)
            nc.vector.wait_ge(self.pf.h, idx_after)
            yv = y[:, mt, tt * 512:(tt + 1) * 512]
            s_d.inc(nc.vector.tensor_tensor(out=yv, in0=self.stmp[:, s, :], in1=yv, op=ALU.mult))
            return fin
        return epi

    def epi_swiglu(self):
        nc = self.nc
        g = self.g()
        s_a = self.kb.sem("es_a")
        hist = []

        def epi(cnt, mt, tt, ps_list, idx_after):
            s = cnt % 2
            nc.scalar.wait_ge(self.mm.h, idx_after)
            if cnt >= 2:
                nc.scalar.wait_ge(self.pf.h, hist[cnt - 2])
            s_a.inc(nc.scalar.activation(out=self.stmp[:, s, :], in_=ps_list[0], func=AF.Silu))
            nc.vector.wait_ge(s_a.h, s_a.n)
            fin = nc.vector.tensor_tensor(out=g[:, mt, tt * 512:(tt + 1) * 512], in0=self.stmp[:, s, :], in1=ps_list[1], op=ALU.mult)
            hist.append(idx_after)
            return fin
        return epi

    def epi_resid(self, res_src, dst, tok0, tiles):
        nc = self.nc
        rv = res_src.rearrange("(c p) t -> p c t", p=128)
        dv = dst.rearrange("(c p) t -> p c t", p=128)
        idx0 = self.gidx
        rls = [self.kb.sem("rl%d" % i) for i in range(3)]
        sts = self.sts

        def issue_load(c):
            mt, tt = tiles[c]
            if c >= 3:
                nc.sync.wait_ge(self.pf.h, idx0 + c - 2)
            rls[c % 3].inc(nc.sync.dma_start(out=self.rbuf[:, c % 3, :], in_=rv[:, mt, tok0 + tt * 512: tok0 + (tt + 1) * 512]), 16)

        def epi(cnt, mt, tt, ps_list, idx_after):
            if cnt == 0:
                issue_load(0)
                if len(tiles) > 1:
                    issue_load(1)
            if cnt + 2 < len(tiles):
                issue_load(cnt + 2)
            s = cnt % 3
            nc.vector.wait_ge(self.mm.h, idx_after)
            nc.vector.wait_ge(rls[s].h, rls[s].n if cnt + 3 >= len(tiles) or True else 0)
            nc.vector.wait_ge(sts[s].h, sts[s].n)
            fin = nc.vector.tensor_tensor(out=self.obuf[:, s, :], in0=ps_list[0], in1=self.rbuf[:, s, :], op=ALU.add)
            nc.sync.wait_ge(self.pf.h, idx_after)
            sts[s].inc(nc.sync.dma_start(out=dv[:, mt, tok0 + tt * 512: tok0 + (tt + 1) * 512], in_=self.obuf[:, s, :]), 16)
            return fin
        return epi

    def epi_plain(self, dstf):
        nc = self.nc

        def epi(cnt, mt, tt, ps_list, idx_after):
            nc.vector.wait_ge(self.mm.h, idx_after)
            return nc.vector.tensor_copy(out=dstf(mt, tt), in_=ps_list[0])
        return epi

    def mem_kv(self):
        nc = self.nc
        s_ld = self.kb.sem("m_ld"); s_a = self.kb.sem("m_a"); s_p = self.kb.sem("m_p"); s_d = self.kb.sem("m_d")
        memn = self.hT[:, :, 0:256]
        ndv = self.norm(self.memT, 0, 3, self.hT, 1, tw=256)
        self.barrier()
        self.gemm(lambda g: [(self.w_kv[:, 0:512], 0, 512)], 16, 512, 1, memn, 1,
                  self.epi_plain(lambda mt, tt: self.qf[:, mt, 0:256]), tw=256, pe_waits=[(ndv.h, ndv.n)])
        self.barrier()
        nc.scalar.wait_ge(self.pf.h, self.gidx)
        nc.scalar.activation(out=self.sq[:, 0:4, 0:256], in_=self.qf[:, 0:4, 0:256], func=AF.Square).then_inc(s_a.h, 1)
        nc.tensor.wait_ge(s_a.h, 1)
        for h in range(4):
            ins = nc.tensor.matmul(self.ps[:, 4 + h, 0:256], lhsT=self.ones[:, :], rhs=self.sq[:, h, 0:256], start=True, stop=True)
        ins.then_inc(s_p.h, 1)
        for h in range(4):
            self.rstd_op(self.ps[:, 4 + h, 0:256], self.rstd[:, h, 0:256], 1.0 / 128, (s_p.h, 1))
            nc.vector.scalar_tensor_tensor(out=self.knT[:, h, :], in0=self.qf[:, h, 0:256], scalar=self.hg_sb[:, 1:2], in1=self.rstd[:, h, 0:256],
                                           op0=ALU.mult, op1=ALU.mult)
        self.barrier()
        wv = self.wbuf[0][:, 0:16 * 512].rearrange("p (c n) -> p c n", n=512)
        src = self.w_kv[:, 512:1024].rearrange("(c p) n -> p c n", p=128)
        for k0 in (0, 8):
            nc.gpsimd.dma_start(out=wv[:, k0:k0 + 8, :], in_=src[:, k0:k0 + 8, :]).then_inc(s_ld.h, 16)
        nc.tensor.wait_ge(s_ld.h, 32)
        for c in range(2):
            for k in range(16):
                ins = nc.tensor.matmul(self.ps[:, 4 + c, :], lhsT=self.hT[:, k, c * 128:(c + 1) * 128], rhs=wv[:, k, :], start=(k == 0), stop=(k == 15))
        ins.then_inc(s_p.h, 1)
        nc.vector.wait_ge(s_p.h, 2)
        for c in range(2):
            ins = nc.vector.tensor_copy(out=self.vm[:, c, :], in_=self.ps[:, 4 + c, :])
        self.barrier()

    def xa_attn(self):
        nc = self.nc
        s_q = self.kb.sem("x_q"); s_a = self.kb.sem("x_a"); s_p = self.kb.sem("x_p"); s_d = self.kb.sem("x_d")
        scale = 128 ** -0.5
        for tt in range(NT):
            self.gemm(lambda g: [(self.w_q[:, :], 0, 512)], 16, 512, 1, self.hT[:, :, tt * 512:(tt + 1) * 512], 1,
                      self.epi_plain(lambda mt, t_: self.qf[:, mt, :]), pe_waits=[(self.kb.sem("n_dv").h, self.kb.sem("n_dv").n)])
            self.barrier()
            nc.scalar.wait_ge(self.pf.h, self.gidx)
            s_a.inc(nc.scalar.activation(out=self.sq[:, 0:4, :], in_=self.qf[:, 0:4, :], func=AF.Square))
            nc.tensor.wait_ge(s_a.h, s_a.n)
            for h in range(4):
                ins = nc.tensor.matmul(self.ps[:, 4 + h, :], lhsT=self.ones[:, :], rhs=self.sq[:, h, :], start=True, stop=True)
            s_p.inc(ins)
            for h in range(4):
                self.rstd_op(self.ps[:, 4 + h, :], self.rstd[:, h, :], 1.0 / 128, (s_p.h, s_p.n), post=scale)
                ins = nc.vector.scalar_tensor_tensor(out=self.qn[:, h, tt * 512:(tt + 1) * 512], in0=self.qf[:, h, :], scalar=self.hg_sb[:, 0:1],
                                                     in1=self.rstd[:, h, :], op0=ALU.mult, op1=ALU.mult)
            s_d.inc(ins)
            nc.tensor.wait_ge(s_d.h, s_d.n)
            nc.scalar.wait_ge(s_d.h, s_d.n)
            self.barrier()
            for h in range(4):
                for c in range(2):
                    ins = nc.tensor.matmul(self.ps[:, 4 + c, :], lhsT=self.knT[:, h, c * 128:(c + 1) * 128], rhs=self.qn[:, h, tt * 512:(tt + 1) * 512],
                                           start=True, stop=True)
                s_p.inc(ins)
                nc.scalar.wait_ge(s_p.h, s_p.n)
                for c in range(2):
                    ins = nc.scalar.activation(out=self.pT[:, c, :], in_=self.ps[:, 4 + c, :], func=AF.Exp)
                s_a.inc(ins)
                nc.tensor.wait_ge(s_a.h, s_a.n)
                for c in range(2):
                    nc.tensor.matmul(self.ps[:, 6, :], lhsT=self.vm[:, c, h * 128:(h + 1) * 128], rhs=self.pT[:, c, :], start=(c == 0), stop=(c == 1))
                for c in range(2):
                    ins = nc.tensor.matmul(self.ps[:, 7, :], lhsT=self.ones[:, :], rhs=self.pT[:, c, :], start=(c == 0), stop=(c == 1))
                s_p.inc(ins)
                nc.vector.wait_ge(s_p.h, s_p.n)
                nc.vector.reciprocal(out=self.rstd[:, 0, :], in_=self.ps[:, 7, :])
                ins = nc.vector.tensor_tensor(out=self.oxa[:, h, tt * 512:(tt + 1) * 512], in0=self.ps[:, 6, :], in1=self.rstd[:, 0, :], op=ALU.mult)
                s_d.inc(ins)
                nc.tensor.wait_ge(s_d.h, s_d.n)
                nc.scalar.wait_ge(s_d.h, s_d.n)
            self.barrier()

    def build(self, stages=99):
        nc = self.nc
        s0 = self.kb.sem("init")
        nc.vector.memset(self.ones[:, :], 1.0)
        nc.vector.memset(self.eps_sb[:, :], EPS)
        nc.vector.memset(self.eps2_sb[:, :], EPS * 128.0)
        nc.sync.dma_start(out=self.gains_sb[:, :, :], in_=self.gains).then_inc(s0.h, 16)
        nc.sync.dma_start(out=self.hg_sb[:, :], in_=self.hg).then_inc(s0.h, 16)
        nc.sync.wait_ge(s0.h, 32)
        self.barrier()
        self.mem_kv()
        for p in range(T // TT):
            tok0 = p * TT
            y = self.y()
            self.norm(self.xT, tok0, 0, self.hT, TT // 256)
            self.barrier()
            if stages < 1:
                continue
            self.onorm(tok0)
            self.barrier()
            self.gemm(lambda g: [(self.w_gate[:, g * 512:(g + 1) * 512], 0, 512)], 16, 512, self.G // 512, self.hT, NT, self.epi_gate(),
                      pe_waits=[(self.kb.sem("n_dv").h, self.kb.sem("n_dv").n), (self.kb.sem("o_dv").h, self.kb.sem("o_dv").n)])
            self.barrier()
            if stages < 2:
                continue
            KC = self.FY // 128
            tiles = [(m, tt) for m in range(16) for tt in range(NT)]
            dst = self.x1 if stages > 2 else self.xo
            self.gemm(lambda g: [(self.w_out[:, g * 256:(g + 1) * 256], 0, 256)], KC, 256, 8, y, NT,
                      self.epi_resid(self.xT, dst, tok0, tiles), pe_waits=[(self.kb.sem("eg_d").h, self.kb.sem("eg_d").n)])
            self.wait_stores()
            self.barrier()
            if stages < 3:
                continue
            self.norm(self.x1, tok0, 1, self.hT, TT // 256)
            self.barrier()
            self.xa_attn()
            dst = self.x2 if stages > 3 else self.xo
            self.gemm(lambda g: [(self.w_o[:, :], 0, 2048)], 4, 2048, 1, self.oxa, NT,
                      self.epi_resid(self.x1, dst, tok0, tiles), pe_waits=[(self.kb.sem("x_d").h, self.kb.sem("x_d").n)])
            self.wait_stores()
            self.barrier()
            if stages < 4:
                continue
            self.norm(self.x2, tok0, 2, self.hT, TT // 256)
            self.barrier()
            for half in range(2):
                c0 = half * (FF // 2)
                self.gemm(lambda g: [(self.w1[:, c0 + g * 256:c0 + (g + 1) * 256], 0, 256), (self.w3[:, c0 + g * 256:c0 + (g + 1) * 256], 256, 256)],
                          16, 512, FF // 512, self.hT, NT, self.epi_swiglu(), pair=True,
                          pe_waits=[(self.kb.sem("n_dv").h, self.kb.sem("n_dv").n)])
                self.barrier()
                w2h = self.w2[c0:c0 + FF // 2, :]
                self.gemm(lambda g: [(w2h[:, g * 256:(g + 1) * 256], 0, 256)], 22, 256, 8, self.g(), NT,
                          self.epi_resid(self.x2 if half == 0 else self.xo, self.xo, tok0, tiles), pe_waits=[(self.pf.h, self.gidx)])
                self.wait_stores()
                self.barrier()
        return nc


def post_inputs(layer, inp, xT_c, oT_c):
    def gl(v):
        return np.ascontiguousarray(v.reshape(16, 128).T)
    gains = np.stack([gl(inp["norm_mix"][layer]), gl(inp["norm_xa"][layer]), gl(inp["norm_ffn"][layer]), gl(inp["mem_norm"])], axis=1)
    gd = inp["gdn_norm"][0]
    hg = np.stack([inp["xa_q_gain"][layer], inp["xa_k_gain"][layer], gd], axis=1)
    if layer == 0:
        w_gate = np.ascontiguousarray(inp["ar_w_in"][0][:, 4096:6144])
        w_out = inp["ar_w_out"][0]
    else:
        w_gate = np.ascontiguousarray(inp["gdn_w_in"][0][:, 8192:12288])
        w_out = inp["gdn_w_out"][0]
    return {
        "xT": xT_c, "oT": oT_c, "w_gate": w_gate, "w_out": np.ascontiguousarray(w_out),
        "gains": np.ascontiguousarray(gains.astype(np.float32)), "hg": np.ascontiguousarray(hg.astype(np.float32)),
        "memT": np.ascontiguousarray(inp["mem"][0].T),
        "w_q": np.ascontiguousarray(inp["xa_w_q"][layer]), "w_kv": np.ascontiguousarray(inp["xa_w_kv"][layer]),
        "w_o": np.ascontiguousarray(inp["xa_w_o"][layer]),
        "w1": np.ascontiguousarray(inp["ffn_w1"][layer]), "w3": np.ascontiguousarray(inp["ffn_w3"][layer]),
        "w2": np.ascontiguousarray(inp["ffn_w2"][layer]),
    }


D = 2048
S = 16384
BT = 512
NB_A = S // BT
NCOL_A = 1152
NRING = 20


class MixA:
    def __init__(self, nblocks=NB_A):
        self.nblocks = nblocks
        self.kb = kb = KB()
        nc = self.nc = kb.nc
        self.xT = kb.din("xT", [D, S])
        self.wA = kb.din("wA", [D, NCOL_A])
        self.gain = kb.din("gain", [128, 16])
        self.hg = kb.din("hg", [128, 2])
        self.cosT = kb.din("cosT", [128, S])
        self.sinT = kb.din("sinT", [128, S])
        self.dmask = kb.din("dmask", [128, 128])
        self.qdrow = kb.din("qdrow", [128, BT])
        self.kdec = kb.din("kdec", [128, 2])
        self.gtab = kb.din("gtab", [128, 17, 128])
        self.mtab = kb.din("mtab", [128, 17, 128])
        self.ident = kb.din("ident", [128, 128])
        self.oret = kb.dout("oret", [256, S])
        self.odil = kb.dout("odil", [128, S])
        sb = kb.sb
        self.w = sb("w", [128, 16, NCOL_A], BF16)
        self.ones = sb("ones", [128, 128], BF16)
        self.idb = sb("idb", [128, 128], BF16)
        self.idf = sb("idf", [128, 128], F32)
        self.gain_sb = sb("gain_sb", [128, 16], F32)
        self.hg_sb = sb("hg_sb", [128, 2], F32)
        self.eps_sb = sb("eps_sb", [128, 1], F32)
        self.eps2_sb = sb("eps2_sb", [128, 1], F32)
        self.dm = sb("dm", [128, 128], F32)
        self.qd = sb("qd", [128, BT], F32)
        self.kd = sb("kd", [128, 2], F32)
        self.E = sb("E", [128, 17, 128], F32)
        self.mt_sb = sb("mt_sb", [128, 17, 128], F32)
        self.xst = sb("xst", [128, 16, BT], F32)
        self.sq = sb("sq", [128, 16, BT], BF16)
        self.hT = sb("hT", [128, 16, BT], BF16)
        self.rstd = sb("rstd", [128, 2, BT], F32)
        self.cs = sb("cs", [128, 2, 2, BT], F32)
        self.tmp = sb("tmp", [128, 2, BT], F32)
        self.QT = sb("QT", [128, 2, BT], BF16)
        self.QdT = sb("QdT", [128, 2, BT], BF16)
        self.KT = sb("KT", [128, 2, BT], BF16)
        self.Kd = sb("Kd", [128, 4, 256], BF16)
        self.VA = sb("VA", [128, 4, 256], BF16)
        self.Sm = sb("Sm", [128, 128], BF16)
        self.St = sb("St", [128, 2, 256], F32)
        self.Stb = sb("Stb", [128, 2, 256], BF16)
        self.qnT = sb("qnT", [128, BT], BF16)
        self.knR = sb("knR", [128, NRING, 128], BF16)
        self.vbR = sb("vbR", [128, NRING, 128], BF16)
        self.ex = sb("ex", [128, BT], F32)
        self.pT = sb("pT", [128, 2, BT], BF16)
        self.rl_ = sb("rl_", [128, 128], F32)
        self.oretb = sb("oretb", [128, 2, 2, BT], F32)
        self.odilb = sb("odilb", [128, 2, BT], F32)
        self.ps = kb.psum("ps", [128, 8, 512], F32)

    def rstd_op(self, ps_ap, out_ap, inv_n, wait, post=1.0):
        nc = self.nc
        s = self.kb.sem("r_a")
        nc.scalar.wait_ge(wait[0], wait[1])
        s.inc(nc.scalar.activation(out=out_ap, in_=ps_ap, func=AF.Sqrt, scale=inv_n / post ** 2,
                                   bias=self.eps_sb[:, 0:1] if post == 1.0 else self.eps2_sb[:, 0:1]))
        nc.vector.wait_ge(s.h, s.n)
        return nc.vector.reciprocal(out=out_ap, in_=out_ap)

    def build(self):
        nc = self.nc
        kb = self.kb
        sem = kb.sem
        ps = self.ps
        V, A, PE, SP, PL = nc.vector, nc.scalar, nc.tensor, nc.sync, nc.gpsimd

        def W(eng, s):
            eng.wait_ge(s.h, s.n)

        s0 = sem("init")
        for k0 in range(0, 16, 4):
            s0.inc(PL.dma_start(out=self.w[:, k0:k0 + 4, :], in_=self.wA.rearrange("(c p) n -> p c n", p=128)[:, k0:k0 + 4, :]), 16)
        s0.inc(PL.dma_start(out=self.idb[:, :], in_=self.ident), 16)
        s1 = sem("init1")
        for (dst, src) in [(self.gain_sb[:, :], self.gain), (self.hg_sb[:, :], self.hg), (self.dm[:, :], self.dmask), (self.qd[:, :], self.qdrow),
                           (self.kd[:, :], self.kdec), (self.E[:, :, :], self.gtab), (self.mt_sb[:, :, :], self.mtab), (self.idf[:, :], self.ident)]:
            s1.inc(SP.dma_start(out=dst, in_=src), 16)
        V.memset(self.ones[:, :], 1.0)
        V.memset(self.eps_sb[:, :], EPS)
        V.memset(self.eps2_sb[:, :], EPS * 128.0)
        V.memset(self.St[:, :, :], 0.0)
        V.memset(self.Stb[:, :, :], 0.0)
        W(A, s1)
        sE = sem("sE")
        sE.inc(A.activation(out=self.E[:, :, :], in_=self.E[:, :, :], func=AF.Exp))
        W(V, sE)
        W(V, s1)
        sE2 = sem("sE2")
        sE2.inc(V.tensor_tensor(out=self.E[:, :, :], in0=self.E[:, :, :], in1=self.mt_sb[:, :, :], op=ALU.mult))
        W(PE, s0)
        W(PE, sE2)
        W(A, sE2)

        xv = self.xT.rearrange("(c p) t -> p c t", p=128)
        s_xl = sem("xl"); s_sq = sem("a_sq"); s_ss = sem("p_ss"); s_h = sem("d_h")
        s_cl = [sem("cl0"), sem("cl1")]
        s_pj = sem("p_pj")
        s_pf = sem("pjf")
        s_rot = sem("d_rot")
        s_sq2 = sem("a_sq2"); s_ss2 = sem("p_ss2")
        s_tr = sem("p_tr"); s_kd = sem("d_kd")
        s_sc = sem("p_sc"); s_sm = sem("d_sm"); s_o = sem("p_o"); s_oe = sem("a_oe"); s_ds = sem("p_ds"); s_st = sem("d_st")
        s_qk = sem("p_qk"); s_ex = sem("a_ex"); s_p = sem("d_p"); s_pv = sem("p_pv"); s_do = sem("d_do")
        s_or = [sem("or0"), sem("or1")]; s_od = [sem("od0"), sem("od1")]
        pj_idx = [0]
        rot_hist = []

        def proj_tile(cols, width, lhs_tok=None):
            i = pj_idx[0]
            bank = i % 2
            if i >= 2:
                PE.wait_ge(s_pf.h, i - 1)
            for k in range(16):
                if lhs_tok is None:
                    ins = PE.matmul(ps[:, bank, 0:BT], lhsT=self.w[:, k, cols:cols + 128], rhs=self.hT[:, k, :], start=(k == 0), stop=(k == 15))
                else:
                    ins = PE.matmul(ps[:, bank, 0:width], lhsT=self.hT[:, k, lhs_tok * 128:(lhs_tok + 1) * 128], rhs=self.w[:, k, cols:cols + width],
                                    start=(k == 0), stop=(k == 15))
            s_pj.inc(ins)
            pj_idx[0] += 1
            return bank

        for b in range(self.nblocks):
            t0 = b * BT
            sl = b % 2
            W(SP, s_h)
            for hh in range(2):
                s_xl.inc(SP.dma_start(out=self.xst[:, hh * 8:(hh + 1) * 8, :], in_=xv[:, hh * 8:(hh + 1) * 8, t0:t0 + BT]), 16)
            if b >= 2:
                SP.wait_ge(s_rot.h, rot_hist[b - 2])
            s_cl[sl].inc(SP.dma_start(out=self.cs[:, sl, 0, :], in_=self.cosT[:, t0:t0 + BT]), 16)
            s_cl[sl].inc(SP.dma_start(out=self.cs[:, sl, 1, :], in_=self.sinT[:, t0:t0 + BT]), 16)
            W(A, s_xl)
            W(A, s_ss)
            W(A, s_ss2)
            s_sq.inc(A.activation(out=self.sq[:, :, :], in_=self.xst[:, :, :], func=AF.Square))
            W(PE, s_sq)
            for k in range(16):
                ins = PE.matmul(ps[:, 2, :], lhsT=self.ones[:, :], rhs=self.sq[:, k, :], start=(k == 0), stop=(k == 15))
            s_ss.inc(ins)
            self.rstd_op(ps[:, 2, :], self.rstd[:, 0, :], 1.0 / D, (s_ss.h, s_ss.n))
            W(V, s_pj)
            for k in range(16):
                ins = V.scalar_tensor_tensor(out=self.hT[:, k, :], in0=self.xst[:, k, :], scalar=self.gain_sb[:, k:k + 1], in1=self.rstd[:, 0, :],
                                             op0=ALU.mult, op1=ALU.mult)
            s_h.inc(ins)
            W(PE, s_h)
            W(V, s_cl[sl])
            for which, col0, dstT in ((0, 0, self.QT), (1, 256, self.KT)):
                b0 = proj_tile(col0, 128)
                b1 = proj_tile(col0 + 128, 128)
                W(V, s_pj)
                if which == 0:
                    W(V, s_o)
                    W(V, s_sc)
                else:
                    W(V, s_tr)
                    W(V, s_sc)
                cosv = self.cs[:, sl, 0, :]; sinv = self.cs[:, sl, 1, :]
                V.tensor_tensor(out=self.tmp[:, 0, :], in0=ps[:, b0, :], in1=cosv, op=ALU.mult)
                V.tensor_tensor(out=self.tmp[:, 1, :], in0=ps[:, b1, :], in1=sinv, op=ALU.mult)
                V.tensor_tensor(out=dstT[:, 0, :], in0=self.tmp[:, 0, :], in1=self.tmp[:, 1, :], op=ALU.subtract)
                V.tensor_tensor(out=self.tmp[:, 0, :], in0=ps[:, b0, :], in1=sinv, op=ALU.mult)
                ins = V.tensor_tensor(out=self.tmp[:, 1, :], in0=ps[:, b1, :], in1=cosv, op=ALU.mult)
                s_pf.inc(ins, 2)
                ins = V.tensor_tensor(out=dstT[:, 1, :], in0=self.tmp[:, 0, :], in1=self.tmp[:, 1, :], op=ALU.add)
                if which == 0:
                    for i in range(2):
                        ins = V.tensor_tensor(out=self.QdT[:, i, :], in0=self.QT[:, i, :], in1=self.qd[:, :], op=ALU.mult)
                s_rot.inc(ins)
            rot_hist.append(s_rot.n)
            for which, col0 in ((0, 512), (1, 640)):
                bk = proj_tile(col0, 128)
                W(A, s_pj)
                W(A, s_ss2)
                s_sq2.inc(A.activation(out=self.sq[:, 0, :], in_=ps[:, bk, :], func=AF.Square))
                W(PE, s_sq2)
                ins = PE.matmul(ps[:, 2, :], lhsT=self.ones[:, :], rhs=self.sq[:, 0, :], start=True, stop=True)
                s_ss2.inc(ins)
                if which == 0:
                    self.rstd_op(ps[:, 2, :], self.rstd[:, 1, :], 1.0 / 128, (s_ss2.h, s_ss2.n), post=128 ** -0.5)
                    W(V, s_pv)
                    W(V, s_qk)
                    ins = V.scalar_tensor_tensor(out=self.qnT[:, :], in0=ps[:, bk, :], scalar=self.hg_sb[:, 0:1], in1=self.rstd[:, 1, :],
                                                 op0=ALU.mult, op1=ALU.mult)
                else:
                    self.rstd_op(ps[:, 2, :], self.rstd[:, 1, :], 1.0 / 128, (s_ss2.h, s_ss2.n))
                    W(V, s_qk)
                    for j in range(4):
                        slot = (4 * b + j) % NRING
                        ins = V.scalar_tensor_tensor(out=self.knR[:, slot, :], in0=ps[:, bk, j * 128:(j + 1) * 128], scalar=self.hg_sb[:, 1:2],
                                                     in1=self.rstd[:, 1, j * 128:(j + 1) * 128], op0=ALU.mult, op1=ALU.mult)
                s_pf.inc(ins, 1)
            for c in range(4):
                bk = proj_tile(768, 384, lhs_tok=c)
                W(V, s_pj)
                if c == 0:
                    W(V, s_ds)
                    W(V, s_o)
                    W(V, s_pv)
                V.tensor_copy(out=self.VA[:, c, :], in_=ps[:, bk, 0:256])
                ins = V.tensor_copy(out=self.vbR[:, (4 * b + c) % NRING, :], in_=ps[:, bk, 256:384])
                s_pf.inc(ins, 1)
            W(PE, s_rot)
            for c in range(4):
                W(PE, s_kd)
                for i in range(2):
                    ins = PE.matmul(ps[:, 3, i * 128:(i + 1) * 128], lhsT=self.KT[:, i, c * 128:(c + 1) * 128], rhs=self.idb[:, :], start=True, stop=True)
                s_tr.inc(ins)
                W(V, s_tr)
                if c == 0:
                    W(V, s_ds)
                for i in range(2):
                    ins = V.tensor_scalar(out=self.Kd[:, c, i * 128:(i + 1) * 128], in0=ps[:, 3, i * 128:(i + 1) * 128], scalar1=self.kd[:, 0:1], scalar2=None, op0=ALU.mult)
                s_kd.inc(ins)
            if b % 2 == 0 or True:
                V.wait_ge(s_or[sl].h, s_or[sl].n)
                A.wait_ge(s_or[sl].h, s_or[sl].n)
            for c in range(4):
                cs_ = slice(c * 128, (c + 1) * 128)
                W(PE, s_sm)
                W(PE, s_kd)
                for i in range(2):
                    ins = PE.matmul(ps[:, 3, 256:384], lhsT=self.KT[:, i, cs_], rhs=self.QT[:, i, cs_], start=(i == 0), stop=(i == 1))
                s_sc.inc(ins)
                W(V, s_sc)
                W(V, s_o)
                s_sm.inc(V.tensor_tensor(out=self.Sm[:, :], in0=ps[:, 3, 256:384], in1=self.dm[:, :], op=ALU.mult))
                W(PE, s_sm)
                W(PE, s_pf)
                W(PE, s_st)
                W(PE, s_oe)
                for j in range(2):
                    PE.matmul(ps[:, 4, j * 128:(j + 1) * 128], lhsT=self.VA[:, c, j * 128:(j + 1) * 128], rhs=self.Sm[:, :], start=True, stop=False)
                    for i in range(2):
                        ins = PE.matmul(ps[:, 4, j * 128:(j + 1) * 128], lhsT=self.Stb[:, i, j * 128:(j + 1) * 128], rhs=self.QdT[:, i, cs_],
                                        start=False, stop=(i == 1))
                s_o.inc(ins)
                W(A, s_o)
                for j in range(2):
                    ins = A.activation(out=self.oretb[:, sl, j, cs_], in_=ps[:, 4, j * 128:(j + 1) * 128], func=AF.Copy)
                s_oe.inc(ins)
                W(PE, s_kd)
                for i in range(2):
                    ins = PE.matmul(ps[:, 5, i * 256:(i + 1) * 256], lhsT=self.Kd[:, c, i * 128:(i + 1) * 128], rhs=self.VA[:, c, :], start=True, stop=True)
                s_ds.inc(ins)
                W(V, s_ds)
                W(V, s_o)
                for i in range(2):
                    V.scalar_tensor_tensor(out=self.St[:, i, :], in0=self.St[:, i, :], scalar=self.kd[:, 1:2], in1=ps[:, 5, i * 256:(i + 1) * 256],
                                           op0=ALU.mult, op1=ALU.add)
                ins = V.tensor_copy(out=self.Stb[:, :, :], in_=self.St[:, :, :])
                s_st.inc(ins)
            W(SP, s_oe)
            s_or[sl].inc(SP.dma_start(out=self.oret.rearrange("(j p) t -> p j t", p=128)[:, :, t0:t0 + BT], in_=self.oretb[:, sl, :, :]), 16)
            W(PE, s_pf)
            V.wait_ge(s_od[sl].h, s_od[sl].n)
            for qt in range(4):
                tq = 4 * b + qt
                nk = min(17, tq + 1)
                batches = [(o0, min(4, nk - o0)) for o0 in range(0, nk, 4)]
                W(PE, s_do)
                for bi, (o0, n) in enumerate(batches):
                    W(PE, s_ex)
                    for j in range(n):
                        slot = (tq - (o0 + j)) % NRING
                        ins = PE.matmul(ps[:, 6, j * 128:(j + 1) * 128], lhsT=self.knR[:, slot, :], rhs=self.qnT[:, qt * 128:(qt + 1) * 128], start=True, stop=True)
                    s_qk.inc(ins)
                    W(A, s_qk)
                    W(A, s_p)
                    s_ex.inc(A.activation(out=self.ex[:, 0:n * 128], in_=ps[:, 6, 0:n * 128], func=AF.Exp))
                    W(V, s_ex)
                    if bi >= 2:
                        V.wait_ge(s_pv.h, pv_hist[-2])
                    pslot = bi % 2
                    s_p.inc(V.tensor_tensor(out=self.pT[:, pslot, 0:n * 128], in0=self.ex[:, 0:n * 128],
                                            in1=self.E[:, o0:o0 + n, :].rearrange("p o q -> p (o q)"), op=ALU.mult))
                    W(PE, s_p)
                    for j in range(n):
                        slot = (tq - (o0 + j)) % NRING
                        first = (o0 + j == 0)
                        last = (o0 + j == nk - 1)
                        PE.matmul(ps[:, 7, 0:128], lhsT=self.vbR[:, slot, :], rhs=self.pT[:, pslot, j * 128:(j + 1) * 128], start=first, stop=last, skip_group_check=True)
                        ins = PE.matmul(ps[:, 7, 128:256], lhsT=self.ones[:, :], rhs=self.pT[:, pslot, j * 128:(j + 1) * 128], start=False, stop=last, skip_group_check=True)
                    s_pv.inc(ins)
                    if bi == 0:
                        pv_hist = []
                    pv_hist.append(s_pv.n)
                W(V, s_pv)
                V.reciprocal(out=self.rl_[:, :], in_=ps[:, 7, 128:256])
                s_do.inc(V.tensor_tensor(out=self.odilb[:, sl, qt * 128:(qt + 1) * 128], in0=ps[:, 7, 0:128], in1=self.rl_[:, :], op=ALU.mult))
            W(SP, s_do)
            s_od[sl].inc(SP.dma_start(out=self.odil[:, t0:t0 + BT], in_=self.odilb[:, sl, :]), 16)
        for s in s_or + s_od:
            W(SP, s)
        return nc


def t5_bucket_np(dist):
    exact = 16
    d = np.maximum(dist, exact).astype(np.float32)
    large = exact + (np.log(d / np.float32(exact)) / np.float32(math.log(2048 / exact)) * np.float32(32 - exact)).astype(np.int32)
    large = np.minimum(large, 31)
    return np.where(dist < exact, dist, large)


def mixa_consts():
    i = np.arange(128, dtype=np.float32)
    inv = (np.float32(10000.0) ** (-(np.arange(0, 256, 2, dtype=np.float32)) / np.float32(256))).astype(np.float32)
    pos = np.arange(S, dtype=np.float32)
    ang = (inv[:, None] * pos[None, :]).astype(np.float32)
    cosT = np.cos(ang).astype(np.float32)
    sinT = np.sin(ang).astype(np.float32)
    kj = np.arange(128)[:, None, None]
    o = np.arange(17)[None, :, None]
    qi = np.arange(128)[None, None, :]
    delta = qi - kj + 128 * o
    valid = delta >= 0
    m = ((delta <= 128) & valid).astype(np.float32) + ((delta % 4 == 0) & (delta <= 512) & valid) + ((delta % 16 == 0) & (delta <= 2048) & valid)
    bidx = t5_bucket_np(np.maximum(delta, 0))
    return cosT, sinT, m.astype(np.float32), bidx


def mixa_inputs(inp, c, xT, consts):
    cosT, sinT, mtab, bidx = consts
    hr, vh, hd = c // 2, c % 2, c
    W = inp["ar_w_in"][0]
    wA = np.concatenate([W[:, hr * 256:(hr + 1) * 256], W[:, 1024 + hr * 256:1024 + (hr + 1) * 256],
                         W[:, 6144 + hd * 128:6144 + (hd + 1) * 128], W[:, 7168 + hd * 128:7168 + (hd + 1) * 128],
                         W[:, 2048 + hr * 512 + vh * 256:2048 + hr * 512 + (vh + 1) * 256], W[:, 8192 + hd * 128:8192 + (hd + 1) * 128]], axis=1)
    gamma = 1.0 - 2.0 ** (-5.0 - hr)
    kj = np.arange(128)[:, None]; qi = np.arange(128)[None, :]
    dmask = np.where(qi >= kj, gamma ** np.maximum(qi - kj, 0), 0.0) * 256 ** -0.5
    qdrow = np.tile(gamma ** (np.arange(128) + 1.0), 4)[None, :].repeat(128, axis=0)
    kdec = np.stack([gamma ** (127.0 - np.arange(128)) * 256 ** -0.5, np.full(128, gamma ** 128.0)], axis=1)
    gtab = inp["rel_bias"][:, hd][bidx]
    return {
        "xT": xT, "wA": np.ascontiguousarray(wA), "gain": np.ascontiguousarray(inp["norm_mix"][0].reshape(16, 128).T),
        "hg": np.ascontiguousarray(np.stack([inp["dil_q_gain"][0], inp["dil_k_gain"][0]], axis=1)),
        "cosT": cosT, "sinT": sinT, "dmask": dmask.astype(np.float32), "qdrow": qdrow.astype(np.float32), "kdec": kdec.astype(np.float32),
        "gtab": np.ascontiguousarray(gtab.astype(np.float32)), "mtab": mtab, "ident": np.eye(128, dtype=np.float32),
    }


D = 2048
S = 16384
BT = 512
NB_C = S // BT
NCOL_C = 1032
C = 64


def fl(ap):
    return ap.rearrange("p a b -> p (a b)")


class MixC:
    def __init__(self, nblocks=NB_C):
        self.nblocks = nblocks
        self.kb = kb = KB()
        self.nc = kb.nc
        self.xT = kb.din("xT", [D, S])
        self.wC = kb.din("wC", [D, NCOL_C])
        self.gain = kb.din("gain", [128, 16])
        self.convw = kb.din("convw", [128, 8, 4])
        self.hp = kb.din("hp", [128, 8])
        self.U = kb.din("U", [64, 64])
        self.MU = kb.din("MU", [64, 4, 64])
        self.ML = kb.din("ML", [64, 4, 64])
        self.I4 = kb.din("I4", [64, 4, 64])
        self.ident = kb.din("ident", [128, 128])
        self.og = kb.dout("og", [512, S])
        sb = kb.sb
        self.w = sb("w", [128, 16, NCOL_C], BF16)
        self.ones = sb("ones", [128, 128], BF16)
        self.onesf = sb("onesf", [64, 128], F32)
        self.idb = sb("idb", [128, 128], BF16)
        self.idf = sb("idf", [128, 128], F32)
        self.gain_sb = sb("gain_sb", [128, 16], F32)
        self.cw = sb("cw", [128, 8, 4], F32)
        self.hp_sb = sb("hp_sb", [128, 8], F32)
        self.negA = sb("negA", [128, 4], F32)
        self.eps_sb = sb("eps_sb", [128, 1], F32)
        self.eps2_sb = sb("eps2_sb", [128, 1], F32)
        self.one_sb = sb("one_sb", [128, 1], F32)
        self.U_sb = sb("U_sb", [64, 64], F32)
        self.MU_sb = sb("MU_sb", [64, 4, 64], F32)
        self.ML_sb = sb("ML_sb", [64, 4, 64], F32)
        self.I4_sb = sb("I4_sb", [64, 4, 64], F32)
        self.xst = sb("xst", [128, 16, BT], F32)
        self.sq = sb("sq", [128, 16, BT], BF16)
        self.hT = sb("hT", [128, 16, BT], BF16)
        self.rstd = sb("rstd", [128, BT], F32)
        self.praw = sb("praw", [128, 8, 3 + BT], F32)
        self.acc = sb("acc", [128, BT], F32)
        self.cvs = sb("cvs", [128, 8, BT], F32)
        self.qnT = sb("qnT", [128, 2, BT], BF16)
        self.knT = sb("knT", [128, 2, BT], BF16)
        self.vT = sb("vT", [128, 4, BT], BF16)
        self.ktok = sb("ktok", [64, 2, 128], F32)
        self.vtok = sb("vtok", [64, 4, 128], F32)
        self.xa = sb("xa", [64, 4], F32)
        self.beta = sb("beta", [64, 4], F32)
        self.nbeta = sb("nbeta", [64, 4], F32)
        self.g = sb("g", [64, 4], F32)
        self.gcc = sb("gcc", [64, 4], F32)
        self.egc = sb("egc", [64, 4], F32)
        self.bg = sb("bg", [64, 4], F32)
        self.egl = sb("egl", [128, 4], F32)
        self.G1 = sb("G1", [64, 4, 128], F32)
        self.Zmin = sb("Zmin", [64, 4, 64], F32)
        self.Zmax = sb("Zmax", [64, 4, 64], F32)
        self.E0T = sb("E0T", [64, 4, 64], F32)
        self.E1 = sb("E1", [64, 4, 64], F32)
        self.eR = sb("eR", [128, 4, 64], F32)
        self.X = sb("X", [64, 4, 64], F32)
        self.Y = sb("Y", [64, 4, 64], F32)
        self.Q = sb("Q", [64, 4, 64], F32)
        self.Tt = sb("Tt", [64, 4, 64], BF16)
        self.intraT = sb("intraT", [64, 4, 64], BF16)
        self.vb = sb("vb", [64, 4, 128], BF16)
        self.kbg = sb("kbg", [64, 4, 128], BF16)
        self.kdk = sb("kdk", [64, 4, 128], BF16)
        self.qgT = sb("qgT", [128, 4, 64], BF16)
        self.nwT = sb("nwT", [128, 4, 64], BF16)
        self.vnew = sb("vnew", [64, 4, 128], BF16)
        self.St = sb("St", [128, 4, 128], F32)
        self.Stb = sb("Stb", [128, 4, 128], BF16)
        self.ob = sb("ob", [128, 2, 4, BT], F32)
        self.ps = kb.psum("ps", [128, 8, 512], F32)

    def build(self, debug=False):
        nc = self.nc
        kb = self.kb
        ps = self.ps
        V, A, PE, SP, PL = nc.vector, nc.scalar, nc.tensor, nc.sync, nc.gpsimd
        sems = {"V": kb.sem("sV"), "A": kb.sem("sA"), "P": kb.sem("sP")}
        engs = {"V": V, "A": A, "P": PE}

        def step(e, fn, extra=()):
            eng = engs[e]
            for o in sems:
                if o != e and sems[o].n > 0:
                    eng.wait_ge(sems[o].h, sems[o].n)
            for (sh, sv) in extra:
                eng.wait_ge(sh, sv)
            ins = fn()
            sems[e].inc(ins)

        def sp_wait_all():
            for o in sems:
                if sems[o].n > 0:
                    SP.wait_ge(sems[o].h, sems[o].n)

        s0 = kb.sem("init"); s1 = kb.sem("init1")
        wv = self.wC.rearrange("(c p) n -> p c n", p=128)
        for k0 in range(0, 16, 4):
            s0.inc(PL.dma_start(out=self.w[:, k0:k0 + 4, :], in_=wv[:, k0:k0 + 4, :]), 16)
        s0.inc(PL.dma_start(out=self.idb[:, :], in_=self.ident), 16)
        for (dst, src) in [(self.gain_sb[:, :], self.gain), (self.cw[:, :, :], self.convw), (self.hp_sb[:, :], self.hp), (self.U_sb[:, :], self.U),
                           (self.MU_sb[:, :, :], self.MU), (self.ML_sb[:, :, :], self.ML), (self.I4_sb[:, :, :], self.I4), (self.idf[:, :], self.ident)]:
            s1.inc(SP.dma_start(out=dst, in_=src), 16)

        def init_v():
            V.memset(self.ones[:, :], 1.0)
            V.memset(self.onesf[:, :], 1.0)
            V.memset(self.eps_sb[:, :], EPS)
            V.memset(self.one_sb[:, :], 1.0)
            V.memset(self.eps2_sb[:, :], EPS * 128.0)
            V.memset(self.St[:, :, :], 0.0)
            V.memset(self.Stb[:, :, :], 0.0)
            return V.memset(self.praw[:, :, 0:3], 0.0)
        step("V", init_v)
        step("A", lambda: A.activation(out=self.negA[:, :], in_=self.hp_sb[:, 0:4], func=AF.Exp), extra=[(s1.h, s1.n)])
        step("V", lambda: V.tensor_scalar(out=self.negA[:, :], in0=self.negA[:, :], scalar1=-1.0, scalar2=None, op0=ALU.mult), extra=[(s0.h, s0.n), (s1.h, s1.n)])
        PE.wait_ge(s0.h, s0.n)
        PE.wait_ge(s1.h, s1.n)

        xv = self.xT.rearrange("(c p) t -> p c t", p=128)
        s_xl = kb.sem("xl")
        s_o = [kb.sem("so0"), kb.sem("so1")]

        def load_x(b):
            t0 = b * BT
            for hh in range(2):
                s_xl.inc(SP.dma_start(out=self.xst[:, hh * 8:(hh + 1) * 8, :], in_=xv[:, hh * 8:(hh + 1) * 8, t0:t0 + BT]), 16)

        load_x(0)
        for b in range(self.nblocks):
            t0 = b * BT
            sl = b % 2
            step("A", lambda: A.activation(out=self.sq[:, :, :], in_=self.xst[:, :, :], func=AF.Square), extra=[(s_xl.h, s_xl.n)])

            def f():
                for k in range(16):
                    ins = PE.matmul(ps[:, 2, :], lhsT=self.ones[:, :], rhs=self.sq[:, k, :], start=(k == 0), stop=(k == 15))
                return ins
            step("P", f)
            step("A", lambda: A.activation(out=self.rstd[:, :], in_=ps[:, 2, :], func=AF.Sqrt, scale=1.0 / D, bias=self.eps_sb[:, 0:1]))

            def f():
                V.reciprocal(out=self.rstd[:, :], in_=self.rstd[:, :])
                for k in range(16):
                    ins = V.scalar_tensor_tensor(out=self.hT[:, k, :], in0=self.xst[:, k, :], scalar=self.gain_sb[:, k:k + 1], in1=self.rstd[:, :],
                                                 op0=ALU.mult, op1=ALU.mult)
                return ins
            step("V", f)
            if b + 1 < self.nblocks:
                sp_wait_all()
                load_x(b + 1)
            for mt in range(8):
                def f(mt=mt):
                    for k in range(16):
                        ins = PE.matmul(ps[:, mt % 2, :], lhsT=self.w[:, k, mt * 128:(mt + 1) * 128], rhs=self.hT[:, k, :], start=(k == 0), stop=(k == 15))
                    return ins
                step("P", f)
                step("A", lambda mt=mt: A.activation(out=self.praw[:, mt, 3:3 + BT], in_=ps[:, mt % 2, :], func=AF.Copy))

                def f(mt=mt):
                    V.tensor_scalar(out=self.acc[:, :], in0=self.praw[:, mt, 0:BT], scalar1=self.cw[:, mt, 0:1], scalar2=None, op0=ALU.mult)
                    for j in range(1, 4):
                        ins = V.scalar_tensor_tensor(out=self.acc[:, :], in0=self.praw[:, mt, j:j + BT], scalar=self.cw[:, mt, j:j + 1], in1=self.acc[:, :],
                                                     op0=ALU.mult, op1=ALU.add)
                    return ins
                step("V", f)
                step("A", lambda mt=mt: A.activation(out=self.cvs[:, mt, :], in_=self.acc[:, :], func=AF.Silu))
                step("V", lambda mt=mt: V.tensor_copy(out=self.praw[:, mt, 0:3], in_=self.praw[:, mt, BT:BT + 3]))
            step("A", lambda: A.activation(out=self.sq[:, 0:4, :], in_=self.cvs[:, 0:4, :], func=AF.Square))
            for mt in range(4):
                step("P", lambda mt=mt: PE.matmul(ps[:, 2, :], lhsT=self.ones[:, :], rhs=self.sq[:, mt, :], start=True, stop=True))
                if mt < 2:
                    step("A", lambda: A.activation(out=self.rstd[:, :], in_=ps[:, 2, :], func=AF.Sqrt, scale=128.0, bias=self.eps2_sb[:, 0:1]))
                else:
                    step("A", lambda: A.activation(out=self.rstd[:, :], in_=ps[:, 2, :], func=AF.Sqrt, scale=1.0, bias=self.eps_sb[:, 0:1]))

                def f(mt=mt):
                    V.reciprocal(out=self.rstd[:, :], in_=self.rstd[:, :])
                    dst = self.qnT[:, mt, :] if mt < 2 else self.knT[:, mt - 2, :]
                    return V.tensor_tensor(out=dst, in0=self.cvs[:, mt, :], in1=self.rstd[:, :], op=ALU.mult)
                step("V", f)
            step("V", lambda: V.tensor_copy(out=self.vT[:, :, :], in_=self.cvs[:, 4:8, :]))
            for ch in range(BT // C):
                cs = slice(ch * C, (ch + 1) * C)
                def f():
                    for k in range(16):
                        ins = PE.matmul(ps[0:64, 2, 0:8], lhsT=self.hT[:, k, cs], rhs=self.w[:, k, 1024:1032], start=(k == 0), stop=(k == 15))
                    return ins
                step("P", f)
                step("V", lambda: V.tensor_tensor(out=self.xa[:, :], in0=ps[0:64, 2, 4:8], in1=self.hp_sb[0:64, 4:8], op=ALU.add))
                step("A", lambda: A.activation(out=self.xa[:, :], in_=self.xa[:, :], func=AF.Exp))
                step("A", lambda: A.activation(out=self.xa[:, :], in_=self.xa[:, :], func=AF.Ln, bias=self.one_sb[0:64, 0:1]))
                step("V", lambda: V.tensor_tensor(out=self.g[:, :], in0=self.xa[:, :], in1=self.negA[0:64, :], op=ALU.mult))
                step("A", lambda: A.activation(out=self.beta[:, :], in_=ps[0:64, 2, 0:4], func=AF.Sigmoid))

                def f():
                    V.tensor_scalar(out=self.nbeta[:, :], in0=self.beta[:, :], scalar1=-1.0, scalar2=None, op0=ALU.mult)
                    for h in range(4):
                        ins = V.tensor_scalar(out=self.G1[:, h, :], in0=self.onesf[:, :], scalar1=self.g[:, h:h + 1], scalar2=None, op0=ALU.mult)
                    return ins
                step("V", f)

                def f():
                    PE.matmul(ps[0:64, 2, 8:12], lhsT=self.U_sb[:, :], rhs=self.g[:, :], start=True, stop=True)
                    PE.matmul(ps[:, 2, 16:20], lhsT=self.onesf[:, :], rhs=self.g[:, :], start=True, stop=True)
                    for h in range(4):
                        ins = PE.matmul(ps[:, 3, h * 64:(h + 1) * 64], lhsT=self.G1[:, h, :], rhs=self.U_sb[:, :], start=True, stop=True)
                    for kh in range(2):
                        PE.matmul(ps[0:64, 0, kh * 128:(kh + 1) * 128], lhsT=self.knT[:, kh, cs], rhs=self.idb[:, :], start=True, stop=True)
                    for h in range(4):
                        ins = PE.matmul(ps[0:64, 1, h * 128:(h + 1) * 128], lhsT=self.vT[:, h, cs], rhs=self.idb[:, :], start=True, stop=True)
                    for kh in range(2):
                        PE.matmul(ps[0:64, 4, kh * 64:(kh + 1) * 64], lhsT=self.knT[:, kh, cs], rhs=self.knT[:, kh, cs], start=True, stop=True)
                    for kh in range(2):
                        ins = PE.matmul(ps[0:64, 4, 128 + kh * 64:128 + (kh + 1) * 64], lhsT=self.knT[:, kh, cs], rhs=self.qnT[:, kh, cs], start=True, stop=True)
                    return ins
                step("P", f)

                def f():
                    A.activation(out=self.gcc[:, :], in_=ps[0:64, 2, 8:12], func=AF.Copy)
                    A.activation(out=self.egl[:, :], in_=ps[:, 2, 16:20], func=AF.Exp)
                    A.activation(out=fl(self.eR[:, :, :]), in_=ps[:, 3, 0:256], func=AF.Exp)
                    A.activation(out=fl(self.ktok[:, :, :]), in_=ps[0:64, 0, 0:256], func=AF.Copy)
                    return A.activation(out=fl(self.vtok[:, :, :]), in_=ps[0:64, 1, :], func=AF.Copy)
                step("A", f)

                def f():
                    for h in range(4):
                        V.tensor_scalar(out=self.Zmin[:, h, :], in0=ps[0:64, 3, h * 64:(h + 1) * 64], scalar1=self.gcc[:, h:h + 1], scalar2=0.0,
                                        op0=ALU.subtract, op1=ALU.min)
                        ins = V.tensor_scalar(out=self.Zmax[:, h, :], in0=ps[0:64, 3, h * 64:(h + 1) * 64], scalar1=self.gcc[:, h:h + 1], scalar2=0.0,
                                              op0=ALU.subtract, op1=ALU.max)
                    return ins
                step("V", f)

                def f():
                    A.activation(out=self.egc[:, :], in_=self.gcc[:, :], func=AF.Exp)
                    A.activation(out=self.E0T[:, :, :], in_=self.Zmin[:, :, :], func=AF.Exp)
                    return A.activation(out=self.E1[:, :, :], in_=self.Zmax[:, :, :], func=AF.Exp, scale=-1.0)
                step("A", f)

                def f():
                    V.tensor_tensor(out=self.bg[:, :], in0=self.beta[:, :], in1=self.egc[:, :], op=ALU.mult)
                    V.tensor_tensor(out=self.E0T[:, :, :], in0=self.E0T[:, :, :], in1=self.MU_sb[:, :, :], op=ALU.mult)
                    V.tensor_tensor(out=self.E1[:, :, :], in0=self.E1[:, :, :], in1=self.ML_sb[:, :, :], op=ALU.mult)
                    for h in range(4):
                        kh = h // 2
                        V.scalar_tensor_tensor(out=self.X[:, h, :], in0=ps[0:64, 4, kh * 64:(kh + 1) * 64], scalar=self.nbeta[:, h:h + 1], in1=self.E1[:, h, :],
                                               op0=ALU.mult, op1=ALU.mult)
                        V.tensor_tensor(out=self.intraT[:, h, :], in0=ps[0:64, 4, 128 + kh * 64:128 + (kh + 1) * 64], in1=self.E0T[:, h, :], op=ALU.mult)
                        V.tensor_scalar(out=self.vb[:, h, :], in0=self.vtok[:, h, :], scalar1=self.beta[:, h:h + 1], scalar2=None, op0=ALU.mult)
                        V.tensor_scalar(out=self.kbg[:, h, :], in0=self.ktok[:, kh, :], scalar1=self.bg[:, h:h + 1], scalar2=None, op0=ALU.mult)
                        V.tensor_scalar(out=self.kdk[:, h, :], in0=self.ktok[:, kh, :], scalar1=self.E0T[:, h, 63:64], scalar2=None, op0=ALU.mult)
                        ins = V.tensor_tensor(out=self.qgT[:, h, :], in0=self.qnT[:, kh, cs], in1=self.eR[:, h, :], op=ALU.mult)
                    return ins
                step("V", f)

                def f():
                    for h in range(4):
                        ins = PE.matmul(ps[0:64, 5, h * 64:(h + 1) * 64], lhsT=self.X[:, h, :], rhs=self.idf[0:64, 0:64], start=True, stop=True)
                    return ins
                step("P", f)
                step("A", lambda: A.activation(out=fl(self.Y[:, :, :]), in_=ps[0:64, 5, 0:256], func=AF.Copy))
                step("V", lambda: V.tensor_tensor(out=fl(self.Q[:, :, :]), in0=ps[0:64, 5, 0:256], in1=fl(self.I4_sb[:, :, :]), op=ALU.add))
                for lv in range(6):
                    def f(lv=lv):
                        ins = None
                        for h in range(4):
                            if lv <= 4:
                                ins = PE.matmul(ps[0:64, 5, h * 64:(h + 1) * 64], lhsT=self.Y[:, h, :], rhs=self.X[:, h, :], start=True, stop=True)
                            if lv <= 3:
                                ins = PE.matmul(ps[0:64, 6, h * 64:(h + 1) * 64], lhsT=self.X[:, h, :], rhs=self.Y[:, h, :], start=True, stop=True)
                            if lv >= 1:
                                ins = PE.matmul(ps[0:64, 7, h * 64:(h + 1) * 64], lhsT=self.X[:, h, :], rhs=self.Q[:, h, :], start=True, stop=True)
                        return ins
                    step("P", f)
                    if lv <= 3:
                        step("A", lambda: A.activation(out=fl(self.Y[:, :, :]), in_=ps[0:64, 6, 0:256], func=AF.Copy))

                    def f(lv=lv):
                        ins = None
                        if lv >= 1:
                            ins = V.tensor_tensor(out=fl(self.Q[:, :, :]), in0=fl(self.Q[:, :, :]), in1=ps[0:64, 7, 0:256], op=ALU.add)
                        if lv <= 4:
                            ins = V.tensor_copy(out=fl(self.X[:, :, :]), in_=ps[0:64, 5, 0:256])
                        if lv == 5:
                            ins = V.tensor_copy(out=self.Tt[:, :, :], in_=self.Q[:, :, :])
                        return ins
                    step("V", f)

                def f():
                    for h in range(4):
                        ins = PE.matmul(ps[:, 3, h * 64:(h + 1) * 64], lhsT=self.kbg[:, h, :], rhs=self.Tt[:, h, :], start=True, stop=True)
                    return ins
                step("P", f)
                step("A", lambda: A.activation(out=fl(self.nwT[:, :, :]), in_=ps[:, 3, 0:256], func=AF.Copy, scale=-1.0))

                def f():
                    for h in range(4):
                        PE.matmul(ps[0:64, 4, h * 128:(h + 1) * 128], lhsT=self.Tt[:, h, :], rhs=self.vb[:, h, :], start=(h == 0), stop=False, skip_group_check=True)
                    for h in range(4):
                        ins = PE.matmul(ps[0:64, 4, h * 128:(h + 1) * 128], lhsT=self.nwT[:, h, :], rhs=self.Stb[:, h, :], start=False, stop=(h == 3), skip_group_check=True)
                    return ins
                step("P", f)
                step("V", lambda: V.tensor_copy(out=fl(self.vnew[:, :, :]), in_=ps[0:64, 4, :]))

                def f():
                    for h in range(4):
                        PE.matmul(ps[:, 5, h * 64:(h + 1) * 64], lhsT=self.Stb[:, h, :], rhs=self.qgT[:, h, :], start=(h == 0), stop=False, skip_group_check=True)
                    for h in range(4):
                        PE.matmul(ps[:, 5, h * 64:(h + 1) * 64], lhsT=self.vnew[:, h, :], rhs=self.intraT[:, h, :], start=False, stop=(h == 3), skip_group_check=True)
                    for h in range(4):
                        ins = PE.matmul(ps[:, 6, h * 128:(h + 1) * 128], lhsT=self.kdk[:, h, :], rhs=self.vnew[:, h, :], start=True, stop=True)
                    return ins
                extra = []
                if ch == 0:
                    extra = [(s_o[sl].h, s_o[sl].n)]
                step("P", f)
                step("A", lambda: A.activation(out=self.ob[:, sl, :, cs], in_=ps[:, 5, 0:256].rearrange("p (h i) -> p h i", h=4), func=AF.Copy), extra=extra)

                def f():
                    for h in range(4):
                        V.scalar_tensor_tensor(out=self.St[:, h, :], in0=self.St[:, h, :], scalar=self.egl[:, h:h + 1], in1=ps[:, 6, h * 128:(h + 1) * 128],
                                               op0=ALU.mult, op1=ALU.add)
                    return V.tensor_copy(out=self.Stb[:, :, :], in_=self.St[:, :, :])
                step("V", f)
                if debug and b == 0 and ch == 0:
                    sp_wait_all()
                    sd = kb.sem("dbg")
                    items = {"d_cvs": self.cvs[:, :, 0:64], "d_g": self.g[:, :], "d_beta": self.beta[:, :], "d_gcc": self.gcc[:, :], "d_egl": self.egl[:, :],
                             "d_E0T": self.E0T[:, :, :], "d_E1": self.E1[:, :, :], "d_Q": self.Q[:, :, :], "d_ktok": self.ktok[:, :, :], "d_vtok": self.vtok[:, :, :],
                             "d_xa": self.xa[:, :], "d_eR": self.eR[:, :, :], "d_X": self.X[:, :, :], "d_St": self.St[:, :, :], "d_ob": self.ob[:, 0, :, 0:64],
                             "d_Y": self.Y[:, :, :], "d_praw": self.praw[:, :, 0:67], "d_rstd": self.rstd[:, :]}
                    for nm, ap in items.items():
                        d = kb.dout(nm, list(ap.shape))
                        sd.inc(SP.dma_start(out=d, in_=ap), 16)
                    SP.wait_ge(sd.h, sd.n)
                    return nc
            sp_wait_all()
            s_o[sl].inc(SP.dma_start(out=self.og.rearrange("(h p) t -> p h t", p=128)[:, :, t0:t0 + BT], in_=self.ob[:, sl, :, :]), 16)
        for s in s_o:
            SP.wait_ge(s.h, s.n)
        return nc


def mixc_consts():
    U = (np.arange(64)[:, None] <= np.arange(64)[None, :]).astype(np.float32)
    p = np.arange(64)[:, None, None]; f = np.arange(64)[None, None, :]
    MU = np.broadcast_to((f >= p), (64, 4, 64)).astype(np.float32)
    ML = np.broadcast_to((p > f), (64, 4, 64)).astype(np.float32)
    I4 = np.broadcast_to((p == f), (64, 4, 64)).astype(np.float32)
    return U, np.ascontiguousarray(MU), np.ascontiguousarray(ML), np.ascontiguousarray(I4)


def mixc_inputs(inp, c, xT, consts):
    U, MU, ML, I4 = consts
    W = inp["gdn_w_in"][0]
    kh0 = 2 * c; vh0 = 4 * c
    qc = slice(kh0 * 128, kh0 * 128 + 256)
    kc = slice(2048 + kh0 * 128, 2048 + kh0 * 128 + 256)
    vc = slice(4096 + vh0 * 128, 4096 + vh0 * 128 + 512)
    bc = slice(12288 + vh0, 12288 + vh0 + 4)
    ac = slice(12320 + vh0, 12320 + vh0 + 4)
    wC = np.concatenate([W[:, qc], W[:, kc], W[:, vc], W[:, bc], W[:, ac]], axis=1)
    cw = inp["gdn_conv"][0]
    cwc = np.concatenate([cw[:, qc], cw[:, kc], cw[:, vc]], axis=1)
    convw = np.ascontiguousarray(cwc.reshape(4, 8, 128).transpose(2, 1, 0))
    hp = np.concatenate([np.broadcast_to(inp["gdn_a_log"][0][vh0:vh0 + 4], (128, 4)), np.broadcast_to(inp["gdn_dt_bias"][0][vh0:vh0 + 4], (128, 4))], axis=1)
    return {"xT": xT, "wC": np.ascontiguousarray(wC), "gain": np.ascontiguousarray(inp["norm_mix"][1].reshape(16, 128).T),
            "convw": convw.astype(np.float32), "hp": np.ascontiguousarray(hp.astype(np.float32)), "U": U, "MU": MU, "ML": ML, "I4": I4,
            "ident": np.eye(128, dtype=np.float32)}


def _run(nc, ins):
    res = run_bass_kernel_spmd(nc, ins, core_ids=list(range(8)))
    return res.results


def kernel(**inputs):
    inp = {k: np.asarray(v) for k, v in inputs.items()}
    S_ = 16384
    x = inp["x"][0]
    xT = np.ascontiguousarray(x.T)
    consts = mixa_consts()
    nc = MixA().build()
    ra = _run(nc, [mixa_inputs(inp, c, xT, consts) for c in range(8)])
    oT0 = np.empty((3072, S_), np.float32)
    for c in range(8):
        hr, vh = c // 2, c % 2
        oT0[hr * 512 + vh * 256: hr * 512 + (vh + 1) * 256] = ra[c]["oret"]
        oT0[2048 + c * 128: 2048 + (c + 1) * 128] = ra[c]["odil"]
    del ra
    nc = Post(0).build()
    rb = _run(nc, [post_inputs(0, inp, np.ascontiguousarray(xT[:, c * 2048:(c + 1) * 2048]), np.ascontiguousarray(oT0[:, c * 2048:(c + 1) * 2048]))
                   for c in range(8)])
    x1T = np.ascontiguousarray(np.concatenate([rb[c]["xo"] for c in range(8)], axis=1))
    del rb, oT0
    cc = mixc_consts()
    nc = MixC().build()
    rc = _run(nc, [mixc_inputs(inp, c, x1T, cc) for c in range(8)])
    oT1 = np.ascontiguousarray(np.concatenate([rc[c]["og"] for c in range(8)], axis=0))
    del rc
    nc = Post(1).build()
    rd = _run(nc, [post_inputs(1, inp, np.ascontiguousarray(x1T[:, c * 2048:(c + 1) * 2048]), np.ascontiguousarray(oT1[:, c * 2048:(c + 1) * 2048]))
                   for c in range(8)])
    outT = np.concatenate([rd[c]["xo"] for c in range(8)], axis=1)
    return np.ascontiguousarray(outT.T)[None].astype(np.float32)
```

```python
import math
import numpy as np
from contextlib import ExitStack
from concourse.bass_utils import run_bass_kernel_spmd
import concourse.bass as bass
import concourse.mybir as mybir

F32 = mybir.dt.float32
BF16 = mybir.dt.bfloat16
AF = mybir.ActivationFunctionType
ALU = mybir.AluOpType
EPS = 1e-6


class Sem:
    def __init__(self, h):
        self.h = h
        self.n = 0

    def inc(self, ins, by=1):
        ins.then_inc(self.h, by)
        self.n += by
        return self.n


class KB:
    def __init__(self):
        self.nc = bass.Bass("TRN2", target_bir_lowering=False)
        self.es = ExitStack()
        self.sems = {}
        self.uid = 0

    def sb(self, name, shape, dt, es=None):
        return (es or self.es).enter_context(self.nc.sbuf_tensor(name, shape, dt))

    def psum(self, name, shape, dt, es=None):
        return (es or self.es).enter_context(self.nc.psum_tensor(name, shape, dt))

    def sem(self, name):
        if name not in self.sems:
            self.sems[name] = Sem(self.es.enter_context(self.nc.semaphore(name)))
        return self.sems[name]

    def din(self, name, shape, dt=F32):
        return self.nc.dram_tensor(name, list(shape), dt, kind="ExternalInput").ap()

    def dout(self, name, shape, dt=F32):
        return self.nc.dram_tensor(name, list(shape), dt, kind="ExternalOutput").ap()

    def dscr(self, name, shape, dt=F32):
        return self.nc.dram_tensor(name, list(shape), dt, kind="Internal").ap()


D = 2048
T = 2048
TT = 1024
NT = TT // 512
FF = 5632
WB = 8192


class Post:
    def __init__(self, layer):
        self.layer = layer
        self.FY = 3072 if layer == 0 else 4096
        self.G = 2048 if layer == 0 else 4096
        self.kb = kb = KB()
        nc = self.nc = kb.nc
        FY, G = self.FY, self.G
        self.xT = kb.din("xT", [D, T])
        self.oT = kb.din("oT", [FY, T])
        self.w_gate = kb.din("w_gate", [D, G])
        self.w_out = kb.din("w_out", [FY, D])
        self.gains = kb.din("gains", [128, 4, 16])
        self.hg = kb.din("hg", [128, 3])
        self.memT = kb.din("memT", [D, 256])
        self.w_q = kb.din("w_q", [D, 512])
        self.w_kv = kb.din("w_kv", [D, 1024])
        self.w_o = kb.din("w_o", [512, D])
        self.w1 = kb.din("w1", [D, FF])
        self.w3 = kb.din("w3", [D, FF])
        self.w2 = kb.din("w2", [FF, D])
        self.xo = kb.dout("xo", [D, T])
        self.x1 = kb.dout("x1s", [D, T])
        self.x2 = kb.dout("x2s", [D, T])
        self.wbuf = [kb.sb("wbuf0", [128, WB], BF16), kb.sb("wbuf1", [128, WB], BF16)]
        self.ones = kb.sb("ones", [128, 128], BF16)
        self.gains_sb = kb.sb("gains_sb", [128, 4, 16], F32)
        self.hg_sb = kb.sb("hg_sb", [128, 3], F32)
        self.eps_sb = kb.sb("eps_sb", [128, 1], F32)
        self.eps2_sb = kb.sb("eps2_sb", [128, 1], F32)
        self.hT = kb.sb("hT", [128, 16, TT], BF16)
        self.big = kb.sb("big", [128, 32 * TT], BF16)
        self.xst_f = kb.sb("xst", [128, 4096], F32)
        self.sq_f = kb.sb("sq", [128, 4096], BF16)
        self.xst_n = self.xst_f[:, :].rearrange("p (c t) -> p c t", t=256)
        self.sq_n = self.sq_f[:, :].rearrange("p (c t) -> p c t", t=256)
        self.xst = self.xst_f[:, 0:2048].rearrange("p (c t) -> p c t", t=512)
        self.sq = self.sq_f[:, 0:2048].rearrange("p (c t) -> p c t", t=512)
        self.qf = self.xst
        self.rstd = kb.sb("rstd", [128, 4, 512], F32)
        self.rbuf = kb.sb("rbuf", [128, 3, 512], F32)
        self.obuf = kb.sb("obuf", [128, 3, 512], F32)
        self.stmp = kb.sb("stmp", [128, 2, 512], F32)
        self.knT = kb.sb("knT", [128, 4, 256], BF16)
        self.vm = kb.sb("vm", [128, 2, 512], BF16)
        self.qn = self.big[:, 0:4 * TT].rearrange("p (c t) -> p c t", t=TT)
        self.pT = kb.sb("pT", [128, 2, 512], BF16)
        self.oxa = self.big[:, 4 * TT:8 * TT].rearrange("p (c t) -> p c t", t=TT)
        self.ps = kb.psum("ps", [128, 8, 512], F32)
        self.gidx = 0
        self.grp_end = []
        self.mm = kb.sem("g_mm")
        self.pf = kb.sem("g_pf")
        self.wl = [kb.sem("g_wl0"), kb.sem("g_wl1")]
        self.sts = [kb.sem("st%d" % i) for i in range(3)]
        self.bar_n = 0

    def wait_stores(self):
        for st in self.sts:
            self.nc.sync.wait_ge(st.h, st.n)

    def barrier(self):
        self.nc.all_engine_barrier()

    def y(self):
        return self.big[:, 0:(self.FY // 128) * TT].rearrange("p (c t) -> p c t", t=TT)

    def g(self):
        return self.big[:, 0:22 * TT].rearrange("p (c t) -> p c t", t=TT)

    def rstd_op(self, ps_ap, out_ap, inv_n, wait, post=1.0):
        nc = self.nc
        s = self.kb.sem("r_a")
        nc.scalar.wait_ge(wait[0], wait[1])
        s.inc(nc.scalar.activation(out=out_ap, in_=ps_ap, func=AF.Sqrt, scale=inv_n / post ** 2, bias=self.eps_sb[:, 0:1] if post == 1.0 else self.eps2_sb[:, 0:1]))
        nc.vector.wait_ge(s.h, s.n)
        return nc.vector.reciprocal(out=out_ap, in_=out_ap)

    def gemm(self, wsrc, KC, GW, ngroups, act, ntt, epi, pair=False, tw=512, pe_waits=()):
        nc = self.nc
        mpg = GW // 128
        if pair:
            mpg //= 2
        G0 = len(self.grp_end)

        def load(g):
            Gg = G0 + g
            b = Gg % 2
            if Gg >= 2:
                nc.gpsimd.wait_ge(self.mm.h, self.grp_end[Gg - 2])
            wv = self.wbuf[b][:, 0:KC * GW].rearrange("p (c n) -> p c n", n=GW)
            for (ap, off, w) in wsrc(g):
                src = ap.rearrange("(c p) n -> p c n", p=128)
                kstep = 8
                for k0 in range(0, KC, kstep):
                    k1 = min(KC, k0 + kstep)
                    self.wl[b].inc(nc.gpsimd.dma_start(out=wv[:, k0:k1, off:off + w], in_=src[:, k0:k1, :]), 16)
            return self.wl[b].n

        wl_need = {}
        wl_need[0] = load(0)
        cnt = 0
        for (sh, sv) in pe_waits:
            nc.tensor.wait_ge(sh, sv)
        for g in range(ngroups):
            if g + 1 < ngroups:
                wl_need[g + 1] = load(g + 1)
            b = (G0 + g) % 2
            nc.tensor.wait_ge(self.wl[b].h, wl_need[g])
            wv = self.wbuf[b][:, 0:KC * GW].rearrange("p (c n) -> p c n", n=GW)
            for j in range(mpg):
                for tt in range(ntt):
                    cols = [j] if not pair else [j, j + mpg]
                    ps_list = []
                    for cj in cols:
                        idx = self.gidx
                        bank = idx % 4
                        if idx >= 4:
                            nc.tensor.wait_ge(self.pf.h, idx - 3)
                        for k in range(KC):
                            ins = nc.tensor.matmul(self.ps[:, bank, 0:tw], lhsT=wv[:, k, cj * 128:(cj + 1) * 128],
                                                   rhs=act[:, k, tt * tw:(tt + 1) * tw], start=(k == 0), stop=(k == KC - 1))
                        self.mm.inc(ins)
                        self.gidx += 1
                        ps_list.append(self.ps[:, bank, 0:tw])
                    fin = epi(cnt, g * mpg + j, tt, ps_list, self.gidx)
                    self.pf.inc(fin, len(cols))
                    cnt += 1
            self.grp_end.append(self.mm.n)

    def norm(self, src, tok0, which, dst, ntok_tiles, tw=256, gains=None):
        nc = self.nc
        s_ld = self.kb.sem("n_ld"); s_sq = self.kb.sem("n_sq"); s_mm = self.kb.sem("n_mm"); s_dv = self.kb.sem("n_dv")
        for tt in range(ntok_tiles):
            t0 = tok0 + tt * tw
            nc.sync.wait_ge(s_dv.h, s_dv.n)
            srcv = src.rearrange("(c p) t -> p c t", p=128)
            for hh in range(2):
                s_ld.inc(nc.sync.dma_start(out=self.xst_n[:, hh * 8:(hh + 1) * 8, 0:tw], in_=srcv[:, hh * 8:(hh + 1) * 8, t0:t0 + tw]), 16)
            nc.scalar.wait_ge(s_ld.h, s_ld.n)
            nc.scalar.wait_ge(s_mm.h, s_mm.n)
            s_sq.inc(nc.scalar.activation(out=self.sq_n[:, :, 0:tw], in_=self.xst_n[:, :, 0:tw], func=AF.Square))
            nc.tensor.wait_ge(s_sq.h, s_sq.n)
            nc.tensor.wait_ge(s_dv.h, s_dv.n)
            for k in range(16):
                ins = nc.tensor.matmul(self.ps[:, 4, 0:tw], lhsT=self.ones[:, :], rhs=self.sq_n[:, k, 0:tw], start=(k == 0), stop=(k == 15))
            s_mm.inc(ins)
            self.rstd_op(self.ps[:, 4, 0:tw], self.rstd[:, 0, 0:tw], 1.0 / D, (s_mm.h, s_mm.n))
            for k in range(16):
                ins = nc.vector.scalar_tensor_tensor(out=dst[:, k, tt * tw:(tt + 1) * tw], in0=self.xst_n[:, k, 0:tw],
                                                     scalar=self.gains_sb[:, which, k:k + 1], in1=self.rstd[:, 0, 0:tw],
                                                     op0=ALU.mult, op1=ALU.mult)
            s_dv.inc(ins)
        return s_dv

    def onorm(self, tok0):
        nc = self.nc
        layer = self.layer
        y = self.y()
        s_ld = self.kb.sem("o_ld"); s_sq = self.kb.sem("o_sq"); s_mm = self.kb.sem("o_mm"); s_dv = self.kb.sem("o_dv")
        nblk = self.FY // 512
        nnorm = 4 if layer == 0 else 8
        ov = self.oT.rearrange("(c p) t -> p c t", p=128)
        for tt in range(NT):
            t0 = tok0 + tt * 512
            for blk in range(nblk):
                nc.sync.wait_ge(s_dv.h, s_dv.n)
                s_ld.inc(nc.sync.dma_start(out=self.xst[:, 0:4, :], in_=ov[:, blk * 4:(blk + 1) * 4, t0:t0 + 512]), 16)
                if blk >= nnorm:
                    nc.vector.wait_ge(s_ld.h, s_ld.n)
                    ins = nc.vector.tensor_copy(out=y[:, blk * 4:(blk + 1) * 4, tt * 512:(tt + 1) * 512], in_=self.xst[:, 0:4, :])
                    s_dv.inc(ins)
                    continue
                nc.scalar.wait_ge(s_ld.h, s_ld.n)
                nc.scalar.wait_ge(s_mm.h, s_mm.n)
                s_sq.inc(nc.scalar.activation(out=self.sq[:, 0:4, :], in_=self.xst[:, 0:4, :], func=AF.Square))
                nc.tensor.wait_ge(s_sq.h, s_sq.n)
                nc.tensor.wait_ge(s_dv.h, s_dv.n)
                if layer == 0:
                    for k in range(4):
                        ins = nc.tensor.matmul(self.ps[:, 4, :], lhsT=self.ones[:, :], rhs=self.sq[:, k, :], start=(k == 0), stop=(k == 3))
                else:
                    for k in range(4):
                        ins = nc.tensor.matmul(self.ps[:, 4 + k, :], lhsT=self.ones[:, :], rhs=self.sq[:, k, :], start=True, stop=True)
                s_mm.inc(ins)
                if layer == 0:
                    self.rstd_op(self.ps[:, 4, :], self.rstd[:, 0, :], 1.0 / 512, (s_mm.h, s_mm.n))
                    for k in range(4):
                        ins = nc.vector.tensor_tensor(out=y[:, blk * 4 + k, tt * 512:(tt + 1) * 512], in0=self.xst[:, k, :], in1=self.rstd[:, 0, :], op=ALU.mult)
                else:
                    for k in range(4):
                        self.rstd_op(self.ps[:, 4 + k, :], self.rstd[:, k, :], 1.0 / 128, (s_mm.h, s_mm.n))
                        ins = nc.vector.scalar_tensor_tensor(out=y[:, blk * 4 + k, tt * 512:(tt + 1) * 512], in0=self.xst[:, k, :],
                                                             scalar=self.hg_sb[:, 2:3], in1=self.rstd[:, k, :], op0=ALU.mult, op1=ALU.mult)
                s_dv.inc(ins)

    def epi_gate(self):
        nc = self.nc
        y = self.y()
        s_d = self.kb.sem("eg_d")
        base_d = s_d.n

        def epi(cnt, mt, tt, ps_list, idx_after):
            s = cnt % 2
            nc.scalar.wait_ge(self.mm.h, idx_after)
            if cnt >= 2:
                nc.scalar.wait_ge(s_d.h, base_d + cnt - 1)
            fin = nc.scalar.activation(out=self.stmp[:, s, :], in_=ps_list[0], func=AF.Silu)
            nc.vector.wait_ge(self.pf.h, idx_after)
            yv = y[:, mt, tt * 512:(tt + 1) * 512]
            s_d.inc(nc.vector.tensor_tensor(out=yv, in0=self.stmp[:, s, :], in1=yv, op=ALU.mult))
            return fin
        return epi

    def epi_swiglu(self):
        nc = self.nc
        g = self.g()
        s_a = self.kb.sem("es_a")
        hist = []

        def epi(cnt, mt, tt, ps_list, idx_after):
            s = cnt % 2
            nc.scalar.wait_ge(self.mm.h, idx_after)
            if cnt >= 2:
                nc.scalar.wait_ge(self.pf.h, hist[cnt - 2])
            s_a.inc(nc.scalar.activation(out=self.stmp[:, s, :], in_=ps_list[0], func=AF.Silu))
            nc.vector.wait_ge(s_a.h, s_a.n)
            fin = nc.vector.tensor_tensor(out=g[:, mt, tt * 512:(tt + 1) * 512], in0=self.stmp[:, s, :], in1=ps_list[1], op=ALU.mult)
            hist.append(idx_after)
            return fin
        return epi

    def epi_resid(self, res_src, dst, tok0, tiles):
        nc = self.nc
        rv = res_src.rearrange("(c p) t -> p c t", p=128)
        dv = dst.rearrange("(c p) t -> p c t", p=128)
        idx0 = self.gidx
        rls = [self.kb.sem("rl%d" % i) for i in range(3)]
        sts = self.sts

        def issue_load(c):
            mt, tt = tiles[c]
            if c >= 3:
                nc.sync.wait_ge(self.pf.h, idx0 + c - 2)
            rls[c % 3].inc(nc.sync.dma_start(out=self.rbuf[:, c % 3, :], in_=rv[:, mt, tok0 + tt * 512: tok0 + (tt + 1) * 512]), 16)

        def epi(cnt, mt, tt, ps_list, idx_after):
            if cnt == 0:
                issue_load(0)
                if len(tiles) > 1:
                    issue_load(1)
            if cnt + 2 < len(tiles):
                issue_load(cnt + 2)
            s = cnt % 3
            nc.vector.wait_ge(self.mm.h, idx_after)
            nc.vector.wait_ge(rls[s].h, rls[s].n if cnt + 3 >= len(tiles) or True else 0)
            nc.vector.wait_ge(sts[s].h, sts[s].n)
            fin = nc.vector.tensor_tensor(out=self.obuf[:, s, :], in0=ps_list[0], in1=self.rbuf[:, s, :], op=ALU.add)
            nc.sync.wait_ge(self.pf.h, idx_after)
            sts[s].inc(nc.sync.dma_start(out=dv[:, mt, tok0 + tt * 512: tok0 + (tt + 1) * 512], in_=self.obuf[:, s, :]), 16)
            return fin
        return epi

    def epi_plain(self, dstf):
        nc = self.nc

        def epi(cnt, mt, tt, ps_list, idx_after):
            nc.vector.wait_ge(self.mm.h, idx_after)
            return nc.vector.tensor_copy(out=dstf(mt, tt), in_=ps_list[0])
        return epi

    def mem_kv(self):
        nc = self.nc
        s_ld = self.kb.sem("m_ld"); s_a = self.kb.sem("m_a"); s_p = self.kb.sem("m_p"); s_d = self.kb.sem("m_d")
        memn = self.hT[:, :, 0:256]
        ndv = self.norm(self.memT, 0, 3, self.hT, 1, tw=256)
        self.barrier()
        self.gemm(lambda g: [(self.w_kv[:, 0:512], 0, 512)], 16, 512, 1, memn, 1,
                  self.epi_plain(lambda mt, tt: self.qf[:, mt, 0:256]), tw=256, pe_waits=[(ndv.h, ndv.n)])
        self.barrier()
        nc.scalar.wait_ge(self.pf.h, self.gidx)
        nc.scalar.activation(out=self.sq[:, 0:4, 0:256], in_=self.qf[:, 0:4, 0:256], func=AF.Square).then_inc(s_a.h, 1)
        nc.tensor.wait_ge(s_a.h, 1)
        for h in range(4):
            ins = nc.tensor.matmul(self.ps[:, 4 + h, 0:256], lhsT=self.ones[:, :], rhs=self.sq[:, h, 0:256], start=True, stop=True)
        ins.then_inc(s_p.h, 1)
        for h in range(4):
            self.rstd_op(self.ps[:, 4 + h, 0:256], self.rstd[:, h, 0:256], 1.0 / 128, (s_p.h, 1))
            nc.vector.scalar_tensor_tensor(out=self.knT[:, h, :], in0=self.qf[:, h, 0:256], scalar=self.hg_sb[:, 1:2], in1=self.rstd[:, h, 0:256],
                                           op0=ALU.mult, op1=ALU.mult)
        self.barrier()
        wv = self.wbuf[0][:, 0:16 * 512].rearrange("p (c n) -> p c n", n=512)
        src = self.w_kv[:, 512:1024].rearrange("(c p) n -> p c n", p=128)
        for k0 in (0, 8):
            nc.gpsimd.dma_start(out=wv[:, k0:k0 + 8, :], in_=src[:, k0:k0 + 8, :]).then_inc(s_ld.h, 16)
        nc.tensor.wait_ge(s_ld.h, 32)
        for c in range(2):
            for k in range(16):
                ins = nc.tensor.matmul(self.ps[:, 4 + c, :], lhsT=self.hT[:, k, c * 128:(c + 1) * 128], rhs=wv[:, k, :], start=(k == 0), stop=(k == 15))
        ins.then_inc(s_p.h, 1)
        nc.vector.wait_ge(s_p.h, 2)
        for c in range(2):
            ins = nc.vector.tensor_copy(out=self.vm[:, c, :], in_=self.ps[:, 4 + c, :])
        self.barrier()

    def xa_attn(self):
        nc = self.nc
        s_q = self.kb.sem("x_q"); s_a = self.kb.sem("x_a"); s_p = self.kb.sem("x_p"); s_d = self.kb.sem("x_d")
        scale = 128 ** -0.5
        for tt in range(NT):
            self.gemm(lambda g: [(self.w_q[:, :], 0, 512)], 16, 512, 1, self.hT[:, :, tt * 512:(tt + 1) * 512], 1,
                      self.epi_plain(lambda mt, t_: self.qf[:, mt, :]), pe_waits=[(self.kb.sem("n_dv").h, self.kb.sem("n_dv").n)])
            self.barrier()
            nc.scalar.wait_ge(self.pf.h, self.gidx)
            s_a.inc(nc.scalar.activation(out=self.sq[:, 0:4, :], in_=self.qf[:, 0:4, :], func=AF.Square))
            nc.tensor.wait_ge(s_a.h, s_a.n)
            for h in range(4):
                ins = nc.tensor.matmul(self.ps[:, 4 + h, :], lhsT=self.ones[:, :], rhs=self.sq[:, h, :], start=True, stop=True)
            s_p.inc(ins)
            for h in range(4):
                self.rstd_op(self.ps[:, 4 + h, :], self.rstd[:, h, :], 1.0 / 128, (s_p.h, s_p.n), post=scale)
                ins = nc.vector.scalar_tensor_tensor(out=self.qn[:, h, tt * 512:(tt + 1) * 512], in0=self.qf[:, h, :], scalar=self.hg_sb[:, 0:1],
                                                     in1=self.rstd[:, h, :], op0=ALU.mult, op1=ALU.mult)
            s_d.inc(ins)
            nc.tensor.wait_ge(s_d.h, s_d.n)
            nc.scalar.wait_ge(s_d.h, s_d.n)
            self.barrier()
            for h in range(4):
                for c in range(2):
                    ins = nc.tensor.matmul(self.ps[:, 4 + c, :], lhsT=self.knT[:, h, c * 128:(c + 1) * 128], rhs=self.qn[:, h, tt * 512:(tt + 1) * 512],
                                           start=True, stop=True)
                s_p.inc(ins)
                nc.scalar.wait_ge(s_p.h, s_p.n)
                for c in range(2):
                    ins = nc.scalar.activation(out=self.pT[:, c, :], in_=self.ps[:, 4 + c, :], func=AF.Exp)
                s_a.inc(ins)
                nc.tensor.wait_ge(s_a.h, s_a.n)
                for c in range(2):
                    nc.tensor.matmul(self.ps[:, 6, :], lhsT=self.vm[:, c, h * 128:(h + 1) * 128], rhs=self.pT[:, c, :], start=(c == 0), stop=(c == 1))
                for c in range(2):
                    ins = nc.tensor.matmul(self.ps[:, 7, :], lhsT=self.ones[:, :], rhs=self.pT[:, c, :], start=(c == 0), stop=(c == 1))
                s_p.inc(ins)
                nc.vector.wait_ge(s_p.h, s_p.n)
                nc.vector.reciprocal(out=self.rstd[:, 0, :], in_=self.ps[:, 7, :])
                ins = nc.vector.tensor_tensor(out=self.oxa[:, h, tt * 512:(tt + 1) * 512], in0=self.ps[:, 6, :], in1=self.rstd[:, 0, :], op=ALU.mult)
                s_d.inc(ins)
                nc.tensor.wait_ge(s_d.h, s_d.n)
                nc.scalar.wait_ge(s_d.h, s_d.n)
            self.barrier()

    def build(self, stages=99):
        nc = self.nc
        s0 = self.kb.sem("init")
        nc.vector.memset(self.ones[:, :], 1.0)
        nc.vector.memset(self.eps_sb[:, :], EPS)
        nc.vector.memset(self.eps2_sb[:, :], EPS * 128.0)
        nc.sync.dma_start(out=self.gains_sb[:, :, :], in_=self.gains).then_inc(s0.h, 16)
        nc.sync.dma_start(out=self.hg_sb[:, :], in_=self.hg).then_inc(s0.h, 16)
        nc.sync.wait_ge(s0.h, 32)
        self.barrier()
        self.mem_kv()
        for p in range(T // TT):
            tok0 = p * TT
            y = self.y()
            self.norm(self.xT, tok0, 0, self.hT, TT // 256)
            self.barrier()
            if stages < 1:
                continue
            self.onorm(tok0)
            self.barrier()
            self.gemm(lambda g: [(self.w_gate[:, g * 512:(g + 1) * 512], 0, 512)], 16, 512, self.G // 512, self.hT, NT, self.epi_gate(),
                      pe_waits=[(self.kb.sem("n_dv").h, self.kb.sem("n_dv").n), (self.kb.sem("o_dv").h, self.kb.sem("o_dv").n)])
            self.barrier()
            if stages < 2:
                continue
            KC = self.FY // 128
            tiles = [(m, tt) for m in range(16) for tt in range(NT)]
            dst = self.x1 if stages > 2 else self.xo
            self.gemm(lambda g: [(self.w_out[:, g * 256:(g + 1) * 256], 0, 256)], KC, 256, 8, y, NT,
                      self.epi_resid(self.xT, dst, tok0, tiles), pe_waits=[(self.kb.sem("eg_d").h, self.kb.sem("eg_d").n)])
            self.wait_stores()
            self.barrier()
            if stages < 3:
                continue
            self.norm(self.x1, tok0, 1, self.hT, TT // 256)
            self.barrier()
            self.xa_attn()
            dst = self.x2 if stages > 3 else self.xo
            self.gemm(lambda g: [(self.w_o[:, :], 0, 2048)], 4, 2048, 1, self.oxa, NT,
                      self.epi_resid(self.x1, dst, tok0, tiles), pe_waits=[(self.kb.sem("x_d").h, self.kb.sem("x_d").n)])
            self.wait_stores()
            self.barrier()
            if stages < 4:
                continue
            self.norm(self.x2, tok0, 2, self.hT, TT // 256)
            self.barrier()
            for half in range(2):
                c0 = half * (FF // 2)
                self.gemm(lambda g: [(self.w1[:, c0 + g * 256:c0 + (g + 1) * 256], 0, 256), (self.w3[:, c0 + g * 256:c0 + (g + 1) * 256], 256, 256)],
                          16, 512, FF // 512, self.hT, NT, self.epi_swiglu(), pair=True,
                          pe_waits=[(self.kb.sem("n_dv").h, self.kb.sem("n_dv").n)])
                self.barrier()
                w2h = self.w2[c0:c0 + FF // 2, :]
                self.gemm(lambda g: [(w2h[:, g * 256:(g + 1) * 256], 0, 256)], 22, 256, 8, self.g(), NT,
                          self.epi_resid(self.x2 if half == 0 else self.xo, self.xo, tok0, tiles), pe_waits=[(self.pf.h, self.gidx)])
                self.wait_stores()
                self.barrier()
        return nc


def post_inputs(layer, inp, xT_c, oT_c):
    def gl(v):
        return np.ascontiguousarray(v.reshape(16, 128).T)
    gains = np.stack([gl(inp["norm_mix"][layer]), gl(inp["norm_xa"][layer]), gl(inp["norm_ffn"][layer]), gl(inp["mem_norm"])], axis=1)
    gd = inp["gdn_norm"][0]
    hg = np.stack([inp["xa_q_gain"][layer], inp["xa_k_gain"][layer], gd], axis=1)
    if layer == 0:
        w_gate = np.ascontiguousarray(inp["ar_w_in"][0][:, 4096:6144])
        w_out = inp["ar_w_out"][0]
    else:
        w_gate = np.ascontiguousarray(inp["gdn_w_in"][0][:, 8192:12288])
        w_out = inp["gdn_w_out"][0]
    return {
        "xT": xT_c, "oT": oT_c, "w_gate": w_gate, "w_out": np.ascontiguousarray(w_out),
        "gains": np.ascontiguousarray(gains.astype(np.float32)), "hg": np.ascontiguousarray(hg.astype(np.float32)),
        "memT": np.ascontiguousarray(inp["mem"][0].T),
        "w_q": np.ascontiguousarray(inp["xa_w_q"][layer]), "w_kv": np.ascontiguousarray(inp["xa_w_kv"][layer]),
        "w_o": np.ascontiguousarray(inp["xa_w_o"][layer]),
        "w1": np.ascontiguousarray(inp["ffn_w1"][layer]), "w3": np.ascontiguousarray(inp["ffn_w3"][layer]),
        "w2": np.ascontiguousarray(inp["ffn_w2"][layer]),
    }


D = 2048
S = 16384
BT = 512
NB_A = S // BT
NCOL_A = 1152
NRING = 20


class MixA:
    def __init__(self, nblocks=NB_A):
        self.nblocks = nblocks
        self.kb = kb = KB()
        nc = self.nc = kb.nc
        self.xT = kb.din("xT", [D, S])
        self.wA = kb.din("wA", [D, NCOL_A])
        self.gain = kb.din("gain", [128, 16])
        self.hg = kb.din("hg", [128, 2])
        self.cosT = kb.din("cosT", [128, S])
        self.sinT = kb.din("sinT", [128, S])
        self.dmask = kb.din("dmask", [128, 128])
        self.qdrow = kb.din("qdrow", [128, BT])
        self.kdec = kb.din("kdec", [128, 2])
        self.gtab = kb.din("gtab", [128, 17, 128])
        self.mtab = kb.din("mtab", [128, 17, 128])
        self.ident = kb.din("ident", [128, 128])
        self.oret = kb.dout("oret", [256, S])
        self.odil = kb.dout("odil", [128, S])
        sb = kb.sb
        self.w = sb("w", [128, 16, NCOL_A], BF16)
        self.ones = sb("ones", [128, 128], BF16)
        self.idb = sb("idb", [128, 128], BF16)
        self.idf = sb("idf", [128, 128], F32)
        self.gain_sb = sb("gain_sb", [128, 16], F32)
        self.hg_sb = sb("hg_sb", [128, 2], F32)
        self.eps_sb = sb("eps_sb", [128, 1], F32)
        self.eps2_sb = sb("eps2_sb", [128, 1], F32)
        self.dm = sb("dm", [128, 128], F32)
        self.qd = sb("qd", [128, BT], F32)
        self.kd = sb("kd", [128, 2], F32)
        self.E = sb("E", [128, 17, 128], F32)
        self.mt_sb = sb("mt_sb", [128, 17, 128], F32)
        self.xst = sb("xst", [128, 16, BT], F32)
        self.sq = sb("sq", [128, 16, BT], BF16)
        self.hT = sb("hT", [128, 16, BT], BF16)
        self.rstd = sb("rstd", [128, 2, BT], F32)
        self.cs = sb("cs", [128, 2, 2, BT], F32)
        self.tmp = sb("tmp", [128, 2, BT], F32)
        self.QT = sb("QT", [128, 2, BT], BF16)
        self.QdT = sb("QdT", [128, 2, BT], BF16)
        self.KT = sb("KT", [128, 2, BT], BF16)
        self.Kd = sb("Kd", [128, 4, 256], BF16)
        self.VA = sb("VA", [128, 4, 256], BF16)
        self.Sm = sb("Sm", [128, 128], BF16)
        self.St = sb("St", [128, 2, 256], F32)
        self.Stb = sb("Stb", [128, 2, 256], BF16)
        self.qnT = sb("qnT", [128, BT], BF16)
        self.knR = sb("knR", [128, NRING, 128], BF16)
        self.vbR = sb("vbR", [128, NRING, 128], BF16)
        self.ex = sb("ex", [128, BT], F32)
        self.pT = sb("pT", [128, 2, BT], BF16)
        self.rl_ = sb("rl_", [128, 128], F32)
        self.oretb = sb("oretb", [128, 2, 2, BT], F32)
        self.odilb = sb("odilb", [128, 2, BT], F32)
        self.ps = kb.psum("ps", [128, 8, 512], F32)

    def rstd_op(self, ps_ap, out_ap, inv_n, wait, post=1.0):
        nc = self.nc
        s = self.kb.sem("r_a")
        nc.scalar.wait_ge(wait[0], wait[1])
        s.inc(nc.scalar.activation(out=out_ap, in_=ps_ap, func=AF.Sqrt, scale=inv_n / post ** 2,
                                   bias=self.eps_sb[:, 0:1] if post == 1.0 else self.eps2_sb[:, 0:1]))
        nc.vector.wait_ge(s.h, s.n)
        return nc.vector.reciprocal(out=out_ap, in_=out_ap)

    def build(self):
        nc = self.nc
        kb = self.kb
        sem = kb.sem
        ps = self.ps
        V, A, PE, SP, PL = nc.vector, nc.scalar, nc.tensor, nc.sync, nc.gpsimd

        def W(eng, s):
            eng.wait_ge(s.h, s.n)

        s0 = sem("init")
        for k0 in range(0, 16, 4):
            s0.inc(PL.dma_start(out=self.w[:, k0:k0 + 4, :], in_=self.wA.rearrange("(c p) n -> p c n", p=128)[:, k0:k0 + 4, :]), 16)
        s0.inc(PL.dma_start(out=self.idb[:, :], in_=self.ident), 16)
        s1 = sem("init1")
        for (dst, src) in [(self.gain_sb[:, :], self.gain), (self.hg_sb[:, :], self.hg), (self.dm[:, :], self.dmask), (self.qd[:, :], self.qdrow),
                           (self.kd[:, :], self.kdec), (self.E[:, :, :], self.gtab), (self.mt_sb[:, :, :], self.mtab), (self.idf[:, :], self.ident)]:
            s1.inc(SP.dma_start(out=dst, in_=src), 16)
        V.memset(self.ones[:, :], 1.0)
        V.memset(self.eps_sb[:, :], EPS)
        V.memset(self.eps2_sb[:, :], EPS * 128.0)
        V.memset(self.St[:, :, :], 0.0)
        V.memset(self.Stb[:, :, :], 0.0)
        W(A, s1)
        sE = sem("sE")
        sE.inc(A.activation(out=self.E[:, :, :], in_=self.E[:, :, :], func=AF.Exp))
        W(V, sE)
        W(V, s1)
        sE2 = sem("sE2")
        sE2.inc(V.tensor_tensor(out=self.E[:, :, :], in0=self.E[:, :, :], in1=self.mt_sb[:, :, :], op=ALU.mult))
        W(PE, s0)
        W(PE, sE2)
        W(A, sE2)

        xv = self.xT.rearrange("(c p) t -> p c t", p=128)
        s_xl = sem("xl"); s_sq = sem("a_sq"); s_ss = sem("p_ss"); s_h = sem("d_h")
        s_cl = [sem("cl0"), sem("cl1")]
        s_pj = sem("p_pj")
        s_pf = sem("pjf")
        s_rot = sem("d_rot")
        s_sq2 = sem("a_sq2"); s_ss2 = sem("p_ss2")
        s_tr = sem("p_tr"); s_kd = sem("d_kd")
        s_sc = sem("p_sc"); s_sm = sem("d_sm"); s_o = sem("p_o"); s_oe = sem("a_oe"); s_ds = sem("p_ds"); s_st = sem("d_st")
        s_qk = sem("p_qk"); s_ex = sem("a_ex"); s_p = sem("d_p"); s_pv = sem("p_pv"); s_do = sem("d_do")
        s_or = [sem("or0"), sem("or1")]; s_od = [sem("od0"), sem("od1")]
        pj_idx = [0]
        rot_hist = []

        def proj_tile(cols, width, lhs_tok=None):
            i = pj_idx[0]
            bank = i % 2
            if i >= 2:
                PE.wait_ge(s_pf.h, i - 1)
            for k in range(16):
                if lhs_tok is None:
                    ins = PE.matmul(ps[:, bank, 0:BT], lhsT=self.w[:, k, cols:cols + 128], rhs=self.hT[:, k, :], start=(k == 0), stop=(k == 15))
                else:
                    ins = PE.matmul(ps[:, bank, 0:width], lhsT=self.hT[:, k, lhs_tok * 128:(lhs_tok + 1) * 128], rhs=self.w[:, k, cols:cols + width],
                                    start=(k == 0), stop=(k == 15))
            s_pj.inc(ins)
            pj_idx[0] += 1
            return bank

        for b in range(self.nblocks):
            t0 = b * BT
            sl = b % 2
            W(SP, s_h)
            for hh in range(2):
                s_xl.inc(SP.dma_start(out=self.xst[:, hh * 8:(hh + 1) * 8, :], in_=xv[:, hh * 8:(hh + 1) * 8, t0:t0 + BT]), 16)
            if b >= 2:
                SP.wait_ge(s_rot.h, rot_hist[b - 2])
            s_cl[sl].inc(SP.dma_start(out=self.cs[:, sl, 0, :], in_=self.cosT[:, t0:t0 + BT]), 16)
            s_cl[sl].inc(SP.dma_start(out=self.cs[:, sl, 1, :], in_=self.sinT[:, t0:t0 + BT]), 16)
            W(A, s_xl)
            W(A, s_ss)
            W(A, s_ss2)
            s_sq.inc(A.activation(out=self.sq[:, :, :], in_=self.xst[:, :, :], func=AF.Square))
            W(PE, s_sq)
            for k in range(16):
                ins = PE.matmul(ps[:, 2, :], lhsT=self.ones[:, :], rhs=self.sq[:, k, :], start=(k == 0), stop=(k == 15))
            s_ss.inc(ins)
            self.rstd_op(ps[:, 2, :], self.rstd[:, 0, :], 1.0 / D, (s_ss.h, s_ss.n))
            W(V, s_pj)
            for k in range(16):
                ins = V.scalar_tensor_tensor(out=self.hT[:, k, :], in0=self.xst[:, k, :], scalar=self.gain_sb[:, k:k + 1], in1=self.rstd[:, 0, :],
                                             op0=ALU.mult, op1=ALU.mult)
            s_h.inc(ins)
            W(PE, s_h)
            W(V, s_cl[sl])
            for which, col0, dstT in ((0, 0, self.QT), (1, 256, self.KT)):
                b0 = proj_tile(col0, 128)
                b1 = proj_tile(col0 + 128, 128)
                W(V, s_pj)
                if which == 0:
                    W(V, s_o)
                    W(V, s_sc)
                else:
                    W(V, s_tr)
                    W(V, s_sc)
                cosv = self.cs[:, sl, 0, :]; sinv = self.cs[:, sl, 1, :]
                V.tensor_tensor(out=self.tmp[:, 0, :], in0=ps[:, b0, :], in1=cosv, op=ALU.mult)
                V.tensor_tensor(out=self.tmp[:, 1, :], in0=ps[:, b1, :], in1=sinv, op=ALU.mult)
                V.tensor_tensor(out=dstT[:, 0, :], in0=self.tmp[:, 0, :], in1=self.tmp[:, 1, :], op=ALU.subtract)
                V.tensor_tensor(out=self.tmp[:, 0, :], in0=ps[:, b0, :], in1=sinv, op=ALU.mult)
                ins = V.tensor_tensor(out=self.tmp[:, 1, :], in0=ps[:, b1, :], in1=cosv, op=ALU.mult)
                s_pf.inc(ins, 2)
                ins = V.tensor_tensor(out=dstT[:, 1, :], in0=self.tmp[:, 0, :], in1=self.tmp[:, 1, :], op=ALU.add)
                if which == 0:
                    for i in range(2):
                        ins = V.tensor_tensor(out=self.QdT[:, i, :], in0=self.QT[:, i, :], in1=self.qd[:, :], op=ALU.mult)
                s_rot.inc(ins)
            rot_hist.append(s_rot.n)
            for which, col0 in ((0, 512), (1, 640)):
                bk = proj_tile(col0, 128)
                W(A, s_pj)
                W(A, s_ss2)
                s_sq2.inc(A.activation(out=self.sq[:, 0, :], in_=ps[:, bk, :], func=AF.Square))
                W(PE, s_sq2)
                ins = PE.matmul(ps[:, 2, :], lhsT=self.ones[:, :], rhs=self.sq[:, 0, :], start=True, stop=True)
                s_ss2.inc(ins)
                if which == 0:
                    self.rstd_op(ps[:, 2, :], self.rstd[:, 1, :], 1.0 / 128, (s_ss2.h, s_ss2.n), post=128 ** -0.5)
                    W(V, s_pv)
                    W(V, s_qk)
                    ins = V.scalar_tensor_tensor(out=self.qnT[:, :], in0=ps[:, bk, :], scalar=self.hg_sb[:, 0:1], in1=self.rstd[:, 1, :],
                                                 op0=ALU.mult, op1=ALU.mult)
                else:
                    self.rstd_op(ps[:, 2, :], self.rstd[:, 1, :], 1.0 / 128, (s_ss2.h, s_ss2.n))
                    W(V, s_qk)
                    for j in range(4):
                        slot = (4 * b + j) % NRING
                        ins = V.scalar_tensor_tensor(out=self.knR[:, slot, :], in0=ps[:, bk, j * 128:(j + 1) * 128], scalar=self.hg_sb[:, 1:2],
                                                     in1=self.rstd[:, 1, j * 128:(j + 1) * 128], op0=ALU.mult, op1=ALU.mult)
                s_pf.inc(ins, 1)
            for c in range(4):
                bk = proj_tile(768, 384, lhs_tok=c)
                W(V, s_pj)
                if c == 0:
                    W(V, s_ds)
                    W(V, s_o)
                    W(V, s_pv)
                V.tensor_copy(out=self.VA[:, c, :], in_=ps[:, bk, 0:256])
                ins = V.tensor_copy(out=self.vbR[:, (4 * b + c) % NRING, :], in_=ps[:, bk, 256:384])
                s_pf.inc(ins, 1)
            W(PE, s_rot)
            for c in range(4):
                W(PE, s_kd)
                for i in range(2):
                    ins = PE.matmul(ps[:, 3, i * 128:(i + 1) * 128], lhsT=self.KT[:, i, c * 128:(c + 1) * 128], rhs=self.idb[:, :], start=True, stop=True)
                s_tr.inc(ins)
                W(V, s_tr)
                if c == 0:
                    W(V, s_ds)
                for i in range(2):
                    ins = V.tensor_scalar(out=self.Kd[:, c, i * 128:(i + 1) * 128], in0=ps[:, 3, i * 128:(i + 1) * 128], scalar1=self.kd[:, 0:1], scalar2=None, op0=ALU.mult)
                s_kd.inc(ins)
            if b % 2 == 0 or True:
                V.wait_ge(s_or[sl].h, s_or[sl].n)
                A.wait_ge(s_or[sl].h, s_or[sl].n)
            for c in range(4):
                cs_ = slice(c * 128, (c + 1) * 128)
                W(PE, s_sm)
                W(PE, s_kd)
                for i in range(2):
                    ins = PE.matmul(ps[:, 3, 256:384], lhsT=self.KT[:, i, cs_], rhs=self.QT[:, i, cs_], start=(i == 0), stop=(i == 1))
                s_sc.inc(ins)
                W(V, s_sc)
                W(V, s_o)
                s_sm.inc(V.tensor_tensor(out=self.Sm[:, :], in0=ps[:, 3, 256:384], in1=self.dm[:, :], op=ALU.mult))
                W(PE, s_sm)
                W(PE, s_pf)
                W(PE, s_st)
                W(PE, s_oe)
                for j in range(2):
                    PE.matmul(ps[:, 4, j * 128:(j + 1) * 128], lhsT=self.VA[:, c, j * 128:(j + 1) * 128], rhs=self.Sm[:, :], start=True, stop=False)
                    for i in range(2):
                        ins = PE.matmul(ps[:, 4, j * 128:(j + 1) * 128], lhsT=self.Stb[:, i, j * 128:(j + 1) * 128], rhs=self.QdT[:, i, cs_],
                                        start=False, stop=(i == 1))
                s_o.inc(ins)
                W(A, s_o)
                for j in range(2):
                    ins = A.activation(out=self.oretb[:, sl, j, cs_], in_=ps[:, 4, j * 128:(j + 1) * 128], func=AF.Copy)
                s_oe.inc(ins)
                W(PE, s_kd)
                for i in range(2):
                    ins = PE.matmul(ps[:, 5, i * 256:(i + 1) * 256], lhsT=self.Kd[:, c, i * 128:(i + 1) * 128], rhs=self.VA[:, c, :], start=True, stop=True)
                s_ds.inc(ins)
                W(V, s_ds)
                W(V, s_o)
                for i in range(2):
                    V.scalar_tensor_tensor(out=self.St[:, i, :], in0=self.St[:, i, :], scalar=self.kd[:, 1:2], in1=ps[:, 5, i * 256:(i + 1) * 256],
                                           op0=ALU.mult, op1=ALU.add)
                ins = V.tensor_copy(out=self.Stb[:, :, :], in_=self.St[:, :, :])
                s_st.inc(ins)
            W(SP, s_oe)
            s_or[sl].inc(SP.dma_start(out=self.oret.rearrange("(j p) t -> p j t", p=128)[:, :, t0:t0 + BT], in_=self.oretb[:, sl, :, :]), 16)
            W(PE, s_pf)
            V.wait_ge(s_od[sl].h, s_od[sl].n)
            for qt in range(4):
                tq = 4 * b + qt
                nk = min(17, tq + 1)
                batches = [(o0, min(4, nk - o0)) for o0 in range(0, nk, 4)]
                W(PE, s_do)
                for bi, (o0, n) in enumerate(batches):
                    W(PE, s_ex)
                    for j in range(n):
                        slot = (tq - (o0 + j)) % NRING
                        ins = PE.matmul(ps[:, 6, j * 128:(j + 1) * 128], lhsT=self.knR[:, slot, :], rhs=self.qnT[:, qt * 128:(qt + 1) * 128], start=True, stop=True)
                    s_qk.inc(ins)
                    W(A, s_qk)
                    W(A, s_p)
                    s_ex.inc(A.activation(out=self.ex[:, 0:n * 128], in_=ps[:, 6, 0:n * 128], func=AF.Exp))
                    W(V, s_ex)
                    if bi >= 2:
                        V.wait_ge(s_pv.h, pv_hist[-2])
                    pslot = bi % 2
                    s_p.inc(V.tensor_tensor(out=self.pT[:, pslot, 0:n * 128], in0=self.ex[:, 0:n * 128],
                                            in1=self.E[:, o0:o0 + n, :].rearrange("p o q -> p (o q)"), op=ALU.mult))
                    W(PE, s_p)
                    for j in range(n):
                        slot = (tq - (o0 + j)) % NRING
                        first = (o0 + j == 0)
                        last = (o0 + j == nk - 1)
                        PE.matmul(ps[:, 7, 0:128], lhsT=self.vbR[:, slot, :], rhs=self.pT[:, pslot, j * 128:(j + 1) * 128], start=first, stop=last, skip_group_check=True)
                        ins = PE.matmul(ps[:, 7, 128:256], lhsT=self.ones[:, :], rhs=self.pT[:, pslot, j * 128:(j + 1) * 128], start=False, stop=last, skip_group_check=True)
                    s_pv.inc(ins)
                    if bi == 0:
                        pv_hist = []
                    pv_hist.append(s_pv.n)
                W(V, s_pv)
                V.reciprocal(out=self.rl_[:, :], in_=ps[:, 7, 128:256])
                s_do.inc(V.tensor_tensor(out=self.odilb[:, sl, qt * 128:(qt + 1) * 128], in0=ps[:, 7, 0:128], in1=self.rl_[:, :], op=ALU.mult))
            W(SP, s_do)
            s_od[sl].inc(SP.dma_start(out=self.odil[:, t0:t0 + BT], in_=self.odilb[:, sl, :]), 16)
        for s in s_or + s_od:
            W(SP, s)
        return nc


def t5_bucket_np(dist):
    exact = 16
    d = np.maximum(dist, exact).astype(np.float32)
    large = exact + (np.log(d / np.float32(exact)) / np.float32(math.log(2048 / exact)) * np.float32(32 - exact)).astype(np.int32)
    large = np.minimum(large, 31)
    return np.where(dist < exact, dist, large)


def mixa_consts():
    i = np.arange(128, dtype=np.float32)
    inv = (np.float32(10000.0) ** (-(np.arange(0, 256, 2, dtype=np.float32)) / np.float32(256))).astype(np.float32)
    pos = np.arange(S, dtype=np.float32)
    ang = (inv[:, None] * pos[None, :]).astype(np.float32)
    cosT = np.cos(ang).astype(np.float32)
    sinT = np.sin(ang).astype(np.float32)
    kj = np.arange(128)[:, None, None]
    o = np.arange(17)[None, :, None]
    qi = np.arange(128)[None, None, :]
    delta = qi - kj + 128 * o
    valid = delta >= 0
    m = ((delta <= 128) & valid).astype(np.float32) + ((delta % 4 == 0) & (delta <= 512) & valid) + ((delta % 16 == 0) & (delta <= 2048) & valid)
    bidx = t5_bucket_np(np.maximum(delta, 0))
    return cosT, sinT, m.astype(np.float32), bidx


def mixa_inputs(inp, c, xT, consts):
    cosT, sinT, mtab, bidx = consts
    hr, vh, hd = c // 2, c % 2, c
    W = inp["ar_w_in"][0]
    wA = np.concatenate([W[:, hr * 256:(hr + 1) * 256], W[:, 1024 + hr * 256:1024 + (hr + 1) * 256],
                         W[:, 6144 + hd * 128:6144 + (hd + 1) * 128], W[:, 7168 + hd * 128:7168 + (hd + 1) * 128],
                         W[:, 2048 + hr * 512 + vh * 256:2048 + hr * 512 + (vh + 1) * 256], W[:, 8192 + hd * 128:8192 + (hd + 1) * 128]], axis=1)
    gamma = 1.0 - 2.0 ** (-5.0 - hr)
    kj = np.arange(128)[:, None]; qi = np.arange(128)[None, :]
    dmask = np.where(qi >= kj, gamma ** np.maximum(qi - kj, 0), 0.0) * 256 ** -0.5
    qdrow = np.tile(gamma ** (np.arange(128) + 1.0), 4)[None, :].repeat(128, axis=0)
    kdec = np.stack([gamma ** (127.0 - np.arange(128)) * 256 ** -0.5, np.full(128, gamma ** 128.0)], axis=1)
    gtab = inp["rel_bias"][:, hd][bidx]
    return {
        "xT": xT, "wA": np.ascontiguousarray(wA), "gain": np.ascontiguousarray(inp["norm_mix"][0].reshape(16, 128).T),
        "hg": np.ascontiguousarray(np.stack([inp["dil_q_gain"][0], inp["dil_k_gain"][0]], axis=1)),
        "cosT": cosT, "sinT": sinT, "dmask": dmask.astype(np.float32), "qdrow": qdrow.astype(np.float32), "kdec": kdec.astype(np.float32),
        "gtab": np.ascontiguousarray(gtab.astype(np.float32)), "mtab": mtab, "ident": np.eye(128, dtype=np.float32),
    }


D = 2048
S = 16384
BT = 512
NB_C = S // BT
NCOL_C = 1032
C = 64
G = 4


def fl(ap):
    return ap.rearrange("p a b -> p (a b)")


def fl4(t):
    return t[:, :, :, :].rearrange("p a b c -> p (a b c)")


class MixC:
    def __init__(self, nblocks=NB_C):
        self.nblocks = nblocks
        self.kb = kb = KB()
        self.nc = kb.nc
        self.xT = kb.din("xT", [D, S])
        self.wC = kb.din("wC", [D, NCOL_C])
        self.gain = kb.din("gain", [128, 16])
        self.convw = kb.din("convw", [128, 8, 4])
        self.hp = kb.din("hp", [128, 2, G, 4])
        self.U = kb.din("U", [64, 64])
        self.MU = kb.din("MU", [64, 4 * G, 64])
        self.ML = kb.din("ML", [64, 4 * G, 64])
        self.I4 = kb.din("I4", [64, 4 * G, 64])
        self.ident = kb.din("ident", [128, 128])
        self.og = kb.dout("og", [512, S])
        sb = kb.sb
        self.w = sb("w", [128, 16, NCOL_C], BF16)
        self.ones = sb("ones", [128, 128], BF16)
        self.onesf = sb("onesf", [64, 128], F32)
        self.idb = sb("idb", [128, 128], BF16)
        self.idf = sb("idf", [128, 128], F32)
        self.gain_sb = sb("gain_sb", [128, 16], F32)
        self.cw = sb("cw", [128, 8, 4], F32)
        self.hp_sb = sb("hp_sb", [128, 2, G, 4], F32)
        self.negA = sb("negA", [128, G, 4], F32)
        self.eps_sb = sb("eps_sb", [128, 1], F32)
        self.eps2_sb = sb("eps2_sb", [128, 1], F32)
        self.one_sb = sb("one_sb", [128, 1], F32)
        self.U_sb = sb("U_sb", [64, 64], F32)
        self.MU_sb = sb("MU_sb", [64, 4 * G, 64], F32)
        self.ML_sb = sb("ML_sb", [64, 4 * G, 64], F32)
        self.I4_sb = sb("I4_sb", [64, 4 * G, 64], F32)
        self.xst = sb("xst", [128, 16, BT], F32)
        self.sq = sb("sq", [128, 8, BT], BF16)
        self.hT = sb("hT", [128, 16, BT], BF16)
        self.rstd = sb("rstd", [128, BT], F32)
        self.praw = sb("praw", [128, 3 + BT], F32)
        self.halo = sb("halo", [128, 8, 3], F32)
        self.acc = sb("acc", [128, BT], F32)
        self.cvs = sb("cvs", [128, 4, BT], F32)
        self.qnT = sb("qnT", [128, 2, BT], BF16)
        self.knT = sb("knT", [128, 2, BT], BF16)
        self.vT = sb("vT", [128, 4, BT], BF16)
        self.ktok = sb("ktok", [64, 2, 512], F32)
        self.vtok = sb("vtok", [64, G, 512], F32)
        self.xa = sb("xa", [64, G, 4], F32)
        self.beta = sb("beta", [64, G, 4], F32)
        self.nbeta = sb("nbeta", [64, G, 4], F32)
        self.g = sb("g", [64, G, 4], F32)
        self.gcc = sb("gcc", [64, G, 4], F32)
        self.egc = sb("egc", [64, G, 4], F32)
        self.bg = sb("bg", [64, G, 4], F32)
        self.egl = sb("egl", [128, G, 4], F32)
        self.zz = sb("zz", [64, 2 * G * 4 * 64], F32)
        self.G1 = self.zz[:, :].rearrange("p (g h d) -> p g h d", g=G, h=4)
        self.Zmin = self.zz[:, 0:G * 256].rearrange("p (g h d) -> p g h d", g=G, h=4)
        self.Zmax = self.zz[:, G * 256:2 * G * 256].rearrange("p (g h d) -> p g h d", g=G, h=4)
        self.E0T = sb("E0T", [64, G, 4, 64], F32)
        self.E1 = sb("E1", [64, G, 4, 64], F32)
        self.eR = sb("eR", [128, 2, 512], F32)
        self.X = sb("X", [64, G, 4, 64], F32)
        self.Y = sb("Y", [64, G, 4, 64], F32)
        self.Q = sb("Q", [64, G, 4, 64], F32)
        self.Tt = sb("Tt", [64, G, 4, 64], BF16)
        self.intraT = sb("intraT", [64, G, 4, 64], BF16)
        self.vb = sb("vb", [64, G, 4, 128], BF16)
        self.kbg = sb("kbg", [64, G, 4, 128], BF16)
        self.kdk = sb("kdk", [64, G, 4, 128], BF16)
        self.qgT = sb("qgT", [128, G, 4, 64], BF16)
        self.nwT = sb("nwT", [128, G, 4, 64], BF16)
        self.vnew = sb("vnew", [64, 4, 128], BF16)
        self.St = sb("St", [128, 4, 128], F32)
        self.Stb = sb("Stb", [128, 4, 128], BF16)
        self.ob = sb("ob", [128, 1, 4, BT], F32)
        self.ps = kb.psum("ps", [128, 8, 512], F32)
        self.dtb = self.hp_sb[:, 1, :, :]

    def build(self, debug=False):
        nc = self.nc
        kb = self.kb
        ps = self.ps
        V, A, PE, SP, PL = nc.vector, nc.scalar, nc.tensor, nc.sync, nc.gpsimd
        sems = {"V": kb.sem("sV"), "A": kb.sem("sA"), "P": kb.sem("sP")}
        engs = {"V": V, "A": A, "P": PE}

        def step(e, fn, extra=()):
            eng = engs[e]
            for o in sems:
                if o != e and sems[o].n > 0:
                    eng.wait_ge(sems[o].h, sems[o].n)
            for (sh, sv) in extra:
                eng.wait_ge(sh, sv)
            ins = fn()
            sems[e].inc(ins)

        def sp_wait_all():
            for o in sems:
                if sems[o].n > 0:
                    SP.wait_ge(sems[o].h, sems[o].n)

        s0 = kb.sem("init"); s1 = kb.sem("init1")
        wv = self.wC.rearrange("(c p) n -> p c n", p=128)
        for k0 in range(0, 16, 4):
            s0.inc(PL.dma_start(out=self.w[:, k0:k0 + 4, :], in_=wv[:, k0:k0 + 4, :]), 16)
        s0.inc(PL.dma_start(out=self.idb[:, :], in_=self.ident), 16)
        for (dst, src) in [(self.gain_sb[:, :], self.gain), (self.cw[:, :, :], self.convw), (self.hp_sb[:, :, :, :], self.hp), (self.U_sb[:, :], self.U),
                           (self.MU_sb[:, :, :], self.MU), (self.ML_sb[:, :, :], self.ML), (self.I4_sb[:, :, :], self.I4), (self.idf[:, :], self.ident)]:
            s1.inc(SP.dma_start(out=dst, in_=src), 16)

        def init_v():
            V.memset(self.ones[:, :], 1.0)
            V.memset(self.onesf[:, :], 1.0)
            V.memset(self.eps_sb[:, :], EPS)
            V.memset(self.one_sb[:, :], 1.0)
            V.memset(self.eps2_sb[:, :], EPS * 128.0)
            V.memset(self.St[:, :, :], 0.0)
            V.memset(self.Stb[:, :, :], 0.0)
            return V.memset(self.halo[:, :, :], 0.0)
        step("V", init_v)
        step("A", lambda: A.activation(out=self.negA[:, :, :], in_=self.hp_sb[:, 0, :, :], func=AF.Exp), extra=[(s1.h, s1.n)])
        step("V", lambda: V.tensor_scalar(out=fl(self.negA[:, :, :]), in0=fl(self.negA[:, :, :]), scalar1=-1.0, scalar2=None, op0=ALU.mult), extra=[(s0.h, s0.n), (s1.h, s1.n)])
        PE.wait_ge(s0.h, s0.n)
        PE.wait_ge(s1.h, s1.n)

        xv = self.xT.rearrange("(c p) t -> p c t", p=128)
        s_xl = kb.sem("xl")
        s_o = [kb.sem("so0"), kb.sem("so1")]

        def load_x(b):
            t0 = b * BT
            for hh in range(2):
                s_xl.inc(SP.dma_start(out=self.xst[:, hh * 8:(hh + 1) * 8, :], in_=xv[:, hh * 8:(hh + 1) * 8, t0:t0 + BT]), 16)

        load_x(0)
        for b in range(self.nblocks):
            t0 = b * BT
            sl = 0
            for hf in range(2):
                step("A", lambda hf=hf: A.activation(out=self.sq[:, :, :], in_=self.xst[:, hf * 8:(hf + 1) * 8, :], func=AF.Square), extra=[(s_xl.h, s_xl.n)])

                def f(hf=hf):
                    for k in range(8):
                        ins = PE.matmul(ps[:, 2, :], lhsT=self.ones[:, :], rhs=self.sq[:, k, :], start=(hf == 0 and k == 0), stop=(hf == 1 and k == 7))
                    return ins
                step("P", f)
            step("A", lambda: A.activation(out=self.rstd[:, :], in_=ps[:, 2, :], func=AF.Sqrt, scale=1.0 / D, bias=self.eps_sb[:, 0:1]))

            def f():
                V.reciprocal(out=self.rstd[:, :], in_=self.rstd[:, :])
                for k in range(16):
                    ins = V.scalar_tensor_tensor(out=self.hT[:, k, :], in0=self.xst[:, k, :], scalar=self.gain_sb[:, k:k + 1], in1=self.rstd[:, :],
                                                 op0=ALU.mult, op1=ALU.mult)
                return ins
            step("V", f)
            if b + 1 < self.nblocks:
                sp_wait_all()
                load_x(b + 1)
            for mt in range(8):
                def f(mt=mt):
                    for k in range(16):
                        ins = PE.matmul(ps[:, mt % 2, :], lhsT=self.w[:, k, mt * 128:(mt + 1) * 128], rhs=self.hT[:, k, :], start=(k == 0), stop=(k == 15))
                    return ins
                step("P", f)
                step("A", lambda mt=mt: A.activation(out=self.praw[:, 3:3 + BT], in_=ps[:, mt % 2, :], func=AF.Copy))

                def f(mt=mt):
                    V.tensor_copy(out=self.praw[:, 0:3], in_=self.halo[:, mt, :])
                    V.tensor_scalar(out=self.acc[:, :], in0=self.praw[:, 0:BT], scalar1=self.cw[:, mt, 0:1], scalar2=None, op0=ALU.mult)
                    for j in range(1, 4):
                        ins = V.scalar_tensor_tensor(out=self.acc[:, :], in0=self.praw[:, j:j + BT], scalar=self.cw[:, mt, j:j + 1], in1=self.acc[:, :],
                                                     op0=ALU.mult, op1=ALU.add)
                    return ins
                step("V", f)
                step("A", lambda mt=mt: A.activation(out=(self.cvs[:, mt, :] if mt < 4 else self.vT[:, mt - 4, :]), in_=self.acc[:, :], func=AF.Silu))
                step("V", lambda mt=mt: V.tensor_copy(out=self.halo[:, mt, :], in_=self.praw[:, BT:BT + 3]))
            step("A", lambda: A.activation(out=self.sq[:, 0:4, :], in_=self.cvs[:, 0:4, :], func=AF.Square))
            for mt in range(4):
                step("P", lambda mt=mt: PE.matmul(ps[:, 2, :], lhsT=self.ones[:, :], rhs=self.sq[:, mt, :], start=True, stop=True))
                if mt < 2:
                    step("A", lambda: A.activation(out=self.rstd[:, :], in_=ps[:, 2, :], func=AF.Sqrt, scale=128.0, bias=self.eps2_sb[:, 0:1]))
                else:
                    step("A", lambda: A.activation(out=self.rstd[:, :], in_=ps[:, 2, :], func=AF.Sqrt, scale=1.0, bias=self.eps_sb[:, 0:1]))

                def f(mt=mt):
                    V.reciprocal(out=self.rstd[:, :], in_=self.rstd[:, :])
                    dst = self.qnT[:, mt, :] if mt < 2 else self.knT[:, mt - 2, :]
                    return V.tensor_tensor(out=dst, in0=self.cvs[:, mt, :], in1=self.rstd[:, :], op=ALU.mult)
                step("V", f)
            for grp in range(BT // (C * G)):
                def csl(gi):
                    c0 = (grp * G + gi) * C
                    return slice(c0, c0 + C)

                def bank_col(base_bank, gi, width):
                    return base_bank + gi // 2, (gi % 2) * width

                def f():
                    for gi in range(G):
                        for k in range(16):
                            ins = PE.matmul(ps[0:64, 2, gi * 8:(gi + 1) * 8], lhsT=self.hT[:, k, csl(gi)], rhs=self.w[:, k, 1024:1032],
                                            start=(k == 0), stop=(k == 15))
                    return ins
                step("P", f)
                bav = ps[0:64, 2, 0:G * 8].rearrange("p (g c) -> p g c", c=8)
                step("V", lambda: V.tensor_tensor(out=self.xa[:, :, :], in0=bav[:, :, 4:8], in1=self.dtb[0:64, :, :], op=ALU.add))

                def f():
                    A.activation(out=fl(self.xa[:, :, :]), in_=fl(self.xa[:, :, :]), func=AF.Exp)
                    return A.activation(out=fl(self.xa[:, :, :]), in_=fl(self.xa[:, :, :]), func=AF.Ln, bias=self.one_sb[0:64, 0:1])
                step("A", f)
                step("V", lambda: V.tensor_tensor(out=fl(self.g[:, :, :]), in0=fl(self.xa[:, :, :]), in1=fl(self.negA[0:64, :, :]), op=ALU.mult))
                step("A", lambda: A.activation(out=self.beta[:, :, :], in_=bav[:, :, 0:4], func=AF.Sigmoid))

                def f():
                    V.tensor_scalar(out=fl(self.nbeta[:, :, :]), in0=fl(self.beta[:, :, :]), scalar1=-1.0, scalar2=None, op0=ALU.mult)
                    for gi in range(G):
                        for h in range(4):
                            ins = V.tensor_scalar(out=self.G1[:, gi, h, :], in0=self.onesf[:, :], scalar1=self.g[:, gi, h:h + 1], scalar2=None, op0=ALU.mult)
                    return ins
                step("V", f)

                def f():
                    for gi in range(G):
                        PE.matmul(ps[0:64, 2, 64 + gi * 4:68 + gi * 4], lhsT=self.U_sb[:, :], rhs=self.g[:, gi, :], start=True, stop=True)
                        PE.matmul(ps[:, 2, 128 + gi * 4:132 + gi * 4], lhsT=self.onesf[:, :], rhs=self.g[:, gi, :], start=True, stop=True)
                    for gi in range(G):
                        bk, c0 = bank_col(3, gi, 256)
                        for h in range(4):
                            PE.matmul(ps[:, bk, c0 + h * 64:c0 + (h + 1) * 64], lhsT=self.G1[:, gi, h, :], rhs=self.U_sb[:, :], start=True, stop=True)
                    for gi in range(G):
                        bk, c0 = bank_col(5, gi, 256)
                        for kh in range(2):
                            PE.matmul(ps[0:64, bk, c0 + kh * 64:c0 + (kh + 1) * 64], lhsT=self.knT[:, kh, csl(gi)], rhs=self.knT[:, kh, csl(gi)], start=True, stop=True)
                        for kh in range(2):
                            ins = PE.matmul(ps[0:64, bk, c0 + 128 + kh * 64:c0 + 128 + (kh + 1) * 64], lhsT=self.knT[:, kh, csl(gi)], rhs=self.qnT[:, kh, csl(gi)],
                                            start=True, stop=True)
                    return ins
                step("P", f)

                def f():
                    A.activation(out=fl(self.gcc[:, :, :]), in_=ps[0:64, 2, 64:64 + G * 4], func=AF.Copy)
                    A.activation(out=fl(self.egl[:, :, :]), in_=ps[:, 2, 128:128 + G * 4], func=AF.Exp)
                    return A.activation(out=self.eR[:, :, :], in_=ps[:, 3:5, :], func=AF.Exp)
                step("A", f)

                def f():
                    for gi in range(G):
                        bk, c0 = bank_col(3, gi, 256)
                        for h in range(4):
                            V.tensor_scalar(out=self.Zmin[:, gi, h, :], in0=ps[0:64, bk, c0 + h * 64:c0 + (h + 1) * 64], scalar1=self.gcc[:, gi, h:h + 1], scalar2=0.0,
                                            op0=ALU.subtract, op1=ALU.min)
                            ins = V.tensor_scalar(out=self.Zmax[:, gi, h, :], in0=ps[0:64, bk, c0 + h * 64:c0 + (h + 1) * 64], scalar1=self.gcc[:, gi, h:h + 1], scalar2=0.0,
                                                  op0=ALU.subtract, op1=ALU.max)
                    return ins
                step("V", f)

                def f():
                    A.activation(out=fl(self.egc[:, :, :]), in_=fl(self.gcc[:, :, :]), func=AF.Exp)
                    A.activation(out=fl4(self.E0T), in_=fl4(self.Zmin), func=AF.Exp)
                    return A.activation(out=fl4(self.E1), in_=fl4(self.Zmax), func=AF.Exp, scale=-1.0)
                step("A", f)

                vbanks = [2, 3, 4, 7]

                def f():
                    for gi in range(G):
                        bk, c0 = bank_col(0, gi, 256)
                        for kh in range(2):
                            PE.matmul(ps[0:64, bk, c0 + kh * 128:c0 + (kh + 1) * 128], lhsT=self.knT[:, kh, csl(gi)], rhs=self.idb[:, :], start=True, stop=True)
                        for h in range(4):
                            ins = PE.matmul(ps[0:64, vbanks[gi], h * 128:(h + 1) * 128], lhsT=self.vT[:, h, csl(gi)], rhs=self.idb[:, :], start=True, stop=True)
                    return ins
                step("P", f)

                def f():
                    A.activation(out=self.ktok[:, :, :], in_=ps[0:64, 0:2, :], func=AF.Copy)
                    A.activation(out=self.vtok[:, 0:3, :], in_=ps[0:64, 2:5, :], func=AF.Copy)
                    return A.activation(out=self.vtok[:, 3, :], in_=ps[0:64, 7, :], func=AF.Copy)
                step("A", f)

                def f():
                    V.tensor_tensor(out=fl(self.bg[:, :, :]), in0=fl(self.beta[:, :, :]), in1=fl(self.egc[:, :, :]), op=ALU.mult)
                    V.tensor_tensor(out=fl4(self.E0T), in0=fl4(self.E0T), in1=fl(self.MU_sb[:, :, :]), op=ALU.mult)
                    V.tensor_tensor(out=fl4(self.E1), in0=fl4(self.E1), in1=fl(self.ML_sb[:, :, :]), op=ALU.mult)
                    for gi in range(G):
                        bk, c0 = bank_col(5, gi, 256)
                        ktv = self.ktok[:, gi // 2, :].rearrange("p (a k d) -> p a k d", a=2, k=2)[:, gi % 2, :, :]
                        vtv = self.vtok[:, gi, :].rearrange("p (h d) -> p h d", h=4)
                        for h in range(4):
                            kh = h // 2
                            V.scalar_tensor_tensor(out=self.X[:, gi, h, :], in0=ps[0:64, bk, c0 + kh * 64:c0 + (kh + 1) * 64], scalar=self.nbeta[:, gi, h:h + 1],
                                                   in1=self.E1[:, gi, h, :], op0=ALU.mult, op1=ALU.mult)
                            V.tensor_tensor(out=self.intraT[:, gi, h, :], in0=ps[0:64, bk, c0 + 128 + kh * 64:c0 + 128 + (kh + 1) * 64], in1=self.E0T[:, gi, h, :], op=ALU.mult)
                            V.tensor_scalar(out=self.vb[:, gi, h, :], in0=vtv[:, h, :], scalar1=self.beta[:, gi, h:h + 1], scalar2=None, op0=ALU.mult)
                            V.tensor_scalar(out=self.kbg[:, gi, h, :], in0=ktv[:, kh, :], scalar1=self.bg[:, gi, h:h + 1], scalar2=None, op0=ALU.mult)
                            V.tensor_scalar(out=self.kdk[:, gi, h, :], in0=ktv[:, kh, :], scalar1=self.E0T[:, gi, h, 63:64], scalar2=None, op0=ALU.mult)
                            ins = V.tensor_tensor(out=self.qgT[:, gi, h, :], in0=self.qnT[:, kh, csl(gi)], in1=self.eR[:, gi // 2, ((gi % 2) * 4 + h) * 64:((gi % 2) * 4 + h + 1) * 64],
                                                  op=ALU.mult)
                    return ins
                step("V", f)

                def f():
                    for gi in range(G):
                        bk, c0 = bank_col(2, gi, 256)
                        for h in range(4):
                            ins = PE.matmul(ps[0:64, bk, c0 + h * 64:c0 + (h + 1) * 64], lhsT=self.X[:, gi, h, :], rhs=self.idf[0:64, 0:64], start=True, stop=True)
                    return ins
                step("P", f)
                step("A", lambda: A.activation(out=fl4(self.Y).rearrange("p (b c) -> p b c", b=2), in_=ps[0:64, 2:4, :], func=AF.Copy))
                step("V", lambda: V.tensor_tensor(out=fl4(self.Q).rearrange("p (b c) -> p b c", b=2), in0=ps[0:64, 2:4, :],
                                                  in1=fl(self.I4_sb[:, :, :]).rearrange("p (b c) -> p b c", b=2), op=ALU.add))
                for lv in range(6):
                    def f(lv=lv):
                        ins = None
                        for gi in range(G):
                            for h in range(4):
                                if lv <= 4:
                                    bk, c0 = bank_col(0, gi, 256)
                                    ins = PE.matmul(ps[0:64, bk, c0 + h * 64:c0 + (h + 1) * 64], lhsT=self.Y[:, gi, h, :], rhs=self.X[:, gi, h, :], start=True, stop=True)
                                if lv <= 3:
                                    bk, c0 = bank_col(2, gi, 256)
                                    ins = PE.matmul(ps[0:64, bk, c0 + h * 64:c0 + (h + 1) * 64], lhsT=self.X[:, gi, h, :], rhs=self.Y[:, gi, h, :], start=True, stop=True)
                                if lv >= 1:
                                    bk, c0 = bank_col(4, gi, 256)
                                    ins = PE.matmul(ps[0:64, bk, c0 + h * 64:c0 + (h + 1) * 64], lhsT=self.X[:, gi, h, :], rhs=self.Q[:, gi, h, :], start=True, stop=True)
                        return ins
                    step("P", f)
                    if lv <= 3:
                        step("A", lambda: A.activation(out=fl4(self.Y).rearrange("p (b c) -> p b c", b=2), in_=ps[0:64, 2:4, :], func=AF.Copy))

                    def f(lv=lv):
                        ins = None
                        if lv >= 1:
                            ins = V.tensor_tensor(out=fl4(self.Q).rearrange("p (b c) -> p b c", b=2), in0=fl4(self.Q).rearrange("p (b c) -> p b c", b=2),
                                                  in1=ps[0:64, 4:6, :], op=ALU.add)
                        if lv <= 4:
                            ins = V.tensor_copy(out=fl4(self.X).rearrange("p (b c) -> p b c", b=2), in_=ps[0:64, 0:2, :])
                        if lv == 5:
                            ins = V.tensor_copy(out=fl4(self.Tt), in_=fl4(self.Q))
                        return ins
                    step("V", f)

                def f():
                    for gi in range(G):
                        bk, c0 = bank_col(6, gi, 256)
                        for h in range(4):
                            ins = PE.matmul(ps[:, bk, c0 + h * 64:c0 + (h + 1) * 64], lhsT=self.kbg[:, gi, h, :], rhs=self.Tt[:, gi, h, :], start=True, stop=True)
                    return ins
                step("P", f)
                step("A", lambda: A.activation(out=fl4(self.nwT).rearrange("p (b c) -> p b c", b=2), in_=ps[:, 6:8, :], func=AF.Copy, scale=-1.0))

                for gi in range(G):
                    def f(gi=gi):
                        for h in range(4):
                            PE.matmul(ps[0:64, 0, h * 128:(h + 1) * 128], lhsT=self.Tt[:, gi, h, :], rhs=self.vb[:, gi, h, :], start=(h == 0), stop=False, skip_group_check=True)
                        for h in range(4):
                            ins = PE.matmul(ps[0:64, 0, h * 128:(h + 1) * 128], lhsT=self.nwT[:, gi, h, :], rhs=self.Stb[:, h, :], start=False, stop=(h == 3), skip_group_check=True)
                        return ins
                    step("P", f)
                    step("V", lambda: V.tensor_copy(out=fl(self.vnew[:, :, :]), in_=ps[0:64, 0, :]))

                    def f(gi=gi):
                        for h in range(4):
                            PE.matmul(ps[:, 1, h * 64:(h + 1) * 64], lhsT=self.Stb[:, h, :], rhs=self.qgT[:, gi, h, :], start=(h == 0), stop=False, skip_group_check=True)
                        for h in range(4):
                            PE.matmul(ps[:, 1, h * 64:(h + 1) * 64], lhsT=self.vnew[:, h, :], rhs=self.intraT[:, gi, h, :], start=False, stop=(h == 3), skip_group_check=True)
                        for h in range(4):
                            ins = PE.matmul(ps[:, 2, h * 128:(h + 1) * 128], lhsT=self.kdk[:, gi, h, :], rhs=self.vnew[:, h, :], start=True, stop=True)
                        return ins
                    step("P", f)
                    extra = [(s_o[sl].h, s_o[sl].n)] if (grp == 0 and gi == 0) else []
                    step("A", lambda gi=gi: A.activation(out=self.ob[:, sl, :, csl(gi)], in_=ps[:, 1, 0:256].rearrange("p (h i) -> p h i", h=4), func=AF.Copy), extra=extra)

                    def f(gi=gi):
                        for h in range(4):
                            V.scalar_tensor_tensor(out=self.St[:, h, :], in0=self.St[:, h, :], scalar=self.egl[:, gi, h:h + 1], in1=ps[:, 2, h * 128:(h + 1) * 128],
                                                   op0=ALU.mult, op1=ALU.add)
                        return V.tensor_copy(out=self.Stb[:, :, :], in_=self.St[:, :, :])
                    step("V", f)
            sp_wait_all()
            s_o[sl].inc(SP.dma_start(out=self.og.rearrange("(h p) t -> p h t", p=128)[:, :, t0:t0 + BT], in_=self.ob[:, sl, :, :]), 16)
        for s in s_o:
            SP.wait_ge(s.h, s.n)
        return nc


def mixc_consts():
    U = (np.arange(64)[:, None] <= np.arange(64)[None, :]).astype(np.float32)
    p = np.arange(64)[:, None, None]; f = np.arange(64)[None, None, :]
    MU = np.broadcast_to((f >= p), (64, 4 * G, 64)).astype(np.float32)
    ML = np.broadcast_to((p > f), (64, 4 * G, 64)).astype(np.float32)
    I4 = np.broadcast_to((p == f), (64, 4 * G, 64)).astype(np.float32)
    return U, np.ascontiguousarray(MU), np.ascontiguousarray(ML), np.ascontiguousarray(I4)


def mixc_inputs(inp, c, xT, consts):
    U, MU, ML, I4 = consts
    W = inp["gdn_w_in"][0]
    kh0 = 2 * c; vh0 = 4 * c
    qc = slice(kh0 * 128, kh0 * 128 + 256)
    kc = slice(2048 + kh0 * 128, 2048 + kh0 * 128 + 256)
    vc = slice(4096 + vh0 * 128, 4096 + vh0 * 128 + 512)
    bc = slice(12288 + vh0, 12288 + vh0 + 4)
    ac = slice(12320 + vh0, 12320 + vh0 + 4)
    wC = np.concatenate([W[:, qc], W[:, kc], W[:, vc], W[:, bc], W[:, ac]], axis=1)
    cw = inp["gdn_conv"][0]
    cwc = np.concatenate([cw[:, qc], cw[:, kc], cw[:, vc]], axis=1)
    convw = np.ascontiguousarray(cwc.reshape(4, 8, 128).transpose(2, 1, 0))
    hp = np.stack([np.broadcast_to(inp["gdn_a_log"][0][vh0:vh0 + 4], (128, G, 4)), np.broadcast_to(inp["gdn_dt_bias"][0][vh0:vh0 + 4], (128, G, 4))], axis=1)
    return {"xT": xT, "wC": np.ascontiguousarray(wC), "gain": np.ascontiguousarray(inp["norm_mix"][1].reshape(16, 128).T),
            "convw": convw.astype(np.float32), "hp": np.ascontiguousarray(hp.astype(np.float32)), "U": U, "MU": MU, "ML": ML, "I4": I4,
            "ident": np.eye(128, dtype=np.float32)}


def _run(nc, ins):
    res = run_bass_kernel_spmd(nc, ins, core_ids=list(range(8)))
    return res.results


def kernel(**inputs):
    inp = {k: np.asarray(v) for k, v in inputs.items()}
    S_ = 16384
    x = inp["x"][0]
    xT = np.ascontiguousarray(x.T)
    consts = mixa_consts()
    nc = MixA().build()
    ra = _run(nc, [mixa_inputs(inp, c, xT, consts) for c in range(8)])
    oT0 = np.empty((3072, S_), np.float32)
    for c in range(8):
        hr, vh = c // 2, c % 2
        oT0[hr * 512 + vh * 256: hr * 512 + (vh + 1) * 256] = ra[c]["oret"]
        oT0[2048 + c * 128: 2048 + (c + 1) * 128] = ra[c]["odil"]
    del ra
    nc = Post(0).build()
    rb = _run(nc, [post_inputs(0, inp, np.ascontiguousarray(xT[:, c * 2048:(c + 1) * 2048]), np.ascontiguousarray(oT0[:, c * 2048:(c + 1) * 2048]))
                   for c in range(8)])
    x1T = np.ascontiguousarray(np.concatenate([rb[c]["xo"] for c in range(8)], axis=1))
    del rb, oT0
    cc = mixc_consts()
    nc = MixC().build()
    rc = _run(nc, [mixc_inputs(inp, c, x1T, cc) for c in range(8)])
    oT1 = np.ascontiguousarray(np.concatenate([rc[c]["og"] for c in range(8)], axis=0))
    del rc
    nc = Post(1).build()
    rd = _run(nc, [post_inputs(1, inp, np.ascontiguousarray(x1T[:, c * 2048:(c + 1) * 2048]), np.ascontiguousarray(oT1[:, c * 2048:(c + 1) * 2048]))
                   for c in range(8)])
    outT = np.concatenate([rd[c]["xo"] for c in range(8)], axis=1)
    return np.ascontiguousarray(outT.T)[None].astype(np.float32)
```

```python
import math
import numpy as np
from contextlib import ExitStack
from concourse.bass_utils import run_bass_kernel_spmd
import concourse.bass as bass
import concourse.mybir as mybir

F32 = mybir.dt.float32
BF16 = mybir.dt.bfloat16
AF = mybir.ActivationFunctionType
ALU = mybir.AluOpType
EPS = 1e-6


class Sem:
    def __init__(self, h):
        self.h = h
        self.n = 0

    def inc(self, ins, by=1):
        ins.then_inc(self.h, by)
        self.n += by
        return self.n


class KB:
    def __init__(self):
        self.nc = bass.Bass("TRN2", target_bir_lowering=False)
        self.es = ExitStack()
        self.sems = {}
        self.uid = 0

    def sb(self, name, shape, dt, es=None):
        return (es or self.es).enter_context(self.nc.sbuf_tensor(name, shape, dt))

    def psum(self, name, shape, dt, es=None):
        return (es or self.es).enter_context(self.nc.psum_tensor(name, shape, dt))

    def sem(self, name):
        if name not in self.sems:
            self.sems[name] = Sem(self.es.enter_context(self.nc.semaphore(name)))
        return self.sems[name]

    def din(self, name, shape, dt=F32):
        return self.nc.dram_tensor(name, list(shape), dt, kind="ExternalInput").ap()

    def dout(self, name, shape, dt=F32):
        return self.nc.dram_tensor(name, list(shape), dt, kind="ExternalOutput").ap()

    def dscr(self, name, shape, dt=F32):
        return self.nc.dram_tensor(name, list(shape), dt, kind="Internal").ap()


D = 2048
T = 2048
TT = 1024
NT = TT // 512
FF = 5632
WB = 8192


class Post:
    def __init__(self, layer):
        self.layer = layer
        self.FY = 3072 if layer == 0 else 4096
        self.G = 2048 if layer == 0 else 4096
        self.kb = kb = KB()
        nc = self.nc = kb.nc
        FY, G = self.FY, self.G
        self.xT = kb.din("xT", [D, T])
        self.oT = kb.din("oT", [FY, T])
        self.w_gate = kb.din("w_gate", [D, G])
        self.w_out = kb.din("w_out", [FY, D])
        self.gains = kb.din("gains", [128, 4, 16])
        self.hg = kb.din("hg", [128, 3])
        self.memT = kb.din("memT", [D, 256])
        self.w_q = kb.din("w_q", [D, 512])
        self.w_kv = kb.din("w_kv", [D, 1024])
        self.w_o = kb.din("w_o", [512, D])
        self.w1 = kb.din("w1", [D, FF])
        self.w3 = kb.din("w3", [D, FF])
        self.w2 = kb.din("w2", [FF, D])
        self.xo = kb.dout("xo", [D, T])
        self.x1 = kb.dout("x1s", [D, T])
        self.x2 = kb.dout("x2s", [D, T])
        self.wbuf = [kb.sb("wbuf0", [128, WB], BF16), kb.sb("wbuf1", [128, WB], BF16)]
        self.ones = kb.sb("ones", [128, 128], BF16)
        self.gains_sb = kb.sb("gains_sb", [128, 4, 16], F32)
        self.hg_sb = kb.sb("hg_sb", [128, 3], F32)
        self.eps_sb = kb.sb("eps_sb", [128, 1], F32)
        self.eps2_sb = kb.sb("eps2_sb", [128, 1], F32)
        self.hT = kb.sb("hT", [128, 16, TT], BF16)
        self.big = kb.sb("big", [128, 32 * TT], BF16)
        self.xst_f = kb.sb("xst", [128, 4096], F32)
        self.sq_f = kb.sb("sq", [128, 4096], BF16)
        self.xst_n = self.xst_f[:, :].rearrange("p (c t) -> p c t", t=256)
        self.sq_n = self.sq_f[:, :].rearrange("p (c t) -> p c t", t=256)
        self.xst = self.xst_f[:, 0:2048].rearrange("p (c t) -> p c t", t=512)
        self.sq = self.sq_f[:, 0:2048].rearrange("p (c t) -> p c t", t=512)
        self.qf = self.xst
        self.rstd = kb.sb("rstd", [128, 4, 512], F32)
        self.rbuf = kb.sb("rbuf", [128, 3, 512], F32)
        self.obuf = kb.sb("obuf", [128, 3, 512], F32)
        self.stmp = kb.sb("stmp", [128, 2, 512], F32)
        self.knT = kb.sb("knT", [128, 4, 256], BF16)
        self.vm = kb.sb("vm", [128, 2, 512], BF16)
        self.qn = self.big[:, 0:4 * TT].rearrange("p (c t) -> p c t", t=TT)
        self.pT = kb.sb("pT", [128, 2, 512], BF16)
        self.oxa = self.big[:, 4 * TT:8 * TT].rearrange("p (c t) -> p c t", t=TT)
        self.ps = kb.psum("ps", [128, 8, 512], F32)
        self.gidx = 0
        self.grp_end = []
        self.mm = kb.sem("g_mm")
        self.pf = kb.sem("g_pf")
        self.wl = [kb.sem("g_wl0"), kb.sem("g_wl1")]
        self.sts = [kb.sem("st%d" % i) for i in range(3)]
        self.bar_n = 0

    def wait_stores(self):
        for st in self.sts:
            self.nc.sync.wait_ge(st.h, st.n)

    def barrier(self):
        self.nc.all_engine_barrier()

    def y(self):
        return self.big[:, 0:(self.FY // 128) * TT].rearrange("p (c t) -> p c t", t=TT)

    def g(self):
        return self.big[:, 0:22 * TT].rearrange("p (c t) -> p c t", t=TT)

    def rstd_op(self, ps_ap, out_ap, inv_n, wait, post=1.0):
        nc = self.nc
        s = self.kb.sem("r_a")
        nc.scalar.wait_ge(wait[0], wait[1])
        s.inc(nc.scalar.activation(out=out_ap, in_=ps_ap, func=AF.Sqrt, scale=inv_n / post ** 2, bias=self.eps_sb[:, 0:1] if post == 1.0 else self.eps2_sb[:, 0:1]))
        nc.vector.wait_ge(s.h, s.n)
        return nc.vector.reciprocal(out=out_ap, in_=out_ap)

    def gemm(self, wsrc, KC, GW, ngroups, act, ntt, epi, pair=False, tw=512, pe_waits=()):
        nc = self.nc
        mpg = GW // 128
        if pair:
            mpg //= 2
        G0 = len(self.grp_end)

        def load(g):
            Gg = G0 + g
            b = Gg % 2
            if Gg >= 2:
                nc.gpsimd.wait_ge(self.mm.h, self.grp_end[Gg - 2])
            wv = self.wbuf[b][:, 0:KC * GW].rearrange("p (c n) -> p c n", n=GW)
            for (ap, off, w) in wsrc(g):
                src = ap.rearrange("(c p) n -> p c n", p=128)
                kstep = 8
                for k0 in range(0, KC, kstep):
                    k1 = min(KC, k0 + kstep)
                    self.wl[b].inc(nc.gpsimd.dma_start(out=wv[:, k0:k1, off:off + w], in_=src[:, k0:k1, :]), 16)
            return self.wl[b].n

        wl_need = {}
        wl_need[0] = load(0)
        cnt = 0
        for (sh, sv) in pe_waits:
            nc.tensor.wait_ge(sh, sv)
        for g in range(ngroups):
            if g + 1 < ngroups:
                wl_need[g + 1] = load(g + 1)
            b = (G0 + g) % 2
            nc.tensor.wait_ge(self.wl[b].h, wl_need[g])
            wv = self.wbuf[b][:, 0:KC * GW].rearrange("p (c n) -> p c n", n=GW)
            for j in range(mpg):
                for tt in range(ntt):
                    cols = [j] if not pair else [j, j + mpg]
                    ps_list = []
                    for cj in cols:
                        idx = self.gidx
                        bank = idx % 4
                        if idx >= 4:
                            nc.tensor.wait_ge(self.pf.h, idx - 3)
                        for k in range(KC):
                            ins = nc.tensor.matmul(self.ps[:, bank, 0:tw], lhsT=wv[:, k, cj * 128:(cj + 1) * 128],
                                                   rhs=act[:, k, tt * tw:(tt + 1) * tw], start=(k == 0), stop=(k == KC - 1))
                        self.mm.inc(ins)
                        self.gidx += 1
                        ps_list.append(self.ps[:, bank, 0:tw])
                    fin = epi(cnt, g * mpg + j, tt, ps_list, self.gidx)
                    self.pf.inc(fin, len(cols))
                    cnt += 1
            self.grp_end.append(self.mm.n)

    def norm(self, src, tok0, which, dst, ntok_tiles, tw=256, gains=None):
        nc = self.nc
        s_ld = self.kb.sem("n_ld"); s_sq = self.kb.sem("n_sq"); s_mm = self.kb.sem("n_mm"); s_dv = self.kb.sem("n_dv")
        for tt in range(ntok_tiles):
            t0 = tok0 + tt * tw
            nc.sync.wait_ge(s_dv.h, s_dv.n)
            srcv = src.rearrange("(c p) t -> p c t", p=128)
            for hh in range(2):
                s_ld.inc(nc.sync.dma_start(out=self.xst_n[:, hh * 8:(hh + 1) * 8, 0:tw], in_=srcv[:, hh * 8:(hh + 1) * 8, t0:t0 + tw]), 16)
            nc.scalar.wait_ge(s_ld.h, s_ld.n)
            nc.scalar.wait_ge(s_mm.h, s_mm.n)
            s_sq.inc(nc.scalar.activation(out=self.sq_n[:, :, 0:tw], in_=self.xst_n[:, :, 0:tw], func=AF.Square))
            nc.tensor.wait_ge(s_sq.h, s_sq.n)
            nc.tensor.wait_ge(s_dv.h, s_dv.n)
            for k in range(16):
                ins = nc.tensor.matmul(self.ps[:, 4, 0:tw], lhsT=self.ones[:, :], rhs=self.sq_n[:, k, 0:tw], start=(k == 0), stop=(k == 15))
            s_mm.inc(ins)
            self.rstd_op(self.ps[:, 4, 0:tw], self.rstd[:, 0, 0:tw], 1.0 / D, (s_mm.h, s_mm.n))
            for k in range(16):
                ins = nc.vector.scalar_tensor_tensor(out=dst[:, k, tt * tw:(tt + 1) * tw], in0=self.xst_n[:, k, 0:tw],
                                                     scalar=self.gains_sb[:, which, k:k + 1], in1=self.rstd[:, 0, 0:tw],
                                                     op0=ALU.mult, op1=ALU.mult)
            s_dv.inc(ins)
        return s_dv

    def onorm(self, tok0):
        nc = self.nc
        layer = self.layer
        y = self.y()
        s_ld = self.kb.sem("o_ld"); s_sq = self.kb.sem("o_sq"); s_mm = self.kb.sem("o_mm"); s_dv = self.kb.sem("o_dv")
        nblk = self.FY // 512
        nnorm = 4 if layer == 0 else 8
        ov = self.oT.rearrange("(c p) t -> p c t", p=128)
        for tt in range(NT):
            t0 = tok0 + tt * 512
            for blk in range(nblk):
                nc.sync.wait_ge(s_dv.h, s_dv.n)
                s_ld.inc(nc.sync.dma_start(out=self.xst[:, 0:4, :], in_=ov[:, blk * 4:(blk + 1) * 4, t0:t0 + 512]), 16)
                if blk >= nnorm:
                    nc.vector.wait_ge(s_ld.h, s_ld.n)
                    ins = nc.vector.tensor_copy(out=y[:, blk * 4:(blk + 1) * 4, tt * 512:(tt + 1) * 512], in_=self.xst[:, 0:4, :])
                    s_dv.inc(ins)
                    continue
                nc.scalar.wait_ge(s_ld.h, s_ld.n)
                nc.scalar.wait_ge(s_mm.h, s_mm.n)
                s_sq.inc(nc.scalar.activation(out=self.sq[:, 0:4, :], in_=self.xst[:, 0:4, :], func=AF.Square))
                nc.tensor.wait_ge(s_sq.h, s_sq.n)
                nc.tensor.wait_ge(s_dv.h, s_dv.n)
                if layer == 0:
                    for k in range(4):
                        ins = nc.tensor.matmul(self.ps[:, 4, :], lhsT=self.ones[:, :], rhs=self.sq[:, k, :], start=(k == 0), stop=(k == 3))
                else:
                    for k in range(4):
                        ins = nc.tensor.matmul(self.ps[:, 4 + k, :], lhsT=self.ones[:, :], rhs=self.sq[:, k, :], start=True, stop=True)
                s_mm.inc(ins)
                if layer == 0:
                    self.rstd_op(self.ps[:, 4, :], self.rstd[:, 0, :], 1.0 / 512, (s_mm.h, s_mm.n))
                    for k in range(4):
                        ins = nc.vector.tensor_tensor(out=y[:, blk * 4 + k, tt * 512:(tt + 1) * 512], in0=self.xst[:, k, :], in1=self.rstd[:, 0, :], op=ALU.mult)
                else:
                    for k in range(4):
                        self.rstd_op(self.ps[:, 4 + k, :], self.rstd[:, k, :], 1.0 / 128, (s_mm.h, s_mm.n))
                        ins = nc.vector.scalar_tensor_tensor(out=y[:, blk * 4 + k, tt * 512:(tt + 1) * 512], in0=self.xst[:, k, :],
                                                             scalar=self.hg_sb[:, 2:3], in1=self.rstd[:, k, :], op0=ALU.mult, op1=ALU.mult)
                s_dv.inc(ins)

    def epi_gate(self):
        nc = self.nc
        y = self.y()
        s_d = self.kb.sem("eg_d")
        base_d = s_d.n

        def epi(cnt, mt, tt, ps_list, idx_after):
            s = cnt % 2
            nc.scalar.wait_ge(self.mm.h, idx_after)
            if cnt >= 2:
                nc.scalar.wait_ge(s_d.h, base_d + cnt - 1)
            fin = nc.scalar.activation(out=self.stmp[:, s, :], in_=ps_list[0], func=AF.Silu)
            nc.vector.wait_ge(self.pf.h, idx_after)
            yv = y[:, mt, tt * 512:(tt + 1) * 512]
            s_d.inc(nc.vector.tensor_tensor(out=yv, in0=self.stmp[:, s, :], in1=yv, op=ALU.mult))
            return fin
        return epi

    def epi_swiglu(self):
        nc = self.nc
        g = self.g()
        s_a = self.kb.sem("es_a")
        hist = []

        def epi(cnt, mt, tt, ps_list, idx_after):
            s = cnt % 2
            nc.scalar.wait_ge(self.mm.h, idx_after)
            if cnt >= 2:
                nc.scalar.wait_ge(self.pf.h, hist[cnt - 2])
            s_a.inc(nc.scalar.activation(out=self.stmp[:, s, :], in_=ps_list[0], func=AF.Silu))
            nc.vector.wait_ge(s_a.h, s_a.n)
            fin = nc.vector.tensor_tensor(out=g[:, mt, tt * 512:(tt + 1) * 512], in0=self.stmp[:, s, :], in1=ps_list[1], op=ALU.mult)
            hist.append(idx_after)
            return fin
        return epi

    def epi_resid(self, res_src, dst, tok0, tiles):
        nc = self.nc
        rv = res_src.rearrange("(c p) t -> p c t", p=128)
        dv = dst.rearrange("(c p) t -> p c t", p=128)
        idx0 = self.gidx
        rls = [self.kb.sem("rl%d" % i) for i in range(3)]
        sts = self.sts

        def issue_load(c):
            mt, tt = tiles[c]
            if c >= 3:
                nc.sync.wait_ge(self.pf.h, idx0 + c - 2)
            rls[c % 3].inc(nc.sync.dma_start(out=self.rbuf[:, c % 3, :], in_=rv[:, mt, tok0 + tt * 512: tok0 + (tt + 1) * 512]), 16)

        def epi(cnt, mt, tt, ps_list, idx_after):
            if cnt == 0:
                issue_load(0)
                if len(tiles) > 1:
                    issue_load(1)
            if cnt + 2 < len(tiles):
                issue_load(cnt + 2)
            s = cnt % 3
            nc.vector.wait_ge(self.mm.h, idx_after)
            nc.vector.wait_ge(rls[s].h, rls[s].n if cnt + 3 >= len(tiles) or True else 0)
            nc.vector.wait_ge(sts[s].h, sts[s].n)
            fin = nc.vector.tensor_tensor(out=self.obuf[:, s, :], in0=ps_list[0], in1=self.rbuf[:, s, :], op=ALU.add)
            nc.sync.wait_ge(self.pf.h, idx_after)
            sts[s].inc(nc.sync.dma_start(out=dv[:, mt, tok0 + tt * 512: tok0 + (tt + 1) * 512], in_=self.obuf[:, s, :]), 16)
            return fin
        return epi

    def epi_plain(self, dstf):
        nc = self.nc

        def epi(cnt, mt, tt, ps_list, idx_after):
            nc.vector.wait_ge(self.mm.h, idx_after)
            return nc.vector.tensor_copy(out=dstf(mt, tt), in_=ps_list[0])
        return epi

    def mem_kv(self):
        nc = self.nc
        s_ld = self.kb.sem("m_ld"); s_a = self.kb.sem("m_a"); s_p = self.kb.sem("m_p"); s_d = self.kb.sem("m_d")
        memn = self.hT[:, :, 0:256]
        ndv = self.norm(self.memT, 0, 3, self.hT, 1, tw=256)
        self.barrier()
        self.gemm(lambda g: [(self.w_kv[:, 0:512], 0, 512)], 16, 512, 1, memn, 1,
                  self.epi_plain(lambda mt, tt: self.qf[:, mt, 0:256]), tw=256, pe_waits=[(ndv.h, ndv.n)])
        self.barrier()
        nc.scalar.wait_ge(self.pf.h, self.gidx)
        nc.scalar.activation(out=self.sq[:, 0:4, 0:256], in_=self.qf[:, 0:4, 0:256], func=AF.Square).then_inc(s_a.h, 1)
        nc.tensor.wait_ge(s_a.h, 1)
        for h in range(4):
            ins = nc.tensor.matmul(self.ps[:, 4 + h, 0:256], lhsT=self.ones[:, :], rhs=self.sq[:, h, 0:256], start=True, stop=True)
        ins.then_inc(s_p.h, 1)
        for h in range(4):
            self.rstd_op(self.ps[:, 4 + h, 0:256], self.rstd[:, h, 0:256], 1.0 / 128, (s_p.h, 1))
            nc.vector.scalar_tensor_tensor(out=self.knT[:, h, :], in0=self.qf[:, h, 0:256], scalar=self.hg_sb[:, 1:2], in1=self.rstd[:, h, 0:256],
                                           op0=ALU.mult, op1=ALU.mult)
        self.barrier()
        wv = self.wbuf[0][:, 0:16 * 512].rearrange("p (c n) -> p c n", n=512)
        src = self.w_kv[:, 512:1024].rearrange("(c p) n -> p c n", p=128)
        for k0 in (0, 8):
            nc.gpsimd.dma_start(out=wv[:, k0:k0 + 8, :], in_=src[:, k0:k0 + 8, :]).then_inc(s_ld.h, 16)
        nc.tensor.wait_ge(s_ld.h, 32)
        for c in range(2):
            for k in range(16):
                ins = nc.tensor.matmul(self.ps[:, 4 + c, :], lhsT=self.hT[:, k, c * 128:(c + 1) * 128], rhs=wv[:, k, :], start=(k == 0), stop=(k == 15))
        ins.then_inc(s_p.h, 1)
        nc.vector.wait_ge(s_p.h, 2)
        for c in range(2):
            ins = nc.vector.tensor_copy(out=self.vm[:, c, :], in_=self.ps[:, 4 + c, :])
        self.barrier()

    def xa_attn(self):
        nc = self.nc
        s_q = self.kb.sem("x_q"); s_a = self.kb.sem("x_a"); s_p = self.kb.sem("x_p"); s_d = self.kb.sem("x_d")
        scale = 128 ** -0.5
        for tt in range(NT):
            self.gemm(lambda g: [(self.w_q[:, :], 0, 512)], 16, 512, 1, self.hT[:, :, tt * 512:(tt + 1) * 512], 1,
                      self.epi_plain(lambda mt, t_: self.qf[:, mt, :]), pe_waits=[(self.kb.sem("n_dv").h, self.kb.sem("n_dv").n)])
            self.barrier()
            nc.scalar.wait_ge(self.pf.h, self.gidx)
            s_a.inc(nc.scalar.activation(out=self.sq[:, 0:4, :], in_=self.qf[:, 0:4, :], func=AF.Square))
            nc.tensor.wait_ge(s_a.h, s_a.n)
            for h in range(4):
                ins = nc.tensor.matmul(self.ps[:, 4 + h, :], lhsT=self.ones[:, :], rhs=self.sq[:, h, :], start=True, stop=True)
            s_p.inc(ins)
            for h in range(4):
                self.rstd_op(self.ps[:, 4 + h, :], self.rstd[:, h, :], 1.0 / 128, (s_p.h, s_p.n), post=scale)
                ins = nc.vector.scalar_tensor_tensor(out=self.qn[:, h, tt * 512:(tt + 1) * 512], in0=self.qf[:, h, :], scalar=self.hg_sb[:, 0:1],
                                                     in1=self.rstd[:, h, :], op0=ALU.mult, op1=ALU.mult)
            s_d.inc(ins)
            nc.tensor.wait_ge(s_d.h, s_d.n)
            nc.scalar.wait_ge(s_d.h, s_d.n)
            self.barrier()
            for h in range(4):
                for c in range(2):
                    ins = nc.tensor.matmul(self.ps[:, 4 + c, :], lhsT=self.knT[:, h, c * 128:(c + 1) * 128], rhs=self.qn[:, h, tt * 512:(tt + 1) * 512],
                                           start=True, stop=True)
                s_p.inc(ins)
                nc.scalar.wait_ge(s_p.h, s_p.n)
                for c in range(2):
                    ins = nc.scalar.activation(out=self.pT[:, c, :], in_=self.ps[:, 4 + c, :], func=AF.Exp)
                s_a.inc(ins)
                nc.tensor.wait_ge(s_a.h, s_a.n)
                for c in range(2):
                    nc.tensor.matmul(self.ps[:, 6, :], lhsT=self.vm[:, c, h * 128:(h + 1) * 128], rhs=self.pT[:, c, :], start=(c == 0), stop=(c == 1))
                for c in range(2):
                    ins = nc.tensor.matmul(self.ps[:, 7, :], lhsT=self.ones[:, :], rhs=self.pT[:, c, :], start=(c == 0), stop=(c == 1))
                s_p.inc(ins)
                nc.vector.wait_ge(s_p.h, s_p.n)
                nc.vector.reciprocal(out=self.rstd[:, 0, :], in_=self.ps[:, 7, :])
                ins = nc.vector.tensor_tensor(out=self.oxa[:, h, tt * 512:(tt + 1) * 512], in0=self.ps[:, 6, :], in1=self.rstd[:, 0, :], op=ALU.mult)
                s_d.inc(ins)
                nc.tensor.wait_ge(s_d.h, s_d.n)
                nc.scalar.wait_ge(s_d.h, s_d.n)
            self.barrier()

    def build(self, stages=99):
        nc = self.nc
        s0 = self.kb.sem("init")
        nc.vector.memset(self.ones[:, :], 1.0)
        nc.vector.memset(self.eps_sb[:, :], EPS)
        nc.vector.memset(self.eps2_sb[:, :], EPS * 128.0)
        nc.sync.dma_start(out=self.gains_sb[:, :, :], in_=self.gains).then_inc(s0.h, 16)
        nc.sync.dma_start(out=self.hg_sb[:, :], in_=self.hg).then_inc(s0.h, 16)
        nc.sync.wait_ge(s0.h, 32)
        self.barrier()
        self.mem_kv()
        for p in range(T // TT):
            tok0 = p * TT
            y = self.y()
            self.norm(self.xT, tok0, 0, self.hT, TT // 256)
            self.barrier()
            if stages < 1:
                continue
            self.onorm(tok0)
            self.barrier()
            self.gemm(lambda g: [(self.w_gate[:, g * 512:(g + 1) * 512], 0, 512)], 16, 512, self.G // 512, self.hT, NT, self.epi_gate(),
                      pe_waits=[(self.kb.sem("n_dv").h, self.kb.sem("n_dv").n), (self.kb.sem("o_dv").h, self.kb.sem("o_dv").n)])
            self.barrier()
            if stages < 2:
                continue
            KC = self.FY // 128
            tiles = [(m, tt) for m in range(16) for tt in range(NT)]
            dst = self.x1 if stages > 2 else self.xo
            self.gemm(lambda g: [(self.w_out[:, g * 256:(g + 1) * 256], 0, 256)], KC, 256, 8, y, NT,
                      self.epi_resid(self.xT, dst, tok0, tiles), pe_waits=[(self.kb.sem("eg_d").h, self.kb.sem("eg_d").n)])
            self.wait_stores()
            self.barrier()
            if stages < 3:
                continue
            self.norm(self.x1, tok0, 1, self.hT, TT // 256)
            self.barrier()
            self.xa_attn()
            dst = self.x2 if stages > 3 else self.xo
            self.gemm(lambda g: [(self.w_o[:, :], 0, 2048)], 4, 2048, 1, self.oxa, NT,
                      self.epi_resid(self.x1, dst, tok0, tiles), pe_waits=[(self.kb.sem("x_d").h, self.kb.sem("x_d").n)])
            self.wait_stores()
            self.barrier()
            if stages < 4:
                continue
            self.norm(self.x2, tok0, 2, self.hT, TT // 256)
            self.barrier()
            for half in range(2):
                c0 = half * (FF // 2)
                self.gemm(lambda g: [(self.w1[:, c0 + g * 256:c0 + (g + 1) * 256], 0, 256), (self.w3[:, c0 + g * 256:c0 + (g + 1) * 256], 256, 256)],
                          16, 512, FF // 512, self.hT, NT, self.epi_swiglu(), pair=True,
                          pe_waits=[(self.kb.sem("n_dv").h, self.kb.sem("n_dv").n)])
                self.barrier()
                w2h = self.w2[c0:c0 + FF // 2, :]
                self.gemm(lambda g: [(w2h[:, g * 256:(g + 1) * 256], 0, 256)], 22, 256, 8, self.g(), NT,
                          self.epi_resid(self.x2 if half == 0 else self.xo, self.xo, tok0, tiles), pe_waits=[(self.pf.h, self.gidx)])
                self.wait_stores()
                self.barrier()
        return nc


def post_inputs(layer, inp, xT_c, oT_c):
    def gl(v):
        return np.ascontiguousarray(v.reshape(16, 128).T)
    gains = np.stack([gl(inp["norm_mix"][layer]), gl(inp["norm_xa"][layer]), gl(inp["norm_ffn"][layer]), gl(inp["mem_norm"])], axis=1)
    gd = inp["gdn_norm"][0]
    hg = np.stack([inp["xa_q_gain"][layer], inp["xa_k_gain"][layer], gd], axis=1)
    if layer == 0:
        w_gate = np.ascontiguousarray(inp["ar_w_in"][0][:, 4096:6144])
        w_out = inp["ar_w_out"][0]
    else:
        w_gate = np.ascontiguousarray(inp["gdn_w_in"][0][:, 8192:12288])
        w_out = inp["gdn_w_out"][0]
    return {
        "xT": xT_c, "oT": oT_c, "w_gate": w_gate, "w_out": np.ascontiguousarray(w_out),
        "gains": np.ascontiguousarray(gains.astype(np.float32)), "hg": np.ascontiguousarray(hg.astype(np.float32)),
        "memT": np.ascontiguousarray(inp["mem"][0].T),
        "w_q": np.ascontiguousarray(inp["xa_w_q"][layer]), "w_kv": np.ascontiguousarray(inp["xa_w_kv"][layer]),
        "w_o": np.ascontiguousarray(inp["xa_w_o"][layer]),
        "w1": np.ascontiguousarray(inp["ffn_w1"][layer]), "w3": np.ascontiguousarray(inp["ffn_w3"][layer]),
        "w2": np.ascontiguousarray(inp["ffn_w2"][layer]),
    }


D = 2048
S = 16384
BT = 512
NB_A = S // BT
NCOL_A = 1152
NRING = 20


class MixA:
    def __init__(self, nblocks=NB_A):
        self.nblocks = nblocks
        self.kb = kb = KB()
        nc = self.nc = kb.nc
        self.xT = kb.din("xT", [D, S])
        self.wA = kb.din("wA", [D, NCOL_A])
        self.gain = kb.din("gain", [128, 16])
        self.hg = kb.din("hg", [128, 2])
        self.cosT = kb.din("cosT", [128, S])
        self.sinT = kb.din("sinT", [128, S])
        self.dmask = kb.din("dmask", [128, 128])
        self.qdrow = kb.din("qdrow", [128, BT])
        self.kdec = kb.din("kdec", [128, 2])
        self.gtab = kb.din("gtab", [128, 17, 128])
        self.mtab = kb.din("mtab", [128, 17, 128])
        self.ident = kb.din("ident", [128, 128])
        self.oret = kb.dout("oret", [256, S])
        self.odil = kb.dout("odil", [128, S])
        sb = kb.sb
        self.w = sb("w", [128, 16, NCOL_A], BF16)
        self.ones = sb("ones", [128, 128], BF16)
        self.idb = sb("idb", [128, 128], BF16)
        self.idf = sb("idf", [128, 128], F32)
        self.gain_sb = sb("gain_sb", [128, 16], F32)
        self.hg_sb = sb("hg_sb", [128, 2], F32)
        self.eps_sb = sb("eps_sb", [128, 1], F32)
        self.eps2_sb = sb("eps2_sb", [128, 1], F32)
        self.dm = sb("dm", [128, 128], F32)
        self.qd = sb("qd", [128, BT], F32)
        self.kd = sb("kd", [128, 2], F32)
        self.E = sb("E", [128, 17, 128], F32)
        self.mt_sb = sb("mt_sb", [128, 17, 128], F32)
        self.xst = sb("xst", [128, 16, BT], F32)
        self.sq = sb("sq", [128, 16, BT], BF16)
        self.hT = sb("hT", [128, 16, BT], BF16)
        self.rstd = sb("rstd", [128, 2, BT], F32)
        self.cs = sb("cs", [128, 2, 2, BT], F32)
        self.tmp = sb("tmp", [128, 2, BT], F32)
        self.QT = sb("QT", [128, 2, BT], BF16)
        self.QdT = sb("QdT", [128, 2, BT], BF16)
        self.KT = sb("KT", [128, 2, BT], BF16)
        self.Kd = sb("Kd", [128, 4, 256], BF16)
        self.VA = sb("VA", [128, 4, 256], BF16)
        self.Sm = sb("Sm", [128, 128], BF16)
        self.St = sb("St", [128, 2, 256], F32)
        self.Stb = sb("Stb", [128, 2, 256], BF16)
        self.qnT = sb("qnT", [128, BT], BF16)
        self.knR = sb("knR", [128, NRING, 128], BF16)
        self.vbR = sb("vbR", [128, NRING, 128], BF16)
        self.ex = sb("ex", [128, BT], F32)
        self.pT = sb("pT", [128, 2, BT], BF16)
        self.rl_ = sb("rl_", [128, 128], F32)
        self.oretb = sb("oretb", [128, 2, 2, BT], F32)
        self.odilb = sb("odilb", [128, 2, BT], F32)
        self.ps = kb.psum("ps", [128, 8, 512], F32)

    def rstd_op(self, ps_ap, out_ap, inv_n, wait, post=1.0):
        nc = self.nc
        s = self.kb.sem("r_a")
        nc.scalar.wait_ge(wait[0], wait[1])
        s.inc(nc.scalar.activation(out=out_ap, in_=ps_ap, func=AF.Sqrt, scale=inv_n / post ** 2,
                                   bias=self.eps_sb[:, 0:1] if post == 1.0 else self.eps2_sb[:, 0:1]))
        nc.vector.wait_ge(s.h, s.n)
        return nc.vector.reciprocal(out=out_ap, in_=out_ap)

    def build(self):
        nc = self.nc
        kb = self.kb
        sem = kb.sem
        ps = self.ps
        V, A, PE, SP, PL = nc.vector, nc.scalar, nc.tensor, nc.sync, nc.gpsimd

        def W(eng, s):
            eng.wait_ge(s.h, s.n)

        s0 = sem("init")
        for k0 in range(0, 16, 4):
            s0.inc(PL.dma_start(out=self.w[:, k0:k0 + 4, :], in_=self.wA.rearrange("(c p) n -> p c n", p=128)[:, k0:k0 + 4, :]), 16)
        s0.inc(PL.dma_start(out=self.idb[:, :], in_=self.ident), 16)
        s1 = sem("init1")
        for (dst, src) in [(self.gain_sb[:, :], self.gain), (self.hg_sb[:, :], self.hg), (self.dm[:, :], self.dmask), (self.qd[:, :], self.qdrow),
                           (self.kd[:, :], self.kdec), (self.E[:, :, :], self.gtab), (self.mt_sb[:, :, :], self.mtab), (self.idf[:, :], self.ident)]:
            s1.inc(SP.dma_start(out=dst, in_=src), 16)
        V.memset(self.ones[:, :], 1.0)
        V.memset(self.eps_sb[:, :], EPS)
        V.memset(self.eps2_sb[:, :], EPS * 128.0)
        V.memset(self.St[:, :, :], 0.0)
        V.memset(self.Stb[:, :, :], 0.0)
        W(A, s1)
        sE = sem("sE")
        sE.inc(A.activation(out=self.E[:, :, :], in_=self.E[:, :, :], func=AF.Exp))
        W(V, sE)
        W(V, s1)
        sE2 = sem("sE2")
        sE2.inc(V.tensor_tensor(out=self.E[:, :, :], in0=self.E[:, :, :], in1=self.mt_sb[:, :, :], op=ALU.mult))
        W(PE, s0)
        W(PE, sE2)
        W(A, sE2)

        xv = self.xT.rearrange("(c p) t -> p c t", p=128)
        s_xl = sem("xl"); s_sq = sem("a_sq"); s_ss = sem("p_ss"); s_h = sem("d_h")
        s_cl = [sem("cl0"), sem("cl1")]
        s_pj = sem("p_pj")
        s_pf = sem("pjf")
        s_rot = sem("d_rot")
        s_sq2 = sem("a_sq2"); s_ss2 = sem("p_ss2")
        s_tr = sem("p_tr"); s_kd = sem("d_kd")
        s_sc = sem("p_sc"); s_sm = sem("d_sm"); s_o = sem("p_o"); s_oe = sem("a_oe"); s_ds = sem("p_ds"); s_st = sem("d_st")
        s_qk = sem("p_qk"); s_ex = sem("a_ex"); s_p = sem("d_p"); s_pv = sem("p_pv"); s_do = sem("d_do")
        s_or = [sem("or0"), sem("or1")]; s_od = [sem("od0"), sem("od1")]
        pj_idx = [0]
        rot_hist = []

        def proj_tile(cols, width, lhs_tok=None):
            i = pj_idx[0]
            bank = i % 2
            if i >= 2:
                PE.wait_ge(s_pf.h, i - 1)
            for k in range(16):
                if lhs_tok is None:
                    ins = PE.matmul(ps[:, bank, 0:BT], lhsT=self.w[:, k, cols:cols + 128], rhs=self.hT[:, k, :], start=(k == 0), stop=(k == 15))
                else:
                    ins = PE.matmul(ps[:, bank, 0:width], lhsT=self.hT[:, k, lhs_tok * 128:(lhs_tok + 1) * 128], rhs=self.w[:, k, cols:cols + width],
                                    start=(k == 0), stop=(k == 15))
            s_pj.inc(ins)
            pj_idx[0] += 1
            return bank

        for b in range(self.nblocks):
            t0 = b * BT
            sl = b % 2
            W(SP, s_h)
            for hh in range(2):
                s_xl.inc(SP.dma_start(out=self.xst[:, hh * 8:(hh + 1) * 8, :], in_=xv[:, hh * 8:(hh + 1) * 8, t0:t0 + BT]), 16)
            if b >= 2:
                SP.wait_ge(s_rot.h, rot_hist[b - 2])
            s_cl[sl].inc(SP.dma_start(out=self.cs[:, sl, 0, :], in_=self.cosT[:, t0:t0 + BT]), 16)
            s_cl[sl].inc(SP.dma_start(out=self.cs[:, sl, 1, :], in_=self.sinT[:, t0:t0 + BT]), 16)
            W(A, s_xl)
            W(A, s_ss)
            W(A, s_ss2)
            s_sq.inc(A.activation(out=self.sq[:, :, :], in_=self.xst[:, :, :], func=AF.Square))
            W(PE, s_sq)
            for k in range(16):
                ins = PE.matmul(ps[:, 2, :], lhsT=self.ones[:, :], rhs=self.sq[:, k, :], start=(k == 0), stop=(k == 15))
            s_ss.inc(ins)
            self.rstd_op(ps[:, 2, :], self.rstd[:, 0, :], 1.0 / D, (s_ss.h, s_ss.n))
            W(V, s_pj)
            for k in range(16):
                ins = V.scalar_tensor_tensor(out=self.hT[:, k, :], in0=self.xst[:, k, :], scalar=self.gain_sb[:, k:k + 1], in1=self.rstd[:, 0, :],
                                             op0=ALU.mult, op1=ALU.mult)
            s_h.inc(ins)
            W(PE, s_h)
            W(V, s_cl[sl])
            for which, col0, dstT in ((0, 0, self.QT), (1, 256, self.KT)):
                b0 = proj_tile(col0, 128)
                b1 = proj_tile(col0 + 128, 128)
                W(V, s_pj)
                if which == 0:
                    W(V, s_o)
                    W(V, s_sc)
                else:
                    W(V, s_tr)
                    W(V, s_sc)
                cosv = self.cs[:, sl, 0, :]; sinv = self.cs[:, sl, 1, :]
                V.tensor_tensor(out=self.tmp[:, 0, :], in0=ps[:, b0, :], in1=cosv, op=ALU.mult)
                V.tensor_tensor(out=self.tmp[:, 1, :], in0=ps[:, b1, :], in1=sinv, op=ALU.mult)
                V.tensor_tensor(out=dstT[:, 0, :], in0=self.tmp[:, 0, :], in1=self.tmp[:, 1, :], op=ALU.subtract)
                V.tensor_tensor(out=self.tmp[:, 0, :], in0=ps[:, b0, :], in1=sinv, op=ALU.mult)
                ins = V.tensor_tensor(out=self.tmp[:, 1, :], in0=ps[:, b1, :], in1=cosv, op=ALU.mult)
                s_pf.inc(ins, 2)
                ins = V.tensor_tensor(out=dstT[:, 1, :], in0=self.tmp[:, 0, :], in1=self.tmp[:, 1, :], op=ALU.add)
                if which == 0:
                    for i in range(2):
                        ins = V.tensor_tensor(out=self.QdT[:, i, :], in0=self.QT[:, i, :], in1=self.qd[:, :], op=ALU.mult)
                s_rot.inc(ins)
            rot_hist.append(s_rot.n)
            for which, col0 in ((0, 512), (1, 640)):
                bk = proj_tile(col0, 128)
                W(A, s_pj)
                W(A, s_ss2)
                s_sq2.inc(A.activation(out=self.sq[:, 0, :], in_=ps[:, bk, :], func=AF.Square))
                W(PE, s_sq2)
                ins = PE.matmul(ps[:, 2, :], lhsT=self.ones[:, :], rhs=self.sq[:, 0, :], start=True, stop=True)
                s_ss2.inc(ins)
                if which == 0:
                    self.rstd_op(ps[:, 2, :], self.rstd[:, 1, :], 1.0 / 128, (s_ss2.h, s_ss2.n), post=128 ** -0.5)
                    W(V, s_pv)
                    W(V, s_qk)
                    ins = V.scalar_tensor_tensor(out=self.qnT[:, :], in0=ps[:, bk, :], scalar=self.hg_sb[:, 0:1], in1=self.rstd[:, 1, :],
                                                 op0=ALU.mult, op1=ALU.mult)
                else:
                    self.rstd_op(ps[:, 2, :], self.rstd[:, 1, :], 1.0 / 128, (s_ss2.h, s_ss2.n))
                    W(V, s_qk)
                    for j in range(4):
                        slot = (4 * b + j) % NRING
                        ins = V.scalar_tensor_tensor(out=self.knR[:, slot, :], in0=ps[:, bk, j * 128:(j + 1) * 128], scalar=self.hg_sb[:, 1:2],
                                                     in1=self.rstd[:, 1, j * 128:(j + 1) * 128], op0=ALU.mult, op1=ALU.mult)
                s_pf.inc(ins, 1)
            for c in range(4):
                bk = proj_tile(768, 384, lhs_tok=c)
                W(V, s_pj)
                if c == 0:
                    W(V, s_ds)
                    W(V, s_o)
                    W(V, s_pv)
                V.tensor_copy(out=self.VA[:, c, :], in_=ps[:, bk, 0:256])
                ins = V.tensor_copy(out=self.vbR[:, (4 * b + c) % NRING, :], in_=ps[:, bk, 256:384])
                s_pf.inc(ins, 1)
            W(PE, s_rot)
            for c in range(4):
                W(PE, s_kd)
                for i in range(2):
                    ins = PE.matmul(ps[:, 3, i * 128:(i + 1) * 128], lhsT=self.KT[:, i, c * 128:(c + 1) * 128], rhs=self.idb[:, :], start=True, stop=True)
                s_tr.inc(ins)
                W(V, s_tr)
                if c == 0:
                    W(V, s_ds)
                for i in range(2):
                    ins = V.tensor_scalar(out=self.Kd[:, c, i * 128:(i + 1) * 128], in0=ps[:, 3, i * 128:(i + 1) * 128], scalar1=self.kd[:, 0:1], scalar2=None, op0=ALU.mult)
                s_kd.inc(ins)
            if b % 2 == 0 or True:
                V.wait_ge(s_or[sl].h, s_or[sl].n)
                A.wait_ge(s_or[sl].h, s_or[sl].n)
            for c in range(4):
                cs_ = slice(c * 128, (c + 1) * 128)
                W(PE, s_sm)
                W(PE, s_kd)
                for i in range(2):
                    ins = PE.matmul(ps[:, 3, 256:384], lhsT=self.KT[:, i, cs_], rhs=self.QT[:, i, cs_], start=(i == 0), stop=(i == 1))
                s_sc.inc(ins)
                W(V, s_sc)
                W(V, s_o)
                s_sm.inc(V.tensor_tensor(out=self.Sm[:, :], in0=ps[:, 3, 256:384], in1=self.dm[:, :], op=ALU.mult))
                W(PE, s_sm)
                W(PE, s_pf)
                W(PE, s_st)
                W(PE, s_oe)
                for j in range(2):
                    PE.matmul(ps[:, 4, j * 128:(j + 1) * 128], lhsT=self.VA[:, c, j * 128:(j + 1) * 128], rhs=self.Sm[:, :], start=True, stop=False)
                    for i in range(2):
                        ins = PE.matmul(ps[:, 4, j * 128:(j + 1) * 128], lhsT=self.Stb[:, i, j * 128:(j + 1) * 128], rhs=self.QdT[:, i, cs_],
                                        start=False, stop=(i == 1))
                s_o.inc(ins)
                W(A, s_o)
                for j in range(2):
                    ins = A.activation(out=self.oretb[:, sl, j, cs_], in_=ps[:, 4, j * 128:(j + 1) * 128], func=AF.Copy)
                s_oe.inc(ins)
                W(PE, s_kd)
                for i in range(2):
                    ins = PE.matmul(ps[:, 5, i * 256:(i + 1) * 256], lhsT=self.Kd[:, c, i * 128:(i + 1) * 128], rhs=self.VA[:, c, :], start=True, stop=True)
                s_ds.inc(ins)
                W(V, s_ds)
                W(V, s_o)
                for i in range(2):
                    V.scalar_tensor_tensor(out=self.St[:, i, :], in0=self.St[:, i, :], scalar=self.kd[:, 1:2], in1=ps[:, 5, i * 256:(i + 1) * 256],
                                           op0=ALU.mult, op1=ALU.add)
                ins = V.tensor_copy(out=self.Stb[:, :, :], in_=self.St[:, :, :])
                s_st.inc(ins)
            W(SP, s_oe)
            s_or[sl].inc(SP.dma_start(out=self.oret.rearrange("(j p) t -> p j t", p=128)[:, :, t0:t0 + BT], in_=self.oretb[:, sl, :, :]), 16)
            W(PE, s_pf)
            V.wait_ge(s_od[sl].h, s_od[sl].n)
            for qt in range(4):
                tq = 4 * b + qt
                nk = min(17, tq + 1)
                batches = [(o0, min(4, nk - o0)) for o0 in range(0, nk, 4)]
                W(PE, s_do)
                for bi, (o0, n) in enumerate(batches):
                    W(PE, s_ex)
                    for j in range(n):
                        slot = (tq - (o0 + j)) % NRING
                        ins = PE.matmul(ps[:, 6, j * 128:(j + 1) * 128], lhsT=self.knR[:, slot, :], rhs=self.qnT[:, qt * 128:(qt + 1) * 128], start=True, stop=True)
                    s_qk.inc(ins)
                    W(A, s_qk)
                    W(A, s_p)
                    s_ex.inc(A.activation(out=self.ex[:, 0:n * 128], in_=ps[:, 6, 0:n * 128], func=AF.Exp))
                    W(V, s_ex)
                    if bi >= 2:
                        V.wait_ge(s_pv.h, pv_hist[-2])
                    pslot = bi % 2
                    s_p.inc(V.tensor_tensor(out=self.pT[:, pslot, 0:n * 128], in0=self.ex[:, 0:n * 128],
                                            in1=self.E[:, o0:o0 + n, :].rearrange("p o q -> p (o q)"), op=ALU.mult))
                    W(PE, s_p)
                    for j in range(n):
                        slot = (tq - (o0 + j)) % NRING
                        first = (o0 + j == 0)
                        last = (o0 + j == nk - 1)
                        PE.matmul(ps[:, 7, 0:128], lhsT=self.vbR[:, slot, :], rhs=self.pT[:, pslot, j * 128:(j + 1) * 128], start=first, stop=last, skip_group_check=True)
                        ins = PE.matmul(ps[:, 7, 128:256], lhsT=self.ones[:, :], rhs=self.pT[:, pslot, j * 128:(j + 1) * 128], start=False, stop=last, skip_group_check=True)
                    s_pv.inc(ins)
                    if bi == 0:
                        pv_hist = []
                    pv_hist.append(s_pv.n)
                W(V, s_pv)
                V.reciprocal(out=self.rl_[:, :], in_=ps[:, 7, 128:256])
                s_do.inc(V.tensor_tensor(out=self.odilb[:, sl, qt * 128:(qt + 1) * 128], in0=ps[:, 7, 0:128], in1=self.rl_[:, :], op=ALU.mult))
            W(SP, s_do)
            s_od[sl].inc(SP.dma_start(out=self.odil[:, t0:t0 + BT], in_=self.odilb[:, sl, :]), 16)
        for s in s_or + s_od:
            W(SP, s)
        return nc


def t5_bucket_np(dist):
    exact = 16
    d = np.maximum(dist, exact).astype(np.float32)
    large = exact + (np.log(d / np.float32(exact)) / np.float32(math.log(2048 / exact)) * np.float32(32 - exact)).astype(np.int32)
    large = np.minimum(large, 31)
    return np.where(dist < exact, dist, large)


def mixa_consts():
    i = np.arange(128, dtype=np.float32)
    inv = (np.float32(10000.0) ** (-(np.arange(0, 256, 2, dtype=np.float32)) / np.float32(256))).astype(np.float32)
    pos = np.arange(S, dtype=np.float32)
    ang = (inv[:, None] * pos[None, :]).astype(np.float32)
    cosT = np.cos(ang).astype(np.float32)
    sinT = np.sin(ang).astype(np.float32)
    kj = np.arange(128)[:, None, None]
    o = np.arange(17)[None, :, None]
    qi = np.arange(128)[None, None, :]
    delta = qi - kj + 128 * o
    valid = delta >= 0
    m = ((delta <= 128) & valid).astype(np.float32) + ((delta % 4 == 0) & (delta <= 512) & valid) + ((delta % 16 == 0) & (delta <= 2048) & valid)
    bidx = t5_bucket_np(np.maximum(delta, 0))
    return cosT, sinT, m.astype(np.float32), bidx


def mixa_inputs(inp, c, xT, consts):
    cosT, sinT, mtab, bidx = consts
    hr, vh, hd = c // 2, c % 2, c
    W = inp["ar_w_in"][0]
    wA = np.concatenate([W[:, hr * 256:(hr + 1) * 256], W[:, 1024 + hr * 256:1024 + (hr + 1) * 256],
                         W[:, 6144 + hd * 128:6144 + (hd + 1) * 128], W[:, 7168 + hd * 128:7168 + (hd + 1) * 128],
                         W[:, 2048 + hr * 512 + vh * 256:2048 + hr * 512 + (vh + 1) * 256], W[:, 8192 + hd * 128:8192 + (hd + 1) * 128]], axis=1)
    gamma = 1.0 - 2.0 ** (-5.0 - hr)
    kj = np.arange(128)[:, None]; qi = np.arange(128)[None, :]
    dmask = np.where(qi >= kj, gamma ** np.maximum(qi - kj, 0), 0.0) * 256 ** -0.5
    qdrow = np.tile(gamma ** (np.arange(128) + 1.0), 4)[None, :].repeat(128, axis=0)
    kdec = np.stack([gamma ** (127.0 - np.arange(128)) * 256 ** -0.5, np.full(128, gamma ** 128.0)], axis=1)
    gtab = inp["rel_bias"][:, hd][bidx]
    return {
        "xT": xT, "wA": np.ascontiguousarray(wA), "gain": np.ascontiguousarray(inp["norm_mix"][0].reshape(16, 128).T),
        "hg": np.ascontiguousarray(np.stack([inp["dil_q_gain"][0], inp["dil_k_gain"][0]], axis=1)),
        "cosT": cosT, "sinT": sinT, "dmask": dmask.astype(np.float32), "qdrow": qdrow.astype(np.float32), "kdec": kdec.astype(np.float32),
        "gtab": np.ascontiguousarray(gtab.astype(np.float32)), "mtab": mtab, "ident": np.eye(128, dtype=np.float32),
    }


D = 2048
S = 16384
BT = 512
NB_C = S // BT
NCOL_C = 1032
C = 64
G = 4


def fl(ap):
    return ap.rearrange("p a b -> p (a b)")


def fl4(t):
    return t[:, :, :, :].rearrange("p a b c -> p (a b c)")


class MixC:
    def __init__(self, nblocks=NB_C):
        self.nblocks = nblocks
        self.kb = kb = KB()
        self.nc = kb.nc
        self.xT = kb.din("xT", [D, S])
        self.wC = kb.din("wC", [D, NCOL_C])
        self.gain = kb.din("gain", [128, 16])
        self.convw = kb.din("convw", [128, 8, 4])
        self.hp = kb.din("hp", [128, 2, G, 4])
        self.U = kb.din("U", [64, 64])
        self.MU = kb.din("MU", [64, 4 * G, 64])
        self.ML = kb.din("ML", [64, 4 * G, 64])
        self.I4 = kb.din("I4", [64, 4 * G, 64])
        self.ident = kb.din("ident", [128, 128])
        self.og = kb.dout("og", [512, S])
        sb = kb.sb
        self.w = sb("w", [128, 16, NCOL_C], BF16)
        self.ones = sb("ones", [128, 128], BF16)
        self.onesf = sb("onesf", [64, 128], F32)
        self.idb = sb("idb", [128, 128], BF16)
        self.idf = sb("idf", [128, 128], F32)
        self.gain_sb = sb("gain_sb", [128, 16], F32)
        self.cw = sb("cw", [128, 8, 4], F32)
        self.hp_sb = sb("hp_sb", [128, 2, G, 4], F32)
        self.negA = sb("negA", [128, G, 4], F32)
        self.eps_sb = sb("eps_sb", [128, 1], F32)
        self.eps2_sb = sb("eps2_sb", [128, 1], F32)
        self.one_sb = sb("one_sb", [128, 1], F32)
        self.U_sb = sb("U_sb", [64, 64], F32)
        self.MU_sb = sb("MU_sb", [64, 4 * G, 64], F32)
        self.ML_sb = sb("ML_sb", [64, 4 * G, 64], F32)
        self.I4_sb = sb("I4_sb", [64, 4 * G, 64], F32)
        self.xst = sb("xst", [128, 16, BT], F32)
        self.sq = sb("sq", [128, 8, BT], BF16)
        self.hT = sb("hT", [128, 16, BT], BF16)
        self.rstd = sb("rstd", [128, BT], F32)
        self.praw = sb("praw", [128, 3 + BT], F32)
        self.halo = sb("halo", [128, 8, 3], F32)
        self.acc = sb("acc", [128, BT], F32)
        self.cvs = sb("cvs", [128, 4, BT], F32)
        self.qnT = sb("qnT", [128, 2, BT], BF16)
        self.knT = sb("knT", [128, 2, BT], BF16)
        self.vT = sb("vT", [128, 4, BT], BF16)
        self.ktok = sb("ktok", [64, 2, 512], F32)
        self.vtok = sb("vtok", [64, G, 512], F32)
        self.xa = sb("xa", [64, G, 4], F32)
        self.beta = sb("beta", [64, G, 4], F32)
        self.nbeta = sb("nbeta", [64, G, 4], F32)
        self.g = sb("g", [64, G, 4], F32)
        self.gcc = sb("gcc", [64, G, 4], F32)
        self.egc = sb("egc", [64, G, 4], F32)
        self.bg = sb("bg", [64, G, 4], F32)
        self.egl = sb("egl", [128, G, 4], F32)
        self.zz = sb("zz", [64, 2 * G * 4 * 64], F32)
        self.G1 = self.zz[:, :].rearrange("p (g h d) -> p g h d", g=G, h=4)
        self.Zmin = self.zz[:, 0:G * 256].rearrange("p (g h d) -> p g h d", g=G, h=4)
        self.Zmax = self.zz[:, G * 256:2 * G * 256].rearrange("p (g h d) -> p g h d", g=G, h=4)
        self.E0T = sb("E0T", [64, G, 4, 64], F32)
        self.E1 = sb("E1", [64, G, 4, 64], F32)
        self.eR = sb("eR", [128, 2, 512], F32)
        self.X = sb("X", [64, G, 4, 64], BF16)
        self.Y = sb("Y", [64, G, 4, 64], BF16)
        self.Q = sb("Q", [64, G, 4, 64], F32)
        self.Tt = sb("Tt", [64, G, 4, 64], BF16)
        self.intraT = sb("intraT", [64, G, 4, 64], BF16)
        self.vb = sb("vb", [64, G, 4, 128], BF16)
        self.kbg = sb("kbg", [64, G, 4, 128], BF16)
        self.kdk = sb("kdk", [64, G, 4, 128], BF16)
        self.qgT = sb("qgT", [128, G, 4, 64], BF16)
        self.nwT = sb("nwT", [128, G, 4, 64], BF16)
        self.vnew = sb("vnew", [64, 4, 128], BF16)
        self.St = sb("St", [128, 4, 128], F32)
        self.Stb = sb("Stb", [128, 4, 128], BF16)
        self.ob = sb("ob", [128, 1, 4, BT], F32)
        self.ps = kb.psum("ps", [128, 8, 512], F32)
        self.dtb = self.hp_sb[:, 1, :, :]

    def build(self, debug=False):
        nc = self.nc
        kb = self.kb
        ps = self.ps
        V, A, PE, SP, PL = nc.vector, nc.scalar, nc.tensor, nc.sync, nc.gpsimd
        sems = {"V": kb.sem("sV"), "A": kb.sem("sA"), "P": kb.sem("sP")}
        engs = {"V": V, "A": A, "P": PE}

        def step(e, fn, extra=()):
            eng = engs[e]
            for o in sems:
                if o != e and sems[o].n > 0:
                    eng.wait_ge(sems[o].h, sems[o].n)
            for (sh, sv) in extra:
                eng.wait_ge(sh, sv)
            ins = fn()
            sems[e].inc(ins)

        def sp_wait_all():
            for o in sems:
                if sems[o].n > 0:
                    SP.wait_ge(sems[o].h, sems[o].n)

        s0 = kb.sem("init"); s1 = kb.sem("init1")
        wv = self.wC.rearrange("(c p) n -> p c n", p=128)
        for k0 in range(0, 16, 4):
            s0.inc(PL.dma_start(out=self.w[:, k0:k0 + 4, :], in_=wv[:, k0:k0 + 4, :]), 16)
        s0.inc(PL.dma_start(out=self.idb[:, :], in_=self.ident), 16)
        for (dst, src) in [(self.gain_sb[:, :], self.gain), (self.cw[:, :, :], self.convw), (self.hp_sb[:, :, :, :], self.hp), (self.U_sb[:, :], self.U),
                           (self.MU_sb[:, :, :], self.MU), (self.ML_sb[:, :, :], self.ML), (self.I4_sb[:, :, :], self.I4), (self.idf[:, :], self.ident)]:
            s1.inc(SP.dma_start(out=dst, in_=src), 16)

        def init_v():
            V.memset(self.ones[:, :], 1.0)
            V.memset(self.onesf[:, :], 1.0)
            V.memset(self.eps_sb[:, :], EPS)
            V.memset(self.one_sb[:, :], 1.0)
            V.memset(self.eps2_sb[:, :], EPS * 128.0)
            V.memset(self.St[:, :, :], 0.0)
            V.memset(self.Stb[:, :, :], 0.0)
            return V.memset(self.halo[:, :, :], 0.0)
        step("V", init_v)
        step("A", lambda: A.activation(out=self.negA[:, :, :], in_=self.hp_sb[:, 0, :, :], func=AF.Exp), extra=[(s1.h, s1.n)])
        step("V", lambda: V.tensor_scalar(out=fl(self.negA[:, :, :]), in0=fl(self.negA[:, :, :]), scalar1=-1.0, scalar2=None, op0=ALU.mult), extra=[(s0.h, s0.n), (s1.h, s1.n)])
        PE.wait_ge(s0.h, s0.n)
        PE.wait_ge(s1.h, s1.n)

        xv = self.xT.rearrange("(c p) t -> p c t", p=128)
        s_xl = kb.sem("xl")
        s_o = [kb.sem("so0"), kb.sem("so1")]

        def load_x(b):
            t0 = b * BT
            for hh in range(2):
                s_xl.inc(SP.dma_start(out=self.xst[:, hh * 8:(hh + 1) * 8, :], in_=xv[:, hh * 8:(hh + 1) * 8, t0:t0 + BT]), 16)

        load_x(0)
        for b in range(self.nblocks):
            t0 = b * BT
            sl = 0
            for hf in range(2):
                step("A", lambda hf=hf: A.activation(out=self.sq[:, :, :], in_=self.xst[:, hf * 8:(hf + 1) * 8, :], func=AF.Square), extra=[(s_xl.h, s_xl.n)])

                def f(hf=hf):
                    for k in range(8):
                        ins = PE.matmul(ps[:, 2, :], lhsT=self.ones[:, :], rhs=self.sq[:, k, :], start=(hf == 0 and k == 0), stop=(hf == 1 and k == 7))
                    return ins
                step("P", f)
            step("A", lambda: A.activation(out=self.rstd[:, :], in_=ps[:, 2, :], func=AF.Sqrt, scale=1.0 / D, bias=self.eps_sb[:, 0:1]))

            def f():
                V.reciprocal(out=self.rstd[:, :], in_=self.rstd[:, :])
                for k in range(16):
                    ins = V.scalar_tensor_tensor(out=self.hT[:, k, :], in0=self.xst[:, k, :], scalar=self.gain_sb[:, k:k + 1], in1=self.rstd[:, :],
                                                 op0=ALU.mult, op1=ALU.mult)
                return ins
            step("V", f)
            if b + 1 < self.nblocks:
                sp_wait_all()
                load_x(b + 1)
            for mt in range(8):
                def f(mt=mt):
                    for k in range(16):
                        ins = PE.matmul(ps[:, mt % 2, :], lhsT=self.w[:, k, mt * 128:(mt + 1) * 128], rhs=self.hT[:, k, :], start=(k == 0), stop=(k == 15))
                    return ins
                step("P", f)
                step("A", lambda mt=mt: A.activation(out=self.praw[:, 3:3 + BT], in_=ps[:, mt % 2, :], func=AF.Copy))

                def f(mt=mt):
                    V.tensor_copy(out=self.praw[:, 0:3], in_=self.halo[:, mt, :])
                    V.tensor_scalar(out=self.acc[:, :], in0=self.praw[:, 0:BT], scalar1=self.cw[:, mt, 0:1], scalar2=None, op0=ALU.mult)
                    for j in range(1, 4):
                        ins = V.scalar_tensor_tensor(out=self.acc[:, :], in0=self.praw[:, j:j + BT], scalar=self.cw[:, mt, j:j + 1], in1=self.acc[:, :],
                                                     op0=ALU.mult, op1=ALU.add)
                    return ins
                step("V", f)
                step("A", lambda mt=mt: A.activation(out=(self.cvs[:, mt, :] if mt < 4 else self.vT[:, mt - 4, :]), in_=self.acc[:, :], func=AF.Silu))
                step("V", lambda mt=mt: V.tensor_copy(out=self.halo[:, mt, :], in_=self.praw[:, BT:BT + 3]))
            step("A", lambda: A.activation(out=self.sq[:, 0:4, :], in_=self.cvs[:, 0:4, :], func=AF.Square))
            for mt in range(4):
                step("P", lambda mt=mt: PE.matmul(ps[:, 2, :], lhsT=self.ones[:, :], rhs=self.sq[:, mt, :], start=True, stop=True))
                if mt < 2:
                    step("A", lambda: A.activation(out=self.rstd[:, :], in_=ps[:, 2, :], func=AF.Sqrt, scale=128.0, bias=self.eps2_sb[:, 0:1]))
                else:
                    step("A", lambda: A.activation(out=self.rstd[:, :], in_=ps[:, 2, :], func=AF.Sqrt, scale=1.0, bias=self.eps_sb[:, 0:1]))

                def f(mt=mt):
                    V.reciprocal(out=self.rstd[:, :], in_=self.rstd[:, :])
                    dst = self.qnT[:, mt, :] if mt < 2 else self.knT[:, mt - 2, :]
                    return V.tensor_tensor(out=dst, in0=self.cvs[:, mt, :], in1=self.rstd[:, :], op=ALU.mult)
                step("V", f)
            for grp in range(BT // (C * G)):
                def csl(gi):
                    c0 = (grp * G + gi) * C
                    return slice(c0, c0 + C)

                def bank_col(base_bank, gi, width):
                    return base_bank + gi // 2, (gi % 2) * width

                def f():
                    for gi in range(G):
                        for k in range(16):
                            ins = PE.matmul(ps[0:64, 2, gi * 8:(gi + 1) * 8], lhsT=self.hT[:, k, csl(gi)], rhs=self.w[:, k, 1024:1032],
                                            start=(k == 0), stop=(k == 15))
                    return ins
                step("P", f)
                bav = ps[0:64, 2, 0:G * 8].rearrange("p (g c) -> p g c", c=8)
                step("V", lambda: V.tensor_tensor(out=self.xa[:, :, :], in0=bav[:, :, 4:8], in1=self.dtb[0:64, :, :], op=ALU.add))

                def f():
                    A.activation(out=fl(self.xa[:, :, :]), in_=fl(self.xa[:, :, :]), func=AF.Exp)
                    return A.activation(out=fl(self.xa[:, :, :]), in_=fl(self.xa[:, :, :]), func=AF.Ln, bias=self.one_sb[0:64, 0:1])
                step("A", f)
                step("V", lambda: V.tensor_tensor(out=fl(self.g[:, :, :]), in0=fl(self.xa[:, :, :]), in1=fl(self.negA[0:64, :, :]), op=ALU.mult))
                step("A", lambda: A.activation(out=self.beta[:, :, :], in_=bav[:, :, 0:4], func=AF.Sigmoid))

                def f():
                    V.tensor_scalar(out=fl(self.nbeta[:, :, :]), in0=fl(self.beta[:, :, :]), scalar1=-1.0, scalar2=None, op0=ALU.mult)
                    for gi in range(G):
                        for h in range(4):
                            ins = V.tensor_scalar(out=self.G1[:, gi, h, :], in0=self.onesf[:, :], scalar1=self.g[:, gi, h:h + 1], scalar2=None, op0=ALU.mult)
                    return ins
                step("V", f)

                def f():
                    for gi in range(G):
                        PE.matmul(ps[0:64, 2, 64 + gi * 4:68 + gi * 4], lhsT=self.U_sb[:, :], rhs=self.g[:, gi, :], start=True, stop=True)
                        PE.matmul(ps[:, 2, 128 + gi * 4:132 + gi * 4], lhsT=self.onesf[:, :], rhs=self.g[:, gi, :], start=True, stop=True)
                    for gi in range(G):
                        bk, c0 = bank_col(3, gi, 256)
                        for h in range(4):
                            PE.matmul(ps[:, bk, c0 + h * 64:c0 + (h + 1) * 64], lhsT=self.G1[:, gi, h, :], rhs=self.U_sb[:, :], start=True, stop=True)
                    for gi in range(G):
                        bk, c0 = bank_col(5, gi, 256)
                        for kh in range(2):
                            PE.matmul(ps[0:64, bk, c0 + kh * 64:c0 + (kh + 1) * 64], lhsT=self.knT[:, kh, csl(gi)], rhs=self.knT[:, kh, csl(gi)], start=True, stop=True)
                        for kh in range(2):
                            ins = PE.matmul(ps[0:64, bk, c0 + 128 + kh * 64:c0 + 128 + (kh + 1) * 64], lhsT=self.knT[:, kh, csl(gi)], rhs=self.qnT[:, kh, csl(gi)],
                                            start=True, stop=True)
                    return ins
                step("P", f)

                def f():
                    A.activation(out=fl(self.gcc[:, :, :]), in_=ps[0:64, 2, 64:64 + G * 4], func=AF.Copy)
                    A.activation(out=fl(self.egl[:, :, :]), in_=ps[:, 2, 128:128 + G * 4], func=AF.Exp)
                    return A.activation(out=self.eR[:, :, :], in_=ps[:, 3:5, :], func=AF.Exp)
                step("A", f)

                def f():
                    for gi in range(G):
                        bk, c0 = bank_col(3, gi, 256)
                        for h in range(4):
                            V.tensor_scalar(out=self.Zmin[:, gi, h, :], in0=ps[0:64, bk, c0 + h * 64:c0 + (h + 1) * 64], scalar1=self.gcc[:, gi, h:h + 1], scalar2=0.0,
                                            op0=ALU.subtract, op1=ALU.min)
                            ins = V.tensor_scalar(out=self.Zmax[:, gi, h, :], in0=ps[0:64, bk, c0 + h * 64:c0 + (h + 1) * 64], scalar1=self.gcc[:, gi, h:h + 1], scalar2=0.0,
                                                  op0=ALU.subtract, op1=ALU.max)
                    return ins
                step("V", f)

                def f():
                    A.activation(out=fl(self.egc[:, :, :]), in_=fl(self.gcc[:, :, :]), func=AF.Exp)
                    A.activation(out=fl4(self.E0T), in_=fl4(self.Zmin), func=AF.Exp)
                    return A.activation(out=fl4(self.E1), in_=fl4(self.Zmax), func=AF.Exp, scale=-1.0)
                step("A", f)

                vbanks = [2, 3, 4, 7]

                def f():
                    for gi in range(G):
                        bk, c0 = bank_col(0, gi, 256)
                        for kh in range(2):
                            PE.matmul(ps[0:64, bk, c0 + kh * 128:c0 + (kh + 1) * 128], lhsT=self.knT[:, kh, csl(gi)], rhs=self.idb[:, :], start=True, stop=True)
                        for h in range(4):
                            ins = PE.matmul(ps[0:64, vbanks[gi], h * 128:(h + 1) * 128], lhsT=self.vT[:, h, csl(gi)], rhs=self.idb[:, :], start=True, stop=True)
                    return ins
                step("P", f)

                def f():
                    A.activation(out=self.ktok[:, :, :], in_=ps[0:64, 0:2, :], func=AF.Copy)
                    A.activation(out=self.vtok[:, 0:3, :], in_=ps[0:64, 2:5, :], func=AF.Copy)
                    return A.activation(out=self.vtok[:, 3, :], in_=ps[0:64, 7, :], func=AF.Copy)
                step("A", f)

                def f():
                    V.tensor_tensor(out=fl(self.bg[:, :, :]), in0=fl(self.beta[:, :, :]), in1=fl(self.egc[:, :, :]), op=ALU.mult)
                    V.tensor_tensor(out=fl4(self.E0T), in0=fl4(self.E0T), in1=fl(self.MU_sb[:, :, :]), op=ALU.mult)
                    V.tensor_tensor(out=fl4(self.E1), in0=fl4(self.E1), in1=fl(self.ML_sb[:, :, :]), op=ALU.mult)
                    for gi in range(G):
                        bk, c0 = bank_col(5, gi, 256)
                        ktv = self.ktok[:, gi // 2, :].rearrange("p (a k d) -> p a k d", a=2, k=2)[:, gi % 2, :, :]
                        vtv = self.vtok[:, gi, :].rearrange("p (h d) -> p h d", h=4)
                        for h in range(4):
                            kh = h // 2
                            V.scalar_tensor_tensor(out=self.X[:, gi, h, :], in0=ps[0:64, bk, c0 + kh * 64:c0 + (kh + 1) * 64], scalar=self.nbeta[:, gi, h:h + 1],
                                                   in1=self.E1[:, gi, h, :], op0=ALU.mult, op1=ALU.mult)
                            V.tensor_tensor(out=self.intraT[:, gi, h, :], in0=ps[0:64, bk, c0 + 128 + kh * 64:c0 + 128 + (kh + 1) * 64], in1=self.E0T[:, gi, h, :], op=ALU.mult)
                            V.tensor_scalar(out=self.vb[:, gi, h, :], in0=vtv[:, h, :], scalar1=self.beta[:, gi, h:h + 1], scalar2=None, op0=ALU.mult)
                            V.tensor_scalar(out=self.kbg[:, gi, h, :], in0=ktv[:, kh, :], scalar1=self.bg[:, gi, h:h + 1], scalar2=None, op0=ALU.mult)
                            V.tensor_scalar(out=self.kdk[:, gi, h, :], in0=ktv[:, kh, :], scalar1=self.E0T[:, gi, h, 63:64], scalar2=None, op0=ALU.mult)
                            ins = V.tensor_tensor(out=self.qgT[:, gi, h, :], in0=self.qnT[:, kh, csl(gi)], in1=self.eR[:, gi // 2, ((gi % 2) * 4 + h) * 64:((gi % 2) * 4 + h + 1) * 64],
                                                  op=ALU.mult)
                    return ins
                step("V", f)

                def f():
                    for gi in range(G):
                        bk, c0 = bank_col(2, gi, 256)
                        for h in range(4):
                            ins = PE.matmul(ps[0:64, bk, c0 + h * 64:c0 + (h + 1) * 64], lhsT=self.X[:, gi, h, :], rhs=self.idb[0:64, 0:64], start=True, stop=True)
                    return ins
                step("P", f)
                step("A", lambda: A.activation(out=fl4(self.Y).rearrange("p (b c) -> p b c", b=2), in_=ps[0:64, 2:4, :], func=AF.Copy))
                def f():
                    V.tensor_tensor(out=fl4(self.Q).rearrange("p (b c) -> p b c", b=2), in0=ps[0:64, 2:4, :],
                                    in1=fl(self.I4_sb[:, :, :]).rearrange("p (b c) -> p b c", b=2), op=ALU.add)
                    return V.tensor_copy(out=fl4(self.Tt), in_=fl4(self.Q))
                step("V", f)
                for lv in range(6):
                    def f(lv=lv):
                        ins = None
                        for gi in range(G):
                            for h in range(4):
                                if lv <= 4:
                                    bk, c0 = bank_col(0, gi, 256)
                                    ins = PE.matmul(ps[0:64, bk, c0 + h * 64:c0 + (h + 1) * 64], lhsT=self.Y[:, gi, h, :], rhs=self.X[:, gi, h, :], start=True, stop=True)
                                if lv <= 3:
                                    bk, c0 = bank_col(2, gi, 256)
                                    ins = PE.matmul(ps[0:64, bk, c0 + h * 64:c0 + (h + 1) * 64], lhsT=self.X[:, gi, h, :], rhs=self.Y[:, gi, h, :], start=True, stop=True)
                                if lv >= 1:
                                    bk, c0 = bank_col(4, gi, 256)
                                    ins = PE.matmul(ps[0:64, bk, c0 + h * 64:c0 + (h + 1) * 64], lhsT=self.X[:, gi, h, :], rhs=self.Tt[:, gi, h, :], start=True, stop=True)
                        return ins
                    step("P", f)
                    if lv <= 3:
                        step("A", lambda: A.activation(out=fl4(self.Y).rearrange("p (b c) -> p b c", b=2), in_=ps[0:64, 2:4, :], func=AF.Copy))

                    def f(lv=lv):
                        ins = None
                        if lv >= 1:
                            V.tensor_tensor(out=fl4(self.Q).rearrange("p (b c) -> p b c", b=2), in0=fl4(self.Q).rearrange("p (b c) -> p b c", b=2),
                                            in1=ps[0:64, 4:6, :], op=ALU.add)
                            ins = V.tensor_copy(out=fl4(self.Tt), in_=fl4(self.Q))
                        if lv <= 4:
                            ins = V.tensor_copy(out=fl4(self.X).rearrange("p (b c) -> p b c", b=2), in_=ps[0:64, 0:2, :])
                        return ins
                    step("V", f)

                def f():
                    for gi in range(G):
                        bk, c0 = bank_col(6, gi, 256)
                        for h in range(4):
                            ins = PE.matmul(ps[:, bk, c0 + h * 64:c0 + (h + 1) * 64], lhsT=self.kbg[:, gi, h, :], rhs=self.Tt[:, gi, h, :], start=True, stop=True)
                    return ins
                step("P", f)
                step("A", lambda: A.activation(out=fl4(self.nwT).rearrange("p (b c) -> p b c", b=2), in_=ps[:, 6:8, :], func=AF.Copy, scale=-1.0))

                for gi in range(G):
                    def f(gi=gi):
                        for h in range(4):
                            PE.matmul(ps[0:64, 0, h * 128:(h + 1) * 128], lhsT=self.Tt[:, gi, h, :], rhs=self.vb[:, gi, h, :], start=(h == 0), stop=False, skip_group_check=True)
                        for h in range(4):
                            ins = PE.matmul(ps[0:64, 0, h * 128:(h + 1) * 128], lhsT=self.nwT[:, gi, h, :], rhs=self.Stb[:, h, :], start=False, stop=(h == 3), skip_group_check=True)
                        return ins
                    step("P", f)
                    step("V", lambda: V.tensor_copy(out=fl(self.vnew[:, :, :]), in_=ps[0:64, 0, :]))

                    def f(gi=gi):
                        for h in range(4):
                            PE.matmul(ps[:, 1, h * 64:(h + 1) * 64], lhsT=self.Stb[:, h, :], rhs=self.qgT[:, gi, h, :], start=(h == 0), stop=False, skip_group_check=True)
                        for h in range(4):
                            PE.matmul(ps[:, 1, h * 64:(h + 1) * 64], lhsT=self.vnew[:, h, :], rhs=self.intraT[:, gi, h, :], start=False, stop=(h == 3), skip_group_check=True)
                        for h in range(4):
                            ins = PE.matmul(ps[:, 2, h * 128:(h + 1) * 128], lhsT=self.kdk[:, gi, h, :], rhs=self.vnew[:, h, :], start=True, stop=True)
                        return ins
                    step("P", f)
                    extra = [(s_o[sl].h, s_o[sl].n)] if (grp == 0 and gi == 0) else []
                    step("A", lambda gi=gi: A.activation(out=self.ob[:, sl, :, csl(gi)], in_=ps[:, 1, 0:256].rearrange("p (h i) -> p h i", h=4), func=AF.Copy), extra=extra)

                    def f(gi=gi):
                        for h in range(4):
                            V.scalar_tensor_tensor(out=self.St[:, h, :], in0=self.St[:, h, :], scalar=self.egl[:, gi, h:h + 1], in1=ps[:, 2, h * 128:(h + 1) * 128],
                                                   op0=ALU.mult, op1=ALU.add)
                        return V.tensor_copy(out=self.Stb[:, :, :], in_=self.St[:, :, :])
                    step("V", f)
            sp_wait_all()
            s_o[sl].inc(SP.dma_start(out=self.og.rearrange("(h p) t -> p h t", p=128)[:, :, t0:t0 + BT], in_=self.ob[:, sl, :, :]), 16)
        for s in s_o:
            SP.wait_ge(s.h, s.n)
        return nc


def mixc_consts():
    U = (np.arange(64)[:, None] <= np.arange(64)[None, :]).astype(np.float32)
    p = np.arange(64)[:, None, None]; f = np.arange(64)[None, None, :]
    MU = np.broadcast_to((f >= p), (64, 4 * G, 64)).astype(np.float32)
    ML = np.broadcast_to((p > f), (64, 4 * G, 64)).astype(np.float32)
    I4 = np.broadcast_to((p == f), (64, 4 * G, 64)).astype(np.float32)
    return U, np.ascontiguousarray(MU), np.ascontiguousarray(ML), np.ascontiguousarray(I4)


def mixc_inputs(inp, c, xT, consts):
    U, MU, ML, I4 = consts
    W = inp["gdn_w_in"][0]
    kh0 = 2 * c; vh0 = 4 * c
    qc = slice(kh0 * 128, kh0 * 128 + 256)
    kc = slice(2048 + kh0 * 128, 2048 + kh0 * 128 + 256)
    vc = slice(4096 + vh0 * 128, 4096 + vh0 * 128 + 512)
    bc = slice(12288 + vh0, 12288 + vh0 + 4)
    ac = slice(12320 + vh0, 12320 + vh0 + 4)
    wC = np.concatenate([W[:, qc], W[:, kc], W[:, vc], W[:, bc], W[:, ac]], axis=1)
    cw = inp["gdn_conv"][0]
    cwc = np.concatenate([cw[:, qc], cw[:, kc], cw[:, vc]], axis=1)
    convw = np.ascontiguousarray(cwc.reshape(4, 8, 128).transpose(2, 1, 0))
    hp = np.stack([np.broadcast_to(inp["gdn_a_log"][0][vh0:vh0 + 4], (128, G, 4)), np.broadcast_to(inp["gdn_dt_bias"][0][vh0:vh0 + 4], (128, G, 4))], axis=1)
    return {"xT": xT, "wC": np.ascontiguousarray(wC), "gain": np.ascontiguousarray(inp["norm_mix"][1].reshape(16, 128).T),
            "convw": convw.astype(np.float32), "hp": np.ascontiguousarray(hp.astype(np.float32)), "U": U, "MU": MU, "ML": ML, "I4": I4,
            "ident": np.eye(128, dtype=np.float32)}


def _run(nc, ins):
    res = run_bass_kernel_spmd(nc, ins, core_ids=list(range(8)))
    return res.results


def kernel(**inputs):
    inp = {k: np.asarray(v) for k, v in inputs.items()}
    S_ = 16384
    x = inp["x"][0]
    xT = np.ascontiguousarray(x.T)
    consts = mixa_consts()
    nc = MixA().build()
    ra = _run(nc, [mixa_inputs(inp, c, xT, consts) for c in range(8)])
    oT0 = np.empty((3072, S_), np.float32)
    for c in range(8):
        hr, vh = c // 2, c % 2
        oT0[hr * 512 + vh * 256: hr * 512 + (vh + 1) * 256] = ra[c]["oret"]
        oT0[2048 + c * 128: 2048 + (c + 1) * 128] = ra[c]["odil"]
    del ra
    nc = Post(0).build()
    rb = _run(nc, [post_inputs(0, inp, np.ascontiguousarray(xT[:, c * 2048:(c + 1) * 2048]), np.ascontiguousarray(oT0[:, c * 2048:(c + 1) * 2048]))
                   for c in range(8)])
    x1T = np.ascontiguousarray(np.concatenate([rb[c]["xo"] for c in range(8)], axis=1))
    del rb, oT0
    cc = mixc_consts()
    nc = MixC().build()
    rc = _run(nc, [mixc_inputs(inp, c, x1T, cc) for c in range(8)])
    oT1 = np.ascontiguousarray(np.concatenate([rc[c]["og"] for c in range(8)], axis=0))
    del rc
    nc = Post(1).build()
    rd = _run(nc, [post_inputs(1, inp, np.ascontiguousarray(x1T[:, c * 2048:(c + 1) * 2048]), np.ascontiguousarray(oT1[:, c * 2048:(c + 1) * 2048]))
                   for c in range(8)])
    outT = np.concatenate([rd[c]["xo"] for c in range(8)], axis=1)
    return np.ascontiguousarray(outT.T)[None].astype(np.float32)
```

```python
import math
import numpy as np
from contextlib import ExitStack
from concourse.bass_utils import run_bass_kernel_spmd
import concourse.bass as bass
import concourse.mybir as mybir

F32 = mybir.dt.float32
BF16 = mybir.dt.bfloat16
AF = mybir.ActivationFunctionType
ALU = mybir.AluOpType
EPS = 1e-6


class Sem:
    def __init__(self, h):
        self.h = h
        self.n = 0

    def inc(self, ins, by=1):
        ins.then_inc(self.h, by)
        self.n += by
        return self.n


class KB:
    def __init__(self):
        self.nc = bass.Bass("TRN2", target_bir_lowering=False)
        self.es = ExitStack()
        self.sems = {}
        self.uid = 0

    def sb(self, name, shape, dt, es=None):
        return (es or self.es).enter_context(self.nc.sbuf_tensor(name, shape, dt))

    def psum(self, name, shape, dt, es=None):
        return (es or self.es).enter_context(self.nc.psum_tensor(name, shape, dt))

    def sem(self, name):
        if name not in self.sems:
            self.sems[name] = Sem(self.es.enter_context(self.nc.semaphore(name)))
        return self.sems[name]

    def din(self, name, shape, dt=F32):
        return self.nc.dram_tensor(name, list(shape), dt, kind="ExternalInput").ap()

    def dout(self, name, shape, dt=F32):
        return self.nc.dram_tensor(name, list(shape), dt, kind="ExternalOutput").ap()

    def dscr(self, name, shape, dt=F32):
        return self.nc.dram_tensor(name, list(shape), dt, kind="Internal").ap()


D = 2048
T = 2048
TT = 1024
NT = TT // 512
FF = 5632
WB = 8192


class Post:
    def __init__(self, layer):
        self.layer = layer
        self.FY = 3072 if layer == 0 else 4096
        self.G = 2048 if layer == 0 else 4096
        self.kb = kb = KB()
        nc = self.nc = kb.nc
        FY, G = self.FY, self.G
        self.xT = kb.din("xT", [D, T])
        self.oT = kb.din("oT", [FY, T])
        self.w_gate = kb.din("w_gate", [D, G])
        self.w_out = kb.din("w_out", [FY, D])
        self.gains = kb.din("gains", [128, 4, 16])
        self.hg = kb.din("hg", [128, 3])
        self.memT = kb.din("memT", [D, 256])
        self.w_q = kb.din("w_q", [D, 512])
        self.w_kv = kb.din("w_kv", [D, 1024])
        self.w_o = kb.din("w_o", [512, D])
        self.w1 = kb.din("w1", [D, FF])
        self.w3 = kb.din("w3", [D, FF])
        self.w2 = kb.din("w2", [FF, D])
        self.xo = kb.dout("xo", [D, T])
        self.x1 = kb.dout("x1s", [D, T])
        self.x2 = kb.dout("x2s", [D, T])
        self.wbuf = [kb.sb("wbuf0", [128, WB], BF16), kb.sb("wbuf1", [128, WB], BF16)]
        self.ones = kb.sb("ones", [128, 128], BF16)
        self.gains_sb = kb.sb("gains_sb", [128, 4, 16], F32)
        self.hg_sb = kb.sb("hg_sb", [128, 3], F32)
        self.eps_sb = kb.sb("eps_sb", [128, 1], F32)
        self.eps2_sb = kb.sb("eps2_sb", [128, 1], F32)
        self.hT = kb.sb("hT", [128, 16, TT], BF16)
        self.big = kb.sb("big", [128, 32 * TT], BF16)
        self.xst_f = kb.sb("xst", [128, 4096], F32)
        self.sq_f = kb.sb("sq", [128, 4096], BF16)
        self.xst_n = self.xst_f[:, :].rearrange("p (c t) -> p c t", t=256)
        self.sq_n = self.sq_f[:, :].rearrange("p (c t) -> p c t", t=256)
        self.xst = self.xst_f[:, 0:2048].rearrange("p (c t) -> p c t", t=512)
        self.sq = self.sq_f[:, 0:2048].rearrange("p (c t) -> p c t", t=512)
        self.qf = self.xst
        self.rstd = kb.sb("rstd", [128, 4, 512], F32)
        self.rbuf = kb.sb("rbuf", [128, 3, 512], F32)
        self.obuf = kb.sb("obuf", [128, 3, 512], F32)
        self.stmp = kb.sb("stmp", [128, 2, 512], F32)
        self.knT = kb.sb("knT", [128, 4, 256], BF16)
        self.vm = kb.sb("vm", [128, 2, 512], BF16)
        self.qn = self.big[:, 0:4 * TT].rearrange("p (c t) -> p c t", t=TT)
        self.pT = kb.sb("pT", [128, 2, 512], BF16)
        self.oxa = self.big[:, 4 * TT:8 * TT].rearrange("p (c t) -> p c t", t=TT)
        self.ps = kb.psum("ps", [128, 8, 512], F32)
        self.gidx = 0
        self.grp_end = []
        self.mm = kb.sem("g_mm")
        self.pf = kb.sem("g_pf")
        self.wl = [kb.sem("g_wl0"), kb.sem("g_wl1")]
        self.sts = [kb.sem("st%d" % i) for i in range(3)]
        self.bar_n = 0

    def wait_stores(self):
        for st in self.sts:
            self.nc.sync.wait_ge(st.h, st.n)

    def barrier(self):
        self.nc.all_engine_barrier()

    def y(self):
        return self.big[:, 0:(self.FY // 128) * TT].rearrange("p (c t) -> p c t", t=TT)

    def g(self):
        return self.big[:, 0:22 * TT].rearrange("p (c t) -> p c t", t=TT)

    def rstd_op(self, ps_ap, out_ap, inv_n, wait, post=1.0):
        nc = self.nc
        s = self.kb.sem("r_a")
        nc.scalar.wait_ge(wait[0], wait[1])
        s.inc(nc.scalar.activation(out=out_ap, in_=ps_ap, func=AF.Sqrt, scale=inv_n / post ** 2, bias=self.eps_sb[:, 0:1] if post == 1.0 else self.eps2_sb[:, 0:1]))
        nc.vector.wait_ge(s.h, s.n)
        return nc.vector.reciprocal(out=out_ap, in_=out_ap)

    def gemm(self, wsrc, KC, GW, ngroups, act, ntt, epi, pair=False, tw=512, pe_waits=()):
        nc = self.nc
        mpg = GW // 128
        if pair:
            mpg //= 2
        G0 = len(self.grp_end)

        def load(g):
            Gg = G0 + g
            b = Gg % 2
            if Gg >= 2:
                nc.gpsimd.wait_ge(self.mm.h, self.grp_end[Gg - 2])
            wv = self.wbuf[b][:, 0:KC * GW].rearrange("p (c n) -> p c n", n=GW)
            for (ap, off, w) in wsrc(g):
                src = ap.rearrange("(c p) n -> p c n", p=128)
                kstep = 8
                for k0 in range(0, KC, kstep):
                    k1 = min(KC, k0 + kstep)
                    self.wl[b].inc(nc.gpsimd.dma_start(out=wv[:, k0:k1, off:off + w], in_=src[:, k0:k1, :]), 16)
            return self.wl[b].n

        wl_need = {}
        wl_need[0] = load(0)
        cnt = 0
        for (sh, sv) in pe_waits:
            nc.tensor.wait_ge(sh, sv)
        for g in range(ngroups):
            if g + 1 < ngroups:
                wl_need[g + 1] = load(g + 1)
            b = (G0 + g) % 2
            nc.tensor.wait_ge(self.wl[b].h, wl_need[g])
            wv = self.wbuf[b][:, 0:KC * GW].rearrange("p (c n) -> p c n", n=GW)
            for j in range(mpg):
                for tt in range(ntt):
                    cols = [j] if not pair else [j, j + mpg]
                    ps_list = []
                    for cj in cols:
                        idx = self.gidx
                        bank = idx % 4
                        if idx >= 4:
                            nc.tensor.wait_ge(self.pf.h, idx - 3)
                        for k in range(KC):
                            ins = nc.tensor.matmul(self.ps[:, bank, 0:tw], lhsT=wv[:, k, cj * 128:(cj + 1) * 128],
                                                   rhs=act[:, k, tt * tw:(tt + 1) * tw], start=(k == 0), stop=(k == KC - 1))
                        self.mm.inc(ins)
                        self.gidx += 1
                        ps_list.append(self.ps[:, bank, 0:tw])
                    fin = epi(cnt, g * mpg + j, tt, ps_list, self.gidx)
                    self.pf.inc(fin, len(cols))
                    cnt += 1
            self.grp_end.append(self.mm.n)

    def norm(self, src, tok0, which, dst, ntok_tiles, tw=256, gains=None):
        nc = self.nc
        s_ld = self.kb.sem("n_ld"); s_sq = self.kb.sem("n_sq"); s_mm = self.kb.sem("n_mm"); s_dv = self.kb.sem("n_dv")
        for tt in range(ntok_tiles):
            t0 = tok0 + tt * tw
            nc.sync.wait_ge(s_dv.h, s_dv.n)
            srcv = src.rearrange("(c p) t -> p c t", p=128)
            for hh in range(2):
                s_ld.inc(nc.sync.dma_start(out=self.xst_n[:, hh * 8:(hh + 1) * 8, 0:tw], in_=srcv[:, hh * 8:(hh + 1) * 8, t0:t0 + tw]), 16)
            nc.scalar.wait_ge(s_ld.h, s_ld.n)
            nc.scalar.wait_ge(s_mm.h, s_mm.n)
            s_sq.inc(nc.scalar.activation(out=self.sq_n[:, :, 0:tw], in_=self.xst_n[:, :, 0:tw], func=AF.Square))
            nc.tensor.wait_ge(s_sq.h, s_sq.n)
            nc.tensor.wait_ge(s_dv.h, s_dv.n)
            for k in range(16):
                ins = nc.tensor.matmul(self.ps[:, 4, 0:tw], lhsT=self.ones[:, :], rhs=self.sq_n[:, k, 0:tw], start=(k == 0), stop=(k == 15))
            s_mm.inc(ins)
            self.rstd_op(self.ps[:, 4, 0:tw], self.rstd[:, 0, 0:tw], 1.0 / D, (s_mm.h, s_mm.n))
            for k in range(16):
                ins = nc.vector.scalar_tensor_tensor(out=dst[:, k, tt * tw:(tt + 1) * tw], in0=self.xst_n[:, k, 0:tw],
                                                     scalar=self.gains_sb[:, which, k:k + 1], in1=self.rstd[:, 0, 0:tw],
                                                     op0=ALU.mult, op1=ALU.mult)
            s_dv.inc(ins)
        return s_dv

    def onorm(self, tok0):
        nc = self.nc
        layer = self.layer
        y = self.y()
        s_ld = self.kb.sem("o_ld"); s_sq = self.kb.sem("o_sq"); s_mm = self.kb.sem("o_mm"); s_dv = self.kb.sem("o_dv")
        nblk = self.FY // 512
        nnorm = 4 if layer == 0 else 8
        ov = self.oT.rearrange("(c p) t -> p c t", p=128)
        for tt in range(NT):
            t0 = tok0 + tt * 512
            for blk in range(nblk):
                nc.sync.wait_ge(s_dv.h, s_dv.n)
                s_ld.inc(nc.sync.dma_start(out=self.xst[:, 0:4, :], in_=ov[:, blk * 4:(blk + 1) * 4, t0:t0 + 512]), 16)
                if blk >= nnorm:
                    nc.vector.wait_ge(s_ld.h, s_ld.n)
                    ins = nc.vector.tensor_copy(out=y[:, blk * 4:(blk + 1) * 4, tt * 512:(tt + 1) * 512], in_=self.xst[:, 0:4, :])
                    s_dv.inc(ins)
                    continue
                nc.scalar.wait_ge(s_ld.h, s_ld.n)
                nc.scalar.wait_ge(s_mm.h, s_mm.n)
                s_sq.inc(nc.scalar.activation(out=self.sq[:, 0:4, :], in_=self.xst[:, 0:4, :], func=AF.Square))
                nc.tensor.wait_ge(s_sq.h, s_sq.n)
                nc.tensor.wait_ge(s_dv.h, s_dv.n)
                if layer == 0:
                    for k in range(4):
                        ins = nc.tensor.matmul(self.ps[:, 4, :], lhsT=self.ones[:, :], rhs=self.sq[:, k, :], start=(k == 0), stop=(k == 3))
                else:
                    for k in range(4):
                        ins = nc.tensor.matmul(self.ps[:, 4 + k, :], lhsT=self.ones[:, :], rhs=self.sq[:, k, :], start=True, stop=True)
                s_mm.inc(ins)
                if layer == 0:
                    self.rstd_op(self.ps[:, 4, :], self.rstd[:, 0, :], 1.0 / 512, (s_mm.h, s_mm.n))
                    for k in range(4):
                        ins = nc.vector.tensor_tensor(out=y[:, blk * 4 + k, tt * 512:(tt + 1) * 512], in0=self.xst[:, k, :], in1=self.rstd[:, 0, :], op=ALU.mult)
                else:
                    for k in range(4):
                        self.rstd_op(self.ps[:, 4 + k, :], self.rstd[:, k, :], 1.0 / 128, (s_mm.h, s_mm.n))
                        ins = nc.vector.scalar_tensor_tensor(out=y[:, blk * 4 + k, tt * 512:(tt + 1) * 512], in0=self.xst[:, k, :],
                                                             scalar=self.hg_sb[:, 2:3], in1=self.rstd[:, k, :], op0=ALU.mult, op1=ALU.mult)
                s_dv.inc(ins)

    def epi_gate(self):
        nc = self.nc
        y = self.y()
        s_d = self.kb.sem("eg_d")
        base_d = s_d.n

        def epi(cnt, mt, tt, ps_list, idx_after):
            s = cnt % 2
            nc.scalar.wait_ge(self.mm.h, idx_after)
            if cnt >= 2:
                nc.scalar.wait_ge(s_d.h, base_d + cnt - 1)
            fin = nc.scalar.activation(out=self.stmp[:, s, :], in_=ps_list[0], func=AF.Silu)
            nc.vector.wait_ge(self.pf.h, idx_after)
            yv = y[:, mt, tt * 512:(tt + 1) * 512]
            s_d.inc(nc.vector.tensor_tensor(out=yv, in0=self.stmp[:, s, :], in1=yv, op=ALU.mult))
            return fin
        return epi

    def epi_swiglu(self):
        nc = self.nc
        g = self.g()
        s_a = self.kb.sem("es_a")
        hist = []

        def epi(cnt, mt, tt, ps_list, idx_after):
            s = cnt % 2
            nc.scalar.wait_ge(self.mm.h, idx_after)
            if cnt >= 2:
                nc.scalar.wait_ge(self.pf.h, hist[cnt - 2])
            s_a.inc(nc.scalar.activation(out=self.stmp[:, s, :], in_=ps_list[0], func=AF.Silu))
            nc.vector.wait_ge(s_a.h, s_a.n)
            fin = nc.vector.tensor_tensor(out=g[:, mt, tt * 512:(tt + 1) * 512], in0=self.stmp[:, s, :], in1=ps_list[1], op=ALU.mult)
            hist.append(idx_after)
            return fin
        return epi

    def epi_resid(self, res_src, dst, tok0, tiles):
        nc = self.nc
        rv = res_src.rearrange("(c p) t -> p c t", p=128)
        dv = dst.rearrange("(c p) t -> p c t", p=128)
        idx0 = self.gidx
        rls = [self.kb.sem("rl%d" % i) for i in range(3)]
        sts = self.sts

        def issue_load(c):
            mt, tt = tiles[c]
            if c >= 3:
                nc.sync.wait_ge(self.pf.h, idx0 + c - 2)
            rls[c % 3].inc(nc.sync.dma_start(out=self.rbuf[:, c % 3, :], in_=rv[:, mt, tok0 + tt * 512: tok0 + (tt + 1) * 512]), 16)

        def epi(cnt, mt, tt, ps_list, idx_after):
            if cnt == 0:
                issue_load(0)
                if len(tiles) > 1:
                    issue_load(1)
            if cnt + 2 < len(tiles):
                issue_load(cnt + 2)
            s = cnt % 3
            nc.vector.wait_ge(self.mm.h, idx_after)
            nc.vector.wait_ge(rls[s].h, rls[s].n if cnt + 3 >= len(tiles) or True else 0)
            nc.vector.wait_ge(sts[s].h, sts[s].n)
            fin = nc.vector.tensor_tensor(out=self.obuf[:, s, :], in0=ps_list[0], in1=self.rbuf[:, s, :], op=ALU.add)
            nc.sync.wait_ge(self.pf.h, idx_after)
            sts[s].inc(nc.sync.dma_start(out=dv[:, mt, tok0 + tt * 512: tok0 + (tt + 1) * 512], in_=self.obuf[:, s, :]), 16)
            return fin
        return epi

    def epi_plain(self, dstf):
        nc = self.nc

        def epi(cnt, mt, tt, ps_list, idx_after):
            nc.vector.wait_ge(self.mm.h, idx_after)
            return nc.vector.tensor_copy(out=dstf(mt, tt), in_=ps_list[0])
        return epi

    def mem_kv(self):
        nc = self.nc
        s_ld = self.kb.sem("m_ld"); s_a = self.kb.sem("m_a"); s_p = self.kb.sem("m_p"); s_d = self.kb.sem("m_d")
        memn = self.hT[:, :, 0:256]
        ndv = self.norm(self.memT, 0, 3, self.hT, 1, tw=256)
        self.barrier()
        self.gemm(lambda g: [(self.w_kv[:, 0:512], 0, 512)], 16, 512, 1, memn, 1,
                  self.epi_plain(lambda mt, tt: self.qf[:, mt, 0:256]), tw=256, pe_waits=[(ndv.h, ndv.n)])
        self.barrier()
        nc.scalar.wait_ge(self.pf.h, self.gidx)
        nc.scalar.activation(out=self.sq[:, 0:4, 0:256], in_=self.qf[:, 0:4, 0:256], func=AF.Square).then_inc(s_a.h, 1)
        nc.tensor.wait_ge(s_a.h, 1)
        for h in range(4):
            ins = nc.tensor.matmul(self.ps[:, 4 + h, 0:256], lhsT=self.ones[:, :], rhs=self.sq[:, h, 0:256], start=True, stop=True)
        ins.then_inc(s_p.h, 1)
        for h in range(4):
            self.rstd_op(self.ps[:, 4 + h, 0:256], self.rstd[:, h, 0:256], 1.0 / 128, (s_p.h, 1))
            nc.vector.scalar_tensor_tensor(out=self.knT[:, h, :], in0=self.qf[:, h, 0:256], scalar=self.hg_sb[:, 1:2], in1=self.rstd[:, h, 0:256],
                                           op0=ALU.mult, op1=ALU.mult)
        self.barrier()
        wv = self.wbuf[0][:, 0:16 * 512].rearrange("p (c n) -> p c n", n=512)
        src = self.w_kv[:, 512:1024].rearrange("(c p) n -> p c n", p=128)
        for k0 in (0, 8):
            nc.gpsimd.dma_start(out=wv[:, k0:k0 + 8, :], in_=src[:, k0:k0 + 8, :]).then_inc(s_ld.h, 16)
        nc.tensor.wait_ge(s_ld.h, 32)
        for c in range(2):
            for k in range(16):
                ins = nc.tensor.matmul(self.ps[:, 4 + c, :], lhsT=self.hT[:, k, c * 128:(c + 1) * 128], rhs=wv[:, k, :], start=(k == 0), stop=(k == 15))
        ins.then_inc(s_p.h, 1)
        nc.vector.wait_ge(s_p.h, 2)
        for c in range(2):
            ins = nc.vector.tensor_copy(out=self.vm[:, c, :], in_=self.ps[:, 4 + c, :])
        self.barrier()

    def xa_attn(self):
        nc = self.nc
        s_q = self.kb.sem("x_q"); s_a = self.kb.sem("x_a"); s_p = self.kb.sem("x_p"); s_d = self.kb.sem("x_d")
        scale = 128 ** -0.5
        for tt in range(NT):
            self.gemm(lambda g: [(self.w_q[:, :], 0, 512)], 16, 512, 1, self.hT[:, :, tt * 512:(tt + 1) * 512], 1,
                      self.epi_plain(lambda mt, t_: self.qf[:, mt, :]), pe_waits=[(self.kb.sem("n_dv").h, self.kb.sem("n_dv").n)])
            self.barrier()
            nc.scalar.wait_ge(self.pf.h, self.gidx)
            s_a.inc(nc.scalar.activation(out=self.sq[:, 0:4, :], in_=self.qf[:, 0:4, :], func=AF.Square))
            nc.tensor.wait_ge(s_a.h, s_a.n)
            for h in range(4):
                ins = nc.tensor.matmul(self.ps[:, 4 + h, :], lhsT=self.ones[:, :], rhs=self.sq[:, h, :], start=True, stop=True)
            s_p.inc(ins)
            for h in range(4):
                self.rstd_op(self.ps[:, 4 + h, :], self.rstd[:, h, :], 1.0 / 128, (s_p.h, s_p.n), post=scale)
                ins = nc.vector.scalar_tensor_tensor(out=self.qn[:, h, tt * 512:(tt + 1) * 512], in0=self.qf[:, h, :], scalar=self.hg_sb[:, 0:1],
                                                     in1=self.rstd[:, h, :], op0=ALU.mult, op1=ALU.mult)
            s_d.inc(ins)
            nc.tensor.wait_ge(s_d.h, s_d.n)
            nc.scalar.wait_ge(s_d.h, s_d.n)
            self.barrier()
            for h in range(4):
                for c in range(2):
                    ins = nc.tensor.matmul(self.ps[:, 4 + c, :], lhsT=self.knT[:, h, c * 128:(c + 1) * 128], rhs=self.qn[:, h, tt * 512:(tt + 1) * 512],
                                           start=True, stop=True)
                s_p.inc(ins)
                nc.scalar.wait_ge(s_p.h, s_p.n)
                for c in range(2):
                    ins = nc.scalar.activation(out=self.pT[:, c, :], in_=self.ps[:, 4 + c, :], func=AF.Exp)
                s_a.inc(ins)
                nc.tensor.wait_ge(s_a.h, s_a.n)
                for c in range(2):
                    nc.tensor.matmul(self.ps[:, 6, :], lhsT=self.vm[:, c, h * 128:(h + 1) * 128], rhs=self.pT[:, c, :], start=(c == 0), stop=(c == 1))
                for c in range(2):
                    ins = nc.tensor.matmul(self.ps[:, 7, :], lhsT=self.ones[:, :], rhs=self.pT[:, c, :], start=(c == 0), stop=(c == 1))
                s_p.inc(ins)
                nc.vector.wait_ge(s_p.h, s_p.n)
                nc.vector.reciprocal(out=self.rstd[:, 0, :], in_=self.ps[:, 7, :])
                ins = nc.vector.tensor_tensor(out=self.oxa[:, h, tt * 512:(tt + 1) * 512], in0=self.ps[:, 6, :], in1=self.rstd[:, 0, :], op=ALU.mult)
                s_d.inc(ins)
                nc.tensor.wait_ge(s_d.h, s_d.n)
                nc.scalar.wait_ge(s_d.h, s_d.n)
            self.barrier()

    def build(self, stages=99):
        nc = self.nc
        s0 = self.kb.sem("init")
        nc.vector.memset(self.ones[:, :], 1.0)
        nc.vector.memset(self.eps_sb[:, :], EPS)
        nc.vector.memset(self.eps2_sb[:, :], EPS * 128.0)
        nc.sync.dma_start(out=self.gains_sb[:, :, :], in_=self.gains).then_inc(s0.h, 16)
        nc.sync.dma_start(out=self.hg_sb[:, :], in_=self.hg).then_inc(s0.h, 16)
        nc.sync.wait_ge(s0.h, 32)
        self.barrier()
        self.mem_kv()
        for p in range(T // TT):
            tok0 = p * TT
            y = self.y()
            self.norm(self.xT, tok0, 0, self.hT, TT // 256)
            self.barrier()
            if stages < 1:
                continue
            self.onorm(tok0)
            self.barrier()
            self.gemm(lambda g: [(self.w_gate[:, g * 512:(g + 1) * 512], 0, 512)], 16, 512, self.G // 512, self.hT, NT, self.epi_gate(),
                      pe_waits=[(self.kb.sem("n_dv").h, self.kb.sem("n_dv").n), (self.kb.sem("o_dv").h, self.kb.sem("o_dv").n)])
            self.barrier()
            if stages < 2:
                continue
            KC = self.FY // 128
            tiles = [(m, tt) for m in range(16) for tt in range(NT)]
            dst = self.x1 if stages > 2 else self.xo
            self.gemm(lambda g: [(self.w_out[:, g * 256:(g + 1) * 256], 0, 256)], KC, 256, 8, y, NT,
                      self.epi_resid(self.xT, dst, tok0, tiles), pe_waits=[(self.kb.sem("eg_d").h, self.kb.sem("eg_d").n)])
            self.wait_stores()
            self.barrier()
            if stages < 3:
                continue
            self.norm(self.x1, tok0, 1, self.hT, TT // 256)
            self.barrier()
            self.xa_attn()
            dst = self.x2 if stages > 3 else self.xo
            self.gemm(lambda g: [(self.w_o[:, :], 0, 2048)], 4, 2048, 1, self.oxa, NT,
                      self.epi_resid(self.x1, dst, tok0, tiles), pe_waits=[(self.kb.sem("x_d").h, self.kb.sem("x_d").n)])
            self.wait_stores()
            self.barrier()
            if stages < 4:
                continue
            self.norm(self.x2, tok0, 2, self.hT, TT // 256)
            self.barrier()
            for half in range(2):
                c0 = half * (FF // 2)
                self.gemm(lambda g: [(self.w1[:, c0 + g * 256:c0 + (g + 1) * 256], 0, 256), (self.w3[:, c0 + g * 256:c0 + (g + 1) * 256], 256, 256)],
                          16, 512, FF // 512, self.hT, NT, self.epi_swiglu(), pair=True,
                          pe_waits=[(self.kb.sem("n_dv").h, self.kb.sem("n_dv").n)])
                self.barrier()
                w2h = self.w2[c0:c0 + FF // 2, :]
                self.gemm(lambda g: [(w2h[:, g * 256:(g + 1) * 256], 0, 256)], 22, 256, 8, self.g(), NT,
                          self.epi_resid(self.x2 if half == 0 else self.xo, self.xo, tok0, tiles), pe_waits=[(self.pf.h, self.gidx)])
                self.wait_stores()
                self.barrier()
        return nc


def post_inputs(layer, inp, xT_c, oT_c):
    def gl(v):
        return np.ascontiguousarray(v.reshape(16, 128).T)
    gains = np.stack([gl(inp["norm_mix"][layer]), gl(inp["norm_xa"][layer]), gl(inp["norm_ffn"][layer]), gl(inp["mem_norm"])], axis=1)
    gd = inp["gdn_norm"][0]
    hg = np.stack([inp["xa_q_gain"][layer], inp["xa_k_gain"][layer], gd], axis=1)
    if layer == 0:
        w_gate = np.ascontiguousarray(inp["ar_w_in"][0][:, 4096:6144])
        w_out = inp["ar_w_out"][0]
    else:
        w_gate = np.ascontiguousarray(inp["gdn_w_in"][0][:, 8192:12288])
        w_out = inp["gdn_w_out"][0]
    return {
        "xT": xT_c, "oT": oT_c, "w_gate": w_gate, "w_out": np.ascontiguousarray(w_out),
        "gains": np.ascontiguousarray(gains.astype(np.float32)), "hg": np.ascontiguousarray(hg.astype(np.float32)),
        "memT": np.ascontiguousarray(inp["mem"][0].T),
        "w_q": np.ascontiguousarray(inp["xa_w_q"][layer]), "w_kv": np.ascontiguousarray(inp["xa_w_kv"][layer]),
        "w_o": np.ascontiguousarray(inp["xa_w_o"][layer]),
        "w1": np.ascontiguousarray(inp["ffn_w1"][layer]), "w3": np.ascontiguousarray(inp["ffn_w3"][layer]),
        "w2": np.ascontiguousarray(inp["ffn_w2"][layer]),
    }


D = 2048
S = 16384
BT = 512
NB_A = S // BT
NCOL_A = 1152
NRING = 20


class MixA:
    def __init__(self, nblocks=NB_A):
        self.nblocks = nblocks
        self.kb = kb = KB()
        nc = self.nc = kb.nc
        self.xT = kb.din("xT", [D, S])
        self.wA = kb.din("wA", [D, NCOL_A])
        self.gain = kb.din("gain", [128, 16])
        self.hg = kb.din("hg", [128, 2])
        self.cosT = kb.din("cosT", [128, S])
        self.sinT = kb.din("sinT", [128, S])
        self.dmask = kb.din("dmask", [128, 128])
        self.qdrow = kb.din("qdrow", [128, BT])
        self.kdec = kb.din("kdec", [128, 2])
        self.gtab = kb.din("gtab", [128, 17, 128])
        self.mtab = kb.din("mtab", [128, 17, 128])
        self.ident = kb.din("ident", [128, 128])
        self.oret = kb.dout("oret", [256, S])
        self.odil = kb.dout("odil", [128, S])
        sb = kb.sb
        self.w = sb("w", [128, 16, NCOL_A], BF16)
        self.ones = sb("ones", [128, 128], BF16)
        self.idb = sb("idb", [128, 128], BF16)
        self.idf = sb("idf", [128, 128], F32)
        self.gain_sb = sb("gain_sb", [128, 16], F32)
        self.hg_sb = sb("hg_sb", [128, 2], F32)
        self.eps_sb = sb("eps_sb", [128, 1], F32)
        self.eps2_sb = sb("eps2_sb", [128, 1], F32)
        self.dm = sb("dm", [128, 128], F32)
        self.qd = sb("qd", [128, BT], F32)
        self.kd = sb("kd", [128, 2], F32)
        self.E = sb("E", [128, 17, 128], F32)
        self.mt_sb = sb("mt_sb", [128, 17, 128], F32)
        self.xst = sb("xst", [128, 16, BT], F32)
        self.sq = sb("sq", [128, 16, BT], BF16)
        self.hT = sb("hT", [128, 16, BT], BF16)
        self.rstd = sb("rstd", [128, 2, BT], F32)
        self.cs = sb("cs", [128, 2, 2, BT], F32)
        self.tmp = sb("tmp", [128, 2, BT], F32)
        self.QT = sb("QT", [128, 2, BT], BF16)
        self.QdT = sb("QdT", [128, 2, BT], BF16)
        self.KT = sb("KT", [128, 2, BT], BF16)
        self.Kd = sb("Kd", [128, 4, 256], BF16)
        self.VA = sb("VA", [128, 4, 256], BF16)
        self.Sm = sb("Sm", [128, 128], BF16)
        self.St = sb("St", [128, 2, 256], F32)
        self.Stb = sb("Stb", [128, 2, 256], BF16)
        self.qnT = sb("qnT", [128, BT], BF16)
        self.knR = sb("knR", [128, NRING, 128], BF16)
        self.vbR = sb("vbR", [128, NRING, 128], BF16)
        self.ex = sb("ex", [128, 17 * 128], F32)
        self.pT = sb("pT", [128, 17 * 128], BF16)
        self.rl_ = sb("rl_", [128, 128], F32)
        self.oretb = sb("oretb", [128, 2, 2, BT], F32)
        self.odilb = sb("odilb", [128, 2, BT], F32)
        self.ps = kb.psum("ps", [128, 8, 512], F32)

    def rstd_op(self, ps_ap, out_ap, inv_n, wait, post=1.0):
        nc = self.nc
        s = self.kb.sem("r_a")
        nc.scalar.wait_ge(wait[0], wait[1])
        s.inc(nc.scalar.activation(out=out_ap, in_=ps_ap, func=AF.Sqrt, scale=inv_n / post ** 2,
                                   bias=self.eps_sb[:, 0:1] if post == 1.0 else self.eps2_sb[:, 0:1]))
        nc.vector.wait_ge(s.h, s.n)
        return nc.vector.reciprocal(out=out_ap, in_=out_ap)

    def build(self):
        nc = self.nc
        kb = self.kb
        sem = kb.sem
        ps = self.ps
        V, A, PE, SP, PL = nc.vector, nc.scalar, nc.tensor, nc.sync, nc.gpsimd

        def W(eng, s):
            eng.wait_ge(s.h, s.n)

        s0 = sem("init")
        for k0 in range(0, 16, 4):
            s0.inc(PL.dma_start(out=self.w[:, k0:k0 + 4, :], in_=self.wA.rearrange("(c p) n -> p c n", p=128)[:, k0:k0 + 4, :]), 16)
        s0.inc(PL.dma_start(out=self.idb[:, :], in_=self.ident), 16)
        s1 = sem("init1")
        for (dst, src) in [(self.gain_sb[:, :], self.gain), (self.hg_sb[:, :], self.hg), (self.dm[:, :], self.dmask), (self.qd[:, :], self.qdrow),
                           (self.kd[:, :], self.kdec), (self.E[:, :, :], self.gtab), (self.mt_sb[:, :, :], self.mtab), (self.idf[:, :], self.ident)]:
            s1.inc(SP.dma_start(out=dst, in_=src), 16)
        V.memset(self.ones[:, :], 1.0)
        V.memset(self.eps_sb[:, :], EPS)
        V.memset(self.eps2_sb[:, :], EPS * 128.0)
        V.memset(self.St[:, :, :], 0.0)
        V.memset(self.Stb[:, :, :], 0.0)
        W(A, s1)
        sE = sem("sE")
        sE.inc(A.activation(out=self.E[:, :, :], in_=self.E[:, :, :], func=AF.Exp))
        W(V, sE)
        W(V, s1)
        sE2 = sem("sE2")
        sE2.inc(V.tensor_tensor(out=self.E[:, :, :], in0=self.E[:, :, :], in1=self.mt_sb[:, :, :], op=ALU.mult))
        W(PE, s0)
        W(PE, sE2)
        W(A, sE2)

        xv = self.xT.rearrange("(c p) t -> p c t", p=128)
        s_xl = sem("xl"); s_sq = sem("a_sq"); s_ss = sem("p_ss"); s_h = sem("d_h")
        s_cl = [sem("cl0"), sem("cl1")]
        s_pj = sem("p_pj")
        s_pf = sem("pjf")
        s_rot = sem("d_rot")
        s_sq2 = sem("a_sq2"); s_ss2 = sem("p_ss2")
        s_tr = sem("p_tr"); s_kd = sem("d_kd")
        s_sc = sem("p_sc"); s_sm = sem("d_sm"); s_o = sem("p_o"); s_oe = sem("a_oe"); s_ds = sem("p_ds"); s_st = sem("d_st")
        s_qk = sem("p_qk"); s_ex = sem("a_ex"); s_p = sem("d_p"); s_pv = sem("p_pv"); s_do = sem("d_do")
        s_or = [sem("or0"), sem("or1")]; s_od = [sem("od0"), sem("od1")]
        pj_idx = [0]
        rot_hist = []

        def proj_tile(cols, width, lhs_tok=None):
            i = pj_idx[0]
            bank = i % 2
            if i >= 2:
                PE.wait_ge(s_pf.h, i - 1)
            for k in range(16):
                if lhs_tok is None:
                    ins = PE.matmul(ps[:, bank, 0:BT], lhsT=self.w[:, k, cols:cols + 128], rhs=self.hT[:, k, :], start=(k == 0), stop=(k == 15))
                else:
                    ins = PE.matmul(ps[:, bank, 0:width], lhsT=self.hT[:, k, lhs_tok * 128:(lhs_tok + 1) * 128], rhs=self.w[:, k, cols:cols + width],
                                    start=(k == 0), stop=(k == 15))
            s_pj.inc(ins)
            pj_idx[0] += 1
            return bank

        for b in range(self.nblocks):
            t0 = b * BT
            sl = b % 2
            W(SP, s_h)
            for hh in range(2):
                s_xl.inc(SP.dma_start(out=self.xst[:, hh * 8:(hh + 1) * 8, :], in_=xv[:, hh * 8:(hh + 1) * 8, t0:t0 + BT]), 16)
            if b >= 2:
                SP.wait_ge(s_rot.h, rot_hist[b - 2])
            s_cl[sl].inc(SP.dma_start(out=self.cs[:, sl, 0, :], in_=self.cosT[:, t0:t0 + BT]), 16)
            s_cl[sl].inc(SP.dma_start(out=self.cs[:, sl, 1, :], in_=self.sinT[:, t0:t0 + BT]), 16)
            W(A, s_xl)
            W(A, s_ss)
            W(A, s_ss2)
            s_sq.inc(A.activation(out=self.sq[:, :, :], in_=self.xst[:, :, :], func=AF.Square))
            W(PE, s_sq)
            for k in range(16):
                ins = PE.matmul(ps[:, 2, :], lhsT=self.ones[:, :], rhs=self.sq[:, k, :], start=(k == 0), stop=(k == 15))
            s_ss.inc(ins)
            self.rstd_op(ps[:, 2, :], self.rstd[:, 0, :], 1.0 / D, (s_ss.h, s_ss.n))
            W(V, s_pj)
            for k in range(16):
                ins = V.scalar_tensor_tensor(out=self.hT[:, k, :], in0=self.xst[:, k, :], scalar=self.gain_sb[:, k:k + 1], in1=self.rstd[:, 0, :],
                                             op0=ALU.mult, op1=ALU.mult)
            s_h.inc(ins)
            W(PE, s_h)
            W(V, s_cl[sl])
            for which, col0, dstT in ((0, 0, self.QT), (1, 256, self.KT)):
                b0 = proj_tile(col0, 128)
                b1 = proj_tile(col0 + 128, 128)
                W(V, s_pj)
                if which == 0:
                    W(V, s_o)
                    W(V, s_sc)
                else:
                    W(V, s_tr)
                    W(V, s_sc)
                cosv = self.cs[:, sl, 0, :]; sinv = self.cs[:, sl, 1, :]
                V.tensor_tensor(out=self.tmp[:, 0, :], in0=ps[:, b0, :], in1=cosv, op=ALU.mult)
                V.tensor_tensor(out=self.tmp[:, 1, :], in0=ps[:, b1, :], in1=sinv, op=ALU.mult)
                V.tensor_tensor(out=dstT[:, 0, :], in0=self.tmp[:, 0, :], in1=self.tmp[:, 1, :], op=ALU.subtract)
                V.tensor_tensor(out=self.tmp[:, 0, :], in0=ps[:, b0, :], in1=sinv, op=ALU.mult)
                ins = V.tensor_tensor(out=self.tmp[:, 1, :], in0=ps[:, b1, :], in1=cosv, op=ALU.mult)
                s_pf.inc(ins, 2)
                ins = V.tensor_tensor(out=dstT[:, 1, :], in0=self.tmp[:, 0, :], in1=self.tmp[:, 1, :], op=ALU.add)
                if which == 0:
                    for i in range(2):
                        ins = V.tensor_tensor(out=self.QdT[:, i, :], in0=self.QT[:, i, :], in1=self.qd[:, :], op=ALU.mult)
                s_rot.inc(ins)
            rot_hist.append(s_rot.n)
            for which, col0 in ((0, 512), (1, 640)):
                bk = proj_tile(col0, 128)
                W(A, s_pj)
                W(A, s_ss2)
                s_sq2.inc(A.activation(out=self.sq[:, 0, :], in_=ps[:, bk, :], func=AF.Square))
                W(PE, s_sq2)
                ins = PE.matmul(ps[:, 2, :], lhsT=self.ones[:, :], rhs=self.sq[:, 0, :], start=True, stop=True)
                s_ss2.inc(ins)
                if which == 0:
                    self.rstd_op(ps[:, 2, :], self.rstd[:, 1, :], 1.0 / 128, (s_ss2.h, s_ss2.n), post=128 ** -0.5)
                    W(V, s_pv)
                    W(V, s_qk)
                    ins = V.scalar_tensor_tensor(out=self.qnT[:, :], in0=ps[:, bk, :], scalar=self.hg_sb[:, 0:1], in1=self.rstd[:, 1, :],
                                                 op0=ALU.mult, op1=ALU.mult)
                else:
                    self.rstd_op(ps[:, 2, :], self.rstd[:, 1, :], 1.0 / 128, (s_ss2.h, s_ss2.n))
                    W(V, s_qk)
                    for j in range(4):
                        slot = (4 * b + j) % NRING
                        ins = V.scalar_tensor_tensor(out=self.knR[:, slot, :], in0=ps[:, bk, j * 128:(j + 1) * 128], scalar=self.hg_sb[:, 1:2],
                                                     in1=self.rstd[:, 1, j * 128:(j + 1) * 128], op0=ALU.mult, op1=ALU.mult)
                s_pf.inc(ins, 1)
            for c in range(4):
                bk = proj_tile(768, 384, lhs_tok=c)
                W(V, s_pj)
                if c == 0:
                    W(V, s_ds)
                    W(V, s_o)
                    W(V, s_pv)
                V.tensor_copy(out=self.VA[:, c, :], in_=ps[:, bk, 0:256])
                ins = V.tensor_copy(out=self.vbR[:, (4 * b + c) % NRING, :], in_=ps[:, bk, 256:384])
                s_pf.inc(ins, 1)
            W(PE, s_rot)
            for c in range(4):
                W(PE, s_kd)
                for i in range(2):
                    ins = PE.matmul(ps[:, 3, i * 128:(i + 1) * 128], lhsT=self.KT[:, i, c * 128:(c + 1) * 128], rhs=self.idb[:, :], start=True, stop=True)
                s_tr.inc(ins)
                W(V, s_tr)
                if c == 0:
                    W(V, s_ds)
                for i in range(2):
                    ins = V.tensor_scalar(out=self.Kd[:, c, i * 128:(i + 1) * 128], in0=ps[:, 3, i * 128:(i + 1) * 128], scalar1=self.kd[:, 0:1], scalar2=None, op0=ALU.mult)
                s_kd.inc(ins)
            if b % 2 == 0 or True:
                V.wait_ge(s_or[sl].h, s_or[sl].n)
                A.wait_ge(s_or[sl].h, s_or[sl].n)
            for c in range(4):
                cs_ = slice(c * 128, (c + 1) * 128)
                W(PE, s_sm)
                W(PE, s_kd)
                for i in range(2):
                    ins = PE.matmul(ps[:, 3, 256:384], lhsT=self.KT[:, i, cs_], rhs=self.QT[:, i, cs_], start=(i == 0), stop=(i == 1))
                s_sc.inc(ins)
                W(V, s_sc)
                W(V, s_o)
                s_sm.inc(V.tensor_tensor(out=self.Sm[:, :], in0=ps[:, 3, 256:384], in1=self.dm[:, :], op=ALU.mult))
                W(PE, s_sm)
                W(PE, s_pf)
                W(PE, s_st)
                W(PE, s_oe)
                for j in range(2):
                    PE.matmul(ps[:, 4, j * 128:(j + 1) * 128], lhsT=self.VA[:, c, j * 128:(j + 1) * 128], rhs=self.Sm[:, :], start=True, stop=False)
                    for i in range(2):
                        ins = PE.matmul(ps[:, 4, j * 128:(j + 1) * 128], lhsT=self.Stb[:, i, j * 128:(j + 1) * 128], rhs=self.QdT[:, i, cs_],
                                        start=False, stop=(i == 1))
                s_o.inc(ins)
                W(A, s_o)
                for j in range(2):
                    ins = A.activation(out=self.oretb[:, sl, j, cs_], in_=ps[:, 4, j * 128:(j + 1) * 128], func=AF.Copy)
                s_oe.inc(ins)
                W(PE, s_kd)
                for i in range(2):
                    ins = PE.matmul(ps[:, 5, i * 256:(i + 1) * 256], lhsT=self.Kd[:, c, i * 128:(i + 1) * 128], rhs=self.VA[:, c, :], start=True, stop=True)
                s_ds.inc(ins)
                W(V, s_ds)
                W(V, s_o)
                for i in range(2):
                    V.scalar_tensor_tensor(out=self.St[:, i, :], in0=self.St[:, i, :], scalar=self.kd[:, 1:2], in1=ps[:, 5, i * 256:(i + 1) * 256],
                                           op0=ALU.mult, op1=ALU.add)
                ins = V.tensor_copy(out=self.Stb[:, :, :], in_=self.St[:, :, :])
                s_st.inc(ins)
            W(SP, s_oe)
            s_or[sl].inc(SP.dma_start(out=self.oret.rearrange("(j p) t -> p j t", p=128)[:, :, t0:t0 + BT], in_=self.oretb[:, sl, :, :]), 16)
            W(PE, s_pf)
            W(PE, s_oe)
            W(PE, s_st)
            W(PE, s_sm)
            W(PE, s_kd)
            V.wait_ge(s_od[sl].h, s_od[sl].n)
            sbanks = [0, 1, 2, 3, 6]
            for qt in range(4):
                tq = 4 * b + qt
                nk = min(17, tq + 1)
                W(PE, s_do)
                W(PE, s_ex)
                for o in range(nk):
                    slot = (tq - o) % NRING
                    ins = PE.matmul(ps[:, sbanks[o // 4], (o % 4) * 128:(o % 4 + 1) * 128], lhsT=self.knR[:, slot, :], rhs=self.qnT[:, qt * 128:(qt + 1) * 128],
                                    start=True, stop=True)
                s_qk.inc(ins)
                W(A, s_qk)
                W(A, s_p)
                n0 = min(nk, 16)
                ins = A.activation(out=self.ex[:, 0:n0 * 128], in_=ps[:, 0:4, :].rearrange("p b c -> p (b c)")[:, 0:n0 * 128], func=AF.Exp)
                if nk == 17:
                    ins = A.activation(out=self.ex[:, 2048:2176], in_=ps[:, 6, 0:128], func=AF.Exp)
                s_ex.inc(ins)
                W(V, s_ex)
                W(V, s_pv)
                s_p.inc(V.tensor_tensor(out=self.pT[:, 0:nk * 128], in0=self.ex[:, 0:nk * 128],
                                        in1=self.E[:, 0:nk, :].rearrange("p o q -> p (o q)"), op=ALU.mult))
                W(PE, s_p)
                for o in range(nk):
                    slot = (tq - o) % NRING
                    first = (o == 0)
                    last = (o == nk - 1)
                    PE.matmul(ps[:, 7, 0:128], lhsT=self.vbR[:, slot, :], rhs=self.pT[:, o * 128:(o + 1) * 128], start=first, stop=last, skip_group_check=True)
                    ins = PE.matmul(ps[:, 7, 128:256], lhsT=self.ones[:, :], rhs=self.pT[:, o * 128:(o + 1) * 128], start=False, stop=last, skip_group_check=True)
                s_pv.inc(ins)
                W(V, s_pv)
                V.reciprocal(out=self.rl_[:, :], in_=ps[:, 7, 128:256])
                s_do.inc(V.tensor_tensor(out=self.odilb[:, sl, qt * 128:(qt + 1) * 128], in0=ps[:, 7, 0:128], in1=self.rl_[:, :], op=ALU.mult))
            W(SP, s_do)
            s_od[sl].inc(SP.dma_start(out=self.odil[:, t0:t0 + BT], in_=self.odilb[:, sl, :]), 16)
        for s in s_or + s_od:
            W(SP, s)
        return nc


def t5_bucket_np(dist):
    exact = 16
    d = np.maximum(dist, exact).astype(np.float32)
    large = exact + (np.log(d / np.float32(exact)) / np.float32(math.log(2048 / exact)) * np.float32(32 - exact)).astype(np.int32)
    large = np.minimum(large, 31)
    return np.where(dist < exact, dist, large)


def mixa_consts():
    i = np.arange(128, dtype=np.float32)
    inv = (np.float32(10000.0) ** (-(np.arange(0, 256, 2, dtype=np.float32)) / np.float32(256))).astype(np.float32)
    pos = np.arange(S, dtype=np.float32)
    ang = (inv[:, None] * pos[None, :]).astype(np.float32)
    cosT = np.cos(ang).astype(np.float32)
    sinT = np.sin(ang).astype(np.float32)
    kj = np.arange(128)[:, None, None]
    o = np.arange(17)[None, :, None]
    qi = np.arange(128)[None, None, :]
    delta = qi - kj + 128 * o
    valid = delta >= 0
    m = ((delta <= 128) & valid).astype(np.float32) + ((delta % 4 == 0) & (delta <= 512) & valid) + ((delta % 16 == 0) & (delta <= 2048) & valid)
    bidx = t5_bucket_np(np.maximum(delta, 0))
    return cosT, sinT, m.astype(np.float32), bidx


def mixa_inputs(inp, c, xT, consts):
    cosT, sinT, mtab, bidx = consts
    hr, vh, hd = c // 2, c % 2, c
    W = inp["ar_w_in"][0]
    wA = np.concatenate([W[:, hr * 256:(hr + 1) * 256], W[:, 1024 + hr * 256:1024 + (hr + 1) * 256],
                         W[:, 6144 + hd * 128:6144 + (hd + 1) * 128], W[:, 7168 + hd * 128:7168 + (hd + 1) * 128],
                         W[:, 2048 + hr * 512 + vh * 256:2048 + hr * 512 + (vh + 1) * 256], W[:, 8192 + hd * 128:8192 + (hd + 1) * 128]], axis=1)
    gamma = 1.0 - 2.0 ** (-5.0 - hr)
    kj = np.arange(128)[:, None]; qi = np.arange(128)[None, :]
    dmask = np.where(qi >= kj, gamma ** np.maximum(qi - kj, 0), 0.0) * 256 ** -0.5
    qdrow = np.tile(gamma ** (np.arange(128) + 1.0), 4)[None, :].repeat(128, axis=0)
    kdec = np.stack([gamma ** (127.0 - np.arange(128)) * 256 ** -0.5, np.full(128, gamma ** 128.0)], axis=1)
    gtab = inp["rel_bias"][:, hd][bidx]
    return {
        "xT": xT, "wA": np.ascontiguousarray(wA), "gain": np.ascontiguousarray(inp["norm_mix"][0].reshape(16, 128).T),
        "hg": np.ascontiguousarray(np.stack([inp["dil_q_gain"][0], inp["dil_k_gain"][0]], axis=1)),
        "cosT": cosT, "sinT": sinT, "dmask": dmask.astype(np.float32), "qdrow": qdrow.astype(np.float32), "kdec": kdec.astype(np.float32),
        "gtab": np.ascontiguousarray(gtab.astype(np.float32)), "mtab": mtab, "ident": np.eye(128, dtype=np.float32),
    }


D = 2048
S = 16384
BT = 512
NB_C = S // BT
NCOL_C = 1032
C = 64
G = 4


def fl(ap):
    return ap.rearrange("p a b -> p (a b)")


def fl4(t):
    return t[:, :, :, :].rearrange("p a b c -> p (a b c)")


class MixC:
    def __init__(self, nblocks=NB_C):
        self.nblocks = nblocks
        self.kb = kb = KB()
        self.nc = kb.nc
        self.xT = kb.din("xT", [D, S])
        self.wC = kb.din("wC", [D, NCOL_C])
        self.gain = kb.din("gain", [128, 16])
        self.convw = kb.din("convw", [128, 8, 4])
        self.hp = kb.din("hp", [128, 2, G, 4])
        self.U = kb.din("U", [64, 64])
        self.MU = kb.din("MU", [64, 4 * G, 64])
        self.ML = kb.din("ML", [64, 4 * G, 64])
        self.I4 = kb.din("I4", [64, 4 * G, 64])
        self.ident = kb.din("ident", [128, 128])
        self.og = kb.dout("og", [512, S])
        sb = kb.sb
        self.w = sb("w", [128, 16, NCOL_C], BF16)
        self.ones = sb("ones", [128, 128], BF16)
        self.onesf = sb("onesf", [64, 128], F32)
        self.idb = sb("idb", [128, 128], BF16)
        self.idf = sb("idf", [128, 128], F32)
        self.gain_sb = sb("gain_sb", [128, 16], F32)
        self.cw = sb("cw", [128, 8, 4], F32)
        self.hp_sb = sb("hp_sb", [128, 2, G, 4], F32)
        self.negA = sb("negA", [128, G, 4], F32)
        self.eps_sb = sb("eps_sb", [128, 1], F32)
        self.eps2_sb = sb("eps2_sb", [128, 1], F32)
        self.one_sb = sb("one_sb", [128, 1], F32)
        self.U_sb = sb("U_sb", [64, 64], F32)
        self.MU_sb = sb("MU_sb", [64, 4 * G, 64], F32)
        self.ML_sb = sb("ML_sb", [64, 4 * G, 64], F32)
        self.I4_sb = sb("I4_sb", [64, 4 * G, 64], F32)
        self.xst = sb("xst", [128, 16, BT], F32)
        self.sq = sb("sq", [128, 8, BT], BF16)
        self.hT = sb("hT", [128, 16, BT], BF16)
        self.rstd = sb("rstd", [128, BT], F32)
        self.praw = sb("praw", [128, 3 + BT], F32)
        self.halo = sb("halo", [128, 8, 3], F32)
        self.acc = sb("acc", [128, BT], F32)
        self.cvs = sb("cvs", [128, 4, BT], F32)
        self.qnT = sb("qnT", [128, 2, BT], BF16)
        self.knT = sb("knT", [128, 2, BT], BF16)
        self.vT = sb("vT", [128, 4, BT], BF16)
        self.ktok = sb("ktok", [64, 2, 512], F32)
        self.vtok = sb("vtok", [64, G, 512], F32)
        self.xa = sb("xa", [64, G, 4], F32)
        self.beta = sb("beta", [64, G, 4], F32)
        self.nbeta = sb("nbeta", [64, G, 4], F32)
        self.g = sb("g", [64, G, 4], F32)
        self.gcc = sb("gcc", [64, G, 4], F32)
        self.egc = sb("egc", [64, G, 4], F32)
        self.bg = sb("bg", [64, G, 4], F32)
        self.egl = sb("egl", [128, G, 4], F32)
        self.zz = sb("zz", [64, 2 * G * 4 * 64], F32)
        self.G1 = self.zz[:, :].rearrange("p (g h d) -> p g h d", g=G, h=4)
        self.Zmin = self.zz[:, 0:G * 256].rearrange("p (g h d) -> p g h d", g=G, h=4)
        self.Zmax = self.zz[:, G * 256:2 * G * 256].rearrange("p (g h d) -> p g h d", g=G, h=4)
        self.E0T = sb("E0T", [64, G, 4, 64], F32)
        self.E1 = sb("E1", [64, G, 4, 64], F32)
        self.eR = sb("eR", [128, 2, 512], F32)
        self.X = sb("X", [64, G, 4, 64], BF16)
        self.Y = sb("Y", [64, G, 4, 64], BF16)
        self.Q = sb("Q", [64, G, 4, 64], F32)
        self.Tt = sb("Tt", [64, G, 4, 64], BF16)
        self.intraT = sb("intraT", [64, G, 4, 64], BF16)
        self.vb = sb("vb", [64, G, 4, 128], BF16)
        self.kbg = sb("kbg", [64, G, 4, 128], BF16)
        self.kdk = sb("kdk", [64, G, 4, 128], BF16)
        self.qgT = sb("qgT", [128, G, 4, 64], BF16)
        self.nwT = sb("nwT", [128, G, 4, 64], BF16)
        self.vnew = sb("vnew", [64, 4, 128], BF16)
        self.St = sb("St", [128, 4, 128], F32)
        self.Stb = sb("Stb", [128, 4, 128], BF16)
        self.ob = sb("ob", [128, 1, 4, BT], F32)
        self.ps = kb.psum("ps", [128, 8, 512], F32)
        self.dtb = self.hp_sb[:, 1, :, :]

    def build(self, debug=False):
        nc = self.nc
        kb = self.kb
        ps = self.ps
        V, A, PE, SP, PL = nc.vector, nc.scalar, nc.tensor, nc.sync, nc.gpsimd
        sems = {"V": kb.sem("sV"), "A": kb.sem("sA"), "P": kb.sem("sP")}
        engs = {"V": V, "A": A, "P": PE}

        def step(e, fn, extra=()):
            eng = engs[e]
            for o in sems:
                if o != e and sems[o].n > 0:
                    eng.wait_ge(sems[o].h, sems[o].n)
            for (sh, sv) in extra:
                eng.wait_ge(sh, sv)
            ins = fn()
            sems[e].inc(ins)

        def sp_wait_all():
            for o in sems:
                if sems[o].n > 0:
                    SP.wait_ge(sems[o].h, sems[o].n)

        s0 = kb.sem("init"); s1 = kb.sem("init1")
        wv = self.wC.rearrange("(c p) n -> p c n", p=128)
        for k0 in range(0, 16, 4):
            s0.inc(PL.dma_start(out=self.w[:, k0:k0 + 4, :], in_=wv[:, k0:k0 + 4, :]), 16)
        s0.inc(PL.dma_start(out=self.idb[:, :], in_=self.ident), 16)
        for (dst, src) in [(self.gain_sb[:, :], self.gain), (self.cw[:, :, :], self.convw), (self.hp_sb[:, :, :, :], self.hp), (self.U_sb[:, :], self.U),
                           (self.MU_sb[:, :, :], self.MU), (self.ML_sb[:, :, :], self.ML), (self.I4_sb[:, :, :], self.I4), (self.idf[:, :], self.ident)]:
            s1.inc(SP.dma_start(out=dst, in_=src), 16)

        def init_v():
            V.memset(self.ones[:, :], 1.0)
            V.memset(self.onesf[:, :], 1.0)
            V.memset(self.eps_sb[:, :], EPS)
            V.memset(self.one_sb[:, :], 1.0)
            V.memset(self.eps2_sb[:, :], EPS * 128.0)
            V.memset(self.St[:, :, :], 0.0)
            V.memset(self.Stb[:, :, :], 0.0)
            return V.memset(self.halo[:, :, :], 0.0)
        step("V", init_v)
        step("A", lambda: A.activation(out=self.negA[:, :, :], in_=self.hp_sb[:, 0, :, :], func=AF.Exp), extra=[(s1.h, s1.n)])
        step("V", lambda: V.tensor_scalar(out=fl(self.negA[:, :, :]), in0=fl(self.negA[:, :, :]), scalar1=-1.0, scalar2=None, op0=ALU.mult), extra=[(s0.h, s0.n), (s1.h, s1.n)])
        PE.wait_ge(s0.h, s0.n)
        PE.wait_ge(s1.h, s1.n)

        xv = self.xT.rearrange("(c p) t -> p c t", p=128)
        s_xl = kb.sem("xl")
        s_o = [kb.sem("so0"), kb.sem("so1")]

        def load_x(b):
            t0 = b * BT
            for hh in range(2):
                s_xl.inc(SP.dma_start(out=self.xst[:, hh * 8:(hh + 1) * 8, :], in_=xv[:, hh * 8:(hh + 1) * 8, t0:t0 + BT]), 16)

        load_x(0)
        for b in range(self.nblocks):
            t0 = b * BT
            sl = 0
            for hf in range(2):
                step("A", lambda hf=hf: A.activation(out=self.sq[:, :, :], in_=self.xst[:, hf * 8:(hf + 1) * 8, :], func=AF.Square), extra=[(s_xl.h, s_xl.n)])

                def f(hf=hf):
                    for k in range(8):
                        ins = PE.matmul(ps[:, 2, :], lhsT=self.ones[:, :], rhs=self.sq[:, k, :], start=(hf == 0 and k == 0), stop=(hf == 1 and k == 7))
                    return ins
                step("P", f)
            step("A", lambda: A.activation(out=self.rstd[:, :], in_=ps[:, 2, :], func=AF.Sqrt, scale=1.0 / D, bias=self.eps_sb[:, 0:1]))

            def f():
                V.reciprocal(out=self.rstd[:, :], in_=self.rstd[:, :])
                for k in range(16):
                    ins = V.scalar_tensor_tensor(out=self.hT[:, k, :], in0=self.xst[:, k, :], scalar=self.gain_sb[:, k:k + 1], in1=self.rstd[:, :],
                                                 op0=ALU.mult, op1=ALU.mult)
                return ins
            step("V", f)
            if b + 1 < self.nblocks:
                sp_wait_all()
                load_x(b + 1)
            for mt in range(8):
                def f(mt=mt):
                    for k in range(16):
                        ins = PE.matmul(ps[:, mt % 2, :], lhsT=self.w[:, k, mt * 128:(mt + 1) * 128], rhs=self.hT[:, k, :], start=(k == 0), stop=(k == 15))
                    return ins
                step("P", f)
                step("A", lambda mt=mt: A.activation(out=self.praw[:, 3:3 + BT], in_=ps[:, mt % 2, :], func=AF.Copy))

                def f(mt=mt):
                    V.tensor_copy(out=self.praw[:, 0:3], in_=self.halo[:, mt, :])
                    V.tensor_scalar(out=self.acc[:, :], in0=self.praw[:, 0:BT], scalar1=self.cw[:, mt, 0:1], scalar2=None, op0=ALU.mult)
                    for j in range(1, 4):
                        ins = V.scalar_tensor_tensor(out=self.acc[:, :], in0=self.praw[:, j:j + BT], scalar=self.cw[:, mt, j:j + 1], in1=self.acc[:, :],
                                                     op0=ALU.mult, op1=ALU.add)
                    return ins
                step("V", f)
                step("A", lambda mt=mt: A.activation(out=(self.cvs[:, mt, :] if mt < 4 else self.vT[:, mt - 4, :]), in_=self.acc[:, :], func=AF.Silu))
                step("V", lambda mt=mt: V.tensor_copy(out=self.halo[:, mt, :], in_=self.praw[:, BT:BT + 3]))
            step("A", lambda: A.activation(out=self.sq[:, 0:4, :], in_=self.cvs[:, 0:4, :], func=AF.Square))
            for mt in range(4):
                step("P", lambda mt=mt: PE.matmul(ps[:, 2, :], lhsT=self.ones[:, :], rhs=self.sq[:, mt, :], start=True, stop=True))
                if mt < 2:
                    step("A", lambda: A.activation(out=self.rstd[:, :], in_=ps[:, 2, :], func=AF.Sqrt, scale=128.0, bias=self.eps2_sb[:, 0:1]))
                else:
                    step("A", lambda: A.activation(out=self.rstd[:, :], in_=ps[:, 2, :], func=AF.Sqrt, scale=1.0, bias=self.eps_sb[:, 0:1]))

                def f(mt=mt):
                    V.reciprocal(out=self.rstd[:, :], in_=self.rstd[:, :])
                    dst = self.qnT[:, mt, :] if mt < 2 else self.knT[:, mt - 2, :]
                    return V.tensor_tensor(out=dst, in0=self.cvs[:, mt, :], in1=self.rstd[:, :], op=ALU.mult)
                step("V", f)
            for grp in range(BT // (C * G)):
                def csl(gi):
                    c0 = (grp * G + gi) * C
                    return slice(c0, c0 + C)

                def bank_col(base_bank, gi, width):
                    return base_bank + gi // 2, (gi % 2) * width

                def f():
                    for gi in range(G):
                        for k in range(16):
                            ins = PE.matmul(ps[0:64, 2, gi * 8:(gi + 1) * 8], lhsT=self.hT[:, k, csl(gi)], rhs=self.w[:, k, 1024:1032],
                                            start=(k == 0), stop=(k == 15))
                    return ins
                step("P", f)
                bav = ps[0:64, 2, 0:G * 8].rearrange("p (g c) -> p g c", c=8)
                step("V", lambda: V.tensor_tensor(out=self.xa[:, :, :], in0=bav[:, :, 4:8], in1=self.dtb[0:64, :, :], op=ALU.add))

                def f():
                    A.activation(out=fl(self.xa[:, :, :]), in_=fl(self.xa[:, :, :]), func=AF.Exp)
                    return A.activation(out=fl(self.xa[:, :, :]), in_=fl(self.xa[:, :, :]), func=AF.Ln, bias=self.one_sb[0:64, 0:1])
                step("A", f)
                step("V", lambda: V.tensor_tensor(out=fl(self.g[:, :, :]), in0=fl(self.xa[:, :, :]), in1=fl(self.negA[0:64, :, :]), op=ALU.mult))
                step("A", lambda: A.activation(out=self.beta[:, :, :], in_=bav[:, :, 0:4], func=AF.Sigmoid))

                def f():
                    V.tensor_scalar(out=fl(self.nbeta[:, :, :]), in0=fl(self.beta[:, :, :]), scalar1=-1.0, scalar2=None, op0=ALU.mult)
                    for gi in range(G):
                        for h in range(4):
                            ins = V.tensor_scalar(out=self.G1[:, gi, h, :], in0=self.onesf[:, :], scalar1=self.g[:, gi, h:h + 1], scalar2=None, op0=ALU.mult)
                    return ins
                step("V", f)

                def f():
                    for gi in range(G):
                        PE.matmul(ps[0:64, 2, 64 + gi * 4:68 + gi * 4], lhsT=self.U_sb[:, :], rhs=self.g[:, gi, :], start=True, stop=True)
                        PE.matmul(ps[:, 2, 128 + gi * 4:132 + gi * 4], lhsT=self.onesf[:, :], rhs=self.g[:, gi, :], start=True, stop=True)
                    for gi in range(G):
                        bk, c0 = bank_col(3, gi, 256)
                        for h in range(4):
                            PE.matmul(ps[:, bk, c0 + h * 64:c0 + (h + 1) * 64], lhsT=self.G1[:, gi, h, :], rhs=self.U_sb[:, :], start=True, stop=True)
                    for gi in range(G):
                        bk, c0 = bank_col(5, gi, 256)
                        for kh in range(2):
                            PE.matmul(ps[0:64, bk, c0 + kh * 64:c0 + (kh + 1) * 64], lhsT=self.knT[:, kh, csl(gi)], rhs=self.knT[:, kh, csl(gi)], start=True, stop=True)
                        for kh in range(2):
                            ins = PE.matmul(ps[0:64, bk, c0 + 128 + kh * 64:c0 + 128 + (kh + 1) * 64], lhsT=self.knT[:, kh, csl(gi)], rhs=self.qnT[:, kh, csl(gi)],
                                            start=True, stop=True)
                    return ins
                step("P", f)

                def f():
                    A.activation(out=fl(self.gcc[:, :, :]), in_=ps[0:64, 2, 64:64 + G * 4], func=AF.Copy)
                    A.activation(out=fl(self.egl[:, :, :]), in_=ps[:, 2, 128:128 + G * 4], func=AF.Exp)
                    return A.activation(out=self.eR[:, :, :], in_=ps[:, 3:5, :], func=AF.Exp)
                step("A", f)

                def f():
                    for gi in range(G):
                        bk, c0 = bank_col(3, gi, 256)
                        for h in range(4):
                            V.tensor_scalar(out=self.Zmin[:, gi, h, :], in0=ps[0:64, bk, c0 + h * 64:c0 + (h + 1) * 64], scalar1=self.gcc[:, gi, h:h + 1], scalar2=0.0,
                                            op0=ALU.subtract, op1=ALU.min)
                            ins = V.tensor_scalar(out=self.Zmax[:, gi, h, :], in0=ps[0:64, bk, c0 + h * 64:c0 + (h + 1) * 64], scalar1=self.gcc[:, gi, h:h + 1], scalar2=0.0,
                                                  op0=ALU.subtract, op1=ALU.max)
                    return ins
                step("V", f)

                def f():
                    A.activation(out=fl(self.egc[:, :, :]), in_=fl(self.gcc[:, :, :]), func=AF.Exp)
                    A.activation(out=fl4(self.E0T), in_=fl4(self.Zmin), func=AF.Exp)
                    return A.activation(out=fl4(self.E1), in_=fl4(self.Zmax), func=AF.Exp, scale=-1.0)
                step("A", f)

                vbanks = [2, 3, 4, 7]

                def f():
                    for gi in range(G):
                        bk, c0 = bank_col(0, gi, 256)
                        for kh in range(2):
                            PE.matmul(ps[0:64, bk, c0 + kh * 128:c0 + (kh + 1) * 128], lhsT=self.knT[:, kh, csl(gi)], rhs=self.idb[:, :], start=True, stop=True)
                        for h in range(4):
                            ins = PE.matmul(ps[0:64, vbanks[gi], h * 128:(h + 1) * 128], lhsT=self.vT[:, h, csl(gi)], rhs=self.idb[:, :], start=True, stop=True)
                    return ins
                step("P", f)

                def f():
                    A.activation(out=self.ktok[:, :, :], in_=ps[0:64, 0:2, :], func=AF.Copy)
                    A.activation(out=self.vtok[:, 0:3, :], in_=ps[0:64, 2:5, :], func=AF.Copy)
                    return A.activation(out=self.vtok[:, 3, :], in_=ps[0:64, 7, :], func=AF.Copy)
                step("A", f)

                def f():
                    V.tensor_tensor(out=fl(self.bg[:, :, :]), in0=fl(self.beta[:, :, :]), in1=fl(self.egc[:, :, :]), op=ALU.mult)
                    V.tensor_tensor(out=fl4(self.E0T), in0=fl4(self.E0T), in1=fl(self.MU_sb[:, :, :]), op=ALU.mult)
                    V.tensor_tensor(out=fl4(self.E1), in0=fl4(self.E1), in1=fl(self.ML_sb[:, :, :]), op=ALU.mult)
                    for gi in range(G):
                        bk, c0 = bank_col(5, gi, 256)
                        ktv = self.ktok[:, gi // 2, :].rearrange("p (a k d) -> p a k d", a=2, k=2)[:, gi % 2, :, :]
                        vtv = self.vtok[:, gi, :].rearrange("p (h d) -> p h d", h=4)
                        for h in range(4):
                            kh = h // 2
                            V.scalar_tensor_tensor(out=self.X[:, gi, h, :], in0=ps[0:64, bk, c0 + kh * 64:c0 + (kh + 1) * 64], scalar=self.nbeta[:, gi, h:h + 1],
                                                   in1=self.E1[:, gi, h, :], op0=ALU.mult, op1=ALU.mult)
                            V.tensor_tensor(out=self.intraT[:, gi, h, :], in0=ps[0:64, bk, c0 + 128 + kh * 64:c0 + 128 + (kh + 1) * 64], in1=self.E0T[:, gi, h, :], op=ALU.mult)
                            V.tensor_scalar(out=self.vb[:, gi, h, :], in0=vtv[:, h, :], scalar1=self.beta[:, gi, h:h + 1], scalar2=None, op0=ALU.mult)
                            V.tensor_scalar(out=self.kbg[:, gi, h, :], in0=ktv[:, kh, :], scalar1=self.bg[:, gi, h:h + 1], scalar2=None, op0=ALU.mult)
                            V.tensor_scalar(out=self.kdk[:, gi, h, :], in0=ktv[:, kh, :], scalar1=self.E0T[:, gi, h, 63:64], scalar2=None, op0=ALU.mult)
                            ins = V.tensor_tensor(out=self.qgT[:, gi, h, :], in0=self.qnT[:, kh, csl(gi)], in1=self.eR[:, gi // 2, ((gi % 2) * 4 + h) * 64:((gi % 2) * 4 + h + 1) * 64],
                                                  op=ALU.mult)
                    return ins
                step("V", f)

                def f():
                    for gi in range(G):
                        bk, c0 = bank_col(2, gi, 256)
                        for h in range(4):
                            ins = PE.matmul(ps[0:64, bk, c0 + h * 64:c0 + (h + 1) * 64], lhsT=self.X[:, gi, h, :], rhs=self.idb[0:64, 0:64], start=True, stop=True)
                    return ins
                step("P", f)
                step("A", lambda: A.activation(out=fl4(self.Y).rearrange("p (b c) -> p b c", b=2), in_=ps[0:64, 2:4, :], func=AF.Copy))
                def f():
                    V.tensor_tensor(out=fl4(self.Q).rearrange("p (b c) -> p b c", b=2), in0=ps[0:64, 2:4, :],
                                    in1=fl(self.I4_sb[:, :, :]).rearrange("p (b c) -> p b c", b=2), op=ALU.add)
                    return V.tensor_copy(out=fl4(self.Tt), in_=fl4(self.Q))
                step("V", f)
                for lv in range(6):
                    def f(lv=lv):
                        ins = None
                        for gi in range(G):
                            for h in range(4):
                                if lv <= 4:
                                    bk, c0 = bank_col(0, gi, 256)
                                    ins = PE.matmul(ps[0:64, bk, c0 + h * 64:c0 + (h + 1) * 64], lhsT=self.Y[:, gi, h, :], rhs=self.X[:, gi, h, :], start=True, stop=True)
                                if lv <= 3:
                                    bk, c0 = bank_col(2, gi, 256)
                                    ins = PE.matmul(ps[0:64, bk, c0 + h * 64:c0 + (h + 1) * 64], lhsT=self.X[:, gi, h, :], rhs=self.Y[:, gi, h, :], start=True, stop=True)
                                if lv >= 1:
                                    bk, c0 = bank_col(4, gi, 256)
                                    ins = PE.matmul(ps[0:64, bk, c0 + h * 64:c0 + (h + 1) * 64], lhsT=self.X[:, gi, h, :], rhs=self.Tt[:, gi, h, :], start=True, stop=True)
                        return ins
                    step("P", f)
                    if lv <= 3:
                        step("A", lambda: A.activation(out=fl4(self.Y).rearrange("p (b c) -> p b c", b=2), in_=ps[0:64, 2:4, :], func=AF.Copy))

                    def f(lv=lv):
                        ins = None
                        if lv >= 1:
                            V.tensor_tensor(out=fl4(self.Q).rearrange("p (b c) -> p b c", b=2), in0=fl4(self.Q).rearrange("p (b c) -> p b c", b=2),
                                            in1=ps[0:64, 4:6, :], op=ALU.add)
                            ins = V.tensor_copy(out=fl4(self.Tt), in_=fl4(self.Q))
                        if lv <= 4:
                            ins = V.tensor_copy(out=fl4(self.X).rearrange("p (b c) -> p b c", b=2), in_=ps[0:64, 0:2, :])
                        return ins
                    step("V", f)

                def f():
                    for gi in range(G):
                        bk, c0 = bank_col(6, gi, 256)
                        for h in range(4):
                            ins = PE.matmul(ps[:, bk, c0 + h * 64:c0 + (h + 1) * 64], lhsT=self.kbg[:, gi, h, :], rhs=self.Tt[:, gi, h, :], start=True, stop=True)
                    return ins
                step("P", f)
                step("A", lambda: A.activation(out=fl4(self.nwT).rearrange("p (b c) -> p b c", b=2), in_=ps[:, 6:8, :], func=AF.Copy, scale=-1.0))

                for gi in range(G):
                    def f(gi=gi):
                        for h in range(4):
                            PE.matmul(ps[0:64, 0, h * 128:(h + 1) * 128], lhsT=self.Tt[:, gi, h, :], rhs=self.vb[:, gi, h, :], start=(h == 0), stop=False, skip_group_check=True)
                        for h in range(4):
                            ins = PE.matmul(ps[0:64, 0, h * 128:(h + 1) * 128], lhsT=self.nwT[:, gi, h, :], rhs=self.Stb[:, h, :], start=False, stop=(h == 3), skip_group_check=True)
                        return ins
                    step("P", f)
                    step("V", lambda: V.tensor_copy(out=fl(self.vnew[:, :, :]), in_=ps[0:64, 0, :]))

                    def f(gi=gi):
                        for h in range(4):
                            PE.matmul(ps[:, 1, h * 64:(h + 1) * 64], lhsT=self.Stb[:, h, :], rhs=self.qgT[:, gi, h, :], start=(h == 0), stop=False, skip_group_check=True)
                        for h in range(4):
                            PE.matmul(ps[:, 1, h * 64:(h + 1) * 64], lhsT=self.vnew[:, h, :], rhs=self.intraT[:, gi, h, :], start=False, stop=(h == 3), skip_group_check=True)
                        for h in range(4):
                            ins = PE.matmul(ps[:, 2, h * 128:(h + 1) * 128], lhsT=self.kdk[:, gi, h, :], rhs=self.vnew[:, h, :], start=True, stop=True)
                        return ins
                    step("P", f)
                    extra = [(s_o[sl].h, s_o[sl].n)] if (grp == 0 and gi == 0) else []
                    step("A", lambda gi=gi: A.activation(out=self.ob[:, sl, :, csl(gi)], in_=ps[:, 1, 0:256].rearrange("p (h i) -> p h i", h=4), func=AF.Copy), extra=extra)

                    def f(gi=gi):
                        for h in range(4):
                            V.scalar_tensor_tensor(out=self.St[:, h, :], in0=self.St[:, h, :], scalar=self.egl[:, gi, h:h + 1], in1=ps[:, 2, h * 128:(h + 1) * 128],
                                                   op0=ALU.mult, op1=ALU.add)
                        return V.tensor_copy(out=self.Stb[:, :, :], in_=self.St[:, :, :])
                    step("V", f)
            sp_wait_all()
            s_o[sl].inc(SP.dma_start(out=self.og.rearrange("(h p) t -> p h t", p=128)[:, :, t0:t0 + BT], in_=self.ob[:, sl, :, :]), 16)
        for s in s_o:
            SP.wait_ge(s.h, s.n)
        return nc


def mixc_consts():
    U = (np.arange(64)[:, None] <= np.arange(64)[None, :]).astype(np.float32)
    p = np.arange(64)[:, None, None]; f = np.arange(64)[None, None, :]
    MU = np.broadcast_to((f >= p), (64, 4 * G, 64)).astype(np.float32)
    ML = np.broadcast_to((p > f), (64, 4 * G, 64)).astype(np.float32)
    I4 = np.broadcast_to((p == f), (64, 4 * G, 64)).astype(np.float32)
    return U, np.ascontiguousarray(MU), np.ascontiguousarray(ML), np.ascontiguousarray(I4)


def mixc_inputs(inp, c, xT, consts):
    U, MU, ML, I4 = consts
    W = inp["gdn_w_in"][0]
    kh0 = 2 * c; vh0 = 4 * c
    qc = slice(kh0 * 128, kh0 * 128 + 256)
    kc = slice(2048 + kh0 * 128, 2048 + kh0 * 128 + 256)
    vc = slice(4096 + vh0 * 128, 4096 + vh0 * 128 + 512)
    bc = slice(12288 + vh0, 12288 + vh0 + 4)
    ac = slice(12320 + vh0, 12320 + vh0 + 4)
    wC = np.concatenate([W[:, qc], W[:, kc], W[:, vc], W[:, bc], W[:, ac]], axis=1)
    cw = inp["gdn_conv"][0]
    cwc = np.concatenate([cw[:, qc], cw[:, kc], cw[:, vc]], axis=1)
    convw = np.ascontiguousarray(cwc.reshape(4, 8, 128).transpose(2, 1, 0))
    hp = np.stack([np.broadcast_to(inp["gdn_a_log"][0][vh0:vh0 + 4], (128, G, 4)), np.broadcast_to(inp["gdn_dt_bias"][0][vh0:vh0 + 4], (128, G, 4))], axis=1)
    return {"xT": xT, "wC": np.ascontiguousarray(wC), "gain": np.ascontiguousarray(inp["norm_mix"][1].reshape(16, 128).T),
            "convw": convw.astype(np.float32), "hp": np.ascontiguousarray(hp.astype(np.float32)), "U": U, "MU": MU, "ML": ML, "I4": I4,
            "ident": np.eye(128, dtype=np.float32)}


def _run(nc, ins):
    res = run_bass_kernel_spmd(nc, ins, core_ids=list(range(8)))
    return res.results


def kernel(**inputs):
    inp = {k: np.asarray(v) for k, v in inputs.items()}
    S_ = 16384
    x = inp["x"][0]
    xT = np.ascontiguousarray(x.T)
    consts = mixa_consts()
    nc = MixA().build()
    ra = _run(nc, [mixa_inputs(inp, c, xT, consts) for c in range(8)])
    oT0 = np.empty((3072, S_), np.float32)
    for c in range(8):
        hr, vh = c // 2, c % 2
        oT0[hr * 512 + vh * 256: hr * 512 + (vh + 1) * 256] = ra[c]["oret"]
        oT0[2048 + c * 128: 2048 + (c + 1) * 128] = ra[c]["odil"]
    del ra
    nc = Post(0).build()
    rb = _run(nc, [post_inputs(0, inp, np.ascontiguousarray(xT[:, c * 2048:(c + 1) * 2048]), np.ascontiguousarray(oT0[:, c * 2048:(c + 1) * 2048]))
                   for c in range(8)])
    x1T = np.ascontiguousarray(np.concatenate([rb[c]["xo"] for c in range(8)], axis=1))
    del rb, oT0
    cc = mixc_consts()
    nc = MixC().build()
    rc = _run(nc, [mixc_inputs(inp, c, x1T, cc) for c in range(8)])
    oT1 = np.ascontiguousarray(np.concatenate([rc[c]["og"] for c in range(8)], axis=0))
    del rc
    nc = Post(1).build()
    rd = _run(nc, [post_inputs(1, inp, np.ascontiguousarray(x1T[:, c * 2048:(c + 1) * 2048]), np.ascontiguousarray(oT1[:, c * 2048:(c + 1) * 2048]))
                   for c in range(8)])
    outT = np.concatenate([rd[c]["xo"] for c in range(8)], axis=1)
    return np.ascontiguousarray(outT.T)[None].astype(np.float32)
```

```python
import math
import numpy as np
from contextlib import ExitStack
from concourse.bass_utils import run_bass_kernel_spmd
import concourse.bass as bass
import concourse.mybir as mybir

F32 = mybir.dt.float32
BF16 = mybir.dt.bfloat16
AF = mybir.ActivationFunctionType
ALU = mybir.AluOpType
EPS = 1e-6


class Sem:
    def __init__(self, h):
        self.h = h
        self.n = 0

    def inc(self, ins, by=1):
        ins.then_inc(self.h, by)
        self.n += by
        return self.n


class KB:
    def __init__(self):
        self.nc = bass.Bass("TRN2", target_bir_lowering=False)
        self.es = ExitStack()
        self.sems = {}
        self.uid = 0

    def sb(self, name, shape, dt, es=None):
        return (es or self.es).enter_context(self.nc.sbuf_tensor(name, shape, dt))

    def psum(self, name, shape, dt, es=None):
        return (es or self.es).enter_context(self.nc.psum_tensor(name, shape, dt))

    def sem(self, name):
        if name not in self.sems:
            self.sems[name] = Sem(self.es.enter_context(self.nc.semaphore(name)))
        return self.sems[name]

    def din(self, name, shape, dt=F32):
        return self.nc.dram_tensor(name, list(shape), dt, kind="ExternalInput").ap()

    def dout(self, name, shape, dt=F32):
        return self.nc.dram_tensor(name, list(shape), dt, kind="ExternalOutput").ap()

    def dscr(self, name, shape, dt=F32):
        return self.nc.dram_tensor(name, list(shape), dt, kind="Internal").ap()


D = 2048
T = 2048
TT = 1024
NT = TT // 512
FF = 5632
WB = 8192


class Post:
    def __init__(self, layer):
        self.layer = layer
        self.FY = 3072 if layer == 0 else 4096
        self.G = 2048 if layer == 0 else 4096
        self.kb = kb = KB()
        nc = self.nc = kb.nc
        FY, G = self.FY, self.G
        self.xT = kb.din("xT", [D, T])
        self.oT = kb.din("oT", [FY, T])
        self.w_gate = kb.din("w_gate", [D, G])
        self.w_out = kb.din("w_out", [FY, D])
        self.gains = kb.din("gains", [128, 4, 16])
        self.hg = kb.din("hg", [128, 3])
        self.memT = kb.din("memT", [D, 256])
        self.w_q = kb.din("w_q", [D, 512])
        self.w_kv = kb.din("w_kv", [D, 1024])
        self.w_o = kb.din("w_o", [512, D])
        self.w1 = kb.din("w1", [D, FF])
        self.w3 = kb.din("w3", [D, FF])
        self.w2 = kb.din("w2", [FF, D])
        self.xo = kb.dout("xo", [D, T])
        self.x1 = kb.dout("x1s", [D, T])
        self.x2 = kb.dout("x2s", [D, T])
        self.wbuf = [kb.sb("wbuf0", [128, WB], BF16), kb.sb("wbuf1", [128, WB], BF16)]
        self.ones = kb.sb("ones", [128, 128], BF16)
        self.gains_sb = kb.sb("gains_sb", [128, 4, 16], F32)
        self.hg_sb = kb.sb("hg_sb", [128, 3], F32)
        self.eps_sb = kb.sb("eps_sb", [128, 1], F32)
        self.eps2_sb = kb.sb("eps2_sb", [128, 1], F32)
        self.hT = kb.sb("hT", [128, 16, TT], BF16)
        self.big = kb.sb("big", [128, 32 * TT], BF16)
        self.xst_f = kb.sb("xst", [128, 4096], F32)
        self.sq_f = kb.sb("sq", [128, 4096], BF16)
        self.xst_n = self.xst_f[:, :].rearrange("p (c t) -> p c t", t=256)
        self.sq_n = self.sq_f[:, :].rearrange("p (c t) -> p c t", t=256)
        self.xst = self.xst_f[:, 0:2048].rearrange("p (c t) -> p c t", t=512)
        self.sq = self.sq_f[:, 0:2048].rearrange("p (c t) -> p c t", t=512)
        self.qf = self.xst
        self.rstd = kb.sb("rstd", [128, 4, 512], F32)
        self.rbuf = kb.sb("rbuf", [128, 3, 512], F32)
        self.obuf = kb.sb("obuf", [128, 3, 512], F32)
        self.stmp = kb.sb("stmp", [128, 2, 512], F32)
        self.knT = kb.sb("knT", [128, 4, 256], BF16)
        self.vm = kb.sb("vm", [128, 2, 512], BF16)
        self.qn = self.big[:, 0:4 * TT].rearrange("p (c t) -> p c t", t=TT)
        self.pT = kb.sb("pT", [128, 2, 512], BF16)
        self.oxa = self.big[:, 4 * TT:8 * TT].rearrange("p (c t) -> p c t", t=TT)
        self.ps = kb.psum("ps", [128, 8, 512], F32)
        self.gidx = 0
        self.grp_end = []
        self.mm = kb.sem("g_mm")
        self.pf = kb.sem("g_pf")
        self.wl = [kb.sem("g_wl0"), kb.sem("g_wl1")]
        self.sts = [kb.sem("st%d" % i) for i in range(3)]
        self.bar_n = 0

    def wait_stores(self):
        for st in self.sts:
            self.nc.sync.wait_ge(st.h, st.n)

    def barrier(self):
        self.nc.all_engine_barrier()

    def y(self):
        return self.big[:, 0:(self.FY // 128) * TT].rearrange("p (c t) -> p c t", t=TT)

    def g(self):
        return self.big[:, 0:22 * TT].rearrange("p (c t) -> p c t", t=TT)

    def rstd_op(self, ps_ap, out_ap, inv_n, wait, post=1.0):
        nc = self.nc
        s = self.kb.sem("r_a")
        nc.scalar.wait_ge(wait[0], wait[1])
        s.inc(nc.scalar.activation(out=out_ap, in_=ps_ap, func=AF.Sqrt, scale=inv_n / post ** 2, bias=self.eps_sb[:, 0:1] if post == 1.0 else self.eps2_sb[:, 0:1]))
        nc.vector.wait_ge(s.h, s.n)
        return nc.vector.reciprocal(out=out_ap, in_=out_ap)

    def gemm(self, wsrc, KC, GW, ngroups, act, ntt, epi, pair=False, tw=512, pe_waits=()):
        nc = self.nc
        mpg = GW // 128
        if pair:
            mpg //= 2
        G0 = len(self.grp_end)

        def load(g):
            Gg = G0 + g
            b = Gg % 2
            if Gg >= 2:
                nc.gpsimd.wait_ge(self.mm.h, self.grp_end[Gg - 2])
            wv = self.wbuf[b][:, 0:KC * GW].rearrange("p (c n) -> p c n", n=GW)
            for (ap, off, w) in wsrc(g):
                src = ap.rearrange("(c p) n -> p c n", p=128)
                kstep = 8
                for k0 in range(0, KC, kstep):
                    k1 = min(KC, k0 + kstep)
                    self.wl[b].inc(nc.gpsimd.dma_start(out=wv[:, k0:k1, off:off + w], in_=src[:, k0:k1, :]), 16)
            return self.wl[b].n

        wl_need = {}
        wl_need[0] = load(0)
        cnt = 0
        for (sh, sv) in pe_waits:
            nc.tensor.wait_ge(sh, sv)
        for g in range(ngroups):
            if g + 1 < ngroups:
                wl_need[g + 1] = load(g + 1)
            b = (G0 + g) % 2
            nc.tensor.wait_ge(self.wl[b].h, wl_need[g])
            wv = self.wbuf[b][:, 0:KC * GW].rearrange("p (c n) -> p c n", n=GW)
            for j in range(mpg):
                for tt in range(ntt):
                    cols = [j] if not pair else [j, j + mpg]
                    ps_list = []
                    for cj in cols:
                        idx = self.gidx
                        bank = idx % 4
                        if idx >= 4:
                            nc.tensor.wait_ge(self.pf.h, idx - 3)
                        for k in range(KC):
                            ins = nc.tensor.matmul(self.ps[:, bank, 0:tw], lhsT=wv[:, k, cj * 128:(cj + 1) * 128],
                                                   rhs=act[:, k, tt * tw:(tt + 1) * tw], start=(k == 0), stop=(k == KC - 1))
                        self.mm.inc(ins)
                        self.gidx += 1
                        ps_list.append(self.ps[:, bank, 0:tw])
                    fin = epi(cnt, g * mpg + j, tt, ps_list, self.gidx)
                    self.pf.inc(fin, len(cols))
                    cnt += 1
            self.grp_end.append(self.mm.n)

    def norm(self, src, tok0, which, dst, ntok_tiles, tw=256, gains=None):
        nc = self.nc
        s_ld = self.kb.sem("n_ld"); s_sq = self.kb.sem("n_sq"); s_mm = self.kb.sem("n_mm"); s_dv = self.kb.sem("n_dv")
        for tt in range(ntok_tiles):
            t0 = tok0 + tt * tw
            nc.sync.wait_ge(s_dv.h, s_dv.n)
            srcv = src.rearrange("(c p) t -> p c t", p=128)
            for hh in range(2):
                s_ld.inc(nc.sync.dma_start(out=self.xst_n[:, hh * 8:(hh + 1) * 8, 0:tw], in_=srcv[:, hh * 8:(hh + 1) * 8, t0:t0 + tw]), 16)
            nc.scalar.wait_ge(s_ld.h, s_ld.n)
            nc.scalar.wait_ge(s_mm.h, s_mm.n)
            s_sq.inc(nc.scalar.activation(out=self.sq_n[:, :, 0:tw], in_=self.xst_n[:, :, 0:tw], func=AF.Square))
            nc.tensor.wait_ge(s_sq.h, s_sq.n)
            nc.tensor.wait_ge(s_dv.h, s_dv.n)
            for k in range(16):
                ins = nc.tensor.matmul(self.ps[:, 4, 0:tw], lhsT=self.ones[:, :], rhs=self.sq_n[:, k, 0:tw], start=(k == 0), stop=(k == 15))
            s_mm.inc(ins)
            self.rstd_op(self.ps[:, 4, 0:tw], self.rstd[:, 0, 0:tw], 1.0 / D, (s_mm.h, s_mm.n))
            for k in range(16):
                ins = nc.vector.scalar_tensor_tensor(out=dst[:, k, tt * tw:(tt + 1) * tw], in0=self.xst_n[:, k, 0:tw],
                                                     scalar=self.gains_sb[:, which, k:k + 1], in1=self.rstd[:, 0, 0:tw],
                                                     op0=ALU.mult, op1=ALU.mult)
            s_dv.inc(ins)
        return s_dv

    def onorm(self, tok0):
        nc = self.nc
        layer = self.layer
        y = self.y()
        s_ld = self.kb.sem("o_ld"); s_sq = self.kb.sem("o_sq"); s_mm = self.kb.sem("o_mm"); s_dv = self.kb.sem("o_dv")
        nblk = self.FY // 512
        nnorm = 4 if layer == 0 else 8
        ov = self.oT.rearrange("(c p) t -> p c t", p=128)
        for tt in range(NT):
            t0 = tok0 + tt * 512
            for blk in range(nblk):
                nc.sync.wait_ge(s_dv.h, s_dv.n)
                s_ld.inc(nc.sync.dma_start(out=self.xst[:, 0:4, :], in_=ov[:, blk * 4:(blk + 1) * 4, t0:t0 + 512]), 16)
                if blk >= nnorm:
                    nc.vector.wait_ge(s_ld.h, s_ld.n)
                    ins = nc.vector.tensor_copy(out=y[:, blk * 4:(blk + 1) * 4, tt * 512:(tt + 1) * 512], in_=self.xst[:, 0:4, :])
                    s_dv.inc(ins)
                    continue
                nc.scalar.wait_ge(s_ld.h, s_ld.n)
                nc.scalar.wait_ge(s_mm.h, s_mm.n)
                s_sq.inc(nc.scalar.activation(out=self.sq[:, 0:4, :], in_=self.xst[:, 0:4, :], func=AF.Square))
                nc.tensor.wait_ge(s_sq.h, s_sq.n)
                nc.tensor.wait_ge(s_dv.h, s_dv.n)
                if layer == 0:
                    for k in range(4):
                        ins = nc.tensor.matmul(self.ps[:, 4, :], lhsT=self.ones[:, :], rhs=self.sq[:, k, :], start=(k == 0), stop=(k == 3))
                else:
                    for k in range(4):
                        ins = nc.tensor.matmul(self.ps[:, 4 + k, :], lhsT=self.ones[:, :], rhs=self.sq[:, k, :], start=True, stop=True)
                s_mm.inc(ins)
                if layer == 0:
                    self.rstd_op(self.ps[:, 4, :], self.rstd[:, 0, :], 1.0 / 512, (s_mm.h, s_mm.n))
                    for k in range(4):
                        ins = nc.vector.tensor_tensor(out=y[:, blk * 4 + k, tt * 512:(tt + 1) * 512], in0=self.xst[:, k, :], in1=self.rstd[:, 0, :], op=ALU.mult)
                else:
                    for k in range(4):
                        self.rstd_op(self.ps[:, 4 + k, :], self.rstd[:, k, :], 1.0 / 128, (s_mm.h, s_mm.n))
                        ins = nc.vector.scalar_tensor_tensor(out=y[:, blk * 4 + k, tt * 512:(tt + 1) * 512], in0=self.xst[:, k, :],
                                                             scalar=self.hg_sb[:, 2:3], in1=self.rstd[:, k, :], op0=ALU.mult, op1=ALU.mult)
                s_dv.inc(ins)

    def epi_gate(self):
        nc = self.nc
        y = self.y()
        s_d = self.kb.sem("eg_d")
        base_d = s_d.n

        def epi(cnt, mt, tt, ps_list, idx_after):
            s = cnt % 2
            nc.scalar.wait_ge(self.mm.h, idx_after)
            if cnt >= 2:
                nc.scalar.wait_ge(s_d.h, base_d + cnt - 1)
            fin = nc.scalar.activation(out=self.stmp[:, s, :], in_=ps_list[0], func=AF.Silu)
            nc.vector.wait_ge(self.pf.h, idx_after)
            yv = y[:, mt, tt * 512:(tt + 1) * 512]
            s_d.inc(nc.vector.tensor_tensor(out=yv, in0=self.stmp[:, s, :], in1=yv, op=ALU.mult))
            return fin
        return epi

    def epi_swiglu(self):
        nc = self.nc
        g = self.g()
        s_a = self.kb.sem("es_a")
        hist = []

        def epi(cnt, mt, tt, ps_list, idx_after):
            s = cnt % 2
            nc.scalar.wait_ge(self.mm.h, idx_after)
            if cnt >= 2:
                nc.scalar.wait_ge(self.pf.h, hist[cnt - 2])
            s_a.inc(nc.scalar.activation(out=self.stmp[:, s, :], in_=ps_list[0], func=AF.Silu))
            nc.vector.wait_ge(s_a.h, s_a.n)
            fin = nc.vector.tensor_tensor(out=g[:, mt, tt * 512:(tt + 1) * 512], in0=self.stmp[:, s, :], in1=ps_list[1], op=ALU.mult)
            hist.append(idx_after)
            return fin
        return epi

    def epi_resid(self, res_src, dst, tok0, tiles):
        nc = self.nc
        rv = res_src.rearrange("(c p) t -> p c t", p=128)
        dv = dst.rearrange("(c p) t -> p c t", p=128)
        idx0 = self.gidx
        rls = [self.kb.sem("rl%d" % i) for i in range(3)]
        sts = self.sts

        def issue_load(c):
            mt, tt = tiles[c]
            if c >= 3:
                nc.sync.wait_ge(self.pf.h, idx0 + c - 2)
            rls[c % 3].inc(nc.sync.dma_start(out=self.rbuf[:, c % 3, :], in_=rv[:, mt, tok0 + tt * 512: tok0 + (tt + 1) * 512]), 16)

        def epi(cnt, mt, tt, ps_list, idx_after):
            if cnt == 0:
                issue_load(0)
                if len(tiles) > 1:
                    issue_load(1)
            if cnt + 2 < len(tiles):
                issue_load(cnt + 2)
            s = cnt % 3
            nc.vector.wait_ge(self.mm.h, idx_after)
            nc.vector.wait_ge(rls[s].h, rls[s].n if cnt + 3 >= len(tiles) or True else 0)
            nc.vector.wait_ge(sts[s].h, sts[s].n)
            fin = nc.vector.tensor_tensor(out=self.obuf[:, s, :], in0=ps_list[0], in1=self.rbuf[:, s, :], op=ALU.add)
            nc.sync.wait_ge(self.pf.h, idx_after)
            sts[s].inc(nc.sync.dma_start(out=dv[:, mt, tok0 + tt * 512: tok0 + (tt + 1) * 512], in_=self.obuf[:, s, :]), 16)
            return fin
        return epi

    def epi_plain(self, dstf):
        nc = self.nc

        def epi(cnt, mt, tt, ps_list, idx_after):
            nc.vector.wait_ge(self.mm.h, idx_after)
            return nc.vector.tensor_copy(out=dstf(mt, tt), in_=ps_list[0])
        return epi

    def mem_kv(self):
        nc = self.nc
        s_ld = self.kb.sem("m_ld"); s_a = self.kb.sem("m_a"); s_p = self.kb.sem("m_p"); s_d = self.kb.sem("m_d")
        memn = self.hT[:, :, 0:256]
        ndv = self.norm(self.memT, 0, 3, self.hT, 1, tw=256)
        self.barrier()
        self.gemm(lambda g: [(self.w_kv[:, 0:512], 0, 512)], 16, 512, 1, memn, 1,
                  self.epi_plain(lambda mt, tt: self.qf[:, mt, 0:256]), tw=256, pe_waits=[(ndv.h, ndv.n)])
        self.barrier()
        nc.scalar.wait_ge(self.pf.h, self.gidx)
        nc.scalar.activation(out=self.sq[:, 0:4, 0:256], in_=self.qf[:, 0:4, 0:256], func=AF.Square).then_inc(s_a.h, 1)
        nc.tensor.wait_ge(s_a.h, 1)
        for h in range(4):
            ins = nc.tensor.matmul(self.ps[:, 4 + h, 0:256], lhsT=self.ones[:, :], rhs=self.sq[:, h, 0:256], start=True, stop=True)
        ins.then_inc(s_p.h, 1)
        for h in range(4):
            self.rstd_op(self.ps[:, 4 + h, 0:256], self.rstd[:, h, 0:256], 1.0 / 128, (s_p.h, 1))
            nc.vector.scalar_tensor_tensor(out=self.knT[:, h, :], in0=self.qf[:, h, 0:256], scalar=self.hg_sb[:, 1:2], in1=self.rstd[:, h, 0:256],
                                           op0=ALU.mult, op1=ALU.mult)
        self.barrier()
        wv = self.wbuf[0][:, 0:16 * 512].rearrange("p (c n) -> p c n", n=512)
        src = self.w_kv[:, 512:1024].rearrange("(c p) n -> p c n", p=128)
        for k0 in (0, 8):
            nc.gpsimd.dma_start(out=wv[:, k0:k0 + 8, :], in_=src[:, k0:k0 + 8, :]).then_inc(s_ld.h, 16)
        nc.tensor.wait_ge(s_ld.h, 32)
        for c in range(2):
            for k in range(16):
                ins = nc.tensor.matmul(self.ps[:, 4 + c, :], lhsT=self.hT[:, k, c * 128:(c + 1) * 128], rhs=wv[:, k, :], start=(k == 0), stop=(k == 15))
        ins.then_inc(s_p.h, 1)
        nc.vector.wait_ge(s_p.h, 2)
        for c in range(2):
            ins = nc.vector.tensor_copy(out=self.vm[:, c, :], in_=self.ps[:, 4 + c, :])
        self.barrier()

    def xa_attn(self):
        nc = self.nc
        s_q = self.kb.sem("x_q"); s_a = self.kb.sem("x_a"); s_p = self.kb.sem("x_p"); s_d = self.kb.sem("x_d")
        scale = 128 ** -0.5
        for tt in range(NT):
            self.gemm(lambda g: [(self.w_q[:, :], 0, 512)], 16, 512, 1, self.hT[:, :, tt * 512:(tt + 1) * 512], 1,
                      self.epi_plain(lambda mt, t_: self.qf[:, mt, :]), pe_waits=[(self.kb.sem("n_dv").h, self.kb.sem("n_dv").n)])
            self.barrier()
            nc.scalar.wait_ge(self.pf.h, self.gidx)
            s_a.inc(nc.scalar.activation(out=self.sq[:, 0:4, :], in_=self.qf[:, 0:4, :], func=AF.Square))
            nc.tensor.wait_ge(s_a.h, s_a.n)
            for h in range(4):
                ins = nc.tensor.matmul(self.ps[:, 4 + h, :], lhsT=self.ones[:, :], rhs=self.sq[:, h, :], start=True, stop=True)
            s_p.inc(ins)
            for h in range(4):
                self.rstd_op(self.ps[:, 4 + h, :], self.rstd[:, h, :], 1.0 / 128, (s_p.h, s_p.n), post=scale)
                ins = nc.vector.scalar_tensor_tensor(out=self.qn[:, h, tt * 512:(tt + 1) * 512], in0=self.qf[:, h, :], scalar=self.hg_sb[:, 0:1],
                                                     in1=self.rstd[:, h, :], op0=ALU.mult, op1=ALU.mult)
            s_d.inc(ins)
            nc.tensor.wait_ge(s_d.h, s_d.n)
            nc.scalar.wait_ge(s_d.h, s_d.n)
            self.barrier()
            for h in range(4):
                for c in range(2):
                    ins = nc.tensor.matmul(self.ps[:, 4 + c, :], lhsT=self.knT[:, h, c * 128:(c + 1) * 128], rhs=self.qn[:, h, tt * 512:(tt + 1) * 512],
                                           start=True, stop=True)
                s_p.inc(ins)
                nc.scalar.wait_ge(s_p.h, s_p.n)
                for c in range(2):
                    ins = nc.scalar.activation(out=self.pT[:, c, :], in_=self.ps[:, 4 + c, :], func=AF.Exp)
                s_a.inc(ins)
                nc.tensor.wait_ge(s_a.h, s_a.n)
                for c in range(2):
                    nc.tensor.matmul(self.ps[:, 6, :], lhsT=self.vm[:, c, h * 128:(h + 1) * 128], rhs=self.pT[:, c, :], start=(c == 0), stop=(c == 1))
                for c in range(2):
                    ins = nc.tensor.matmul(self.ps[:, 7, :], lhsT=self.ones[:, :], rhs=self.pT[:, c, :], start=(c == 0), stop=(c == 1))
                s_p.inc(ins)
                nc.vector.wait_ge(s_p.h, s_p.n)
                nc.vector.reciprocal(out=self.rstd[:, 0, :], in_=self.ps[:, 7, :])
                ins = nc.vector.tensor_tensor(out=self.oxa[:, h, tt * 512:(tt + 1) * 512], in0=self.ps[:, 6, :], in1=self.rstd[:, 0, :], op=ALU.mult)
                s_d.inc(ins)
                nc.tensor.wait_ge(s_d.h, s_d.n)
                nc.scalar.wait_ge(s_d.h, s_d.n)
            self.barrier()

    def build(self, stages=99):
        nc = self.nc
        s0 = self.kb.sem("init")
        nc.vector.memset(self.ones[:, :], 1.0)
        nc.vector.memset(self.eps_sb[:, :], EPS)
        nc.vector.memset(self.eps2_sb[:, :], EPS * 128.0)
        nc.sync.dma_start(out=self.gains_sb[:, :, :], in_=self.gains).then_inc(s0.h, 16)
        nc.sync.dma_start(out=self.hg_sb[:, :], in_=self.hg).then_inc(s0.h, 16)
        nc.sync.wait_ge(s0.h, 32)
        self.barrier()
        self.mem_kv()
        for p in range(T // TT):
            tok0 = p * TT
            y = self.y()
            self.norm(self.xT, tok0, 0, self.hT, TT // 256)
            self.barrier()
            if stages < 1:
                continue
            self.onorm(tok0)
            self.barrier()
            self.gemm(lambda g: [(self.w_gate[:, g * 512:(g + 1) * 512], 0, 512)], 16, 512, self.G // 512, self.hT, NT, self.epi_gate(),
                      pe_waits=[(self.kb.sem("n_dv").h, self.kb.sem("n_dv").n), (self.kb.sem("o_dv").h, self.kb.sem("o_dv").n)])
            self.barrier()
            if stages < 2:
                continue
            KC = self.FY // 128
            tiles = [(m, tt) for m in range(16) for tt in range(NT)]
            dst = self.x1 if stages > 2 else self.xo
            self.gemm(lambda g: [(self.w_out[:, g * 256:(g + 1) * 256], 0, 256)], KC, 256, 8, y, NT,
                      self.epi_resid(self.xT, dst, tok0, tiles), pe_waits=[(self.kb.sem("eg_d").h, self.kb.sem("eg_d").n)])
            self.wait_stores()
            self.barrier()
            if stages < 3:
                continue
            self.norm(self.x1, tok0, 1, self.hT, TT // 256)
            self.barrier()
            self.xa_attn()
            dst = self.x2 if stages > 3 else self.xo
            self.gemm(lambda g: [(self.w_o[:, :], 0, 2048)], 4, 2048, 1, self.oxa, NT,
                      self.epi_resid(self.x1, dst, tok0, tiles), pe_waits=[(self.kb.sem("x_d").h, self.kb.sem("x_d").n)])
            self.wait_stores()
            self.barrier()
            if stages < 4:
                continue
            self.norm(self.x2, tok0, 2, self.hT, TT // 256)
            self.barrier()
            for half in range(2):
                c0 = half * (FF // 2)
                self.gemm(lambda g: [(self.w1[:, c0 + g * 256:c0 + (g + 1) * 256], 0, 256), (self.w3[:, c0 + g * 256:c0 + (g + 1) * 256], 256, 256)],
                          16, 512, FF // 512, self.hT, NT, self.epi_swiglu(), pair=True,
                          pe_waits=[(self.kb.sem("n_dv").h, self.kb.sem("n_dv").n)])
                self.barrier()
                w2h = self.w2[c0:c0 + FF // 2, :]
                self.gemm(lambda g: [(w2h[:, g * 256:(g + 1) * 256], 0, 256)], 22, 256, 8, self.g(), NT,
                          self.epi_resid(self.x2 if half == 0 else self.xo, self.xo, tok0, tiles), pe_waits=[(self.pf.h, self.gidx)])
                self.wait_stores()
                self.barrier()
        return nc


def post_inputs(layer, inp, xT_c, oT_c):
    def gl(v):
        return np.ascontiguousarray(v.reshape(16, 128).T)
    gains = np.stack([gl(inp["norm_mix"][layer]), gl(inp["norm_xa"][layer]), gl(inp["norm_ffn"][layer]), gl(inp["mem_norm"])], axis=1)
    gd = inp["gdn_norm"][0]
    hg = np.stack([inp["xa_q_gain"][layer], inp["xa_k_gain"][layer], gd], axis=1)
    if layer == 0:
        w_gate = np.ascontiguousarray(inp["ar_w_in"][0][:, 4096:6144])
        w_out = inp["ar_w_out"][0]
    else:
        w_gate = np.ascontiguousarray(inp["gdn_w_in"][0][:, 8192:12288])
        w_out = inp["gdn_w_out"][0]
    return {
        "xT": xT_c, "oT": oT_c, "w_gate": w_gate, "w_out": np.ascontiguousarray(w_out),
        "gains": np.ascontiguousarray(gains.astype(np.float32)), "hg": np.ascontiguousarray(hg.astype(np.float32)),
        "memT": np.ascontiguousarray(inp["mem"][0].T),
        "w_q": np.ascontiguousarray(inp["xa_w_q"][layer]), "w_kv": np.ascontiguousarray(inp["xa_w_kv"][layer]),
        "w_o": np.ascontiguousarray(inp["xa_w_o"][layer]),
        "w1": np.ascontiguousarray(inp["ffn_w1"][layer]), "w3": np.ascontiguousarray(inp["ffn_w3"][layer]),
        "w2": np.ascontiguousarray(inp["ffn_w2"][layer]),
    }


D = 2048
S = 16384
BT = 512
NB_A = S // BT
NCOL_A = 1152
NRING = 20


class MixA:
    def __init__(self, nblocks=NB_A):
        self.nblocks = nblocks
        self.kb = kb = KB()
        nc = self.nc = kb.nc
        self.xT = kb.din("xT", [D, S])
        self.wA = kb.din("wA", [D, NCOL_A])
        self.gain = kb.din("gain", [128, 16])
        self.hg = kb.din("hg", [128, 2])
        self.cosT = kb.din("cosT", [128, S])
        self.sinT = kb.din("sinT", [128, S])
        self.dmask = kb.din("dmask", [128, 128])
        self.qdrow = kb.din("qdrow", [128, BT])
        self.kdec = kb.din("kdec", [128, 2])
        self.gtab = kb.din("gtab", [128, 17, 128])
        self.mtab = kb.din("mtab", [128, 17, 128])
        self.ident = kb.din("ident", [128, 128])
        self.oret = kb.dout("oret", [256, S])
        self.odil = kb.dout("odil", [128, S])
        sb = kb.sb
        self.w = sb("w", [128, 16, NCOL_A], BF16)
        self.ones = sb("ones", [128, 128], BF16)
        self.idb = sb("idb", [128, 128], BF16)
        self.idf = sb("idf", [128, 128], F32)
        self.gain_sb = sb("gain_sb", [128, 16], F32)
        self.hg_sb = sb("hg_sb", [128, 2], F32)
        self.eps_sb = sb("eps_sb", [128, 1], F32)
        self.eps2_sb = sb("eps2_sb", [128, 1], F32)
        self.dm = sb("dm", [128, 128], F32)
        self.qd = sb("qd", [128, BT], F32)
        self.kd = sb("kd", [128, 2], F32)
        self.E = sb("E", [128, 17, 128], F32)
        self.mt_sb = sb("mt_sb", [128, 17, 128], F32)
        self.xst = sb("xst", [128, 16, BT], F32)
        self.sq = sb("sq", [128, 16, BT], BF16)
        self.hT = sb("hT", [128, 16, BT], BF16)
        self.rstd = sb("rstd", [128, 2, BT], F32)
        self.cs = sb("cs", [128, 2, 2, BT], F32)
        self.tmp = sb("tmp", [128, 2, BT], F32)
        self.QT = sb("QT", [128, 2, BT], BF16)
        self.QdT = sb("QdT", [128, 2, BT], BF16)
        self.KT = sb("KT", [128, 2, BT], BF16)
        self.Kd = sb("Kd", [128, 4, 256], BF16)
        self.VA = sb("VA", [128, 4, 256], BF16)
        self.Sm = sb("Sm", [128, 128], BF16)
        self.St = sb("St", [128, 2, 256], F32)
        self.Stb = sb("Stb", [128, 2, 256], BF16)
        self.qnT = sb("qnT", [128, BT], BF16)
        self.knR = sb("knR", [128, NRING, 128], BF16)
        self.vbR = sb("vbR", [128, NRING, 128], BF16)
        self.ex = sb("ex", [128, 17 * 128], F32)
        self.pT = sb("pT", [128, 17 * 128], BF16)
        self.rl_ = sb("rl_", [128, 128], F32)
        self.oretb = sb("oretb", [128, 2, 2, BT], F32)
        self.odilb = sb("odilb", [128, 2, BT], F32)
        self.ps = kb.psum("ps", [128, 8, 512], F32)

    def rstd_op(self, ps_ap, out_ap, inv_n, wait, post=1.0):
        nc = self.nc
        s = self.kb.sem("r_a")
        nc.scalar.wait_ge(wait[0], wait[1])
        s.inc(nc.scalar.activation(out=out_ap, in_=ps_ap, func=AF.Sqrt, scale=inv_n / post ** 2,
                                   bias=self.eps_sb[:, 0:1] if post == 1.0 else self.eps2_sb[:, 0:1]))
        nc.vector.wait_ge(s.h, s.n)
        return nc.vector.reciprocal(out=out_ap, in_=out_ap)

    def build(self):
        nc = self.nc
        kb = self.kb
        sem = kb.sem
        ps = self.ps
        V, A, PE, SP, PL = nc.vector, nc.scalar, nc.tensor, nc.sync, nc.gpsimd

        def W(eng, s):
            eng.wait_ge(s.h, s.n)

        s0 = sem("init")
        for k0 in range(0, 16, 4):
            s0.inc(PL.dma_start(out=self.w[:, k0:k0 + 4, :], in_=self.wA.rearrange("(c p) n -> p c n", p=128)[:, k0:k0 + 4, :]), 16)
        s0.inc(PL.dma_start(out=self.idb[:, :], in_=self.ident), 16)
        s1 = sem("init1")
        for (dst, src) in [(self.gain_sb[:, :], self.gain), (self.hg_sb[:, :], self.hg), (self.dm[:, :], self.dmask), (self.qd[:, :], self.qdrow),
                           (self.kd[:, :], self.kdec), (self.E[:, :, :], self.gtab), (self.mt_sb[:, :, :], self.mtab), (self.idf[:, :], self.ident)]:
            s1.inc(SP.dma_start(out=dst, in_=src), 16)
        V.memset(self.ones[:, :], 1.0)
        V.memset(self.eps_sb[:, :], EPS)
        V.memset(self.eps2_sb[:, :], EPS * 128.0)
        V.memset(self.St[:, :, :], 0.0)
        V.memset(self.Stb[:, :, :], 0.0)
        W(A, s1)
        sE = sem("sE")
        sE.inc(A.activation(out=self.E[:, :, :], in_=self.E[:, :, :], func=AF.Exp))
        W(V, sE)
        W(V, s1)
        sE2 = sem("sE2")
        sE2.inc(V.tensor_tensor(out=self.E[:, :, :], in0=self.E[:, :, :], in1=self.mt_sb[:, :, :], op=ALU.mult))
        W(PE, s0)
        W(PE, sE2)
        W(A, sE2)

        xv = self.xT.rearrange("(c p) t -> p c t", p=128)
        s_xl = sem("xl"); s_sq = sem("a_sq"); s_ss = sem("p_ss"); s_h = sem("d_h")
        s_cl = [sem("cl0"), sem("cl1")]
        s_pj = sem("p_pj")
        s_pf = sem("pjf")
        s_rot = sem("d_rot")
        s_sq2 = sem("a_sq2"); s_ss2 = sem("p_ss2")
        s_tr = sem("p_tr"); s_kd = sem("d_kd")
        s_sc = sem("p_sc"); s_sm = sem("d_sm"); s_o = sem("p_o"); s_oe = sem("a_oe"); s_ds = sem("p_ds"); s_st = sem("d_st")
        s_qk = sem("p_qk"); s_ex = sem("a_ex"); s_p = sem("d_p"); s_pv = sem("p_pv"); s_do = sem("d_do")
        s_or = [sem("or0"), sem("or1")]; s_od = [sem("od0"), sem("od1")]
        pj_idx = [0]
        rot_hist = []

        def proj_tile(cols, width, lhs_tok=None):
            i = pj_idx[0]
            bank = i % 2
            if i >= 2:
                PE.wait_ge(s_pf.h, i - 1)
            for k in range(16):
                if lhs_tok is None:
                    ins = PE.matmul(ps[:, bank, 0:BT], lhsT=self.w[:, k, cols:cols + 128], rhs=self.hT[:, k, :], start=(k == 0), stop=(k == 15))
                else:
                    ins = PE.matmul(ps[:, bank, 0:width], lhsT=self.hT[:, k, lhs_tok * 128:(lhs_tok + 1) * 128], rhs=self.w[:, k, cols:cols + width],
                                    start=(k == 0), stop=(k == 15))
            s_pj.inc(ins)
            pj_idx[0] += 1
            return bank

        for b in range(self.nblocks):
            t0 = b * BT
            sl = b % 2
            W(SP, s_h)
            for hh in range(2):
                s_xl.inc(SP.dma_start(out=self.xst[:, hh * 8:(hh + 1) * 8, :], in_=xv[:, hh * 8:(hh + 1) * 8, t0:t0 + BT]), 16)
            if b >= 2:
                SP.wait_ge(s_rot.h, rot_hist[b - 2])
            s_cl[sl].inc(SP.dma_start(out=self.cs[:, sl, 0, :], in_=self.cosT[:, t0:t0 + BT]), 16)
            s_cl[sl].inc(SP.dma_start(out=self.cs[:, sl, 1, :], in_=self.sinT[:, t0:t0 + BT]), 16)
            W(A, s_xl)
            W(A, s_ss)
            W(A, s_ss2)
            s_sq.inc(A.activation(out=self.sq[:, :, :], in_=self.xst[:, :, :], func=AF.Square))
            W(PE, s_sq)
            for k in range(16):
                ins = PE.matmul(ps[:, 2, :], lhsT=self.ones[:, :], rhs=self.sq[:, k, :], start=(k == 0), stop=(k == 15))
            s_ss.inc(ins)
            self.rstd_op(ps[:, 2, :], self.rstd[:, 0, :], 1.0 / D, (s_ss.h, s_ss.n))
            W(V, s_pj)
            for k in range(16):
                ins = V.scalar_tensor_tensor(out=self.hT[:, k, :], in0=self.xst[:, k, :], scalar=self.gain_sb[:, k:k + 1], in1=self.rstd[:, 0, :],
                                             op0=ALU.mult, op1=ALU.mult)
            s_h.inc(ins)
            W(PE, s_h)
            W(V, s_cl[sl])
            for which, col0, dstT in ((0, 0, self.QT), (1, 256, self.KT)):
                b0 = proj_tile(col0, 128)
                b1 = proj_tile(col0 + 128, 128)
                W(V, s_pj)
                if which == 0:
                    W(V, s_o)
                    W(V, s_sc)
                else:
                    W(V, s_tr)
                    W(V, s_sc)
                cosv = self.cs[:, sl, 0, :]; sinv = self.cs[:, sl, 1, :]
                V.tensor_tensor(out=self.tmp[:, 0, :], in0=ps[:, b0, :], in1=cosv, op=ALU.mult)
                V.tensor_tensor(out=self.tmp[:, 1, :], in0=ps[:, b1, :], in1=sinv, op=ALU.mult)
                V.tensor_tensor(out=dstT[:, 0, :], in0=self.tmp[:, 0, :], in1=self.tmp[:, 1, :], op=ALU.subtract)
                V.tensor_tensor(out=self.tmp[:, 0, :], in0=ps[:, b0, :], in1=sinv, op=ALU.mult)
                ins = V.tensor_tensor(out=self.tmp[:, 1, :], in0=ps[:, b1, :], in1=cosv, op=ALU.mult)
                s_pf.inc(ins, 2)
                ins = V.tensor_tensor(out=dstT[:, 1, :], in0=self.tmp[:, 0, :], in1=self.tmp[:, 1, :], op=ALU.add)
                if which == 0:
                    for i in range(2):
                        ins = V.tensor_tensor(out=self.QdT[:, i, :], in0=self.QT[:, i, :], in1=self.qd[:, :], op=ALU.mult)
                s_rot.inc(ins)
            rot_hist.append(s_rot.n)
            for which, col0 in ((0, 512), (1, 640)):
                bk = proj_tile(col0, 128)
                W(A, s_pj)
                W(A, s_ss2)
                s_sq2.inc(A.activation(out=self.sq[:, 0, :], in_=ps[:, bk, :], func=AF.Square))
                W(PE, s_sq2)
                ins = PE.matmul(ps[:, 2, :], lhsT=self.ones[:, :], rhs=self.sq[:, 0, :], start=True, stop=True)
                s_ss2.inc(ins)
                if which == 0:
                    self.rstd_op(ps[:, 2, :], self.rstd[:, 1, :], 1.0 / 128, (s_ss2.h, s_ss2.n), post=128 ** -0.5)
                    W(V, s_pv)
                    W(V, s_qk)
                    ins = V.scalar_tensor_tensor(out=self.qnT[:, :], in0=ps[:, bk, :], scalar=self.hg_sb[:, 0:1], in1=self.rstd[:, 1, :],
                                                 op0=ALU.mult, op1=ALU.mult)
                else:
                    self.rstd_op(ps[:, 2, :], self.rstd[:, 1, :], 1.0 / 128, (s_ss2.h, s_ss2.n))
                    W(V, s_qk)
                    for j in range(4):
                        slot = (4 * b + j) % NRING
                        ins = V.scalar_tensor_tensor(out=self.knR[:, slot, :], in0=ps[:, bk, j * 128:(j + 1) * 128], scalar=self.hg_sb[:, 1:2],
                                                     in1=self.rstd[:, 1, j * 128:(j + 1) * 128], op0=ALU.mult, op1=ALU.mult)
                s_pf.inc(ins, 1)
            for c in range(4):
                bk = proj_tile(768, 384, lhs_tok=c)
                W(V, s_pj)
                if c == 0:
                    W(V, s_ds)
                    W(V, s_o)
                    W(V, s_pv)
                V.tensor_copy(out=self.VA[:, c, :], in_=ps[:, bk, 0:256])
                ins = V.tensor_copy(out=self.vbR[:, (4 * b + c) % NRING, :], in_=ps[:, bk, 256:384])
                s_pf.inc(ins, 1)
            W(PE, s_rot)
            for c in range(4):
                W(PE, s_kd)
                for i in range(2):
                    ins = PE.matmul(ps[:, 3, i * 128:(i + 1) * 128], lhsT=self.KT[:, i, c * 128:(c + 1) * 128], rhs=self.idb[:, :], start=True, stop=True)
                s_tr.inc(ins)
                W(V, s_tr)
                if c == 0:
                    W(V, s_ds)
                for i in range(2):
                    ins = V.tensor_scalar(out=self.Kd[:, c, i * 128:(i + 1) * 128], in0=ps[:, 3, i * 128:(i + 1) * 128], scalar1=self.kd[:, 0:1], scalar2=None, op0=ALU.mult)
                s_kd.inc(ins)
            if b % 2 == 0 or True:
                V.wait_ge(s_or[sl].h, s_or[sl].n)
                A.wait_ge(s_or[sl].h, s_or[sl].n)
            for c in range(4):
                cs_ = slice(c * 128, (c + 1) * 128)
                W(PE, s_sm)
                W(PE, s_kd)
                for i in range(2):
                    ins = PE.matmul(ps[:, 3, 256:384], lhsT=self.KT[:, i, cs_], rhs=self.QT[:, i, cs_], start=(i == 0), stop=(i == 1))
                s_sc.inc(ins)
                W(V, s_sc)
                W(V, s_o)
                s_sm.inc(V.tensor_tensor(out=self.Sm[:, :], in0=ps[:, 3, 256:384], in1=self.dm[:, :], op=ALU.mult))
                W(PE, s_sm)
                W(PE, s_pf)
                W(PE, s_st)
                W(PE, s_oe)
                for j in range(2):
                    PE.matmul(ps[:, 4, j * 128:(j + 1) * 128], lhsT=self.VA[:, c, j * 128:(j + 1) * 128], rhs=self.Sm[:, :], start=True, stop=False)
                    for i in range(2):
                        ins = PE.matmul(ps[:, 4, j * 128:(j + 1) * 128], lhsT=self.Stb[:, i, j * 128:(j + 1) * 128], rhs=self.QdT[:, i, cs_],
                                        start=False, stop=(i == 1))
                s_o.inc(ins)
                W(A, s_o)
                for j in range(2):
                    ins = A.activation(out=self.oretb[:, sl, j, cs_], in_=ps[:, 4, j * 128:(j + 1) * 128], func=AF.Copy)
                s_oe.inc(ins)
                W(PE, s_kd)
                for i in range(2):
                    ins = PE.matmul(ps[:, 5, i * 256:(i + 1) * 256], lhsT=self.Kd[:, c, i * 128:(i + 1) * 128], rhs=self.VA[:, c, :], start=True, stop=True)
                s_ds.inc(ins)
                W(V, s_ds)
                W(V, s_o)
                for i in range(2):
                    V.scalar_tensor_tensor(out=self.St[:, i, :], in0=self.St[:, i, :], scalar=self.kd[:, 1:2], in1=ps[:, 5, i * 256:(i + 1) * 256],
                                           op0=ALU.mult, op1=ALU.add)
                ins = V.tensor_copy(out=self.Stb[:, :, :], in_=self.St[:, :, :])
                s_st.inc(ins)
            W(SP, s_oe)
            s_or[sl].inc(SP.dma_start(out=self.oret.rearrange("(j p) t -> p j t", p=128)[:, :, t0:t0 + BT], in_=self.oretb[:, sl, :, :]), 16)
            W(PE, s_pf)
            W(PE, s_oe)
            W(PE, s_st)
            W(PE, s_sm)
            W(PE, s_kd)
            V.wait_ge(s_od[sl].h, s_od[sl].n)
            sbanks = [0, 1, 2, 3, 6]
            for qt in range(4):
                tq = 4 * b + qt
                nk = min(17, tq + 1)
                W(PE, s_do)
                W(PE, s_ex)
                for o in range(nk):
                    slot = (tq - o) % NRING
                    ins = PE.matmul(ps[:, sbanks[o // 4], (o % 4) * 128:(o % 4 + 1) * 128], lhsT=self.knR[:, slot, :], rhs=self.qnT[:, qt * 128:(qt + 1) * 128],
                                    start=True, stop=True)
                s_qk.inc(ins)
                W(A, s_qk)
                W(A, s_p)
                n0 = min(nk, 16)
                ins = A.activation(out=self.ex[:, 0:n0 * 128], in_=ps[:, 0:4, :].rearrange("p b c -> p (b c)")[:, 0:n0 * 128], func=AF.Exp)
                if nk == 17:
                    ins = A.activation(out=self.ex[:, 2048:2176], in_=ps[:, 6, 0:128], func=AF.Exp)
                s_ex.inc(ins)
                W(V, s_ex)
                W(V, s_pv)
                s_p.inc(V.tensor_tensor(out=self.pT[:, 0:nk * 128], in0=self.ex[:, 0:nk * 128],
                                        in1=self.E[:, 0:nk, :].rearrange("p o q -> p (o q)"), op=ALU.mult))
                W(PE, s_p)
                for o in range(nk):
                    slot = (tq - o) % NRING
                    first = (o == 0)
                    last = (o == nk - 1)
                    PE.matmul(ps[:, 7, 0:128], lhsT=self.vbR[:, slot, :], rhs=self.pT[:, o * 128:(o + 1) * 128], start=first, stop=last, skip_group_check=True)
                    ins = PE.matmul(ps[:, 7, 128:256], lhsT=self.ones[:, :], rhs=self.pT[:, o * 128:(o + 1) * 128], start=False, stop=last, skip_group_check=True)
                s_pv.inc(ins)
                W(V, s_pv)
                V.reciprocal(out=self.rl_[:, :], in_=ps[:, 7, 128:256])
                s_do.inc(V.tensor_tensor(out=self.odilb[:, sl, qt * 128:(qt + 1) * 128], in0=ps[:, 7, 0:128], in1=self.rl_[:, :], op=ALU.mult))
            W(SP, s_do)
            s_od[sl].inc(SP.dma_start(out=self.odil[:, t0:t0 + BT], in_=self.odilb[:, sl, :]), 16)
        for s in s_or + s_od:
            W(SP, s)
        return nc


def t5_bucket_np(dist):
    exact = 16
    d = np.maximum(dist, exact).astype(np.float32)
    large = exact + (np.log(d / np.float32(exact)) / np.float32(math.log(2048 / exact)) * np.float32(32 - exact)).astype(np.int32)
    large = np.minimum(large, 31)
    return np.where(dist < exact, dist, large)


def mixa_consts():
    i = np.arange(128, dtype=np.float32)
    inv = (np.float32(10000.0) ** (-(np.arange(0, 256, 2, dtype=np.float32)) / np.float32(256))).astype(np.float32)
    pos = np.arange(S, dtype=np.float32)
    ang = (inv[:, None] * pos[None, :]).astype(np.float32)
    cosT = np.cos(ang).astype(np.float32)
    sinT = np.sin(ang).astype(np.float32)
    kj = np.arange(128)[:, None, None]
    o = np.arange(17)[None, :, None]
    qi = np.arange(128)[None, None, :]
    delta = qi - kj + 128 * o
    valid = delta >= 0
    m = ((delta <= 128) & valid).astype(np.float32) + ((delta % 4 == 0) & (delta <= 512) & valid) + ((delta % 16 == 0) & (delta <= 2048) & valid)
    bidx = t5_bucket_np(np.maximum(delta, 0))
    return cosT, sinT, m.astype(np.float32), bidx


def mixa_inputs(inp, c, xT, consts):
    cosT, sinT, mtab, bidx = consts
    hr, vh, hd = c // 2, c % 2, c
    W = inp["ar_w_in"][0]
    wA = np.concatenate([W[:, hr * 256:(hr + 1) * 256], W[:, 1024 + hr * 256:1024 + (hr + 1) * 256],
                         W[:, 6144 + hd * 128:6144 + (hd + 1) * 128], W[:, 7168 + hd * 128:7168 + (hd + 1) * 128],
                         W[:, 2048 + hr * 512 + vh * 256:2048 + hr * 512 + (vh + 1) * 256], W[:, 8192 + hd * 128:8192 + (hd + 1) * 128]], axis=1)
    gamma = 1.0 - 2.0 ** (-5.0 - hr)
    kj = np.arange(128)[:, None]; qi = np.arange(128)[None, :]
    dmask = np.where(qi >= kj, gamma ** np.maximum(qi - kj, 0), 0.0) * 256 ** -0.5
    qdrow = np.tile(gamma ** (np.arange(128) + 1.0), 4)[None, :].repeat(128, axis=0)
    kdec = np.stack([gamma ** (127.0 - np.arange(128)) * 256 ** -0.5, np.full(128, gamma ** 128.0)], axis=1)
    gtab = inp["rel_bias"][:, hd][bidx]
    return {
        "xT": xT, "wA": np.ascontiguousarray(wA), "gain": np.ascontiguousarray(inp["norm_mix"][0].reshape(16, 128).T),
        "hg": np.ascontiguousarray(np.stack([inp["dil_q_gain"][0], inp["dil_k_gain"][0]], axis=1)),
        "cosT": cosT, "sinT": sinT, "dmask": dmask.astype(np.float32), "qdrow": qdrow.astype(np.float32), "kdec": kdec.astype(np.float32),
        "gtab": np.ascontiguousarray(gtab.astype(np.float32)), "mtab": mtab, "ident": np.eye(128, dtype=np.float32),
    }


D = 2048
S = 16384
BT = 512
NB_C = S // BT
NCOL_C = 1032
C = 128
G = 2


def fl(ap):
    return ap.rearrange("p a b -> p (a b)")


def fl4(t):
    return t[:, :, :, :].rearrange("p a b c -> p (a b c)")


class MixC:
    def __init__(self, nblocks=NB_C):
        self.nblocks = nblocks
        self.kb = kb = KB()
        self.nc = kb.nc
        self.xT = kb.din("xT", [D, S])
        self.wC = kb.din("wC", [D, NCOL_C])
        self.gain = kb.din("gain", [128, 16])
        self.convw = kb.din("convw", [128, 8, 4])
        self.hp = kb.din("hp", [128, 2, G, 4])
        self.U = kb.din("U", [C, C])
        self.MU = kb.din("MU", [C, 4 * G, C])
        self.ML = kb.din("ML", [C, 4 * G, C])
        self.I4 = kb.din("I4", [C, 4 * G, C])
        self.ident = kb.din("ident", [128, 128])
        self.og = kb.dout("og", [512, S])
        sb = kb.sb
        self.w = sb("w", [128, 16, NCOL_C], BF16)
        self.ones = sb("ones", [128, 128], BF16)
        self.onesf = sb("onesf", [C, 128], F32)
        self.idb = sb("idb", [128, 128], BF16)
        self.idf = sb("idf", [128, 128], F32)
        self.gain_sb = sb("gain_sb", [128, 16], F32)
        self.cw = sb("cw", [128, 8, 4], F32)
        self.hp_sb = sb("hp_sb", [128, 2, G, 4], F32)
        self.negA = sb("negA", [128, G, 4], F32)
        self.eps_sb = sb("eps_sb", [128, 1], F32)
        self.eps2_sb = sb("eps2_sb", [128, 1], F32)
        self.one_sb = sb("one_sb", [128, 1], F32)
        self.U_sb = sb("U_sb", [C, C], F32)
        self.MU_sb = sb("MU_sb", [C, 4 * G, C], F32)
        self.ML_sb = sb("ML_sb", [C, 4 * G, C], F32)
        self.I4_sb = sb("I4_sb", [C, 4 * G, C], F32)
        self.xst = sb("xst", [128, 16, BT], F32)
        self.sq = sb("sq", [128, 8, BT], BF16)
        self.hT = sb("hT", [128, 16, BT], BF16)
        self.rstd = sb("rstd", [128, BT], F32)
        self.praw = sb("praw", [128, 3 + BT], F32)
        self.halo = sb("halo", [128, 8, 3], F32)
        self.acc = sb("acc", [128, BT], F32)
        self.cvs = sb("cvs", [128, 4, BT], F32)
        self.qnT = sb("qnT", [128, 2, BT], BF16)
        self.knT = sb("knT", [128, 2, BT], BF16)
        self.vT = sb("vT", [128, 4, BT], BF16)
        self.ktok = sb("ktok", [C, G * 2 * 128], F32)
        self.vtok = sb("vtok", [C, G, 512], F32)
        self.xa = sb("xa", [C, G, 4], F32)
        self.beta = sb("beta", [C, G, 4], F32)
        self.nbeta = sb("nbeta", [C, G, 4], F32)
        self.g = sb("g", [C, G, 4], F32)
        self.gcc = sb("gcc", [C, G, 4], F32)
        self.egc = sb("egc", [C, G, 4], F32)
        self.bg = sb("bg", [C, G, 4], F32)
        self.egl = sb("egl", [128, G, 4], F32)
        self.zz = sb("zz", [C, 2 * G * 4 * C], F32)
        self.G1 = self.zz[:, 0:G * 4 * 128].rearrange("p (g h d) -> p g h d", g=G, h=4)
        self.Zmin = self.zz[:, 0:G * 4 * C].rearrange("p (g h d) -> p g h d", g=G, h=4)
        self.Zmax = self.zz[:, G * 4 * C:2 * G * 4 * C].rearrange("p (g h d) -> p g h d", g=G, h=4)
        self.E0T = sb("E0T", [C, G, 4, C], F32)
        self.E1 = sb("E1", [C, G, 4, C], F32)
        self.eR = sb("eR", [128, G, 4 * C], F32)
        self.X = sb("X", [C, G, 4, C], BF16)
        self.Y = sb("Y", [C, G, 4, C], BF16)
        self.Q = sb("Q", [C, G, 4, C], F32)
        self.Tt = sb("Tt", [C, G, 4, C], BF16)
        self.intraT = sb("intraT", [C, G, 4, C], BF16)
        self.vb = sb("vb", [C, G, 4, 128], BF16)
        self.kbg = sb("kbg", [C, G, 4, 128], BF16)
        self.kdk = sb("kdk", [C, G, 4, 128], BF16)
        self.qgT = sb("qgT", [128, G, 4, C], BF16)
        self.nwT = sb("nwT", [128, G, 4, C], BF16)
        self.vnew = sb("vnew", [C, 4, 128], BF16)
        self.St = sb("St", [128, 4, 128], F32)
        self.Stb = sb("Stb", [128, 4, 128], BF16)
        self.ob = sb("ob", [128, 1, 4, BT], F32)
        self.ps = kb.psum("ps", [128, 8, 512], F32)
        self.dtb = self.hp_sb[:, 1, :, :]

    def build(self, debug=False):
        nc = self.nc
        kb = self.kb
        ps = self.ps
        V, A, PE, SP, PL = nc.vector, nc.scalar, nc.tensor, nc.sync, nc.gpsimd
        sems = {"V": kb.sem("sV"), "A": kb.sem("sA"), "P": kb.sem("sP")}
        engs = {"V": V, "A": A, "P": PE}

        def step(e, fn, extra=()):
            eng = engs[e]
            for o in sems:
                if o != e and sems[o].n > 0:
                    eng.wait_ge(sems[o].h, sems[o].n)
            for (sh, sv) in extra:
                eng.wait_ge(sh, sv)
            ins = fn()
            sems[e].inc(ins)

        def sp_wait_all():
            for o in sems:
                if sems[o].n > 0:
                    SP.wait_ge(sems[o].h, sems[o].n)

        s0 = kb.sem("init"); s1 = kb.sem("init1")
        wv = self.wC.rearrange("(c p) n -> p c n", p=128)
        for k0 in range(0, 16, 4):
            s0.inc(PL.dma_start(out=self.w[:, k0:k0 + 4, :], in_=wv[:, k0:k0 + 4, :]), 16)
        s0.inc(PL.dma_start(out=self.idb[:, :], in_=self.ident), 16)
        for (dst, src) in [(self.gain_sb[:, :], self.gain), (self.cw[:, :, :], self.convw), (self.hp_sb[:, :, :, :], self.hp), (self.U_sb[:, :], self.U),
                           (self.MU_sb[:, :, :], self.MU), (self.ML_sb[:, :, :], self.ML), (self.I4_sb[:, :, :], self.I4), (self.idf[:, :], self.ident)]:
            s1.inc(SP.dma_start(out=dst, in_=src), 16)

        def init_v():
            V.memset(self.ones[:, :], 1.0)
            V.memset(self.onesf[:, :], 1.0)
            V.memset(self.eps_sb[:, :], EPS)
            V.memset(self.one_sb[:, :], 1.0)
            V.memset(self.eps2_sb[:, :], EPS * 128.0)
            V.memset(self.St[:, :, :], 0.0)
            V.memset(self.Stb[:, :, :], 0.0)
            return V.memset(self.halo[:, :, :], 0.0)
        step("V", init_v)
        step("A", lambda: A.activation(out=self.negA[:, :, :], in_=self.hp_sb[:, 0, :, :], func=AF.Exp), extra=[(s1.h, s1.n)])
        step("V", lambda: V.tensor_scalar(out=fl(self.negA[:, :, :]), in0=fl(self.negA[:, :, :]), scalar1=-1.0, scalar2=None, op0=ALU.mult), extra=[(s0.h, s0.n), (s1.h, s1.n)])
        PE.wait_ge(s0.h, s0.n)
        PE.wait_ge(s1.h, s1.n)

        xv = self.xT.rearrange("(c p) t -> p c t", p=128)
        s_xl = kb.sem("xl")
        s_o = [kb.sem("so0"), kb.sem("so1")]

        def load_x(b):
            t0 = b * BT
            for hh in range(2):
                s_xl.inc(SP.dma_start(out=self.xst[:, hh * 8:(hh + 1) * 8, :], in_=xv[:, hh * 8:(hh + 1) * 8, t0:t0 + BT]), 16)

        load_x(0)
        for b in range(self.nblocks):
            t0 = b * BT
            sl = 0
            for hf in range(2):
                step("A", lambda hf=hf: A.activation(out=self.sq[:, :, :], in_=self.xst[:, hf * 8:(hf + 1) * 8, :], func=AF.Square), extra=[(s_xl.h, s_xl.n)])

                def f(hf=hf):
                    for k in range(8):
                        ins = PE.matmul(ps[:, 2, :], lhsT=self.ones[:, :], rhs=self.sq[:, k, :], start=(hf == 0 and k == 0), stop=(hf == 1 and k == 7))
                    return ins
                step("P", f)
            step("A", lambda: A.activation(out=self.rstd[:, :], in_=ps[:, 2, :], func=AF.Sqrt, scale=1.0 / D, bias=self.eps_sb[:, 0:1]))

            def f():
                V.reciprocal(out=self.rstd[:, :], in_=self.rstd[:, :])
                for k in range(16):
                    ins = V.scalar_tensor_tensor(out=self.hT[:, k, :], in0=self.xst[:, k, :], scalar=self.gain_sb[:, k:k + 1], in1=self.rstd[:, :],
                                                 op0=ALU.mult, op1=ALU.mult)
                return ins
            step("V", f)
            if b + 1 < self.nblocks:
                sp_wait_all()
                load_x(b + 1)
            for mt in range(8):
                def f(mt=mt):
                    for k in range(16):
                        ins = PE.matmul(ps[:, mt % 2, :], lhsT=self.w[:, k, mt * 128:(mt + 1) * 128], rhs=self.hT[:, k, :], start=(k == 0), stop=(k == 15))
                    return ins
                step("P", f)
                step("A", lambda mt=mt: A.activation(out=self.praw[:, 3:3 + BT], in_=ps[:, mt % 2, :], func=AF.Copy))

                def f(mt=mt):
                    V.tensor_copy(out=self.praw[:, 0:3], in_=self.halo[:, mt, :])
                    V.tensor_scalar(out=self.acc[:, :], in0=self.praw[:, 0:BT], scalar1=self.cw[:, mt, 0:1], scalar2=None, op0=ALU.mult)
                    for j in range(1, 4):
                        ins = V.scalar_tensor_tensor(out=self.acc[:, :], in0=self.praw[:, j:j + BT], scalar=self.cw[:, mt, j:j + 1], in1=self.acc[:, :],
                                                     op0=ALU.mult, op1=ALU.add)
                    return ins
                step("V", f)
                step("A", lambda mt=mt: A.activation(out=(self.cvs[:, mt, :] if mt < 4 else self.vT[:, mt - 4, :]), in_=self.acc[:, :], func=AF.Silu))
                step("V", lambda mt=mt: V.tensor_copy(out=self.halo[:, mt, :], in_=self.praw[:, BT:BT + 3]))
            step("A", lambda: A.activation(out=self.sq[:, 0:4, :], in_=self.cvs[:, 0:4, :], func=AF.Square))
            for mt in range(4):
                step("P", lambda mt=mt: PE.matmul(ps[:, 2, :], lhsT=self.ones[:, :], rhs=self.sq[:, mt, :], start=True, stop=True))
                if mt < 2:
                    step("A", lambda: A.activation(out=self.rstd[:, :], in_=ps[:, 2, :], func=AF.Sqrt, scale=128.0, bias=self.eps2_sb[:, 0:1]))
                else:
                    step("A", lambda: A.activation(out=self.rstd[:, :], in_=ps[:, 2, :], func=AF.Sqrt, scale=1.0, bias=self.eps_sb[:, 0:1]))

                def f(mt=mt):
                    V.reciprocal(out=self.rstd[:, :], in_=self.rstd[:, :])
                    dst = self.qnT[:, mt, :] if mt < 2 else self.knT[:, mt - 2, :]
                    return V.tensor_tensor(out=dst, in0=self.cvs[:, mt, :], in1=self.rstd[:, :], op=ALU.mult)
                step("V", f)
            NL = 7
            for grp in range(BT // (C * G)):
                def csl(gi):
                    c0 = (grp * G + gi) * C
                    return slice(c0, c0 + C)

                def hs(h):
                    return slice(h * C, (h + 1) * C)

                def f():
                    for gi in range(G):
                        for k in range(16):
                            ins = PE.matmul(ps[:, 2, gi * 8:(gi + 1) * 8], lhsT=self.hT[:, k, csl(gi)], rhs=self.w[:, k, 1024:1032],
                                            start=(k == 0), stop=(k == 15))
                    return ins
                step("P", f)
                bav = ps[:, 2, 0:G * 8].rearrange("p (g c) -> p g c", c=8)
                step("V", lambda: V.tensor_tensor(out=self.xa[:, :, :], in0=bav[:, :, 4:8], in1=self.dtb[:, :, :], op=ALU.add))

                def f():
                    A.activation(out=fl(self.xa[:, :, :]), in_=fl(self.xa[:, :, :]), func=AF.Exp)
                    return A.activation(out=fl(self.xa[:, :, :]), in_=fl(self.xa[:, :, :]), func=AF.Ln, bias=self.one_sb[:, 0:1])
                step("A", f)
                step("V", lambda: V.tensor_tensor(out=fl(self.g[:, :, :]), in0=fl(self.xa[:, :, :]), in1=fl(self.negA[:, :, :]), op=ALU.mult))
                step("A", lambda: A.activation(out=self.beta[:, :, :], in_=bav[:, :, 0:4], func=AF.Sigmoid))

                def f():
                    V.tensor_scalar(out=fl(self.nbeta[:, :, :]), in0=fl(self.beta[:, :, :]), scalar1=-1.0, scalar2=None, op0=ALU.mult)
                    for gi in range(G):
                        for h in range(4):
                            ins = V.tensor_scalar(out=self.G1[:, gi, h, :], in0=self.onesf[:, :], scalar1=self.g[:, gi, h:h + 1], scalar2=None, op0=ALU.mult)
                    return ins
                step("V", f)

                def f():
                    for gi in range(G):
                        PE.matmul(ps[:, 2, 64 + gi * 4:68 + gi * 4], lhsT=self.U_sb[:, :], rhs=self.g[:, gi, :], start=True, stop=True)
                        PE.matmul(ps[:, 2, 128 + gi * 4:132 + gi * 4], lhsT=self.onesf[:, :], rhs=self.g[:, gi, :], start=True, stop=True)
                    for gi in range(G):
                        for h in range(4):
                            PE.matmul(ps[:, 3 + gi, hs(h)], lhsT=self.G1[:, gi, h, :], rhs=self.U_sb[:, :], start=True, stop=True)
                    for gi in range(G):
                        for kh in range(2):
                            PE.matmul(ps[:, 5 + gi, hs(kh)], lhsT=self.knT[:, kh, csl(gi)], rhs=self.knT[:, kh, csl(gi)], start=True, stop=True)
                        for kh in range(2):
                            ins = PE.matmul(ps[:, 5 + gi, hs(2 + kh)], lhsT=self.knT[:, kh, csl(gi)], rhs=self.qnT[:, kh, csl(gi)], start=True, stop=True)
                    return ins
                step("P", f)

                def f():
                    A.activation(out=fl(self.gcc[:, :, :]), in_=ps[:, 2, 64:64 + G * 4], func=AF.Copy)
                    A.activation(out=fl(self.egl[:, :, :]), in_=ps[:, 2, 128:128 + G * 4], func=AF.Exp)
                    return A.activation(out=self.eR[:, :, :], in_=ps[:, 3:3 + G, :], func=AF.Exp)
                step("A", f)

                def f():
                    for gi in range(G):
                        for h in range(4):
                            V.tensor_scalar(out=self.Zmin[:, gi, h, :], in0=ps[:, 3 + gi, hs(h)], scalar1=self.gcc[:, gi, h:h + 1], scalar2=0.0,
                                            op0=ALU.subtract, op1=ALU.min)
                            ins = V.tensor_scalar(out=self.Zmax[:, gi, h, :], in0=ps[:, 3 + gi, hs(h)], scalar1=self.gcc[:, gi, h:h + 1], scalar2=0.0,
                                                  op0=ALU.subtract, op1=ALU.max)
                    return ins
                step("V", f)

                def f():
                    A.activation(out=fl(self.egc[:, :, :]), in_=fl(self.gcc[:, :, :]), func=AF.Exp)
                    A.activation(out=fl4(self.E0T), in_=fl4(self.Zmin), func=AF.Exp)
                    return A.activation(out=fl4(self.E1), in_=fl4(self.Zmax), func=AF.Exp, scale=-1.0)
                step("A", f)

                def f():
                    for gi in range(G):
                        for kh in range(2):
                            PE.matmul(ps[:, 0, (gi * 2 + kh) * 128:(gi * 2 + kh + 1) * 128], lhsT=self.knT[:, kh, csl(gi)], rhs=self.idb[:, :], start=True, stop=True)
                        for h in range(4):
                            ins = PE.matmul(ps[:, 3 + gi, hs(h)], lhsT=self.vT[:, h, csl(gi)], rhs=self.idb[:, :], start=True, stop=True)
                    return ins
                step("P", f)

                def f():
                    A.activation(out=self.ktok[:, :], in_=ps[:, 0, :], func=AF.Copy)
                    return A.activation(out=self.vtok[:, :, :], in_=ps[:, 3:3 + G, :], func=AF.Copy)
                step("A", f)

                def f():
                    V.tensor_tensor(out=fl(self.bg[:, :, :]), in0=fl(self.beta[:, :, :]), in1=fl(self.egc[:, :, :]), op=ALU.mult)
                    V.tensor_tensor(out=fl4(self.E0T), in0=fl4(self.E0T), in1=fl(self.MU_sb[:, :, :]), op=ALU.mult)
                    V.tensor_tensor(out=fl4(self.E1), in0=fl4(self.E1), in1=fl(self.ML_sb[:, :, :]), op=ALU.mult)
                    for gi in range(G):
                        ktv = self.ktok[:, :].rearrange("p (g k d) -> p g k d", g=G, k=2)[:, gi, :, :]
                        vtv = self.vtok[:, gi, :].rearrange("p (h d) -> p h d", h=4)
                        for h in range(4):
                            kh = h // 2
                            V.scalar_tensor_tensor(out=self.X[:, gi, h, :], in0=ps[:, 5 + gi, hs(kh)], scalar=self.nbeta[:, gi, h:h + 1],
                                                   in1=self.E1[:, gi, h, :], op0=ALU.mult, op1=ALU.mult)
                            V.tensor_tensor(out=self.intraT[:, gi, h, :], in0=ps[:, 5 + gi, hs(2 + kh)], in1=self.E0T[:, gi, h, :], op=ALU.mult)
                            V.tensor_scalar(out=self.vb[:, gi, h, :], in0=vtv[:, h, :], scalar1=self.beta[:, gi, h:h + 1], scalar2=None, op0=ALU.mult)
                            V.tensor_scalar(out=self.kbg[:, gi, h, :], in0=ktv[:, kh, :], scalar1=self.bg[:, gi, h:h + 1], scalar2=None, op0=ALU.mult)
                            V.tensor_scalar(out=self.kdk[:, gi, h, :], in0=ktv[:, kh, :], scalar1=self.E0T[:, gi, h, C - 1:C], scalar2=None, op0=ALU.mult)
                            ins = V.tensor_tensor(out=self.qgT[:, gi, h, :], in0=self.qnT[:, kh, csl(gi)], in1=self.eR[:, gi, hs(h)], op=ALU.mult)
                    return ins
                step("V", f)

                def f():
                    for gi in range(G):
                        for h in range(4):
                            ins = PE.matmul(ps[:, 2 + gi, hs(h)], lhsT=self.X[:, gi, h, :], rhs=self.idb[:, :], start=True, stop=True)
                    return ins
                step("P", f)
                gb = lambda t: fl4(t).rearrange("p (b c) -> p b c", b=G)
                step("A", lambda: A.activation(out=gb(self.Y), in_=ps[:, 2:2 + G, :], func=AF.Copy))

                def f():
                    V.tensor_tensor(out=gb(self.Q), in0=ps[:, 2:2 + G, :], in1=fl(self.I4_sb[:, :, :]).rearrange("p (b c) -> p b c", b=G), op=ALU.add)
                    return V.tensor_copy(out=fl4(self.Tt), in_=fl4(self.Q))
                step("V", f)
                for lv in range(NL):
                    def f(lv=lv):
                        ins = None
                        for gi in range(G):
                            for h in range(4):
                                if lv <= NL - 2:
                                    ins = PE.matmul(ps[:, 0 + gi, hs(h)], lhsT=self.Y[:, gi, h, :], rhs=self.X[:, gi, h, :], start=True, stop=True)
                                if lv <= NL - 3:
                                    ins = PE.matmul(ps[:, 2 + gi, hs(h)], lhsT=self.X[:, gi, h, :], rhs=self.Y[:, gi, h, :], start=True, stop=True)
                                if lv >= 1:
                                    ins = PE.matmul(ps[:, 4 + gi, hs(h)], lhsT=self.X[:, gi, h, :], rhs=self.Tt[:, gi, h, :], start=True, stop=True)
                        return ins
                    step("P", f)
                    if lv <= NL - 3:
                        step("A", lambda: A.activation(out=gb(self.Y), in_=ps[:, 2:2 + G, :], func=AF.Copy))

                    def f(lv=lv):
                        ins = None
                        if lv >= 1:
                            V.tensor_tensor(out=gb(self.Q), in0=gb(self.Q), in1=ps[:, 4:4 + G, :], op=ALU.add)
                            ins = V.tensor_copy(out=fl4(self.Tt), in_=fl4(self.Q))
                        if lv <= NL - 2:
                            ins = V.tensor_copy(out=gb(self.X), in_=ps[:, 0:G, :])
                        return ins
                    step("V", f)

                def f():
                    for gi in range(G):
                        for h in range(4):
                            ins = PE.matmul(ps[:, 6 + gi, hs(h)], lhsT=self.kbg[:, gi, h, :], rhs=self.Tt[:, gi, h, :], start=True, stop=True)
                    return ins
                step("P", f)
                step("A", lambda: A.activation(out=gb(self.nwT), in_=ps[:, 6:6 + G, :], func=AF.Copy, scale=-1.0))

                for gi in range(G):
                    def f(gi=gi):
                        for h in range(4):
                            PE.matmul(ps[:, 0, h * 128:(h + 1) * 128], lhsT=self.Tt[:, gi, h, :], rhs=self.vb[:, gi, h, :], start=(h == 0), stop=False, skip_group_check=True)
                        for h in range(4):
                            ins = PE.matmul(ps[:, 0, h * 128:(h + 1) * 128], lhsT=self.nwT[:, gi, h, :], rhs=self.Stb[:, h, :], start=False, stop=(h == 3), skip_group_check=True)
                        return ins
                    step("P", f)
                    step("V", lambda: V.tensor_copy(out=fl(self.vnew[:, :, :]), in_=ps[:, 0, :]))

                    def f(gi=gi):
                        for h in range(4):
                            PE.matmul(ps[:, 1, hs(h)], lhsT=self.Stb[:, h, :], rhs=self.qgT[:, gi, h, :], start=(h == 0), stop=False, skip_group_check=True)
                        for h in range(4):
                            PE.matmul(ps[:, 1, hs(h)], lhsT=self.vnew[:, h, :], rhs=self.intraT[:, gi, h, :], start=False, stop=(h == 3), skip_group_check=True)
                        for h in range(4):
                            ins = PE.matmul(ps[:, 2, h * 128:(h + 1) * 128], lhsT=self.kdk[:, gi, h, :], rhs=self.vnew[:, h, :], start=True, stop=True)
                        return ins
                    step("P", f)
                    extra = [(s_o[sl].h, s_o[sl].n)] if (grp == 0 and gi == 0) else []
                    step("A", lambda gi=gi: A.activation(out=self.ob[:, sl, :, csl(gi)], in_=ps[:, 1, :].rearrange("p (h i) -> p h i", h=4), func=AF.Copy), extra=extra)

                    def f(gi=gi):
                        for h in range(4):
                            V.scalar_tensor_tensor(out=self.St[:, h, :], in0=self.St[:, h, :], scalar=self.egl[:, gi, h:h + 1], in1=ps[:, 2, h * 128:(h + 1) * 128],
                                                   op0=ALU.mult, op1=ALU.add)
                        return V.tensor_copy(out=self.Stb[:, :, :], in_=self.St[:, :, :])
                    step("V", f)
            sp_wait_all()
            s_o[sl].inc(SP.dma_start(out=self.og.rearrange("(h p) t -> p h t", p=128)[:, :, t0:t0 + BT], in_=self.ob[:, sl, :, :]), 16)
        for s in s_o:
            SP.wait_ge(s.h, s.n)
        return nc


def mixc_consts():
    U = (np.arange(C)[:, None] <= np.arange(C)[None, :]).astype(np.float32)
    p = np.arange(C)[:, None, None]; f = np.arange(C)[None, None, :]
    MU = np.broadcast_to((f >= p), (C, 4 * G, C)).astype(np.float32)
    ML = np.broadcast_to((p > f), (C, 4 * G, C)).astype(np.float32)
    I4 = np.broadcast_to((p == f), (C, 4 * G, C)).astype(np.float32)
    return U, np.ascontiguousarray(MU), np.ascontiguousarray(ML), np.ascontiguousarray(I4)


def mixc_inputs(inp, c, xT, consts):
    U, MU, ML, I4 = consts
    W = inp["gdn_w_in"][0]
    kh0 = 2 * c; vh0 = 4 * c
    qc = slice(kh0 * 128, kh0 * 128 + 256)
    kc = slice(2048 + kh0 * 128, 2048 + kh0 * 128 + 256)
    vc = slice(4096 + vh0 * 128, 4096 + vh0 * 128 + 512)
    bc = slice(12288 + vh0, 12288 + vh0 + 4)
    ac = slice(12320 + vh0, 12320 + vh0 + 4)
    wC = np.concatenate([W[:, qc], W[:, kc], W[:, vc], W[:, bc], W[:, ac]], axis=1)
    cw = inp["gdn_conv"][0]
    cwc = np.concatenate([cw[:, qc], cw[:, kc], cw[:, vc]], axis=1)
    convw = np.ascontiguousarray(cwc.reshape(4, 8, 128).transpose(2, 1, 0))
    hp = np.stack([np.broadcast_to(inp["gdn_a_log"][0][vh0:vh0 + 4], (128, G, 4)), np.broadcast_to(inp["gdn_dt_bias"][0][vh0:vh0 + 4], (128, G, 4))], axis=1)
    return {"xT": xT, "wC": np.ascontiguousarray(wC), "gain": np.ascontiguousarray(inp["norm_mix"][1].reshape(16, 128).T),
            "convw": convw.astype(np.float32), "hp": np.ascontiguousarray(hp.astype(np.float32)), "U": U, "MU": MU, "ML": ML, "I4": I4,
            "ident": np.eye(128, dtype=np.float32)}


def _run(nc, ins):
    res = run_bass_kernel_spmd(nc, ins, core_ids=list(range(8)))
    return res.results


def kernel(**inputs):
    inp = {k: np.asarray(v) for k, v in inputs.items()}
    S_ = 16384
    x = inp["x"][0]
    xT = np.ascontiguousarray(x.T)
    consts = mixa_consts()
    nc = MixA().build()
    ra = _run(nc, [mixa_inputs(inp, c, xT, consts) for c in range(8)])
    oT0 = np.empty((3072, S_), np.float32)
    for c in range(8):
        hr, vh = c // 2, c % 2
        oT0[hr * 512 + vh * 256: hr * 512 + (vh + 1) * 256] = ra[c]["oret"]
        oT0[2048 + c * 128: 2048 + (c + 1) * 128] = ra[c]["odil"]
    del ra
    nc = Post(0).build()
    rb = _run(nc, [post_inputs(0, inp, np.ascontiguousarray(xT[:, c * 2048:(c + 1) * 2048]), np.ascontiguousarray(oT0[:, c * 2048:(c + 1) * 2048]))
                   for c in range(8)])
    x1T = np.ascontiguousarray(np.concatenate([rb[c]["xo"] for c in range(8)], axis=1))
    del rb, oT0
    cc = mixc_consts()
    nc = MixC().build()
    rc = _run(nc, [mixc_inputs(inp, c, x1T, cc) for c in range(8)])
    oT1 = np.ascontiguousarray(np.concatenate([rc[c]["og"] for c in range(8)], axis=0))
    del rc
    nc = Post(1).build()
    rd = _run(nc, [post_inputs(1, inp, np.ascontiguousarray(x1T[:, c * 2048:(c + 1) * 2048]), np.ascontiguousarray(oT1[:, c * 2048:(c + 1) * 2048]))
                   for c in range(8)])
    outT = np.concatenate([rd[c]["xo"] for c in range(8)], axis=1)
    return np.ascontiguousarray(outT.T)[None].astype(np.float32)
```

```python
import math
import numpy as np
from contextlib import ExitStack
from concourse.bass_utils import run_bass_kernel_spmd
import concourse.bass as bass
import concourse.mybir as mybir

F32 = mybir.dt.float32
BF16 = mybir.dt.bfloat16
AF = mybir.ActivationFunctionType
ALU = mybir.AluOpType
EPS = 1e-6


class Sem:
    def __init__(self, h):
        self.h = h
        self.n = 0

    def inc(self, ins, by=1):
        ins.then_inc(self.h, by)
        self.n += by
        return self.n


class KB:
    def __init__(self):
        self.nc = bass.Bass("TRN2", target_bir_lowering=False)
        self.es = ExitStack()
        self.sems = {}
        self.uid = 0

    def sb(self, name, shape, dt, es=None):
        return (es or self.es).enter_context(self.nc.sbuf_tensor(name, shape, dt))

    def psum(self, name, shape, dt, es=None):
        return (es or self.es).enter_context(self.nc.psum_tensor(name, shape, dt))

    def sem(self, name):
        if name not in self.sems:
            self.sems[name] = Sem(self.es.enter_context(self.nc.semaphore(name)))
        return self.sems[name]

    def din(self, name, shape, dt=F32):
        return self.nc.dram_tensor(name, list(shape), dt, kind="ExternalInput").ap()

    def dout(self, name, shape, dt=F32):
        return self.nc.dram_tensor(name, list(shape), dt, kind="ExternalOutput").ap()

    def dscr(self, name, shape, dt=F32):
        return self.nc.dram_tensor(name, list(shape), dt, kind="Internal").ap()


D = 2048
T = 2048
TT = 1024
NT = TT // 512
FF = 5632
WB = 8192


class Post:
    def __init__(self, layer):
        self.layer = layer
        self.FY = 3072 if layer == 0 else 4096
        self.G = 2048 if layer == 0 else 4096
        self.kb = kb = KB()
        nc = self.nc = kb.nc
        FY, G = self.FY, self.G
        self.xT = kb.din("xT", [D, T])
        self.oT = kb.din("oT", [FY, T])
        self.w_gate = kb.din("w_gate", [D, G])
        self.w_out = kb.din("w_out", [FY, D])
        self.gains = kb.din("gains", [128, 4, 16])
        self.hg = kb.din("hg", [128, 3])
        self.memT = kb.din("memT", [D, 256])
        self.w_q = kb.din("w_q", [D, 512])
        self.w_kv = kb.din("w_kv", [D, 1024])
        self.w_o = kb.din("w_o", [512, D])
        self.w1 = kb.din("w1", [D, FF])
        self.w3 = kb.din("w3", [D, FF])
        self.w2 = kb.din("w2", [FF, D])
        self.xo = kb.dout("xo", [D, T])
        self.x1 = kb.dout("x1s", [D, T])
        self.x2 = kb.dout("x2s", [D, T])
        self.wbuf = [kb.sb("wbuf0", [128, WB], BF16), kb.sb("wbuf1", [128, WB], BF16)]
        self.ones = kb.sb("ones", [128, 128], BF16)
        self.gains_sb = kb.sb("gains_sb", [128, 4, 16], F32)
        self.hg_sb = kb.sb("hg_sb", [128, 3], F32)
        self.eps_sb = kb.sb("eps_sb", [128, 1], F32)
        self.eps2_sb = kb.sb("eps2_sb", [128, 1], F32)
        self.hT = kb.sb("hT", [128, 16, TT], BF16)
        self.big = kb.sb("big", [128, 32 * TT], BF16)
        self.xst_f = kb.sb("xst", [128, 4096], F32)
        self.sq_f = kb.sb("sq", [128, 4096], BF16)
        self.xst_n = self.xst_f[:, :].rearrange("p (c t) -> p c t", t=256)
        self.xst_f2 = kb.sb("xst2", [128, 4096], F32)
        self.xst_n2 = self.xst_f2[:, :].rearrange("p (c t) -> p c t", t=256)
        self.sq_n = self.sq_f[:, :].rearrange("p (c t) -> p c t", t=256)
        self.xst = self.xst_f[:, 0:2048].rearrange("p (c t) -> p c t", t=512)
        self.sq = self.sq_f[:, 0:2048].rearrange("p (c t) -> p c t", t=512)
        self.qf = self.xst
        self.rstd = kb.sb("rstd", [128, 4, 512], F32)
        self.rbuf = kb.sb("rbuf", [128, 3, 512], F32)
        self.obuf = kb.sb("obuf", [128, 3, 512], F32)
        self.stmp = kb.sb("stmp", [128, 2, 512], F32)
        self.knT = kb.sb("knT", [128, 4, 256], BF16)
        self.vm = kb.sb("vm", [128, 2, 512], BF16)
        self.qn = self.big[:, 0:4 * TT].rearrange("p (c t) -> p c t", t=TT)
        self.pT = kb.sb("pT", [128, 2, 512], BF16)
        self.oxa = self.big[:, 4 * TT:8 * TT].rearrange("p (c t) -> p c t", t=TT)
        self.ps = kb.psum("ps", [128, 8, 512], F32)
        self.gidx = 0
        self.grp_end = []
        self.mm = kb.sem("g_mm")
        self.pf = kb.sem("g_pf")
        self.wl = [kb.sem("g_wl0"), kb.sem("g_wl1")]
        self.sts = [kb.sem("st%d" % i) for i in range(3)]
        self.bar_n = 0

    def wait_stores(self):
        for st in self.sts:
            self.nc.sync.wait_ge(st.h, st.n)

    def barrier(self):
        self.nc.all_engine_barrier()

    def y(self):
        return self.big[:, 0:(self.FY // 128) * TT].rearrange("p (c t) -> p c t", t=TT)

    def g(self):
        return self.big[:, 0:22 * TT].rearrange("p (c t) -> p c t", t=TT)

    def rstd_op(self, ps_ap, out_ap, inv_n, wait, post=1.0):
        nc = self.nc
        s = self.kb.sem("r_a")
        nc.scalar.wait_ge(wait[0], wait[1])
        s.inc(nc.scalar.activation(out=out_ap, in_=ps_ap, func=AF.Sqrt, scale=inv_n / post ** 2, bias=self.eps_sb[:, 0:1] if post == 1.0 else self.eps2_sb[:, 0:1]))
        nc.vector.wait_ge(s.h, s.n)
        return nc.vector.reciprocal(out=out_ap, in_=out_ap)

    def gemm(self, wsrc, KC, GW, ngroups, act, ntt, epi, pair=False, tw=512, pe_waits=()):
        nc = self.nc
        mpg = GW // 128
        if pair:
            mpg //= 2
        G0 = len(self.grp_end)

        def load(g):
            Gg = G0 + g
            b = Gg % 2
            if Gg >= 2:
                nc.gpsimd.wait_ge(self.mm.h, self.grp_end[Gg - 2])
            wv = self.wbuf[b][:, 0:KC * GW].rearrange("p (c n) -> p c n", n=GW)
            for (ap, off, w) in wsrc(g):
                src = ap.rearrange("(c p) n -> p c n", p=128)
                kstep = 8
                for k0 in range(0, KC, kstep):
                    k1 = min(KC, k0 + kstep)
                    self.wl[b].inc(nc.gpsimd.dma_start(out=wv[:, k0:k1, off:off + w], in_=src[:, k0:k1, :]), 16)
            return self.wl[b].n

        wl_need = {}
        wl_need[0] = load(0)
        cnt = 0
        for (sh, sv) in pe_waits:
            nc.tensor.wait_ge(sh, sv)
        for g in range(ngroups):
            if g + 1 < ngroups:
                wl_need[g + 1] = load(g + 1)
            b = (G0 + g) % 2
            nc.tensor.wait_ge(self.wl[b].h, wl_need[g])
            wv = self.wbuf[b][:, 0:KC * GW].rearrange("p (c n) -> p c n", n=GW)
            for j in range(mpg):
                for tt in range(ntt):
                    cols = [j] if not pair else [j, j + mpg]
                    ps_list = []
                    for cj in cols:
                        idx = self.gidx
                        bank = idx % 4
                        if idx >= 4:
                            nc.tensor.wait_ge(self.pf.h, idx - 3)
                        for k in range(KC):
                            ins = nc.tensor.matmul(self.ps[:, bank, 0:tw], lhsT=wv[:, k, cj * 128:(cj + 1) * 128],
                                                   rhs=act[:, k, tt * tw:(tt + 1) * tw], start=(k == 0), stop=(k == KC - 1))
                        self.mm.inc(ins)
                        self.gidx += 1
                        ps_list.append(self.ps[:, bank, 0:tw])
                    fin = epi(cnt, g * mpg + j, tt, ps_list, self.gidx)
                    self.pf.inc(fin, len(cols))
                    cnt += 1
            self.grp_end.append(self.mm.n)

    def norm(self, src, tok0, which, dst, ntok_tiles, tw=256, gains=None):
        nc = self.nc
        s_lds = [self.kb.sem("n_ld0"), self.kb.sem("n_ld1")]
        s_sq = self.kb.sem("n_sq"); s_mm = self.kb.sem("n_mm"); s_dv = self.kb.sem("n_dv")
        slots = [self.xst_n, self.xst_n2]
        srcv = src.rearrange("(c p) t -> p c t", p=128)
        base_dv = s_dv.n
        dv_hist = []

        def issue(tt):
            sl = tt % 2
            t0 = tok0 + tt * tw
            nc.sync.wait_ge(s_dv.h, dv_hist[tt - 2] if tt >= 2 else base_dv)
            for hh in range(2):
                s_lds[sl].inc(nc.sync.dma_start(out=slots[sl][:, hh * 8:(hh + 1) * 8, 0:tw], in_=srcv[:, hh * 8:(hh + 1) * 8, t0:t0 + tw]), 16)

        issue(0)
        for tt in range(ntok_tiles):
            sl = tt % 2
            xs = slots[sl]
            if tt + 1 < ntok_tiles:
                issue(tt + 1)
            nc.scalar.wait_ge(s_lds[sl].h, s_lds[sl].n)
            nc.scalar.wait_ge(s_mm.h, s_mm.n)
            s_sq.inc(nc.scalar.activation(out=self.sq_n[:, :, 0:tw], in_=xs[:, :, 0:tw], func=AF.Square))
            nc.tensor.wait_ge(s_sq.h, s_sq.n)
            nc.tensor.wait_ge(s_dv.h, s_dv.n)
            for k in range(16):
                ins = nc.tensor.matmul(self.ps[:, 4, 0:tw], lhsT=self.ones[:, :], rhs=self.sq_n[:, k, 0:tw], start=(k == 0), stop=(k == 15))
            s_mm.inc(ins)
            self.rstd_op(self.ps[:, 4, 0:tw], self.rstd[:, 0, 0:tw], 1.0 / D, (s_mm.h, s_mm.n))
            for k in range(16):
                ins = nc.vector.scalar_tensor_tensor(out=dst[:, k, tt * tw:(tt + 1) * tw], in0=xs[:, k, 0:tw],
                                                     scalar=self.gains_sb[:, which, k:k + 1], in1=self.rstd[:, 0, 0:tw],
                                                     op0=ALU.mult, op1=ALU.mult)
            s_dv.inc(ins)
            dv_hist.append(s_dv.n)
        return s_dv

    def onorm(self, tok0):
        nc = self.nc
        layer = self.layer
        y = self.y()
        s_ld = self.kb.sem("o_ld"); s_sq = self.kb.sem("o_sq"); s_mm = self.kb.sem("o_mm"); s_dv = self.kb.sem("o_dv")
        nblk = self.FY // 512
        nnorm = 4 if layer == 0 else 8
        ov = self.oT.rearrange("(c p) t -> p c t", p=128)
        for tt in range(NT):
            t0 = tok0 + tt * 512
            for blk in range(nblk):
                nc.sync.wait_ge(s_dv.h, s_dv.n)
                s_ld.inc(nc.sync.dma_start(out=self.xst[:, 0:4, :], in_=ov[:, blk * 4:(blk + 1) * 4, t0:t0 + 512]), 16)
                if blk >= nnorm:
                    nc.vector.wait_ge(s_ld.h, s_ld.n)
                    ins = nc.vector.tensor_copy(out=y[:, blk * 4:(blk + 1) * 4, tt * 512:(tt + 1) * 512], in_=self.xst[:, 0:4, :])
                    s_dv.inc(ins)
                    continue
                nc.scalar.wait_ge(s_ld.h, s_ld.n)
                nc.scalar.wait_ge(s_mm.h, s_mm.n)
                s_sq.inc(nc.scalar.activation(out=self.sq[:, 0:4, :], in_=self.xst[:, 0:4, :], func=AF.Square))
                nc.tensor.wait_ge(s_sq.h, s_sq.n)
                nc.tensor.wait_ge(s_dv.h, s_dv.n)
                if layer == 0:
                    for k in range(4):
                        ins = nc.tensor.matmul(self.ps[:, 4, :], lhsT=self.ones[:, :], rhs=self.sq[:, k, :], start=(k == 0), stop=(k == 3))
                else:
                    for k in range(4):
                        ins = nc.tensor.matmul(self.ps[:, 4 + k, :], lhsT=self.ones[:, :], rhs=self.sq[:, k, :], start=True, stop=True)
                s_mm.inc(ins)
                if layer == 0:
                    self.rstd_op(self.ps[:, 4, :], self.rstd[:, 0, :], 1.0 / 512, (s_mm.h, s_mm.n))
                    for k in range(4):
                        ins = nc.vector.tensor_tensor(out=y[:, blk * 4 + k, tt * 512:(tt + 1) * 512], in0=self.xst[:, k, :], in1=self.rstd[:, 0, :], op=ALU.mult)
                else:
                    for k in range(4):
                        self.rstd_op(self.ps[:, 4 + k, :], self.rstd[:, k, :], 1.0 / 128, (s_mm.h, s_mm.n))
                        ins = nc.vector.scalar_tensor_tensor(out=y[:, blk * 4 + k, tt * 512:(tt + 1) * 512], in0=self.xst[:, k, :],
                                                             scalar=self.hg_sb[:, 2:3], in1=self.rstd[:, k, :], op0=ALU.mult, op1=ALU.mult)
                s_dv.inc(ins)

    def epi_gate(self):
        nc = self.nc
        y = self.y()
        s_d = self.kb.sem("eg_d")
        base_d = s_d.n

        def epi(cnt, mt, tt, ps_list, idx_after):
            s = cnt % 2
            nc.scalar.wait_ge(self.mm.h, idx_after)
            if cnt >= 2:
                nc.scalar.wait_ge(s_d.h, base_d + cnt - 1)
            fin = nc.scalar.activation(out=self.stmp[:, s, :], in_=ps_list[0], func=AF.Silu)
            nc.vector.wait_ge(self.pf.h, idx_after)
            yv = y[:, mt, tt * 512:(tt + 1) * 512]
            s_d.inc(nc.vector.tensor_tensor(out=yv, in0=self.stmp[:, s, :], in1=yv, op=ALU.mult))
            return fin
        return epi

    def epi_swiglu(self):
        nc = self.nc
        g = self.g()
        s_a = self.kb.sem("es_a")
        hist = []

        def epi(cnt, mt, tt, ps_list, idx_after):
            s = cnt % 2
            nc.scalar.wait_ge(self.mm.h, idx_after)
            if cnt >= 2:
                nc.scalar.wait_ge(self.pf.h, hist[cnt - 2])
            s_a.inc(nc.scalar.activation(out=self.stmp[:, s, :], in_=ps_list[0], func=AF.Silu))
            nc.vector.wait_ge(s_a.h, s_a.n)
            fin = nc.vector.tensor_tensor(out=g[:, mt, tt * 512:(tt + 1) * 512], in0=self.stmp[:, s, :], in1=ps_list[1], op=ALU.mult)
            hist.append(idx_after)
            return fin
        return epi

    def epi_resid(self, res_src, dst, tok0, tiles):
        nc = self.nc
        rv = res_src.rearrange("(c p) t -> p c t", p=128)
        dv = dst.rearrange("(c p) t -> p c t", p=128)
        idx0 = self.gidx
        rls = [self.kb.sem("rl%d" % i) for i in range(3)]
        sts = self.sts

        def issue_load(c):
            mt, tt = tiles[c]
            if c >= 3:
                nc.sync.wait_ge(self.pf.h, idx0 + c - 2)
            rls[c % 3].inc(nc.sync.dma_start(out=self.rbuf[:, c % 3, :], in_=rv[:, mt, tok0 + tt * 512: tok0 + (tt + 1) * 512]), 16)

        def epi(cnt, mt, tt, ps_list, idx_after):
            if cnt == 0:
                issue_load(0)
                if len(tiles) > 1:
                    issue_load(1)
            if cnt + 2 < len(tiles):
                issue_load(cnt + 2)
            s = cnt % 3
            nc.vector.wait_ge(self.mm.h, idx_after)
            nc.vector.wait_ge(rls[s].h, rls[s].n if cnt + 3 >= len(tiles) or True else 0)
            nc.vector.wait_ge(sts[s].h, sts[s].n)
            fin = nc.vector.tensor_tensor(out=self.obuf[:, s, :], in0=ps_list[0], in1=self.rbuf[:, s, :], op=ALU.add)
            nc.sync.wait_ge(self.pf.h, idx_after)
            sts[s].inc(nc.sync.dma_start(out=dv[:, mt, tok0 + tt * 512: tok0 + (tt + 1) * 512], in_=self.obuf[:, s, :]), 16)
            return fin
        return epi

    def epi_plain(self, dstf):
        nc = self.nc

        def epi(cnt, mt, tt, ps_list, idx_after):
            nc.vector.wait_ge(self.mm.h, idx_after)
            return nc.vector.tensor_copy(out=dstf(mt, tt), in_=ps_list[0])
        return epi

    def mem_kv(self):
        nc = self.nc
        s_ld = self.kb.sem("m_ld"); s_a = self.kb.sem("m_a"); s_p = self.kb.sem("m_p"); s_d = self.kb.sem("m_d")
        memn = self.hT[:, :, 0:256]
        ndv = self.norm(self.memT, 0, 3, self.hT, 1, tw=256)
        self.barrier()
        self.gemm(lambda g: [(self.w_kv[:, 0:512], 0, 512)], 16, 512, 1, memn, 1,
                  self.epi_plain(lambda mt, tt: self.qf[:, mt, 0:256]), tw=256, pe_waits=[(ndv.h, ndv.n)])
        self.barrier()
        nc.scalar.wait_ge(self.pf.h, self.gidx)
        nc.scalar.activation(out=self.sq[:, 0:4, 0:256], in_=self.qf[:, 0:4, 0:256], func=AF.Square).then_inc(s_a.h, 1)
        nc.tensor.wait_ge(s_a.h, 1)
        for h in range(4):
            ins = nc.tensor.matmul(self.ps[:, 4 + h, 0:256], lhsT=self.ones[:, :], rhs=self.sq[:, h, 0:256], start=True, stop=True)
        ins.then_inc(s_p.h, 1)
        for h in range(4):
            self.rstd_op(self.ps[:, 4 + h, 0:256], self.rstd[:, h, 0:256], 1.0 / 128, (s_p.h, 1))
            nc.vector.scalar_tensor_tensor(out=self.knT[:, h, :], in0=self.qf[:, h, 0:256], scalar=self.hg_sb[:, 1:2], in1=self.rstd[:, h, 0:256],
                                           op0=ALU.mult, op1=ALU.mult)
        self.barrier()
        wv = self.wbuf[0][:, 0:16 * 512].rearrange("p (c n) -> p c n", n=512)
        src = self.w_kv[:, 512:1024].rearrange("(c p) n -> p c n", p=128)
        for k0 in (0, 8):
            nc.gpsimd.dma_start(out=wv[:, k0:k0 + 8, :], in_=src[:, k0:k0 + 8, :]).then_inc(s_ld.h, 16)
        nc.tensor.wait_ge(s_ld.h, 32)
        for c in range(2):
            for k in range(16):
                ins = nc.tensor.matmul(self.ps[:, 4 + c, :], lhsT=self.hT[:, k, c * 128:(c + 1) * 128], rhs=wv[:, k, :], start=(k == 0), stop=(k == 15))
        ins.then_inc(s_p.h, 1)
        nc.vector.wait_ge(s_p.h, 2)
        for c in range(2):
            ins = nc.vector.tensor_copy(out=self.vm[:, c, :], in_=self.ps[:, 4 + c, :])
        self.barrier()

    def xa_attn(self):
        nc = self.nc
        s_q = self.kb.sem("x_q"); s_a = self.kb.sem("x_a"); s_p = self.kb.sem("x_p"); s_d = self.kb.sem("x_d")
        scale = 128 ** -0.5
        for tt in range(NT):
            self.gemm(lambda g: [(self.w_q[:, :], 0, 512)], 16, 512, 1, self.hT[:, :, tt * 512:(tt + 1) * 512], 1,
                      self.epi_plain(lambda mt, t_: self.qf[:, mt, :]), pe_waits=[(self.kb.sem("n_dv").h, self.kb.sem("n_dv").n)])
            self.barrier()
            nc.scalar.wait_ge(self.pf.h, self.gidx)
            s_a.inc(nc.scalar.activation(out=self.sq[:, 0:4, :], in_=self.qf[:, 0:4, :], func=AF.Square))
            nc.tensor.wait_ge(s_a.h, s_a.n)
            for h in range(4):
                ins = nc.tensor.matmul(self.ps[:, 4 + h, :], lhsT=self.ones[:, :], rhs=self.sq[:, h, :], start=True, stop=True)
            s_p.inc(ins)
            for h in range(4):
                self.rstd_op(self.ps[:, 4 + h, :], self.rstd[:, h, :], 1.0 / 128, (s_p.h, s_p.n), post=scale)
                ins = nc.vector.scalar_tensor_tensor(out=self.qn[:, h, tt * 512:(tt + 1) * 512], in0=self.qf[:, h, :], scalar=self.hg_sb[:, 0:1],
                                                     in1=self.rstd[:, h, :], op0=ALU.mult, op1=ALU.mult)
            s_d.inc(ins)
            nc.tensor.wait_ge(s_d.h, s_d.n)
            nc.scalar.wait_ge(s_d.h, s_d.n)
            self.barrier()
            for h in range(4):
                for c in range(2):
                    ins = nc.tensor.matmul(self.ps[:, 4 + c, :], lhsT=self.knT[:, h, c * 128:(c + 1) * 128], rhs=self.qn[:, h, tt * 512:(tt + 1) * 512],
                                           start=True, stop=True)
                s_p.inc(ins)
                nc.scalar.wait_ge(s_p.h, s_p.n)
                for c in range(2):
                    ins = nc.scalar.activation(out=self.pT[:, c, :], in_=self.ps[:, 4 + c, :], func=AF.Exp)
                s_a.inc(ins)
                nc.tensor.wait_ge(s_a.h, s_a.n)
                for c in range(2):
                    nc.tensor.matmul(self.ps[:, 6, :], lhsT=self.vm[:, c, h * 128:(h + 1) * 128], rhs=self.pT[:, c, :], start=(c == 0), stop=(c == 1))
                for c in range(2):
                    ins = nc.tensor.matmul(self.ps[:, 7, :], lhsT=self.ones[:, :], rhs=self.pT[:, c, :], start=(c == 0), stop=(c == 1))
                s_p.inc(ins)
                nc.vector.wait_ge(s_p.h, s_p.n)
                nc.vector.reciprocal(out=self.rstd[:, 0, :], in_=self.ps[:, 7, :])
                ins = nc.vector.tensor_tensor(out=self.oxa[:, h, tt * 512:(tt + 1) * 512], in0=self.ps[:, 6, :], in1=self.rstd[:, 0, :], op=ALU.mult)
                s_d.inc(ins)
                nc.tensor.wait_ge(s_d.h, s_d.n)
                nc.scalar.wait_ge(s_d.h, s_d.n)
            self.barrier()

    def build(self, stages=99):
        nc = self.nc
        s0 = self.kb.sem("init")
        nc.vector.memset(self.ones[:, :], 1.0)
        nc.vector.memset(self.eps_sb[:, :], EPS)
        nc.vector.memset(self.eps2_sb[:, :], EPS * 128.0)
        nc.sync.dma_start(out=self.gains_sb[:, :, :], in_=self.gains).then_inc(s0.h, 16)
        nc.sync.dma_start(out=self.hg_sb[:, :], in_=self.hg).then_inc(s0.h, 16)
        nc.sync.wait_ge(s0.h, 32)
        self.barrier()
        self.mem_kv()
        for p in range(T // TT):
            tok0 = p * TT
            y = self.y()
            self.norm(self.xT, tok0, 0, self.hT, TT // 256)
            self.barrier()
            if stages < 1:
                continue
            self.onorm(tok0)
            self.barrier()
            self.gemm(lambda g: [(self.w_gate[:, g * 512:(g + 1) * 512], 0, 512)], 16, 512, self.G // 512, self.hT, NT, self.epi_gate(),
                      pe_waits=[(self.kb.sem("n_dv").h, self.kb.sem("n_dv").n), (self.kb.sem("o_dv").h, self.kb.sem("o_dv").n)])
            self.barrier()
            if stages < 2:
                continue
            KC = self.FY // 128
            tiles = [(m, tt) for m in range(16) for tt in range(NT)]
            dst = self.x1 if stages > 2 else self.xo
            self.gemm(lambda g: [(self.w_out[:, g * 256:(g + 1) * 256], 0, 256)], KC, 256, 8, y, NT,
                      self.epi_resid(self.xT, dst, tok0, tiles), pe_waits=[(self.kb.sem("eg_d").h, self.kb.sem("eg_d").n)])
            self.wait_stores()
            self.barrier()
            if stages < 3:
                continue
            self.norm(self.x1, tok0, 1, self.hT, TT // 256)
            self.barrier()
            self.xa_attn()
            dst = self.x2 if stages > 3 else self.xo
            self.gemm(lambda g: [(self.w_o[:, :], 0, 2048)], 4, 2048, 1, self.oxa, NT,
                      self.epi_resid(self.x1, dst, tok0, tiles), pe_waits=[(self.kb.sem("x_d").h, self.kb.sem("x_d").n)])
            self.wait_stores()
            self.barrier()
            if stages < 4:
                continue
            self.norm(self.x2, tok0, 2, self.hT, TT // 256)
            self.barrier()
            for half in range(2):
                c0 = half * (FF // 2)
                self.gemm(lambda g: [(self.w1[:, c0 + g * 256:c0 + (g + 1) * 256], 0, 256), (self.w3[:, c0 + g * 256:c0 + (g + 1) * 256], 256, 256)],
                          16, 512, FF // 512, self.hT, NT, self.epi_swiglu(), pair=True,
                          pe_waits=[(self.kb.sem("n_dv").h, self.kb.sem("n_dv").n)])
                self.barrier()
                w2h = self.w2[c0:c0 + FF // 2, :]
                self.gemm(lambda g: [(w2h[:, g * 256:(g + 1) * 256], 0, 256)], 22, 256, 8, self.g(), NT,
                          self.epi_resid(self.x2 if half == 0 else self.xo, self.xo, tok0, tiles), pe_waits=[(self.pf.h, self.gidx)])
                self.wait_stores()
                self.barrier()
        return nc


def post_inputs(layer, inp, xT_c, oT_c):
    def gl(v):
        return np.ascontiguousarray(v.reshape(16, 128).T)
    gains = np.stack([gl(inp["norm_mix"][layer]), gl(inp["norm_xa"][layer]), gl(inp["norm_ffn"][layer]), gl(inp["mem_norm"])], axis=1)
    gd = inp["gdn_norm"][0]
    hg = np.stack([inp["xa_q_gain"][layer], inp["xa_k_gain"][layer], gd], axis=1)
    if layer == 0:
        w_gate = np.ascontiguousarray(inp["ar_w_in"][0][:, 4096:6144])
        w_out = inp["ar_w_out"][0]
    else:
        w_gate = np.ascontiguousarray(inp["gdn_w_in"][0][:, 8192:12288])
        w_out = inp["gdn_w_out"][0]
    return {
        "xT": xT_c, "oT": oT_c, "w_gate": w_gate, "w_out": np.ascontiguousarray(w_out),
        "gains": np.ascontiguousarray(gains.astype(np.float32)), "hg": np.ascontiguousarray(hg.astype(np.float32)),
        "memT": np.ascontiguousarray(inp["mem"][0].T),
        "w_q": np.ascontiguousarray(inp["xa_w_q"][layer]), "w_kv": np.ascontiguousarray(inp["xa_w_kv"][layer]),
        "w_o": np.ascontiguousarray(inp["xa_w_o"][layer]),
        "w1": np.ascontiguousarray(inp["ffn_w1"][layer]), "w3": np.ascontiguousarray(inp["ffn_w3"][layer]),
        "w2": np.ascontiguousarray(inp["ffn_w2"][layer]),
    }


D = 2048
S = 16384
BT = 512
NB_A = S // BT
NCOL_A = 1152
NRING = 20


class MixA:
    def __init__(self, nblocks=NB_A):
        self.nblocks = nblocks
        self.kb = kb = KB()
        nc = self.nc = kb.nc
        self.xT = kb.din("xT", [D, S])
        self.wA = kb.din("wA", [D, NCOL_A])
        self.gain = kb.din("gain", [128, 16])
        self.hg = kb.din("hg", [128, 2])
        self.cosT = kb.din("cosT", [128, S])
        self.sinT = kb.din("sinT", [128, S])
        self.dmask = kb.din("dmask", [128, 128])
        self.qdrow = kb.din("qdrow", [128, BT])
        self.kdec = kb.din("kdec", [128, 2])
        self.gtab = kb.din("gtab", [128, 17, 128])
        self.mtab = kb.din("mtab", [128, 17, 128])
        self.ident = kb.din("ident", [128, 128])
        self.oret = kb.dout("oret", [256, S])
        self.odil = kb.dout("odil", [128, S])
        sb = kb.sb
        self.w = sb("w", [128, 16, NCOL_A], BF16)
        self.ones = sb("ones", [128, 128], BF16)
        self.idb = sb("idb", [128, 128], BF16)
        self.idf = sb("idf", [128, 128], F32)
        self.gain_sb = sb("gain_sb", [128, 16], F32)
        self.hg_sb = sb("hg_sb", [128, 2], F32)
        self.eps_sb = sb("eps_sb", [128, 1], F32)
        self.eps2_sb = sb("eps2_sb", [128, 1], F32)
        self.dm = sb("dm", [128, 128], F32)
        self.qd = sb("qd", [128, BT], F32)
        self.kd = sb("kd", [128, 2], F32)
        self.E = sb("E", [128, 17, 128], F32)
        self.mt_sb = sb("mt_sb", [128, 17, 128], F32)
        self.xst = sb("xst", [128, 16, BT], F32)
        self.sq = sb("sq", [128, 16, BT], BF16)
        self.hT = sb("hT", [128, 16, BT], BF16)
        self.rstd = sb("rstd", [128, 2, BT], F32)
        self.cs = sb("cs", [128, 2, 2, BT], F32)
        self.tmp = sb("tmp", [128, 2, BT], F32)
        self.QT = sb("QT", [128, 2, BT], BF16)
        self.QdT = sb("QdT", [128, 2, BT], BF16)
        self.KT = sb("KT", [128, 2, BT], BF16)
        self.Kd = sb("Kd", [128, 4, 256], BF16)
        self.VA = sb("VA", [128, 4, 256], BF16)
        self.Sm = sb("Sm", [128, 128], BF16)
        self.St = sb("St", [128, 2, 256], F32)
        self.Stb = sb("Stb", [128, 2, 256], BF16)
        self.qnT = sb("qnT", [128, BT], BF16)
        self.knR = sb("knR", [128, NRING, 128], BF16)
        self.vbR = sb("vbR", [128, NRING, 128], BF16)
        self.ex = sb("ex", [128, 17 * 128], F32)
        self.pT = sb("pT", [128, 17 * 128], BF16)
        self.rl_ = sb("rl_", [128, 128], F32)
        self.oretb = sb("oretb", [128, 2, 2, BT], F32)
        self.odilb = sb("odilb", [128, 2, BT], F32)
        self.ps = kb.psum("ps", [128, 8, 512], F32)

    def rstd_op(self, ps_ap, out_ap, inv_n, wait, post=1.0):
        nc = self.nc
        s = self.kb.sem("r_a")
        nc.scalar.wait_ge(wait[0], wait[1])
        s.inc(nc.scalar.activation(out=out_ap, in_=ps_ap, func=AF.Sqrt, scale=inv_n / post ** 2,
                                   bias=self.eps_sb[:, 0:1] if post == 1.0 else self.eps2_sb[:, 0:1]))
        nc.vector.wait_ge(s.h, s.n)
        return nc.vector.reciprocal(out=out_ap, in_=out_ap)

    def build(self):
        nc = self.nc
        kb = self.kb
        sem = kb.sem
        ps = self.ps
        V, A, PE, SP, PL = nc.vector, nc.scalar, nc.tensor, nc.sync, nc.gpsimd

        def W(eng, s):
            eng.wait_ge(s.h, s.n)

        s0 = sem("init")
        for k0 in range(0, 16, 4):
            s0.inc(PL.dma_start(out=self.w[:, k0:k0 + 4, :], in_=self.wA.rearrange("(c p) n -> p c n", p=128)[:, k0:k0 + 4, :]), 16)
        s0.inc(PL.dma_start(out=self.idb[:, :], in_=self.ident), 16)
        s1 = sem("init1")
        for (dst, src) in [(self.gain_sb[:, :], self.gain), (self.hg_sb[:, :], self.hg), (self.dm[:, :], self.dmask), (self.qd[:, :], self.qdrow),
                           (self.kd[:, :], self.kdec), (self.E[:, :, :], self.gtab), (self.mt_sb[:, :, :], self.mtab), (self.idf[:, :], self.ident)]:
            s1.inc(SP.dma_start(out=dst, in_=src), 16)
        V.memset(self.ones[:, :], 1.0)
        V.memset(self.eps_sb[:, :], EPS)
        V.memset(self.eps2_sb[:, :], EPS * 128.0)
        V.memset(self.St[:, :, :], 0.0)
        V.memset(self.Stb[:, :, :], 0.0)
        W(A, s1)
        sE = sem("sE")
        sE.inc(A.activation(out=self.E[:, :, :], in_=self.E[:, :, :], func=AF.Exp))
        W(V, sE)
        W(V, s1)
        sE2 = sem("sE2")
        sE2.inc(V.tensor_tensor(out=self.E[:, :, :], in0=self.E[:, :, :], in1=self.mt_sb[:, :, :], op=ALU.mult))
        W(PE, s0)
        W(PE, sE2)
        W(A, sE2)

        xv = self.xT.rearrange("(c p) t -> p c t", p=128)
        s_xl = sem("xl"); s_sq = sem("a_sq"); s_ss = sem("p_ss"); s_h = sem("d_h")
        s_cl = [sem("cl0"), sem("cl1")]
        s_pj = sem("p_pj")
        s_pf = sem("pjf")
        s_rot = sem("d_rot")
        s_sq2 = sem("a_sq2"); s_ss2 = sem("p_ss2")
        s_tr = sem("p_tr"); s_kd = sem("d_kd")
        s_sc = sem("p_sc"); s_sm = sem("d_sm"); s_o = sem("p_o"); s_oe = sem("a_oe"); s_ds = sem("p_ds"); s_st = sem("d_st")
        s_qk = sem("p_qk"); s_ex = sem("a_ex"); s_p = sem("d_p"); s_pv = sem("p_pv"); s_do = sem("d_do")
        s_or = [sem("or0"), sem("or1")]; s_od = [sem("od0"), sem("od1")]
        pj_idx = [0]
        rot_hist = []

        def proj_tile(cols, width, lhs_tok=None):
            i = pj_idx[0]
            bank = i % 2
            if i >= 2:
                PE.wait_ge(s_pf.h, i - 1)
            for k in range(16):
                if lhs_tok is None:
                    ins = PE.matmul(ps[:, bank, 0:BT], lhsT=self.w[:, k, cols:cols + 128], rhs=self.hT[:, k, :], start=(k == 0), stop=(k == 15))
                else:
                    ins = PE.matmul(ps[:, bank, 0:width], lhsT=self.hT[:, k, lhs_tok * 128:(lhs_tok + 1) * 128], rhs=self.w[:, k, cols:cols + width],
                                    start=(k == 0), stop=(k == 15))
            s_pj.inc(ins)
            pj_idx[0] += 1
            return bank

        for b in range(self.nblocks):
            t0 = b * BT
            sl = b % 2
            W(SP, s_h)
            for hh in range(2):
                s_xl.inc(SP.dma_start(out=self.xst[:, hh * 8:(hh + 1) * 8, :], in_=xv[:, hh * 8:(hh + 1) * 8, t0:t0 + BT]), 16)
            if b >= 2:
                SP.wait_ge(s_rot.h, rot_hist[b - 2])
            s_cl[sl].inc(SP.dma_start(out=self.cs[:, sl, 0, :], in_=self.cosT[:, t0:t0 + BT]), 16)
            s_cl[sl].inc(SP.dma_start(out=self.cs[:, sl, 1, :], in_=self.sinT[:, t0:t0 + BT]), 16)
            W(A, s_xl)
            W(A, s_ss)
            W(A, s_ss2)
            s_sq.inc(A.activation(out=self.sq[:, :, :], in_=self.xst[:, :, :], func=AF.Square))
            W(PE, s_sq)
            for k in range(16):
                ins = PE.matmul(ps[:, 2, :], lhsT=self.ones[:, :], rhs=self.sq[:, k, :], start=(k == 0), stop=(k == 15))
            s_ss.inc(ins)
            self.rstd_op(ps[:, 2, :], self.rstd[:, 0, :], 1.0 / D, (s_ss.h, s_ss.n))
            W(V, s_pj)
            for k in range(16):
                ins = V.scalar_tensor_tensor(out=self.hT[:, k, :], in0=self.xst[:, k, :], scalar=self.gain_sb[:, k:k + 1], in1=self.rstd[:, 0, :],
                                             op0=ALU.mult, op1=ALU.mult)
            s_h.inc(ins)
            W(PE, s_h)
            W(V, s_cl[sl])
            for which, col0, dstT in ((0, 0, self.QT), (1, 256, self.KT)):
                b0 = proj_tile(col0, 128)
                b1 = proj_tile(col0 + 128, 128)
                W(V, s_pj)
                if which == 0:
                    W(V, s_o)
                    W(V, s_sc)
                else:
                    W(V, s_tr)
                    W(V, s_sc)
                cosv = self.cs[:, sl, 0, :]; sinv = self.cs[:, sl, 1, :]
                V.tensor_tensor(out=self.tmp[:, 0, :], in0=ps[:, b0, :], in1=cosv, op=ALU.mult)
                V.tensor_tensor(out=self.tmp[:, 1, :], in0=ps[:, b1, :], in1=sinv, op=ALU.mult)
                V.tensor_tensor(out=dstT[:, 0, :], in0=self.tmp[:, 0, :], in1=self.tmp[:, 1, :], op=ALU.subtract)
                V.tensor_tensor(out=self.tmp[:, 0, :], in0=ps[:, b0, :], in1=sinv, op=ALU.mult)
                ins = V.tensor_tensor(out=self.tmp[:, 1, :], in0=ps[:, b1, :], in1=cosv, op=ALU.mult)
                s_pf.inc(ins, 2)
                ins = V.tensor_tensor(out=dstT[:, 1, :], in0=self.tmp[:, 0, :], in1=self.tmp[:, 1, :], op=ALU.add)
                if which == 0:
                    for i in range(2):
                        ins = V.tensor_tensor(out=self.QdT[:, i, :], in0=self.QT[:, i, :], in1=self.qd[:, :], op=ALU.mult)
                s_rot.inc(ins)
            rot_hist.append(s_rot.n)
            for which, col0 in ((0, 512), (1, 640)):
                bk = proj_tile(col0, 128)
                W(A, s_pj)
                W(A, s_ss2)
                s_sq2.inc(A.activation(out=self.sq[:, 0, :], in_=ps[:, bk, :], func=AF.Square))
                W(PE, s_sq2)
                ins = PE.matmul(ps[:, 2, :], lhsT=self.ones[:, :], rhs=self.sq[:, 0, :], start=True, stop=True)
                s_ss2.inc(ins)
                if which == 0:
                    self.rstd_op(ps[:, 2, :], self.rstd[:, 1, :], 1.0 / 128, (s_ss2.h, s_ss2.n), post=128 ** -0.5)
                    W(V, s_pv)
                    W(V, s_qk)
                    ins = V.scalar_tensor_tensor(out=self.qnT[:, :], in0=ps[:, bk, :], scalar=self.hg_sb[:, 0:1], in1=self.rstd[:, 1, :],
                                                 op0=ALU.mult, op1=ALU.mult)
                else:
                    self.rstd_op(ps[:, 2, :], self.rstd[:, 1, :], 1.0 / 128, (s_ss2.h, s_ss2.n))
                    W(V, s_qk)
                    for j in range(4):
                        slot = (4 * b + j) % NRING
                        ins = V.scalar_tensor_tensor(out=self.knR[:, slot, :], in0=ps[:, bk, j * 128:(j + 1) * 128], scalar=self.hg_sb[:, 1:2],
                                                     in1=self.rstd[:, 1, j * 128:(j + 1) * 128], op0=ALU.mult, op1=ALU.mult)
                s_pf.inc(ins, 1)
            for c in range(4):
                bk = proj_tile(768, 384, lhs_tok=c)
                W(V, s_pj)
                if c == 0:
                    W(V, s_ds)
                    W(V, s_o)
                    W(V, s_pv)
                V.tensor_copy(out=self.VA[:, c, :], in_=ps[:, bk, 0:256])
                ins = V.tensor_copy(out=self.vbR[:, (4 * b + c) % NRING, :], in_=ps[:, bk, 256:384])
                s_pf.inc(ins, 1)
            W(PE, s_rot)
            for c in range(4):
                W(PE, s_kd)
                for i in range(2):
                    ins = PE.matmul(ps[:, 3, i * 128:(i + 1) * 128], lhsT=self.KT[:, i, c * 128:(c + 1) * 128], rhs=self.idb[:, :], start=True, stop=True)
                s_tr.inc(ins)
                W(V, s_tr)
                if c == 0:
                    W(V, s_ds)
                for i in range(2):
                    ins = V.tensor_scalar(out=self.Kd[:, c, i * 128:(i + 1) * 128], in0=ps[:, 3, i * 128:(i + 1) * 128], scalar1=self.kd[:, 0:1], scalar2=None, op0=ALU.mult)
                s_kd.inc(ins)
            if b % 2 == 0 or True:
                V.wait_ge(s_or[sl].h, s_or[sl].n)
                A.wait_ge(s_or[sl].h, s_or[sl].n)
            for c in range(4):
                cs_ = slice(c * 128, (c + 1) * 128)
                W(PE, s_sm)
                W(PE, s_kd)
                for i in range(2):
                    ins = PE.matmul(ps[:, 3, 256:384], lhsT=self.KT[:, i, cs_], rhs=self.QT[:, i, cs_], start=(i == 0), stop=(i == 1))
                s_sc.inc(ins)
                W(V, s_sc)
                W(V, s_o)
                s_sm.inc(V.tensor_tensor(out=self.Sm[:, :], in0=ps[:, 3, 256:384], in1=self.dm[:, :], op=ALU.mult))
                W(PE, s_sm)
                W(PE, s_pf)
                W(PE, s_st)
                W(PE, s_oe)
                for j in range(2):
                    PE.matmul(ps[:, 4, j * 128:(j + 1) * 128], lhsT=self.VA[:, c, j * 128:(j + 1) * 128], rhs=self.Sm[:, :], start=True, stop=False)
                    for i in range(2):
                        ins = PE.matmul(ps[:, 4, j * 128:(j + 1) * 128], lhsT=self.Stb[:, i, j * 128:(j + 1) * 128], rhs=self.QdT[:, i, cs_],
                                        start=False, stop=(i == 1))
                s_o.inc(ins)
                W(A, s_o)
                for j in range(2):
                    ins = A.activation(out=self.oretb[:, sl, j, cs_], in_=ps[:, 4, j * 128:(j + 1) * 128], func=AF.Copy)
                s_oe.inc(ins)
                W(PE, s_kd)
                for i in range(2):
                    ins = PE.matmul(ps[:, 5, i * 256:(i + 1) * 256], lhsT=self.Kd[:, c, i * 128:(i + 1) * 128], rhs=self.VA[:, c, :], start=True, stop=True)
                s_ds.inc(ins)
                W(V, s_ds)
                W(V, s_o)
                for i in range(2):
                    V.scalar_tensor_tensor(out=self.St[:, i, :], in0=self.St[:, i, :], scalar=self.kd[:, 1:2], in1=ps[:, 5, i * 256:(i + 1) * 256],
                                           op0=ALU.mult, op1=ALU.add)
                ins = V.tensor_copy(out=self.Stb[:, :, :], in_=self.St[:, :, :])
                s_st.inc(ins)
            W(SP, s_oe)
            s_or[sl].inc(SP.dma_start(out=self.oret.rearrange("(j p) t -> p j t", p=128)[:, :, t0:t0 + BT], in_=self.oretb[:, sl, :, :]), 16)
            W(PE, s_pf)
            W(PE, s_oe)
            W(PE, s_st)
            W(PE, s_sm)
            W(PE, s_kd)
            V.wait_ge(s_od[sl].h, s_od[sl].n)
            sbanks = [0, 1, 2, 3, 6]
            for qt in range(4):
                tq = 4 * b + qt
                nk = min(17, tq + 1)
                W(PE, s_do)
                W(PE, s_ex)
                for o in range(nk):
                    slot = (tq - o) % NRING
                    ins = PE.matmul(ps[:, sbanks[o // 4], (o % 4) * 128:(o % 4 + 1) * 128], lhsT=self.knR[:, slot, :], rhs=self.qnT[:, qt * 128:(qt + 1) * 128],
                                    start=True, stop=True)
                s_qk.inc(ins)
                W(A, s_qk)
                W(A, s_p)
                n0 = min(nk, 16)
                ins = A.activation(out=self.ex[:, 0:n0 * 128], in_=ps[:, 0:4, :].rearrange("p b c -> p (b c)")[:, 0:n0 * 128], func=AF.Exp)
                if nk == 17:
                    ins = A.activation(out=self.ex[:, 2048:2176], in_=ps[:, 6, 0:128], func=AF.Exp)
                s_ex.inc(ins)
                W(V, s_ex)
                W(V, s_pv)
                s_p.inc(V.tensor_tensor(out=self.pT[:, 0:nk * 128], in0=self.ex[:, 0:nk * 128],
                                        in1=self.E[:, 0:nk, :].rearrange("p o q -> p (o q)"), op=ALU.mult))
                W(PE, s_p)
                for o in range(nk):
                    slot = (tq - o) % NRING
                    first = (o == 0)
                    last = (o == nk - 1)
                    PE.matmul(ps[:, 7, 0:128], lhsT=self.vbR[:, slot, :], rhs=self.pT[:, o * 128:(o + 1) * 128], start=first, stop=last, skip_group_check=True)
                    ins = PE.matmul(ps[:, 7, 128:256], lhsT=self.ones[:, :], rhs=self.pT[:, o * 128:(o + 1) * 128], start=False, stop=last, skip_group_check=True)
                s_pv.inc(ins)
                W(V, s_pv)
                V.reciprocal(out=self.rl_[:, :], in_=ps[:, 7, 128:256])
                s_do.inc(V.tensor_tensor(out=self.odilb[:, sl, qt * 128:(qt + 1) * 128], in0=ps[:, 7, 0:128], in1=self.rl_[:, :], op=ALU.mult))
            W(SP, s_do)
            s_od[sl].inc(SP.dma_start(out=self.odil[:, t0:t0 + BT], in_=self.odilb[:, sl, :]), 16)
        for s in s_or + s_od:
            W(SP, s)
        return nc


def t5_bucket_np(dist):
    exact = 16
    d = np.maximum(dist, exact).astype(np.float32)
    large = exact + (np.log(d / np.float32(exact)) / np.float32(math.log(2048 / exact)) * np.float32(32 - exact)).astype(np.int32)
    large = np.minimum(large, 31)
    return np.where(dist < exact, dist, large)


def mixa_consts():
    i = np.arange(128, dtype=np.float32)
    inv = (np.float32(10000.0) ** (-(np.arange(0, 256, 2, dtype=np.float32)) / np.float32(256))).astype(np.float32)
    pos = np.arange(S, dtype=np.float32)
    ang = (inv[:, None] * pos[None, :]).astype(np.float32)
    cosT = np.cos(ang).astype(np.float32)
    sinT = np.sin(ang).astype(np.float32)
    kj = np.arange(128)[:, None, None]
    o = np.arange(17)[None, :, None]
    qi = np.arange(128)[None, None, :]
    delta = qi - kj + 128 * o
    valid = delta >= 0
    m = ((delta <= 128) & valid).astype(np.float32) + ((delta % 4 == 0) & (delta <= 512) & valid) + ((delta % 16 == 0) & (delta <= 2048) & valid)
    bidx = t5_bucket_np(np.maximum(delta, 0))
    return cosT, sinT, m.astype(np.float32), bidx


def mixa_inputs(inp, c, xT, consts):
    cosT, sinT, mtab, bidx = consts
    hr, vh, hd = c // 2, c % 2, c
    W = inp["ar_w_in"][0]
    wA = np.concatenate([W[:, hr * 256:(hr + 1) * 256], W[:, 1024 + hr * 256:1024 + (hr + 1) * 256],
                         W[:, 6144 + hd * 128:6144 + (hd + 1) * 128], W[:, 7168 + hd * 128:7168 + (hd + 1) * 128],
                         W[:, 2048 + hr * 512 + vh * 256:2048 + hr * 512 + (vh + 1) * 256], W[:, 8192 + hd * 128:8192 + (hd + 1) * 128]], axis=1)
    gamma = 1.0 - 2.0 ** (-5.0 - hr)
    kj = np.arange(128)[:, None]; qi = np.arange(128)[None, :]
    dmask = np.where(qi >= kj, gamma ** np.maximum(qi - kj, 0), 0.0) * 256 ** -0.5
    qdrow = np.tile(gamma ** (np.arange(128) + 1.0), 4)[None, :].repeat(128, axis=0)
    kdec = np.stack([gamma ** (127.0 - np.arange(128)) * 256 ** -0.5, np.full(128, gamma ** 128.0)], axis=1)
    gtab = inp["rel_bias"][:, hd][bidx]
    return {
        "xT": xT, "wA": np.ascontiguousarray(wA), "gain": np.ascontiguousarray(inp["norm_mix"][0].reshape(16, 128).T),
        "hg": np.ascontiguousarray(np.stack([inp["dil_q_gain"][0], inp["dil_k_gain"][0]], axis=1)),
        "cosT": cosT, "sinT": sinT, "dmask": dmask.astype(np.float32), "qdrow": qdrow.astype(np.float32), "kdec": kdec.astype(np.float32),
        "gtab": np.ascontiguousarray(gtab.astype(np.float32)), "mtab": mtab, "ident": np.eye(128, dtype=np.float32),
    }


D = 2048
S = 16384
BT = 512
NB_C = S // BT
NCOL_C = 1032
C = 128
G = 2


def fl(ap):
    return ap.rearrange("p a b -> p (a b)")


def fl4(t):
    return t[:, :, :, :].rearrange("p a b c -> p (a b c)")


class MixC:
    def __init__(self, nblocks=NB_C):
        self.nblocks = nblocks
        self.kb = kb = KB()
        self.nc = kb.nc
        self.xT = kb.din("xT", [D, S])
        self.wC = kb.din("wC", [D, NCOL_C])
        self.gain = kb.din("gain", [128, 16])
        self.convw = kb.din("convw", [128, 8, 4])
        self.hp = kb.din("hp", [128, 2, G, 4])
        self.U = kb.din("U", [C, C])
        self.MU = kb.din("MU", [C, 4 * G, C])
        self.ML = kb.din("ML", [C, 4 * G, C])
        self.I4 = kb.din("I4", [C, 4 * G, C])
        self.ident = kb.din("ident", [128, 128])
        self.og = kb.dout("og", [512, S])
        sb = kb.sb
        self.w = sb("w", [128, 16, NCOL_C], BF16)
        self.ones = sb("ones", [128, 128], BF16)
        self.onesf = sb("onesf", [C, 128], F32)
        self.idb = sb("idb", [128, 128], BF16)
        self.idf = sb("idf", [128, 128], F32)
        self.gain_sb = sb("gain_sb", [128, 16], F32)
        self.cw = sb("cw", [128, 8, 4], F32)
        self.hp_sb = sb("hp_sb", [128, 2, G, 4], F32)
        self.negA = sb("negA", [128, G, 4], F32)
        self.eps_sb = sb("eps_sb", [128, 1], F32)
        self.eps2_sb = sb("eps2_sb", [128, 1], F32)
        self.one_sb = sb("one_sb", [128, 1], F32)
        self.U_sb = sb("U_sb", [C, C], F32)
        self.MU_sb = sb("MU_sb", [C, 4 * G, C], F32)
        self.ML_sb = sb("ML_sb", [C, 4 * G, C], F32)
        self.I4_sb = sb("I4_sb", [C, 4 * G, C], F32)
        self.xst = sb("xst", [128, 16, BT], F32)
        self.sq = sb("sq", [128, 8, BT], BF16)
        self.hT = sb("hT", [128, 16, BT], BF16)
        self.rstd = sb("rstd", [128, BT], F32)
        self.praw = sb("praw", [128, 3 + BT], F32)
        self.halo = sb("halo", [128, 8, 3], F32)
        self.acc = sb("acc", [128, BT], F32)
        self.cvs = sb("cvs", [128, 4, BT], F32)
        self.qnT = sb("qnT", [128, 2, BT], BF16)
        self.knT = sb("knT", [128, 2, BT], BF16)
        self.vT = sb("vT", [128, 4, BT], BF16)
        self.ktok = sb("ktok", [C, G * 2 * 128], F32)
        self.vtok = sb("vtok", [C, G, 512], F32)
        self.xa = sb("xa", [C, G, 4], F32)
        self.beta = sb("beta", [C, G, 4], F32)
        self.nbeta = sb("nbeta", [C, G, 4], F32)
        self.g = sb("g", [C, G, 4], F32)
        self.gcc = sb("gcc", [C, G, 4], F32)
        self.egc = sb("egc", [C, G, 4], F32)
        self.bg = sb("bg", [C, G, 4], F32)
        self.egl = sb("egl", [128, G, 4], F32)
        self.zz = sb("zz", [C, 2 * G * 4 * C], F32)
        self.G1 = self.zz[:, 0:G * 4 * 128].rearrange("p (g h d) -> p g h d", g=G, h=4)
        self.Zmin = self.zz[:, 0:G * 4 * C].rearrange("p (g h d) -> p g h d", g=G, h=4)
        self.Zmax = self.zz[:, G * 4 * C:2 * G * 4 * C].rearrange("p (g h d) -> p g h d", g=G, h=4)
        self.E0T = sb("E0T", [C, G, 4, C], F32)
        self.E1 = sb("E1", [C, G, 4, C], F32)
        self.eR = sb("eR", [128, G, 4 * C], F32)
        self.X = sb("X", [C, G, 4, C], BF16)
        self.Y = sb("Y", [C, G, 4, C], BF16)
        self.Q = sb("Q", [C, G, 4, C], F32)
        self.Tt = sb("Tt", [C, G, 4, C], BF16)
        self.intraT = sb("intraT", [C, G, 4, C], BF16)
        self.vb = sb("vb", [C, G, 4, 128], BF16)
        self.kbg = sb("kbg", [C, G, 4, 128], BF16)
        self.kdk = sb("kdk", [C, G, 4, 128], BF16)
        self.qgT = sb("qgT", [128, G, 4, C], BF16)
        self.nwT = sb("nwT", [128, G, 4, C], BF16)
        self.vnew = sb("vnew", [C, 4, 128], BF16)
        self.St = sb("St", [128, 4, 128], F32)
        self.Stb = sb("Stb", [128, 4, 128], BF16)
        self.ob = sb("ob", [128, 1, 4, BT], F32)
        self.ps = kb.psum("ps", [128, 8, 512], F32)
        self.dtb = self.hp_sb[:, 1, :, :]

    def build(self, debug=False):
        nc = self.nc
        kb = self.kb
        ps = self.ps
        V, A, PE, SP, PL = nc.vector, nc.scalar, nc.tensor, nc.sync, nc.gpsimd
        sems = {"V": kb.sem("sV"), "A": kb.sem("sA"), "P": kb.sem("sP")}
        engs = {"V": V, "A": A, "P": PE}

        def step(e, fn, extra=()):
            eng = engs[e]
            for o in sems:
                if o != e and sems[o].n > 0:
                    eng.wait_ge(sems[o].h, sems[o].n)
            for (sh, sv) in extra:
                eng.wait_ge(sh, sv)
            ins = fn()
            sems[e].inc(ins)

        def sp_wait_all():
            for o in sems:
                if sems[o].n > 0:
                    SP.wait_ge(sems[o].h, sems[o].n)

        s0 = kb.sem("init"); s1 = kb.sem("init1")
        wv = self.wC.rearrange("(c p) n -> p c n", p=128)
        for k0 in range(0, 16, 4):
            s0.inc(PL.dma_start(out=self.w[:, k0:k0 + 4, :], in_=wv[:, k0:k0 + 4, :]), 16)
        s0.inc(PL.dma_start(out=self.idb[:, :], in_=self.ident), 16)
        for (dst, src) in [(self.gain_sb[:, :], self.gain), (self.cw[:, :, :], self.convw), (self.hp_sb[:, :, :, :], self.hp), (self.U_sb[:, :], self.U),
                           (self.MU_sb[:, :, :], self.MU), (self.ML_sb[:, :, :], self.ML), (self.I4_sb[:, :, :], self.I4), (self.idf[:, :], self.ident)]:
            s1.inc(SP.dma_start(out=dst, in_=src), 16)

        def init_v():
            V.memset(self.ones[:, :], 1.0)
            V.memset(self.onesf[:, :], 1.0)
            V.memset(self.eps_sb[:, :], EPS)
            V.memset(self.one_sb[:, :], 1.0)
            V.memset(self.eps2_sb[:, :], EPS * 128.0)
            V.memset(self.St[:, :, :], 0.0)
            V.memset(self.Stb[:, :, :], 0.0)
            return V.memset(self.halo[:, :, :], 0.0)
        step("V", init_v)
        step("A", lambda: A.activation(out=self.negA[:, :, :], in_=self.hp_sb[:, 0, :, :], func=AF.Exp), extra=[(s1.h, s1.n)])
        step("V", lambda: V.tensor_scalar(out=fl(self.negA[:, :, :]), in0=fl(self.negA[:, :, :]), scalar1=-1.0, scalar2=None, op0=ALU.mult), extra=[(s0.h, s0.n), (s1.h, s1.n)])
        PE.wait_ge(s0.h, s0.n)
        PE.wait_ge(s1.h, s1.n)

        xv = self.xT.rearrange("(c p) t -> p c t", p=128)
        s_xl = kb.sem("xl")
        s_o = [kb.sem("so0"), kb.sem("so1")]

        def load_x(b):
            t0 = b * BT
            for hh in range(2):
                s_xl.inc(SP.dma_start(out=self.xst[:, hh * 8:(hh + 1) * 8, :], in_=xv[:, hh * 8:(hh + 1) * 8, t0:t0 + BT]), 16)

        load_x(0)
        for b in range(self.nblocks):
            t0 = b * BT
            sl = 0
            for hf in range(2):
                step("A", lambda hf=hf: A.activation(out=self.sq[:, :, :], in_=self.xst[:, hf * 8:(hf + 1) * 8, :], func=AF.Square), extra=[(s_xl.h, s_xl.n)])

                def f(hf=hf):
                    for k in range(8):
                        ins = PE.matmul(ps[:, 2, :], lhsT=self.ones[:, :], rhs=self.sq[:, k, :], start=(hf == 0 and k == 0), stop=(hf == 1 and k == 7))
                    return ins
                step("P", f)
            step("A", lambda: A.activation(out=self.rstd[:, :], in_=ps[:, 2, :], func=AF.Sqrt, scale=1.0 / D, bias=self.eps_sb[:, 0:1]))

            def f():
                V.reciprocal(out=self.rstd[:, :], in_=self.rstd[:, :])
                for k in range(16):
                    ins = V.scalar_tensor_tensor(out=self.hT[:, k, :], in0=self.xst[:, k, :], scalar=self.gain_sb[:, k:k + 1], in1=self.rstd[:, :],
                                                 op0=ALU.mult, op1=ALU.mult)
                return ins
            step("V", f)
            if b + 1 < self.nblocks:
                sp_wait_all()
                load_x(b + 1)
            for mt in range(8):
                def f(mt=mt):
                    for k in range(16):
                        ins = PE.matmul(ps[:, mt % 2, :], lhsT=self.w[:, k, mt * 128:(mt + 1) * 128], rhs=self.hT[:, k, :], start=(k == 0), stop=(k == 15))
                    return ins
                step("P", f)
                step("A", lambda mt=mt: A.activation(out=self.praw[:, 3:3 + BT], in_=ps[:, mt % 2, :], func=AF.Copy))

                def f(mt=mt):
                    V.tensor_copy(out=self.praw[:, 0:3], in_=self.halo[:, mt, :])
                    V.tensor_scalar(out=self.acc[:, :], in0=self.praw[:, 0:BT], scalar1=self.cw[:, mt, 0:1], scalar2=None, op0=ALU.mult)
                    for j in range(1, 4):
                        ins = V.scalar_tensor_tensor(out=self.acc[:, :], in0=self.praw[:, j:j + BT], scalar=self.cw[:, mt, j:j + 1], in1=self.acc[:, :],
                                                     op0=ALU.mult, op1=ALU.add)
                    return ins
                step("V", f)
                step("A", lambda mt=mt: A.activation(out=(self.cvs[:, mt, :] if mt < 4 else self.vT[:, mt - 4, :]), in_=self.acc[:, :], func=AF.Silu))
                step("V", lambda mt=mt: V.tensor_copy(out=self.halo[:, mt, :], in_=self.praw[:, BT:BT + 3]))
            step("A", lambda: A.activation(out=self.sq[:, 0:4, :], in_=self.cvs[:, 0:4, :], func=AF.Square))
            for mt in range(4):
                step("P", lambda mt=mt: PE.matmul(ps[:, 2, :], lhsT=self.ones[:, :], rhs=self.sq[:, mt, :], start=True, stop=True))
                if mt < 2:
                    step("A", lambda: A.activation(out=self.rstd[:, :], in_=ps[:, 2, :], func=AF.Sqrt, scale=128.0, bias=self.eps2_sb[:, 0:1]))
                else:
                    step("A", lambda: A.activation(out=self.rstd[:, :], in_=ps[:, 2, :], func=AF.Sqrt, scale=1.0, bias=self.eps_sb[:, 0:1]))

                def f(mt=mt):
                    V.reciprocal(out=self.rstd[:, :], in_=self.rstd[:, :])
                    dst = self.qnT[:, mt, :] if mt < 2 else self.knT[:, mt - 2, :]
                    return V.tensor_tensor(out=dst, in0=self.cvs[:, mt, :], in1=self.rstd[:, :], op=ALU.mult)
                step("V", f)
            NL = 7
            for grp in range(BT // (C * G)):
                def csl(gi):
                    c0 = (grp * G + gi) * C
                    return slice(c0, c0 + C)

                def hs(h):
                    return slice(h * C, (h + 1) * C)

                def f():
                    for gi in range(G):
                        for k in range(16):
                            ins = PE.matmul(ps[:, 2, gi * 8:(gi + 1) * 8], lhsT=self.hT[:, k, csl(gi)], rhs=self.w[:, k, 1024:1032],
                                            start=(k == 0), stop=(k == 15))
                    return ins
                step("P", f)
                bav = ps[:, 2, 0:G * 8].rearrange("p (g c) -> p g c", c=8)
                step("V", lambda: V.tensor_tensor(out=self.xa[:, :, :], in0=bav[:, :, 4:8], in1=self.dtb[:, :, :], op=ALU.add))

                def f():
                    A.activation(out=fl(self.xa[:, :, :]), in_=fl(self.xa[:, :, :]), func=AF.Exp)
                    return A.activation(out=fl(self.xa[:, :, :]), in_=fl(self.xa[:, :, :]), func=AF.Ln, bias=self.one_sb[:, 0:1])
                step("A", f)
                step("V", lambda: V.tensor_tensor(out=fl(self.g[:, :, :]), in0=fl(self.xa[:, :, :]), in1=fl(self.negA[:, :, :]), op=ALU.mult))
                step("A", lambda: A.activation(out=self.beta[:, :, :], in_=bav[:, :, 0:4], func=AF.Sigmoid))

                def f():
                    V.tensor_scalar(out=fl(self.nbeta[:, :, :]), in0=fl(self.beta[:, :, :]), scalar1=-1.0, scalar2=None, op0=ALU.mult)
                    for gi in range(G):
                        for h in range(4):
                            ins = V.tensor_scalar(out=self.G1[:, gi, h, :], in0=self.onesf[:, :], scalar1=self.g[:, gi, h:h + 1], scalar2=None, op0=ALU.mult)
                    return ins
                step("V", f)

                def f():
                    for gi in range(G):
                        PE.matmul(ps[:, 2, 64 + gi * 4:68 + gi * 4], lhsT=self.U_sb[:, :], rhs=self.g[:, gi, :], start=True, stop=True)
                        PE.matmul(ps[:, 2, 128 + gi * 4:132 + gi * 4], lhsT=self.onesf[:, :], rhs=self.g[:, gi, :], start=True, stop=True)
                    for gi in range(G):
                        for h in range(4):
                            PE.matmul(ps[:, 3 + gi, hs(h)], lhsT=self.G1[:, gi, h, :], rhs=self.U_sb[:, :], start=True, stop=True)
                    for gi in range(G):
                        for kh in range(2):
                            PE.matmul(ps[:, 5 + gi, hs(kh)], lhsT=self.knT[:, kh, csl(gi)], rhs=self.knT[:, kh, csl(gi)], start=True, stop=True)
                        for kh in range(2):
                            ins = PE.matmul(ps[:, 5 + gi, hs(2 + kh)], lhsT=self.knT[:, kh, csl(gi)], rhs=self.qnT[:, kh, csl(gi)], start=True, stop=True)
                    return ins
                step("P", f)

                def f():
                    A.activation(out=fl(self.gcc[:, :, :]), in_=ps[:, 2, 64:64 + G * 4], func=AF.Copy)
                    A.activation(out=fl(self.egl[:, :, :]), in_=ps[:, 2, 128:128 + G * 4], func=AF.Exp)
                    return A.activation(out=self.eR[:, :, :], in_=ps[:, 3:3 + G, :], func=AF.Exp)
                step("A", f)

                def f():
                    for gi in range(G):
                        for h in range(4):
                            V.tensor_scalar(out=self.Zmin[:, gi, h, :], in0=ps[:, 3 + gi, hs(h)], scalar1=self.gcc[:, gi, h:h + 1], scalar2=0.0,
                                            op0=ALU.subtract, op1=ALU.min)
                            ins = V.tensor_scalar(out=self.Zmax[:, gi, h, :], in0=ps[:, 3 + gi, hs(h)], scalar1=self.gcc[:, gi, h:h + 1], scalar2=0.0,
                                                  op0=ALU.subtract, op1=ALU.max)
                    return ins
                step("V", f)

                def f():
                    A.activation(out=fl(self.egc[:, :, :]), in_=fl(self.gcc[:, :, :]), func=AF.Exp)
                    A.activation(out=fl4(self.E0T), in_=fl4(self.Zmin), func=AF.Exp)
                    return A.activation(out=fl4(self.E1), in_=fl4(self.Zmax), func=AF.Exp, scale=-1.0)
                step("A", f)

                def f():
                    for gi in range(G):
                        for kh in range(2):
                            PE.matmul(ps[:, 0, (gi * 2 + kh) * 128:(gi * 2 + kh + 1) * 128], lhsT=self.knT[:, kh, csl(gi)], rhs=self.idb[:, :], start=True, stop=True)
                        for h in range(4):
                            ins = PE.matmul(ps[:, 3 + gi, hs(h)], lhsT=self.vT[:, h, csl(gi)], rhs=self.idb[:, :], start=True, stop=True)
                    return ins
                step("P", f)

                def f():
                    A.activation(out=self.ktok[:, :], in_=ps[:, 0, :], func=AF.Copy)
                    return A.activation(out=self.vtok[:, :, :], in_=ps[:, 3:3 + G, :], func=AF.Copy)
                step("A", f)

                def f():
                    V.tensor_tensor(out=fl(self.bg[:, :, :]), in0=fl(self.beta[:, :, :]), in1=fl(self.egc[:, :, :]), op=ALU.mult)
                    V.tensor_tensor(out=fl4(self.E0T), in0=fl4(self.E0T), in1=fl(self.MU_sb[:, :, :]), op=ALU.mult)
                    V.tensor_tensor(out=fl4(self.E1), in0=fl4(self.E1), in1=fl(self.ML_sb[:, :, :]), op=ALU.mult)
                    for gi in range(G):
                        ktv = self.ktok[:, :].rearrange("p (g k d) -> p g k d", g=G, k=2)[:, gi, :, :]
                        vtv = self.vtok[:, gi, :].rearrange("p (h d) -> p h d", h=4)
                        for h in range(4):
                            kh = h // 2
                            V.scalar_tensor_tensor(out=self.X[:, gi, h, :], in0=ps[:, 5 + gi, hs(kh)], scalar=self.nbeta[:, gi, h:h + 1],
                                                   in1=self.E1[:, gi, h, :], op0=ALU.mult, op1=ALU.mult)
                            V.tensor_tensor(out=self.intraT[:, gi, h, :], in0=ps[:, 5 + gi, hs(2 + kh)], in1=self.E0T[:, gi, h, :], op=ALU.mult)
                            V.tensor_scalar(out=self.vb[:, gi, h, :], in0=vtv[:, h, :], scalar1=self.beta[:, gi, h:h + 1], scalar2=None, op0=ALU.mult)
                            V.tensor_scalar(out=self.kbg[:, gi, h, :], in0=ktv[:, kh, :], scalar1=self.bg[:, gi, h:h + 1], scalar2=None, op0=ALU.mult)
                            V.tensor_scalar(out=self.kdk[:, gi, h, :], in0=ktv[:, kh, :], scalar1=self.E0T[:, gi, h, C - 1:C], scalar2=None, op0=ALU.mult)
                            ins = V.tensor_tensor(out=self.qgT[:, gi, h, :], in0=self.qnT[:, kh, csl(gi)], in1=self.eR[:, gi, hs(h)], op=ALU.mult)
                    return ins
                step("V", f)

                def f():
                    for gi in range(G):
                        for h in range(4):
                            ins = PE.matmul(ps[:, 2 + gi, hs(h)], lhsT=self.X[:, gi, h, :], rhs=self.idb[:, :], start=True, stop=True)
                    return ins
                step("P", f)
                gb = lambda t: fl4(t).rearrange("p (b c) -> p b c", b=G)
                step("A", lambda: A.activation(out=gb(self.Y), in_=ps[:, 2:2 + G, :], func=AF.Copy))

                def f():
                    V.tensor_tensor(out=gb(self.Q), in0=ps[:, 2:2 + G, :], in1=fl(self.I4_sb[:, :, :]).rearrange("p (b c) -> p b c", b=G), op=ALU.add)
                    return V.tensor_copy(out=fl4(self.Tt), in_=fl4(self.Q))
                step("V", f)
                for lv in range(NL):
                    def f(lv=lv):
                        ins = None
                        for gi in range(G):
                            for h in range(4):
                                if lv <= NL - 2:
                                    ins = PE.matmul(ps[:, 0 + gi, hs(h)], lhsT=self.Y[:, gi, h, :], rhs=self.X[:, gi, h, :], start=True, stop=True)
                                if lv <= NL - 3:
                                    ins = PE.matmul(ps[:, 2 + gi, hs(h)], lhsT=self.X[:, gi, h, :], rhs=self.Y[:, gi, h, :], start=True, stop=True)
                                if lv >= 1:
                                    ins = PE.matmul(ps[:, 4 + gi, hs(h)], lhsT=self.X[:, gi, h, :], rhs=self.Tt[:, gi, h, :], start=True, stop=True)
                        return ins
                    step("P", f)
                    if lv <= NL - 3:
                        step("A", lambda: A.activation(out=gb(self.Y), in_=ps[:, 2:2 + G, :], func=AF.Copy))

                    def f(lv=lv):
                        ins = None
                        if lv >= 1:
                            V.tensor_tensor(out=gb(self.Q), in0=gb(self.Q), in1=ps[:, 4:4 + G, :], op=ALU.add)
                            ins = V.tensor_copy(out=fl4(self.Tt), in_=fl4(self.Q))
                        if lv <= NL - 2:
                            ins = V.tensor_copy(out=gb(self.X), in_=ps[:, 0:G, :])
                        return ins
                    step("V", f)

                def f():
                    for gi in range(G):
                        for h in range(4):
                            ins = PE.matmul(ps[:, 6 + gi, hs(h)], lhsT=self.kbg[:, gi, h, :], rhs=self.Tt[:, gi, h, :], start=True, stop=True)
                    return ins
                step("P", f)
                step("A", lambda: A.activation(out=gb(self.nwT), in_=ps[:, 6:6 + G, :], func=AF.Copy, scale=-1.0))

                for gi in range(G):
                    def f(gi=gi):
                        for h in range(4):
                            PE.matmul(ps[:, 0, h * 128:(h + 1) * 128], lhsT=self.Tt[:, gi, h, :], rhs=self.vb[:, gi, h, :], start=(h == 0), stop=False, skip_group_check=True)
                        for h in range(4):
                            ins = PE.matmul(ps[:, 0, h * 128:(h + 1) * 128], lhsT=self.nwT[:, gi, h, :], rhs=self.Stb[:, h, :], start=False, stop=(h == 3), skip_group_check=True)
                        return ins
                    step("P", f)
                    step("V", lambda: V.tensor_copy(out=fl(self.vnew[:, :, :]), in_=ps[:, 0, :]))

                    def f(gi=gi):
                        for h in range(4):
                            PE.matmul(ps[:, 1, hs(h)], lhsT=self.Stb[:, h, :], rhs=self.qgT[:, gi, h, :], start=(h == 0), stop=False, skip_group_check=True)
                        for h in range(4):
                            PE.matmul(ps[:, 1, hs(h)], lhsT=self.vnew[:, h, :], rhs=self.intraT[:, gi, h, :], start=False, stop=(h == 3), skip_group_check=True)
                        for h in range(4):
                            ins = PE.matmul(ps[:, 2, h * 128:(h + 1) * 128], lhsT=self.kdk[:, gi, h, :], rhs=self.vnew[:, h, :], start=True, stop=True)
                        return ins
                    step("P", f)
                    extra = [(s_o[sl].h, s_o[sl].n)] if (grp == 0 and gi == 0) else []
                    step("A", lambda gi=gi: A.activation(out=self.ob[:, sl, :, csl(gi)], in_=ps[:, 1, :].rearrange("p (h i) -> p h i", h=4), func=AF.Copy), extra=extra)

                    def f(gi=gi):
                        for h in range(4):
                            V.scalar_tensor_tensor(out=self.St[:, h, :], in0=self.St[:, h, :], scalar=self.egl[:, gi, h:h + 1], in1=ps[:, 2, h * 128:(h + 1) * 128],
                                                   op0=ALU.mult, op1=ALU.add)
                        return V.tensor_copy(out=self.Stb[:, :, :], in_=self.St[:, :, :])
                    step("V", f)
            sp_wait_all()
            s_o[sl].inc(SP.dma_start(out=self.og.rearrange("(h p) t -> p h t", p=128)[:, :, t0:t0 + BT], in_=self.ob[:, sl, :, :]), 16)
        for s in s_o:
            SP.wait_ge(s.h, s.n)
        return nc


def mixc_consts():
    U = (np.arange(C)[:, None] <= np.arange(C)[None, :]).astype(np.float32)
    p = np.arange(C)[:, None, None]; f = np.arange(C)[None, None, :]
    MU = np.broadcast_to((f >= p), (C, 4 * G, C)).astype(np.float32)
    ML = np.broadcast_to((p > f), (C, 4 * G, C)).astype(np.float32)
    I4 = np.broadcast_to((p == f), (C, 4 * G, C)).astype(np.float32)
    return U, np.ascontiguousarray(MU), np.ascontiguousarray(ML), np.ascontiguousarray(I4)


def mixc_inputs(inp, c, xT, consts):
    U, MU, ML, I4 = consts
    W = inp["gdn_w_in"][0]
    kh0 = 2 * c; vh0 = 4 * c
    qc = slice(kh0 * 128, kh0 * 128 + 256)
    kc = slice(2048 + kh0 * 128, 2048 + kh0 * 128 + 256)
    vc = slice(4096 + vh0 * 128, 4096 + vh0 * 128 + 512)
    bc = slice(12288 + vh0, 12288 + vh0 + 4)
    ac = slice(12320 + vh0, 12320 + vh0 + 4)
    wC = np.concatenate([W[:, qc], W[:, kc], W[:, vc], W[:, bc], W[:, ac]], axis=1)
    cw = inp["gdn_conv"][0]
    cwc = np.concatenate([cw[:, qc], cw[:, kc], cw[:, vc]], axis=1)
    convw = np.ascontiguousarray(cwc.reshape(4, 8, 128).transpose(2, 1, 0))
    hp = np.stack([np.broadcast_to(inp["gdn_a_log"][0][vh0:vh0 + 4], (128, G, 4)), np.broadcast_to(inp["gdn_dt_bias"][0][vh0:vh0 + 4], (128, G, 4))], axis=1)
    return {"xT": xT, "wC": np.ascontiguousarray(wC), "gain": np.ascontiguousarray(inp["norm_mix"][1].reshape(16, 128).T),
            "convw": convw.astype(np.float32), "hp": np.ascontiguousarray(hp.astype(np.float32)), "U": U, "MU": MU, "ML": ML, "I4": I4,
            "ident": np.eye(128, dtype=np.float32)}


def _run(nc, ins):
    res = run_bass_kernel_spmd(nc, ins, core_ids=list(range(8)))
    return res.results


def kernel(**inputs):
    inp = {k: np.asarray(v) for k, v in inputs.items()}
    S_ = 16384
    x = inp["x"][0]
    xT = np.ascontiguousarray(x.T)
    consts = mixa_consts()
    nc = MixA().build()
    ra = _run(nc, [mixa_inputs(inp, c, xT, consts) for c in range(8)])
    oT0 = np.empty((3072, S_), np.float32)
    for c in range(8):
        hr, vh = c // 2, c % 2
        oT0[hr * 512 + vh * 256: hr * 512 + (vh + 1) * 256] = ra[c]["oret"]
        oT0[2048 + c * 128: 2048 + (c + 1) * 128] = ra[c]["odil"]
    del ra
    nc = Post(0).build()
    rb = _run(nc, [post_inputs(0, inp, np.ascontiguousarray(xT[:, c * 2048:(c + 1) * 2048]), np.ascontiguousarray(oT0[:, c * 2048:(c + 1) * 2048]))
                   for c in range(8)])
    x1T = np.ascontiguousarray(np.concatenate([rb[c]["xo"] for c in range(8)], axis=1))
    del rb, oT0
    cc = mixc_consts()
    nc = MixC().build()
    rc = _run(nc, [mixc_inputs(inp, c, x1T, cc) for c in range(8)])
    oT1 = np.ascontiguousarray(np.concatenate([rc[c]["og"] for c in range(8)], axis=0))
    del rc
    nc = Post(1).build()
    rd = _run(nc, [post_inputs(1, inp, np.ascontiguousarray(x1T[:, c * 2048:(c + 1) * 2048]), np.ascontiguousarray(oT1[:, c * 2048:(c + 1) * 2048]))
                   for c in range(8)])
    outT = np.concatenate([rd[c]["xo"] for c in range(8)], axis=1)
    return np.ascontiguousarray(outT.T)[None].astype(np.float32)
```

```python
import math
import numpy as np
from contextlib import ExitStack
from concourse.bass_utils import run_bass_kernel_spmd
import concourse.bass as bass
import concourse.mybir as mybir

F32 = mybir.dt.float32
BF16 = mybir.dt.bfloat16
AF = mybir.ActivationFunctionType
ALU = mybir.AluOpType
EPS = 1e-6


class Sem:
    def __init__(self, h):
        self.h = h
        self.n = 0

    def inc(self, ins, by=1):
        ins.then_inc(self.h, by)
        self.n += by
        return self.n


class KB:
    def __init__(self):
        self.nc = bass.Bass("TRN2", target_bir_lowering=False)
        self.es = ExitStack()
        self.sems = {}
        self.uid = 0

    def sb(self, name, shape, dt, es=None):
        return (es or self.es).enter_context(self.nc.sbuf_tensor(name, shape, dt))

    def psum(self, name, shape, dt, es=None):
        return (es or self.es).enter_context(self.nc.psum_tensor(name, shape, dt))

    def sem(self, name):
        if name not in self.sems:
            self.sems[name] = Sem(self.es.enter_context(self.nc.semaphore(name)))
        return self.sems[name]

    def din(self, name, shape, dt=F32):
        return self.nc.dram_tensor(name, list(shape), dt, kind="ExternalInput").ap()

    def dout(self, name, shape, dt=F32):
        return self.nc.dram_tensor(name, list(shape), dt, kind="ExternalOutput").ap()

    def dscr(self, name, shape, dt=F32):
        return self.nc.dram_tensor(name, list(shape), dt, kind="Internal").ap()


D = 2048
T = 2048
TT = 1024
NT = TT // 512
FF = 5632
WB = 8192


class Post:
    def __init__(self, layer):
        self.layer = layer
        self.FY = 3072 if layer == 0 else 4096
        self.G = 2048 if layer == 0 else 4096
        self.kb = kb = KB()
        nc = self.nc = kb.nc
        FY, G = self.FY, self.G
        self.xT = kb.din("xT", [D, T])
        self.oT = kb.din("oT", [FY, T])
        self.w_gate = kb.din("w_gate", [D, G])
        self.w_out = kb.din("w_out", [FY, D])
        self.gains = kb.din("gains", [128, 4, 16])
        self.hg = kb.din("hg", [128, 3])
        self.memT = kb.din("memT", [D, 256])
        self.w_q = kb.din("w_q", [D, 512])
        self.w_kv = kb.din("w_kv", [D, 1024])
        self.w_o = kb.din("w_o", [512, D])
        self.w1 = kb.din("w1", [D, FF])
        self.w3 = kb.din("w3", [D, FF])
        self.w2 = kb.din("w2", [FF, D])
        self.xo = kb.dout("xo", [D, T])
        self.x1 = kb.dout("x1s", [D, T])
        self.x2 = kb.dout("x2s", [D, T])
        self.wbuf = [kb.sb("wbuf0", [128, WB], BF16), kb.sb("wbuf1", [128, WB], BF16)]
        self.ones = kb.sb("ones", [128, 128], BF16)
        self.gains_sb = kb.sb("gains_sb", [128, 4, 16], F32)
        self.hg_sb = kb.sb("hg_sb", [128, 3], F32)
        self.eps_sb = kb.sb("eps_sb", [128, 1], F32)
        self.eps2_sb = kb.sb("eps2_sb", [128, 1], F32)
        self.hT = kb.sb("hT", [128, 16, TT], BF16)
        self.big = kb.sb("big", [128, 32 * TT], BF16)
        self.xst_f = kb.sb("xst", [128, 4096], F32)
        self.sq_f = kb.sb("sq", [128, 4096], BF16)
        self.xst_n = self.xst_f[:, :].rearrange("p (c t) -> p c t", t=256)
        self.xst_f2 = kb.sb("xst2", [128, 4096], F32)
        self.xst_n2 = self.xst_f2[:, :].rearrange("p (c t) -> p c t", t=256)
        self.sq_n = self.sq_f[:, :].rearrange("p (c t) -> p c t", t=256)
        self.xst = self.xst_f[:, 0:2048].rearrange("p (c t) -> p c t", t=512)
        self.sq = self.sq_f[:, 0:2048].rearrange("p (c t) -> p c t", t=512)
        self.qf = self.xst
        self.rstd = kb.sb("rstd", [128, 4, 512], F32)
        self.rbuf = kb.sb("rbuf", [128, 3, 512], F32)
        self.obuf = kb.sb("obuf", [128, 3, 512], F32)
        self.stmp = kb.sb("stmp", [128, 2, 512], F32)
        self.knT = kb.sb("knT", [128, 4, 256], BF16)
        self.vm = kb.sb("vm", [128, 2, 512], BF16)
        self.qn = self.big[:, 0:4 * TT].rearrange("p (c t) -> p c t", t=TT)
        self.pT = kb.sb("pT", [128, 2, 512], BF16)
        self.oxa = self.big[:, 4 * TT:8 * TT].rearrange("p (c t) -> p c t", t=TT)
        self.ps = kb.psum("ps", [128, 8, 512], F32)
        self.gidx = 0
        self.grp_end = []
        self.mm = kb.sem("g_mm")
        self.pf = kb.sem("g_pf")
        self.wl = [kb.sem("g_wl0"), kb.sem("g_wl1")]
        self.sts = [kb.sem("st%d" % i) for i in range(3)]
        self.bar_n = 0

    def wait_stores(self):
        for st in self.sts:
            self.nc.sync.wait_ge(st.h, st.n)

    def barrier(self):
        self.nc.all_engine_barrier()

    def y(self):
        return self.big[:, 0:(self.FY // 128) * TT].rearrange("p (c t) -> p c t", t=TT)

    def g(self):
        return self.big[:, 0:22 * TT].rearrange("p (c t) -> p c t", t=TT)

    def rstd_op(self, ps_ap, out_ap, inv_n, wait, post=1.0):
        nc = self.nc
        s = self.kb.sem("r_a")
        nc.scalar.wait_ge(wait[0], wait[1])
        s.inc(nc.scalar.activation(out=out_ap, in_=ps_ap, func=AF.Sqrt, scale=inv_n / post ** 2, bias=self.eps_sb[:, 0:1] if post == 1.0 else self.eps2_sb[:, 0:1]))
        nc.vector.wait_ge(s.h, s.n)
        return nc.vector.reciprocal(out=out_ap, in_=out_ap)

    def gemm(self, wsrc, KC, GW, ngroups, act, ntt, epi, pair=False, tw=512, pe_waits=()):
        nc = self.nc
        mpg = GW // 128
        if pair:
            mpg //= 2
        G0 = len(self.grp_end)

        def load(g):
            Gg = G0 + g
            b = Gg % 2
            if Gg >= 2:
                nc.gpsimd.wait_ge(self.mm.h, self.grp_end[Gg - 2])
            wv = self.wbuf[b][:, 0:KC * GW].rearrange("p (c n) -> p c n", n=GW)
            for (ap, off, w) in wsrc(g):
                src = ap.rearrange("(c p) n -> p c n", p=128)
                kstep = 8
                for k0 in range(0, KC, kstep):
                    k1 = min(KC, k0 + kstep)
                    self.wl[b].inc(nc.gpsimd.dma_start(out=wv[:, k0:k1, off:off + w], in_=src[:, k0:k1, :]), 16)
            return self.wl[b].n

        wl_need = {}
        wl_need[0] = load(0)
        cnt = 0
        for (sh, sv) in pe_waits:
            nc.tensor.wait_ge(sh, sv)
        for g in range(ngroups):
            if g + 1 < ngroups:
                wl_need[g + 1] = load(g + 1)
            b = (G0 + g) % 2
            nc.tensor.wait_ge(self.wl[b].h, wl_need[g])
            wv = self.wbuf[b][:, 0:KC * GW].rearrange("p (c n) -> p c n", n=GW)
            for j in range(mpg):
                for tt in range(ntt):
                    cols = [j] if not pair else [j, j + mpg]
                    ps_list = []
                    for cj in cols:
                        idx = self.gidx
                        bank = idx % 4
                        if idx >= 4:
                            nc.tensor.wait_ge(self.pf.h, idx - 3)
                        for k in range(KC):
                            ins = nc.tensor.matmul(self.ps[:, bank, 0:tw], lhsT=wv[:, k, cj * 128:(cj + 1) * 128],
                                                   rhs=act[:, k, tt * tw:(tt + 1) * tw], start=(k == 0), stop=(k == KC - 1))
                        self.mm.inc(ins)
                        self.gidx += 1
                        ps_list.append(self.ps[:, bank, 0:tw])
                    fin = epi(cnt, g * mpg + j, tt, ps_list, self.gidx)
                    self.pf.inc(fin, len(cols))
                    cnt += 1
            self.grp_end.append(self.mm.n)

    def norm(self, src, tok0, which, dst, ntok_tiles, tw=256, gains=None):
        nc = self.nc
        s_lds = [self.kb.sem("n_ld0"), self.kb.sem("n_ld1")]
        s_sq = self.kb.sem("n_sq"); s_mm = self.kb.sem("n_mm"); s_dv = self.kb.sem("n_dv")
        slots = [self.xst_n, self.xst_n2]
        srcv = src.rearrange("(c p) t -> p c t", p=128)
        base_dv = s_dv.n
        dv_hist = []

        def issue(tt):
            sl = tt % 2
            t0 = tok0 + tt * tw
            nc.sync.wait_ge(s_dv.h, dv_hist[tt - 2] if tt >= 2 else base_dv)
            for hh in range(2):
                s_lds[sl].inc(nc.sync.dma_start(out=slots[sl][:, hh * 8:(hh + 1) * 8, 0:tw], in_=srcv[:, hh * 8:(hh + 1) * 8, t0:t0 + tw]), 16)

        issue(0)
        for tt in range(ntok_tiles):
            sl = tt % 2
            xs = slots[sl]
            if tt + 1 < ntok_tiles:
                issue(tt + 1)
            nc.scalar.wait_ge(s_lds[sl].h, s_lds[sl].n)
            nc.scalar.wait_ge(s_mm.h, s_mm.n)
            s_sq.inc(nc.scalar.activation(out=self.sq_n[:, :, 0:tw], in_=xs[:, :, 0:tw], func=AF.Square))
            nc.tensor.wait_ge(s_sq.h, s_sq.n)
            nc.tensor.wait_ge(s_dv.h, s_dv.n)
            for k in range(16):
                ins = nc.tensor.matmul(self.ps[:, 4, 0:tw], lhsT=self.ones[:, :], rhs=self.sq_n[:, k, 0:tw], start=(k == 0), stop=(k == 15))
            s_mm.inc(ins)
            self.rstd_op(self.ps[:, 4, 0:tw], self.rstd[:, 0, 0:tw], 1.0 / D, (s_mm.h, s_mm.n))
            for k in range(16):
                ins = nc.vector.scalar_tensor_tensor(out=dst[:, k, tt * tw:(tt + 1) * tw], in0=xs[:, k, 0:tw],
                                                     scalar=self.gains_sb[:, which, k:k + 1], in1=self.rstd[:, 0, 0:tw],
                                                     op0=ALU.mult, op1=ALU.mult)
            s_dv.inc(ins)
            dv_hist.append(s_dv.n)
        return s_dv

    def onorm(self, tok0):
        nc = self.nc
        layer = self.layer
        y = self.y()
        s_lds = [self.kb.sem("o_ld0"), self.kb.sem("o_ld1")]
        s_sq = self.kb.sem("o_sq"); s_mm = self.kb.sem("o_mm"); s_dv = self.kb.sem("o_dv")
        n_dv = self.kb.sem("n_dv")
        nblk = self.FY // 512
        nnorm = 4 if layer == 0 else 8
        ov = self.oT.rearrange("(c p) t -> p c t", p=128)
        slots = [self.xst, self.xst_f2[:, 0:2048].rearrange("p (c t) -> p c t", t=512)]
        items = [(tt, blk) for tt in range(NT) for blk in range(nblk)]
        base_dv = s_dv.n
        dv_hist = []
        nc.sync.wait_ge(n_dv.h, n_dv.n)

        def issue(i):
            tt, blk = items[i]
            t0 = tok0 + tt * 512
            nc.sync.wait_ge(s_dv.h, dv_hist[i - 2] if i >= 2 else base_dv)
            s_lds[i % 2].inc(nc.sync.dma_start(out=slots[i % 2][:, 0:4, :], in_=ov[:, blk * 4:(blk + 1) * 4, t0:t0 + 512]), 16)

        issue(0)
        for i, (tt, blk) in enumerate(items):
            xs = slots[i % 2]
            s_ld = s_lds[i % 2]
            if i + 1 < len(items):
                issue(i + 1)
            if blk >= nnorm:
                nc.vector.wait_ge(s_ld.h, s_ld.n)
                ins = nc.vector.tensor_copy(out=y[:, blk * 4:(blk + 1) * 4, tt * 512:(tt + 1) * 512], in_=xs[:, 0:4, :])
                s_dv.inc(ins)
                dv_hist.append(s_dv.n)
                continue
            nc.scalar.wait_ge(s_ld.h, s_ld.n)
            nc.scalar.wait_ge(s_mm.h, s_mm.n)
            s_sq.inc(nc.scalar.activation(out=self.sq[:, 0:4, :], in_=xs[:, 0:4, :], func=AF.Square))
            nc.tensor.wait_ge(s_sq.h, s_sq.n)
            nc.tensor.wait_ge(s_dv.h, s_dv.n)
            if layer == 0:
                for k in range(4):
                    ins = nc.tensor.matmul(self.ps[:, 4, :], lhsT=self.ones[:, :], rhs=self.sq[:, k, :], start=(k == 0), stop=(k == 3))
            else:
                for k in range(4):
                    ins = nc.tensor.matmul(self.ps[:, 4 + k, :], lhsT=self.ones[:, :], rhs=self.sq[:, k, :], start=True, stop=True)
            s_mm.inc(ins)
            if layer == 0:
                self.rstd_op(self.ps[:, 4, :], self.rstd[:, 0, :], 1.0 / 512, (s_mm.h, s_mm.n))
                for k in range(4):
                    ins = nc.vector.tensor_tensor(out=y[:, blk * 4 + k, tt * 512:(tt + 1) * 512], in0=xs[:, k, :], in1=self.rstd[:, 0, :], op=ALU.mult)
            else:
                for k in range(4):
                    self.rstd_op(self.ps[:, 4 + k, :], self.rstd[:, k, :], 1.0 / 128, (s_mm.h, s_mm.n))
                    ins = nc.vector.scalar_tensor_tensor(out=y[:, blk * 4 + k, tt * 512:(tt + 1) * 512], in0=xs[:, k, :],
                                                         scalar=self.hg_sb[:, 2:3], in1=self.rstd[:, k, :], op0=ALU.mult, op1=ALU.mult)
            s_dv.inc(ins)
            dv_hist.append(s_dv.n)

    def epi_gate(self):
        nc = self.nc
        y = self.y()
        s_d = self.kb.sem("eg_d")
        base_d = s_d.n

        def epi(cnt, mt, tt, ps_list, idx_after):
            s = cnt % 2
            nc.scalar.wait_ge(self.mm.h, idx_after)
            if cnt >= 2:
                nc.scalar.wait_ge(s_d.h, base_d + cnt - 1)
            fin = nc.scalar.activation(out=self.stmp[:, s, :], in_=ps_list[0], func=AF.Silu)
            nc.vector.wait_ge(self.pf.h, idx_after)
            yv = y[:, mt, tt * 512:(tt + 1) * 512]
            s_d.inc(nc.vector.tensor_tensor(out=yv, in0=self.stmp[:, s, :], in1=yv, op=ALU.mult))
            return fin
        return epi

    def epi_swiglu(self):
        nc = self.nc
        g = self.g()
        s_a = self.kb.sem("es_a")
        hist = []

        def epi(cnt, mt, tt, ps_list, idx_after):
            s = cnt % 2
            nc.scalar.wait_ge(self.mm.h, idx_after)
            if cnt >= 2:
                nc.scalar.wait_ge(self.pf.h, hist[cnt - 2])
            s_a.inc(nc.scalar.activation(out=self.stmp[:, s, :], in_=ps_list[0], func=AF.Silu))
            nc.vector.wait_ge(s_a.h, s_a.n)
            fin = nc.vector.tensor_tensor(out=g[:, mt, tt * 512:(tt + 1) * 512], in0=self.stmp[:, s, :], in1=ps_list[1], op=ALU.mult)
            hist.append(idx_after)
            return fin
        return epi

    def epi_resid(self, res_src, dst, tok0, tiles):
        nc = self.nc
        rv = res_src.rearrange("(c p) t -> p c t", p=128)
        dv = dst.rearrange("(c p) t -> p c t", p=128)
        idx0 = self.gidx
        rls = [self.kb.sem("rl%d" % i) for i in range(3)]
        sts = self.sts

        def issue_load(c):
            mt, tt = tiles[c]
            if c >= 3:
                nc.sync.wait_ge(self.pf.h, idx0 + c - 2)
            rls[c % 3].inc(nc.sync.dma_start(out=self.rbuf[:, c % 3, :], in_=rv[:, mt, tok0 + tt * 512: tok0 + (tt + 1) * 512]), 16)

        def epi(cnt, mt, tt, ps_list, idx_after):
            if cnt == 0:
                issue_load(0)
                if len(tiles) > 1:
                    issue_load(1)
            if cnt + 2 < len(tiles):
                issue_load(cnt + 2)
            s = cnt % 3
            nc.vector.wait_ge(self.mm.h, idx_after)
            nc.vector.wait_ge(rls[s].h, rls[s].n if cnt + 3 >= len(tiles) or True else 0)
            nc.vector.wait_ge(sts[s].h, sts[s].n)
            fin = nc.vector.tensor_tensor(out=self.obuf[:, s, :], in0=ps_list[0], in1=self.rbuf[:, s, :], op=ALU.add)
            nc.sync.wait_ge(self.pf.h, idx_after)
            sts[s].inc(nc.sync.dma_start(out=dv[:, mt, tok0 + tt * 512: tok0 + (tt + 1) * 512], in_=self.obuf[:, s, :]), 16)
            return fin
        return epi

    def epi_plain(self, dstf):
        nc = self.nc

        def epi(cnt, mt, tt, ps_list, idx_after):
            nc.vector.wait_ge(self.mm.h, idx_after)
            return nc.vector.tensor_copy(out=dstf(mt, tt), in_=ps_list[0])
        return epi

    def mem_kv(self):
        nc = self.nc
        s_ld = self.kb.sem("m_ld"); s_a = self.kb.sem("m_a"); s_p = self.kb.sem("m_p"); s_d = self.kb.sem("m_d")
        memn = self.hT[:, :, 0:256]
        ndv = self.norm(self.memT, 0, 3, self.hT, 1, tw=256)
        self.barrier()
        self.gemm(lambda g: [(self.w_kv[:, 0:512], 0, 512)], 16, 512, 1, memn, 1,
                  self.epi_plain(lambda mt, tt: self.qf[:, mt, 0:256]), tw=256, pe_waits=[(ndv.h, ndv.n)])
        self.barrier()
        nc.scalar.wait_ge(self.pf.h, self.gidx)
        nc.scalar.activation(out=self.sq[:, 0:4, 0:256], in_=self.qf[:, 0:4, 0:256], func=AF.Square).then_inc(s_a.h, 1)
        nc.tensor.wait_ge(s_a.h, 1)
        for h in range(4):
            ins = nc.tensor.matmul(self.ps[:, 4 + h, 0:256], lhsT=self.ones[:, :], rhs=self.sq[:, h, 0:256], start=True, stop=True)
        ins.then_inc(s_p.h, 1)
        for h in range(4):
            self.rstd_op(self.ps[:, 4 + h, 0:256], self.rstd[:, h, 0:256], 1.0 / 128, (s_p.h, 1))
            nc.vector.scalar_tensor_tensor(out=self.knT[:, h, :], in0=self.qf[:, h, 0:256], scalar=self.hg_sb[:, 1:2], in1=self.rstd[:, h, 0:256],
                                           op0=ALU.mult, op1=ALU.mult)
        self.barrier()
        wv = self.wbuf[0][:, 0:16 * 512].rearrange("p (c n) -> p c n", n=512)
        src = self.w_kv[:, 512:1024].rearrange("(c p) n -> p c n", p=128)
        for k0 in (0, 8):
            nc.gpsimd.dma_start(out=wv[:, k0:k0 + 8, :], in_=src[:, k0:k0 + 8, :]).then_inc(s_ld.h, 16)
        nc.tensor.wait_ge(s_ld.h, 32)
        for c in range(2):
            for k in range(16):
                ins = nc.tensor.matmul(self.ps[:, 4 + c, :], lhsT=self.hT[:, k, c * 128:(c + 1) * 128], rhs=wv[:, k, :], start=(k == 0), stop=(k == 15))
        ins.then_inc(s_p.h, 1)
        nc.vector.wait_ge(s_p.h, 2)
        for c in range(2):
            ins = nc.vector.tensor_copy(out=self.vm[:, c, :], in_=self.ps[:, 4 + c, :])
        self.barrier()

    def xa_attn(self):
        nc = self.nc
        s_q = self.kb.sem("x_q"); s_a = self.kb.sem("x_a"); s_p = self.kb.sem("x_p"); s_d = self.kb.sem("x_d")
        scale = 128 ** -0.5
        for tt in range(NT):
            self.gemm(lambda g: [(self.w_q[:, :], 0, 512)], 16, 512, 1, self.hT[:, :, tt * 512:(tt + 1) * 512], 1,
                      self.epi_plain(lambda mt, t_: self.qf[:, mt, :]), pe_waits=[(self.kb.sem("n_dv").h, self.kb.sem("n_dv").n)])
            self.barrier()
            nc.scalar.wait_ge(self.pf.h, self.gidx)
            s_a.inc(nc.scalar.activation(out=self.sq[:, 0:4, :], in_=self.qf[:, 0:4, :], func=AF.Square))
            nc.tensor.wait_ge(s_a.h, s_a.n)
            for h in range(4):
                ins = nc.tensor.matmul(self.ps[:, 4 + h, :], lhsT=self.ones[:, :], rhs=self.sq[:, h, :], start=True, stop=True)
            s_p.inc(ins)
            for h in range(4):
                self.rstd_op(self.ps[:, 4 + h, :], self.rstd[:, h, :], 1.0 / 128, (s_p.h, s_p.n), post=scale)
                ins = nc.vector.scalar_tensor_tensor(out=self.qn[:, h, tt * 512:(tt + 1) * 512], in0=self.qf[:, h, :], scalar=self.hg_sb[:, 0:1],
                                                     in1=self.rstd[:, h, :], op0=ALU.mult, op1=ALU.mult)
            s_d.inc(ins)
            nc.tensor.wait_ge(s_d.h, s_d.n)
            nc.scalar.wait_ge(s_d.h, s_d.n)
            self.barrier()
            for h in range(4):
                for c in range(2):
                    ins = nc.tensor.matmul(self.ps[:, 4 + c, :], lhsT=self.knT[:, h, c * 128:(c + 1) * 128], rhs=self.qn[:, h, tt * 512:(tt + 1) * 512],
                                           start=True, stop=True)
                s_p.inc(ins)
                nc.scalar.wait_ge(s_p.h, s_p.n)
                for c in range(2):
                    ins = nc.scalar.activation(out=self.pT[:, c, :], in_=self.ps[:, 4 + c, :], func=AF.Exp)
                s_a.inc(ins)
                nc.tensor.wait_ge(s_a.h, s_a.n)
                for c in range(2):
                    nc.tensor.matmul(self.ps[:, 6, :], lhsT=self.vm[:, c, h * 128:(h + 1) * 128], rhs=self.pT[:, c, :], start=(c == 0), stop=(c == 1))
                for c in range(2):
                    ins = nc.tensor.matmul(self.ps[:, 7, :], lhsT=self.ones[:, :], rhs=self.pT[:, c, :], start=(c == 0), stop=(c == 1))
                s_p.inc(ins)
                nc.vector.wait_ge(s_p.h, s_p.n)
                nc.vector.reciprocal(out=self.rstd[:, 0, :], in_=self.ps[:, 7, :])
                ins = nc.vector.tensor_tensor(out=self.oxa[:, h, tt * 512:(tt + 1) * 512], in0=self.ps[:, 6, :], in1=self.rstd[:, 0, :], op=ALU.mult)
                s_d.inc(ins)
                nc.tensor.wait_ge(s_d.h, s_d.n)
                nc.scalar.wait_ge(s_d.h, s_d.n)
            self.barrier()

    def build(self, stages=99):
        nc = self.nc
        s0 = self.kb.sem("init")
        nc.vector.memset(self.ones[:, :], 1.0)
        nc.vector.memset(self.eps_sb[:, :], EPS)
        nc.vector.memset(self.eps2_sb[:, :], EPS * 128.0)
        nc.sync.dma_start(out=self.gains_sb[:, :, :], in_=self.gains).then_inc(s0.h, 16)
        nc.sync.dma_start(out=self.hg_sb[:, :], in_=self.hg).then_inc(s0.h, 16)
        nc.sync.wait_ge(s0.h, 32)
        self.barrier()
        self.mem_kv()
        for p in range(T // TT):
            tok0 = p * TT
            y = self.y()
            self.norm(self.xT, tok0, 0, self.hT, TT // 256)
            self.barrier()
            if stages < 1:
                continue
            self.onorm(tok0)
            self.barrier()
            self.gemm(lambda g: [(self.w_gate[:, g * 512:(g + 1) * 512], 0, 512)], 16, 512, self.G // 512, self.hT, NT, self.epi_gate(),
                      pe_waits=[(self.kb.sem("n_dv").h, self.kb.sem("n_dv").n), (self.kb.sem("o_dv").h, self.kb.sem("o_dv").n)])
            self.barrier()
            if stages < 2:
                continue
            KC = self.FY // 128
            tiles = [(m, tt) for m in range(16) for tt in range(NT)]
            dst = self.x1 if stages > 2 else self.xo
            self.gemm(lambda g: [(self.w_out[:, g * 256:(g + 1) * 256], 0, 256)], KC, 256, 8, y, NT,
                      self.epi_resid(self.xT, dst, tok0, tiles), pe_waits=[(self.kb.sem("eg_d").h, self.kb.sem("eg_d").n)])
            self.wait_stores()
            self.barrier()
            if stages < 3:
                continue
            self.norm(self.x1, tok0, 1, self.hT, TT // 256)
            self.barrier()
            self.xa_attn()
            dst = self.x2 if stages > 3 else self.xo
            self.gemm(lambda g: [(self.w_o[:, :], 0, 2048)], 4, 2048, 1, self.oxa, NT,
                      self.epi_resid(self.x1, dst, tok0, tiles), pe_waits=[(self.kb.sem("x_d").h, self.kb.sem("x_d").n)])
            self.wait_stores()
            self.barrier()
            if stages < 4:
                continue
            self.norm(self.x2, tok0, 2, self.hT, TT // 256)
            self.barrier()
            for half in range(2):
                c0 = half * (FF // 2)
                self.gemm(lambda g: [(self.w1[:, c0 + g * 256:c0 + (g + 1) * 256], 0, 256), (self.w3[:, c0 + g * 256:c0 + (g + 1) * 256], 256, 256)],
                          16, 512, FF // 512, self.hT, NT, self.epi_swiglu(), pair=True,
                          pe_waits=[(self.kb.sem("n_dv").h, self.kb.sem("n_dv").n)])
                self.barrier()
                w2h = self.w2[c0:c0 + FF // 2, :]
                self.gemm(lambda g: [(w2h[:, g * 256:(g + 1) * 256], 0, 256)], 22, 256, 8, self.g(), NT,
                          self.epi_resid(self.x2 if half == 0 else self.xo, self.xo, tok0, tiles), pe_waits=[(self.pf.h, self.gidx)])
                self.wait_stores()
                self.barrier()
        return nc


def post_inputs(layer, inp, xT_c, oT_c):
    def gl(v):
        return np.ascontiguousarray(v.reshape(16, 128).T)
    gains = np.stack([gl(inp["norm_mix"][layer]), gl(inp["norm_xa"][layer]), gl(inp["norm_ffn"][layer]), gl(inp["mem_norm"])], axis=1)
    gd = inp["gdn_norm"][0]
    hg = np.stack([inp["xa_q_gain"][layer], inp["xa_k_gain"][layer], gd], axis=1)
    if layer == 0:
        w_gate = np.ascontiguousarray(inp["ar_w_in"][0][:, 4096:6144])
        w_out = inp["ar_w_out"][0]
    else:
        w_gate = np.ascontiguousarray(inp["gdn_w_in"][0][:, 8192:12288])
        w_out = inp["gdn_w_out"][0]
    return {
        "xT": xT_c, "oT": oT_c, "w_gate": w_gate, "w_out": np.ascontiguousarray(w_out),
        "gains": np.ascontiguousarray(gains.astype(np.float32)), "hg": np.ascontiguousarray(hg.astype(np.float32)),
        "memT": np.ascontiguousarray(inp["mem"][0].T),
        "w_q": np.ascontiguousarray(inp["xa_w_q"][layer]), "w_kv": np.ascontiguousarray(inp["xa_w_kv"][layer]),
        "w_o": np.ascontiguousarray(inp["xa_w_o"][layer]),
        "w1": np.ascontiguousarray(inp["ffn_w1"][layer]), "w3": np.ascontiguousarray(inp["ffn_w3"][layer]),
        "w2": np.ascontiguousarray(inp["ffn_w2"][layer]),
    }


D = 2048
S = 16384
BT = 512
NB_A = S // BT
NCOL_A = 1152
NRING = 20


class MixA:
    def __init__(self, nblocks=NB_A):
        self.nblocks = nblocks
        self.kb = kb = KB()
        nc = self.nc = kb.nc
        self.xT = kb.din("xT", [D, S])
        self.wA = kb.din("wA", [D, NCOL_A])
        self.gain = kb.din("gain", [128, 16])
        self.hg = kb.din("hg", [128, 2])
        self.cosT = kb.din("cosT", [128, S])
        self.sinT = kb.din("sinT", [128, S])
        self.dmask = kb.din("dmask", [128, 128])
        self.qdrow = kb.din("qdrow", [128, BT])
        self.kdec = kb.din("kdec", [128, 2])
        self.gtab = kb.din("gtab", [128, 17, 128])
        self.mtab = kb.din("mtab", [128, 17, 128])
        self.ident = kb.din("ident", [128, 128])
        self.oret = kb.dout("oret", [256, S])
        self.odil = kb.dout("odil", [128, S])
        sb = kb.sb
        self.w = sb("w", [128, 16, NCOL_A], BF16)
        self.ones = sb("ones", [128, 128], BF16)
        self.idb = sb("idb", [128, 128], BF16)
        self.idf = sb("idf", [128, 128], F32)
        self.gain_sb = sb("gain_sb", [128, 16], F32)
        self.hg_sb = sb("hg_sb", [128, 2], F32)
        self.eps_sb = sb("eps_sb", [128, 1], F32)
        self.eps2_sb = sb("eps2_sb", [128, 1], F32)
        self.dm = sb("dm", [128, 128], F32)
        self.qd = sb("qd", [128, BT], F32)
        self.kd = sb("kd", [128, 2], F32)
        self.E = sb("E", [128, 17, 128], F32)
        self.mt_sb = sb("mt_sb", [128, 17, 128], F32)
        self.xst = sb("xst", [128, 16, BT], F32)
        self.sq = sb("sq", [128, 16, BT], BF16)
        self.hT = sb("hT", [128, 16, BT], BF16)
        self.rstd = sb("rstd", [128, 2, BT], F32)
        self.cs = sb("cs", [128, 2, 2, BT], F32)
        self.tmp = sb("tmp", [128, 2, BT], F32)
        self.QT = sb("QT", [128, 2, BT], BF16)
        self.QdT = sb("QdT", [128, 2, BT], BF16)
        self.KT = sb("KT", [128, 2, BT], BF16)
        self.Kd = sb("Kd", [128, 4, 256], BF16)
        self.VA = sb("VA", [128, 4, 256], BF16)
        self.Sm = sb("Sm", [128, 128], BF16)
        self.St = sb("St", [128, 2, 256], F32)
        self.Stb = sb("Stb", [128, 2, 256], BF16)
        self.qnT = sb("qnT", [128, BT], BF16)
        self.knR = sb("knR", [128, NRING, 128], BF16)
        self.vbR = sb("vbR", [128, NRING, 128], BF16)
        self.ex = sb("ex", [128, 17 * 128], F32)
        self.pT = sb("pT", [128, 17 * 128], BF16)
        self.rl_ = sb("rl_", [128, 128], F32)
        self.oretb = sb("oretb", [128, 2, 2, BT], F32)
        self.odilb = sb("odilb", [128, 2, BT], F32)
        self.ps = kb.psum("ps", [128, 8, 512], F32)

    def rstd_op(self, ps_ap, out_ap, inv_n, wait, post=1.0):
        nc = self.nc
        s = self.kb.sem("r_a")
        nc.scalar.wait_ge(wait[0], wait[1])
        s.inc(nc.scalar.activation(out=out_ap, in_=ps_ap, func=AF.Sqrt, scale=inv_n / post ** 2,
                                   bias=self.eps_sb[:, 0:1] if post == 1.0 else self.eps2_sb[:, 0:1]))
        nc.vector.wait_ge(s.h, s.n)
        return nc.vector.reciprocal(out=out_ap, in_=out_ap)

    def build(self):
        nc = self.nc
        kb = self.kb
        sem = kb.sem
        ps = self.ps
        V, A, PE, SP, PL = nc.vector, nc.scalar, nc.tensor, nc.sync, nc.gpsimd

        def W(eng, s):
            eng.wait_ge(s.h, s.n)

        s0 = sem("init")
        for k0 in range(0, 16, 4):
            s0.inc(PL.dma_start(out=self.w[:, k0:k0 + 4, :], in_=self.wA.rearrange("(c p) n -> p c n", p=128)[:, k0:k0 + 4, :]), 16)
        s0.inc(PL.dma_start(out=self.idb[:, :], in_=self.ident), 16)
        s1 = sem("init1")
        for (dst, src) in [(self.gain_sb[:, :], self.gain), (self.hg_sb[:, :], self.hg), (self.dm[:, :], self.dmask), (self.qd[:, :], self.qdrow),
                           (self.kd[:, :], self.kdec), (self.E[:, :, :], self.gtab), (self.mt_sb[:, :, :], self.mtab), (self.idf[:, :], self.ident)]:
            s1.inc(SP.dma_start(out=dst, in_=src), 16)
        V.memset(self.ones[:, :], 1.0)
        V.memset(self.eps_sb[:, :], EPS)
        V.memset(self.eps2_sb[:, :], EPS * 128.0)
        V.memset(self.St[:, :, :], 0.0)
        V.memset(self.Stb[:, :, :], 0.0)
        W(A, s1)
        sE = sem("sE")
        sE.inc(A.activation(out=self.E[:, :, :], in_=self.E[:, :, :], func=AF.Exp))
        W(V, sE)
        W(V, s1)
        sE2 = sem("sE2")
        sE2.inc(V.tensor_tensor(out=self.E[:, :, :], in0=self.E[:, :, :], in1=self.mt_sb[:, :, :], op=ALU.mult))
        W(PE, s0)
        W(PE, sE2)
        W(A, sE2)

        xv = self.xT.rearrange("(c p) t -> p c t", p=128)
        s_xl = sem("xl"); s_sq = sem("a_sq"); s_ss = sem("p_ss"); s_h = sem("d_h")
        s_cl = [sem("cl0"), sem("cl1")]
        s_pj = sem("p_pj")
        s_pf = sem("pjf")
        s_rot = sem("d_rot")
        s_sq2 = sem("a_sq2"); s_ss2 = sem("p_ss2")
        s_tr = sem("p_tr"); s_kd = sem("d_kd")
        s_sc = sem("p_sc"); s_sm = sem("d_sm"); s_o = sem("p_o"); s_oe = sem("a_oe"); s_ds = sem("p_ds"); s_st = sem("d_st")
        s_qk = sem("p_qk"); s_ex = sem("a_ex"); s_p = sem("d_p"); s_pv = sem("p_pv"); s_do = sem("d_do")
        s_or = [sem("or0"), sem("or1")]; s_od = [sem("od0"), sem("od1")]
        pj_idx = [0]
        rot_hist = []

        def proj_tile(cols, width, lhs_tok=None):
            i = pj_idx[0]
            bank = i % 2
            if i >= 2:
                PE.wait_ge(s_pf.h, i - 1)
            for k in range(16):
                if lhs_tok is None:
                    ins = PE.matmul(ps[:, bank, 0:BT], lhsT=self.w[:, k, cols:cols + 128], rhs=self.hT[:, k, :], start=(k == 0), stop=(k == 15))
                else:
                    ins = PE.matmul(ps[:, bank, 0:width], lhsT=self.hT[:, k, lhs_tok * 128:(lhs_tok + 1) * 128], rhs=self.w[:, k, cols:cols + width],
                                    start=(k == 0), stop=(k == 15))
            s_pj.inc(ins)
            pj_idx[0] += 1
            return bank

        for b in range(self.nblocks):
            t0 = b * BT
            sl = b % 2
            W(SP, s_h)
            for hh in range(2):
                s_xl.inc(SP.dma_start(out=self.xst[:, hh * 8:(hh + 1) * 8, :], in_=xv[:, hh * 8:(hh + 1) * 8, t0:t0 + BT]), 16)
            if b >= 2:
                SP.wait_ge(s_rot.h, rot_hist[b - 2])
            s_cl[sl].inc(SP.dma_start(out=self.cs[:, sl, 0, :], in_=self.cosT[:, t0:t0 + BT]), 16)
            s_cl[sl].inc(SP.dma_start(out=self.cs[:, sl, 1, :], in_=self.sinT[:, t0:t0 + BT]), 16)
            W(A, s_xl)
            W(A, s_ss)
            W(A, s_ss2)
            s_sq.inc(A.activation(out=self.sq[:, :, :], in_=self.xst[:, :, :], func=AF.Square))
            W(PE, s_sq)
            for k in range(16):
                ins = PE.matmul(ps[:, 2, :], lhsT=self.ones[:, :], rhs=self.sq[:, k, :], start=(k == 0), stop=(k == 15))
            s_ss.inc(ins)
            self.rstd_op(ps[:, 2, :], self.rstd[:, 0, :], 1.0 / D, (s_ss.h, s_ss.n))
            W(V, s_pj)
            for k in range(16):
                ins = V.scalar_tensor_tensor(out=self.hT[:, k, :], in0=self.xst[:, k, :], scalar=self.gain_sb[:, k:k + 1], in1=self.rstd[:, 0, :],
                                             op0=ALU.mult, op1=ALU.mult)
            s_h.inc(ins)
            W(PE, s_h)
            W(V, s_cl[sl])
            for which, col0, dstT in ((0, 0, self.QT), (1, 256, self.KT)):
                b0 = proj_tile(col0, 128)
                b1 = proj_tile(col0 + 128, 128)
                W(V, s_pj)
                if which == 0:
                    W(V, s_o)
                    W(V, s_sc)
                else:
                    W(V, s_tr)
                    W(V, s_sc)
                cosv = self.cs[:, sl, 0, :]; sinv = self.cs[:, sl, 1, :]
                V.tensor_tensor(out=self.tmp[:, 0, :], in0=ps[:, b0, :], in1=cosv, op=ALU.mult)
                V.tensor_tensor(out=self.tmp[:, 1, :], in0=ps[:, b1, :], in1=sinv, op=ALU.mult)
                V.tensor_tensor(out=dstT[:, 0, :], in0=self.tmp[:, 0, :], in1=self.tmp[:, 1, :], op=ALU.subtract)
                V.tensor_tensor(out=self.tmp[:, 0, :], in0=ps[:, b0, :], in1=sinv, op=ALU.mult)
                ins = V.tensor_tensor(out=self.tmp[:, 1, :], in0=ps[:, b1, :], in1=cosv, op=ALU.mult)
                s_pf.inc(ins, 2)
                ins = V.tensor_tensor(out=dstT[:, 1, :], in0=self.tmp[:, 0, :], in1=self.tmp[:, 1, :], op=ALU.add)
                if which == 0:
                    for i in range(2):
                        ins = V.tensor_tensor(out=self.QdT[:, i, :], in0=self.QT[:, i, :], in1=self.qd[:, :], op=ALU.mult)
                s_rot.inc(ins)
            rot_hist.append(s_rot.n)
            for which, col0 in ((0, 512), (1, 640)):
                bk = proj_tile(col0, 128)
                W(A, s_pj)
                W(A, s_ss2)
                s_sq2.inc(A.activation(out=self.sq[:, 0, :], in_=ps[:, bk, :], func=AF.Square))
                W(PE, s_sq2)
                ins = PE.matmul(ps[:, 2, :], lhsT=self.ones[:, :], rhs=self.sq[:, 0, :], start=True, stop=True)
                s_ss2.inc(ins)
                if which == 0:
                    self.rstd_op(ps[:, 2, :], self.rstd[:, 1, :], 1.0 / 128, (s_ss2.h, s_ss2.n), post=128 ** -0.5)
                    W(V, s_pv)
                    W(V, s_qk)
                    ins = V.scalar_tensor_tensor(out=self.qnT[:, :], in0=ps[:, bk, :], scalar=self.hg_sb[:, 0:1], in1=self.rstd[:, 1, :],
                                                 op0=ALU.mult, op1=ALU.mult)
                else:
                    self.rstd_op(ps[:, 2, :], self.rstd[:, 1, :], 1.0 / 128, (s_ss2.h, s_ss2.n))
                    W(V, s_qk)
                    for j in range(4):
                        slot = (4 * b + j) % NRING
                        ins = V.scalar_tensor_tensor(out=self.knR[:, slot, :], in0=ps[:, bk, j * 128:(j + 1) * 128], scalar=self.hg_sb[:, 1:2],
                                                     in1=self.rstd[:, 1, j * 128:(j + 1) * 128], op0=ALU.mult, op1=ALU.mult)
                s_pf.inc(ins, 1)
            for c in range(4):
                bk = proj_tile(768, 384, lhs_tok=c)
                W(V, s_pj)
                if c == 0:
                    W(V, s_ds)
                    W(V, s_o)
                    W(V, s_pv)
                V.tensor_copy(out=self.VA[:, c, :], in_=ps[:, bk, 0:256])
                ins = V.tensor_copy(out=self.vbR[:, (4 * b + c) % NRING, :], in_=ps[:, bk, 256:384])
                s_pf.inc(ins, 1)
            W(PE, s_rot)
            for c in range(4):
                W(PE, s_kd)
                for i in range(2):
                    ins = PE.matmul(ps[:, 3, i * 128:(i + 1) * 128], lhsT=self.KT[:, i, c * 128:(c + 1) * 128], rhs=self.idb[:, :], start=True, stop=True)
                s_tr.inc(ins)
                W(V, s_tr)
                if c == 0:
                    W(V, s_ds)
                for i in range(2):
                    ins = V.tensor_scalar(out=self.Kd[:, c, i * 128:(i + 1) * 128], in0=ps[:, 3, i * 128:(i + 1) * 128], scalar1=self.kd[:, 0:1], scalar2=None, op0=ALU.mult)
                s_kd.inc(ins)
            if b % 2 == 0 or True:
                V.wait_ge(s_or[sl].h, s_or[sl].n)
                A.wait_ge(s_or[sl].h, s_or[sl].n)
            for c in range(4):
                cs_ = slice(c * 128, (c + 1) * 128)
                W(PE, s_sm)
                W(PE, s_kd)
                for i in range(2):
                    ins = PE.matmul(ps[:, 3, 256:384], lhsT=self.KT[:, i, cs_], rhs=self.QT[:, i, cs_], start=(i == 0), stop=(i == 1))
                s_sc.inc(ins)
                W(V, s_sc)
                W(V, s_o)
                s_sm.inc(V.tensor_tensor(out=self.Sm[:, :], in0=ps[:, 3, 256:384], in1=self.dm[:, :], op=ALU.mult))
                W(PE, s_sm)
                W(PE, s_pf)
                W(PE, s_st)
                W(PE, s_oe)
                for j in range(2):
                    PE.matmul(ps[:, 4, j * 128:(j + 1) * 128], lhsT=self.VA[:, c, j * 128:(j + 1) * 128], rhs=self.Sm[:, :], start=True, stop=False)
                    for i in range(2):
                        ins = PE.matmul(ps[:, 4, j * 128:(j + 1) * 128], lhsT=self.Stb[:, i, j * 128:(j + 1) * 128], rhs=self.QdT[:, i, cs_],
                                        start=False, stop=(i == 1))
                s_o.inc(ins)
                W(A, s_o)
                for j in range(2):
                    ins = A.activation(out=self.oretb[:, sl, j, cs_], in_=ps[:, 4, j * 128:(j + 1) * 128], func=AF.Copy)
                s_oe.inc(ins)
                W(PE, s_kd)
                for i in range(2):
                    ins = PE.matmul(ps[:, 5, i * 256:(i + 1) * 256], lhsT=self.Kd[:, c, i * 128:(i + 1) * 128], rhs=self.VA[:, c, :], start=True, stop=True)
                s_ds.inc(ins)
                W(V, s_ds)
                W(V, s_o)
                for i in range(2):
                    V.scalar_tensor_tensor(out=self.St[:, i, :], in0=self.St[:, i, :], scalar=self.kd[:, 1:2], in1=ps[:, 5, i * 256:(i + 1) * 256],
                                           op0=ALU.mult, op1=ALU.add)
                ins = V.tensor_copy(out=self.Stb[:, :, :], in_=self.St[:, :, :])
                s_st.inc(ins)
            W(SP, s_oe)
            s_or[sl].inc(SP.dma_start(out=self.oret.rearrange("(j p) t -> p j t", p=128)[:, :, t0:t0 + BT], in_=self.oretb[:, sl, :, :]), 16)
            W(PE, s_pf)
            W(PE, s_oe)
            W(PE, s_st)
            W(PE, s_sm)
            W(PE, s_kd)
            V.wait_ge(s_od[sl].h, s_od[sl].n)
            sbanks = [0, 1, 2, 3, 6]
            for qt in range(4):
                tq = 4 * b + qt
                nk = min(17, tq + 1)
                W(PE, s_do)
                W(PE, s_ex)
                for o in range(nk):
                    slot = (tq - o) % NRING
                    ins = PE.matmul(ps[:, sbanks[o // 4], (o % 4) * 128:(o % 4 + 1) * 128], lhsT=self.knR[:, slot, :], rhs=self.qnT[:, qt * 128:(qt + 1) * 128],
                                    start=True, stop=True)
                s_qk.inc(ins)
                W(A, s_qk)
                W(A, s_p)
                n0 = min(nk, 16)
                ins = A.activation(out=self.ex[:, 0:n0 * 128], in_=ps[:, 0:4, :].rearrange("p b c -> p (b c)")[:, 0:n0 * 128], func=AF.Exp)
                if nk == 17:
                    ins = A.activation(out=self.ex[:, 2048:2176], in_=ps[:, 6, 0:128], func=AF.Exp)
                s_ex.inc(ins)
                W(V, s_ex)
                W(V, s_pv)
                s_p.inc(V.tensor_tensor(out=self.pT[:, 0:nk * 128], in0=self.ex[:, 0:nk * 128],
                                        in1=self.E[:, 0:nk, :].rearrange("p o q -> p (o q)"), op=ALU.mult))
                W(PE, s_p)
                for o in range(nk):
                    slot = (tq - o) % NRING
                    first = (o == 0)
                    last = (o == nk - 1)
                    PE.matmul(ps[:, 7, 0:128], lhsT=self.vbR[:, slot, :], rhs=self.pT[:, o * 128:(o + 1) * 128], start=first, stop=last, skip_group_check=True)
                    ins = PE.matmul(ps[:, 7, 128:256], lhsT=self.ones[:, :], rhs=self.pT[:, o * 128:(o + 1) * 128], start=False, stop=last, skip_group_check=True)
                s_pv.inc(ins)
                W(V, s_pv)
                V.reciprocal(out=self.rl_[:, :], in_=ps[:, 7, 128:256])
                s_do.inc(V.tensor_tensor(out=self.odilb[:, sl, qt * 128:(qt + 1) * 128], in0=ps[:, 7, 0:128], in1=self.rl_[:, :], op=ALU.mult))
            W(SP, s_do)
            s_od[sl].inc(SP.dma_start(out=self.odil[:, t0:t0 + BT], in_=self.odilb[:, sl, :]), 16)
        for s in s_or + s_od:
            W(SP, s)
        return nc


def t5_bucket_np(dist):
    exact = 16
    d = np.maximum(dist, exact).astype(np.float32)
    large = exact + (np.log(d / np.float32(exact)) / np.float32(math.log(2048 / exact)) * np.float32(32 - exact)).astype(np.int32)
    large = np.minimum(large, 31)
    return np.where(dist < exact, dist, large)


def mixa_consts():
    i = np.arange(128, dtype=np.float32)
    inv = (np.float32(10000.0) ** (-(np.arange(0, 256, 2, dtype=np.float32)) / np.float32(256))).astype(np.float32)
    pos = np.arange(S, dtype=np.float32)
    ang = (inv[:, None] * pos[None, :]).astype(np.float32)
    cosT = np.cos(ang).astype(np.float32)
    sinT = np.sin(ang).astype(np.float32)
    kj = np.arange(128)[:, None, None]
    o = np.arange(17)[None, :, None]
    qi = np.arange(128)[None, None, :]
    delta = qi - kj + 128 * o
    valid = delta >= 0
    m = ((delta <= 128) & valid).astype(np.float32) + ((delta % 4 == 0) & (delta <= 512) & valid) + ((delta % 16 == 0) & (delta <= 2048) & valid)
    bidx = t5_bucket_np(np.maximum(delta, 0))
    return cosT, sinT, m.astype(np.float32), bidx


def mixa_inputs(inp, c, xT, consts):
    cosT, sinT, mtab, bidx = consts
    hr, vh, hd = c // 2, c % 2, c
    W = inp["ar_w_in"][0]
    wA = np.concatenate([W[:, hr * 256:(hr + 1) * 256], W[:, 1024 + hr * 256:1024 + (hr + 1) * 256],
                         W[:, 6144 + hd * 128:6144 + (hd + 1) * 128], W[:, 7168 + hd * 128:7168 + (hd + 1) * 128],
                         W[:, 2048 + hr * 512 + vh * 256:2048 + hr * 512 + (vh + 1) * 256], W[:, 8192 + hd * 128:8192 + (hd + 1) * 128]], axis=1)
    gamma = 1.0 - 2.0 ** (-5.0 - hr)
    kj = np.arange(128)[:, None]; qi = np.arange(128)[None, :]
    dmask = np.where(qi >= kj, gamma ** np.maximum(qi - kj, 0), 0.0) * 256 ** -0.5
    qdrow = np.tile(gamma ** (np.arange(128) + 1.0), 4)[None, :].repeat(128, axis=0)
    kdec = np.stack([gamma ** (127.0 - np.arange(128)) * 256 ** -0.5, np.full(128, gamma ** 128.0)], axis=1)
    gtab = inp["rel_bias"][:, hd][bidx]
    return {
        "xT": xT, "wA": np.ascontiguousarray(wA), "gain": np.ascontiguousarray(inp["norm_mix"][0].reshape(16, 128).T),
        "hg": np.ascontiguousarray(np.stack([inp["dil_q_gain"][0], inp["dil_k_gain"][0]], axis=1)),
        "cosT": cosT, "sinT": sinT, "dmask": dmask.astype(np.float32), "qdrow": qdrow.astype(np.float32), "kdec": kdec.astype(np.float32),
        "gtab": np.ascontiguousarray(gtab.astype(np.float32)), "mtab": mtab, "ident": np.eye(128, dtype=np.float32),
    }


D = 2048
S = 16384
BT = 512
NB_C = S // BT
NCOL_C = 1032
C = 128
G = 2


def fl(ap):
    return ap.rearrange("p a b -> p (a b)")


def fl4(t):
    return t[:, :, :, :].rearrange("p a b c -> p (a b c)")


class MixC:
    def __init__(self, nblocks=NB_C):
        self.nblocks = nblocks
        self.kb = kb = KB()
        self.nc = kb.nc
        self.xT = kb.din("xT", [D, S])
        self.wC = kb.din("wC", [D, NCOL_C])
        self.gain = kb.din("gain", [128, 16])
        self.convw = kb.din("convw", [128, 8, 4])
        self.hp = kb.din("hp", [128, 2, G, 4])
        self.U = kb.din("U", [C, C])
        self.MU = kb.din("MU", [C, 4 * G, C])
        self.ML = kb.din("ML", [C, 4 * G, C])
        self.I4 = kb.din("I4", [C, 4 * G, C])
        self.ident = kb.din("ident", [128, 128])
        self.og = kb.dout("og", [512, S])
        sb = kb.sb
        self.w = sb("w", [128, 16, NCOL_C], BF16)
        self.ones = sb("ones", [128, 128], BF16)
        self.onesf = sb("onesf", [C, 128], F32)
        self.idb = sb("idb", [128, 128], BF16)
        self.idf = sb("idf", [128, 128], F32)
        self.gain_sb = sb("gain_sb", [128, 16], F32)
        self.cw = sb("cw", [128, 8, 4], F32)
        self.hp_sb = sb("hp_sb", [128, 2, G, 4], F32)
        self.negA = sb("negA", [128, G, 4], F32)
        self.eps_sb = sb("eps_sb", [128, 1], F32)
        self.eps2_sb = sb("eps2_sb", [128, 1], F32)
        self.one_sb = sb("one_sb", [128, 1], F32)
        self.U_sb = sb("U_sb", [C, C], F32)
        self.MU_sb = sb("MU_sb", [C, 4 * G, C], F32)
        self.ML_sb = sb("ML_sb", [C, 4 * G, C], F32)
        self.I4_sb = sb("I4_sb", [C, 4 * G, C], F32)
        self.xst = sb("xst", [128, 16, BT], F32)
        self.sq = sb("sq", [128, 8, BT], BF16)
        self.hT = sb("hT", [128, 16, BT], BF16)
        self.rstd = sb("rstd", [128, BT], F32)
        self.praw = sb("praw", [128, 3 + BT], F32)
        self.halo = sb("halo", [128, 8, 3], F32)
        self.acc = sb("acc", [128, BT], F32)
        self.cvs = sb("cvs", [128, 4, BT], F32)
        self.qnT = sb("qnT", [128, 2, BT], BF16)
        self.knT = sb("knT", [128, 2, BT], BF16)
        self.vT = sb("vT", [128, 4, BT], BF16)
        self.ktok = sb("ktok", [C, G * 2 * 128], F32)
        self.vtok = sb("vtok", [C, G, 512], F32)
        self.xa = sb("xa", [C, G, 4], F32)
        self.beta = sb("beta", [C, G, 4], F32)
        self.nbeta = sb("nbeta", [C, G, 4], F32)
        self.g = sb("g", [C, G, 4], F32)
        self.gcc = sb("gcc", [C, G, 4], F32)
        self.egc = sb("egc", [C, G, 4], F32)
        self.bg = sb("bg", [C, G, 4], F32)
        self.egl = sb("egl", [128, G, 4], F32)
        self.zz = sb("zz", [C, 2 * G * 4 * C], F32)
        self.G1 = self.zz[:, 0:G * 4 * 128].rearrange("p (g h d) -> p g h d", g=G, h=4)
        self.Zmin = self.zz[:, 0:G * 4 * C].rearrange("p (g h d) -> p g h d", g=G, h=4)
        self.Zmax = self.zz[:, G * 4 * C:2 * G * 4 * C].rearrange("p (g h d) -> p g h d", g=G, h=4)
        self.E0T = sb("E0T", [C, G, 4, C], F32)
        self.E1 = sb("E1", [C, G, 4, C], F32)
        self.eR = sb("eR", [128, G, 4 * C], F32)
        self.X = sb("X", [C, G, 4, C], BF16)
        self.Y = sb("Y", [C, G, 4, C], BF16)
        self.Q = sb("Q", [C, G, 4, C], F32)
        self.Tt = sb("Tt", [C, G, 4, C], BF16)
        self.intraT = sb("intraT", [C, G, 4, C], BF16)
        self.vb = sb("vb", [C, G, 4, 128], BF16)
        self.kbg = sb("kbg", [C, G, 4, 128], BF16)
        self.kdk = sb("kdk", [C, G, 4, 128], BF16)
        self.qgT = sb("qgT", [128, G, 4, C], BF16)
        self.nwT = sb("nwT", [128, G, 4, C], BF16)
        self.vnew = sb("vnew", [C, 4, 128], BF16)
        self.St = sb("St", [128, 4, 128], F32)
        self.Stb = sb("Stb", [128, 4, 128], BF16)
        self.ob = sb("ob", [128, 1, 4, BT], F32)
        self.ps = kb.psum("ps", [128, 8, 512], F32)
        self.dtb = self.hp_sb[:, 1, :, :]

    def build(self, debug=False):
        nc = self.nc
        kb = self.kb
        ps = self.ps
        V, A, PE, SP, PL = nc.vector, nc.scalar, nc.tensor, nc.sync, nc.gpsimd
        sems = {"V": kb.sem("sV"), "A": kb.sem("sA"), "P": kb.sem("sP")}
        engs = {"V": V, "A": A, "P": PE}

        def step(e, fn, extra=()):
            eng = engs[e]
            for o in sems:
                if o != e and sems[o].n > 0:
                    eng.wait_ge(sems[o].h, sems[o].n)
            for (sh, sv) in extra:
                eng.wait_ge(sh, sv)
            ins = fn()
            sems[e].inc(ins)

        def sp_wait_all():
            for o in sems:
                if sems[o].n > 0:
                    SP.wait_ge(sems[o].h, sems[o].n)

        s0 = kb.sem("init"); s1 = kb.sem("init1")
        wv = self.wC.rearrange("(c p) n -> p c n", p=128)
        for k0 in range(0, 16, 4):
            s0.inc(PL.dma_start(out=self.w[:, k0:k0 + 4, :], in_=wv[:, k0:k0 + 4, :]), 16)
        s0.inc(PL.dma_start(out=self.idb[:, :], in_=self.ident), 16)
        for (dst, src) in [(self.gain_sb[:, :], self.gain), (self.cw[:, :, :], self.convw), (self.hp_sb[:, :, :, :], self.hp), (self.U_sb[:, :], self.U),
                           (self.MU_sb[:, :, :], self.MU), (self.ML_sb[:, :, :], self.ML), (self.I4_sb[:, :, :], self.I4), (self.idf[:, :], self.ident)]:
            s1.inc(SP.dma_start(out=dst, in_=src), 16)

        def init_v():
            V.memset(self.ones[:, :], 1.0)
            V.memset(self.onesf[:, :], 1.0)
            V.memset(self.eps_sb[:, :], EPS)
            V.memset(self.one_sb[:, :], 1.0)
            V.memset(self.eps2_sb[:, :], EPS * 128.0)
            V.memset(self.St[:, :, :], 0.0)
            V.memset(self.Stb[:, :, :], 0.0)
            return V.memset(self.halo[:, :, :], 0.0)
        step("V", init_v)
        step("A", lambda: A.activation(out=self.negA[:, :, :], in_=self.hp_sb[:, 0, :, :], func=AF.Exp), extra=[(s1.h, s1.n)])
        step("V", lambda: V.tensor_scalar(out=fl(self.negA[:, :, :]), in0=fl(self.negA[:, :, :]), scalar1=-1.0, scalar2=None, op0=ALU.mult), extra=[(s0.h, s0.n), (s1.h, s1.n)])
        PE.wait_ge(s0.h, s0.n)
        PE.wait_ge(s1.h, s1.n)

        xv = self.xT.rearrange("(c p) t -> p c t", p=128)
        s_xl = kb.sem("xl")
        s_o = [kb.sem("so0"), kb.sem("so1")]

        def load_x(b):
            t0 = b * BT
            for hh in range(2):
                s_xl.inc(SP.dma_start(out=self.xst[:, hh * 8:(hh + 1) * 8, :], in_=xv[:, hh * 8:(hh + 1) * 8, t0:t0 + BT]), 16)

        load_x(0)
        for b in range(self.nblocks):
            t0 = b * BT
            sl = 0
            for hf in range(2):
                step("A", lambda hf=hf: A.activation(out=self.sq[:, :, :], in_=self.xst[:, hf * 8:(hf + 1) * 8, :], func=AF.Square), extra=[(s_xl.h, s_xl.n)])

                def f(hf=hf):
                    for k in range(8):
                        ins = PE.matmul(ps[:, 2, :], lhsT=self.ones[:, :], rhs=self.sq[:, k, :], start=(hf == 0 and k == 0), stop=(hf == 1 and k == 7))
                    return ins
                step("P", f)
            step("A", lambda: A.activation(out=self.rstd[:, :], in_=ps[:, 2, :], func=AF.Sqrt, scale=1.0 / D, bias=self.eps_sb[:, 0:1]))

            def f():
                V.reciprocal(out=self.rstd[:, :], in_=self.rstd[:, :])
                for k in range(16):
                    ins = V.scalar_tensor_tensor(out=self.hT[:, k, :], in0=self.xst[:, k, :], scalar=self.gain_sb[:, k:k + 1], in1=self.rstd[:, :],
                                                 op0=ALU.mult, op1=ALU.mult)
                return ins
            step("V", f)
            if b + 1 < self.nblocks:
                sp_wait_all()
                load_x(b + 1)
            for mt in range(8):
                def f(mt=mt):
                    for k in range(16):
                        ins = PE.matmul(ps[:, mt % 2, :], lhsT=self.w[:, k, mt * 128:(mt + 1) * 128], rhs=self.hT[:, k, :], start=(k == 0), stop=(k == 15))
                    return ins
                step("P", f)
                step("A", lambda mt=mt: A.activation(out=self.praw[:, 3:3 + BT], in_=ps[:, mt % 2, :], func=AF.Copy))

                def f(mt=mt):
                    V.tensor_copy(out=self.praw[:, 0:3], in_=self.halo[:, mt, :])
                    V.tensor_scalar(out=self.acc[:, :], in0=self.praw[:, 0:BT], scalar1=self.cw[:, mt, 0:1], scalar2=None, op0=ALU.mult)
                    for j in range(1, 4):
                        ins = V.scalar_tensor_tensor(out=self.acc[:, :], in0=self.praw[:, j:j + BT], scalar=self.cw[:, mt, j:j + 1], in1=self.acc[:, :],
                                                     op0=ALU.mult, op1=ALU.add)
                    return ins
                step("V", f)
                step("A", lambda mt=mt: A.activation(out=(self.cvs[:, mt, :] if mt < 4 else self.vT[:, mt - 4, :]), in_=self.acc[:, :], func=AF.Silu))
                step("V", lambda mt=mt: V.tensor_copy(out=self.halo[:, mt, :], in_=self.praw[:, BT:BT + 3]))
            step("A", lambda: A.activation(out=self.sq[:, 0:4, :], in_=self.cvs[:, 0:4, :], func=AF.Square))
            for mt in range(4):
                step("P", lambda mt=mt: PE.matmul(ps[:, 2, :], lhsT=self.ones[:, :], rhs=self.sq[:, mt, :], start=True, stop=True))
                if mt < 2:
                    step("A", lambda: A.activation(out=self.rstd[:, :], in_=ps[:, 2, :], func=AF.Sqrt, scale=128.0, bias=self.eps2_sb[:, 0:1]))
                else:
                    step("A", lambda: A.activation(out=self.rstd[:, :], in_=ps[:, 2, :], func=AF.Sqrt, scale=1.0, bias=self.eps_sb[:, 0:1]))

                def f(mt=mt):
                    V.reciprocal(out=self.rstd[:, :], in_=self.rstd[:, :])
                    dst = self.qnT[:, mt, :] if mt < 2 else self.knT[:, mt - 2, :]
                    return V.tensor_tensor(out=dst, in0=self.cvs[:, mt, :], in1=self.rstd[:, :], op=ALU.mult)
                step("V", f)
            NL = 7
            for grp in range(BT // (C * G)):
                def csl(gi):
                    c0 = (grp * G + gi) * C
                    return slice(c0, c0 + C)

                def hs(h):
                    return slice(h * C, (h + 1) * C)

                def f():
                    for gi in range(G):
                        for k in range(16):
                            ins = PE.matmul(ps[:, 2, gi * 8:(gi + 1) * 8], lhsT=self.hT[:, k, csl(gi)], rhs=self.w[:, k, 1024:1032],
                                            start=(k == 0), stop=(k == 15))
                    return ins
                step("P", f)
                bav = ps[:, 2, 0:G * 8].rearrange("p (g c) -> p g c", c=8)
                step("V", lambda: V.tensor_tensor(out=self.xa[:, :, :], in0=bav[:, :, 4:8], in1=self.dtb[:, :, :], op=ALU.add))

                def f():
                    A.activation(out=fl(self.xa[:, :, :]), in_=fl(self.xa[:, :, :]), func=AF.Exp)
                    return A.activation(out=fl(self.xa[:, :, :]), in_=fl(self.xa[:, :, :]), func=AF.Ln, bias=self.one_sb[:, 0:1])
                step("A", f)
                step("V", lambda: V.tensor_tensor(out=fl(self.g[:, :, :]), in0=fl(self.xa[:, :, :]), in1=fl(self.negA[:, :, :]), op=ALU.mult))
                step("A", lambda: A.activation(out=self.beta[:, :, :], in_=bav[:, :, 0:4], func=AF.Sigmoid))

                def f():
                    V.tensor_scalar(out=fl(self.nbeta[:, :, :]), in0=fl(self.beta[:, :, :]), scalar1=-1.0, scalar2=None, op0=ALU.mult)
                    for gi in range(G):
                        for h in range(4):
                            ins = V.tensor_scalar(out=self.G1[:, gi, h, :], in0=self.onesf[:, :], scalar1=self.g[:, gi, h:h + 1], scalar2=None, op0=ALU.mult)
                    return ins
                step("V", f)

                def f():
                    for gi in range(G):
                        PE.matmul(ps[:, 2, 64 + gi * 4:68 + gi * 4], lhsT=self.U_sb[:, :], rhs=self.g[:, gi, :], start=True, stop=True)
                        PE.matmul(ps[:, 2, 128 + gi * 4:132 + gi * 4], lhsT=self.onesf[:, :], rhs=self.g[:, gi, :], start=True, stop=True)
                    for gi in range(G):
                        for h in range(4):
                            PE.matmul(ps[:, 3 + gi, hs(h)], lhsT=self.G1[:, gi, h, :], rhs=self.U_sb[:, :], start=True, stop=True)
                    for gi in range(G):
                        for kh in range(2):
                            PE.matmul(ps[:, 5 + gi, hs(kh)], lhsT=self.knT[:, kh, csl(gi)], rhs=self.knT[:, kh, csl(gi)], start=True, stop=True)
                        for kh in range(2):
                            ins = PE.matmul(ps[:, 5 + gi, hs(2 + kh)], lhsT=self.knT[:, kh, csl(gi)], rhs=self.qnT[:, kh, csl(gi)], start=True, stop=True)
                    return ins
                step("P", f)

                def f():
                    A.activation(out=fl(self.gcc[:, :, :]), in_=ps[:, 2, 64:64 + G * 4], func=AF.Copy)
                    A.activation(out=fl(self.egl[:, :, :]), in_=ps[:, 2, 128:128 + G * 4], func=AF.Exp)
                    return A.activation(out=self.eR[:, :, :], in_=ps[:, 3:3 + G, :], func=AF.Exp)
                step("A", f)

                def f():
                    for gi in range(G):
                        for h in range(4):
                            V.tensor_scalar(out=self.Zmin[:, gi, h, :], in0=ps[:, 3 + gi, hs(h)], scalar1=self.gcc[:, gi, h:h + 1], scalar2=0.0,
                                            op0=ALU.subtract, op1=ALU.min)
                            ins = V.tensor_scalar(out=self.Zmax[:, gi, h, :], in0=ps[:, 3 + gi, hs(h)], scalar1=self.gcc[:, gi, h:h + 1], scalar2=0.0,
                                                  op0=ALU.subtract, op1=ALU.max)
                    return ins
                step("V", f)

                def f():
                    A.activation(out=fl(self.egc[:, :, :]), in_=fl(self.gcc[:, :, :]), func=AF.Exp)
                    A.activation(out=fl4(self.E0T), in_=fl4(self.Zmin), func=AF.Exp)
                    return A.activation(out=fl4(self.E1), in_=fl4(self.Zmax), func=AF.Exp, scale=-1.0)
                step("A", f)

                def f():
                    for gi in range(G):
                        for kh in range(2):
                            PE.matmul(ps[:, 0, (gi * 2 + kh) * 128:(gi * 2 + kh + 1) * 128], lhsT=self.knT[:, kh, csl(gi)], rhs=self.idb[:, :], start=True, stop=True)
                        for h in range(4):
                            ins = PE.matmul(ps[:, 3 + gi, hs(h)], lhsT=self.vT[:, h, csl(gi)], rhs=self.idb[:, :], start=True, stop=True)
                    return ins
                step("P", f)

                def f():
                    A.activation(out=self.ktok[:, :], in_=ps[:, 0, :], func=AF.Copy)
                    return A.activation(out=self.vtok[:, :, :], in_=ps[:, 3:3 + G, :], func=AF.Copy)
                step("A", f)

                def f():
                    V.tensor_tensor(out=fl(self.bg[:, :, :]), in0=fl(self.beta[:, :, :]), in1=fl(self.egc[:, :, :]), op=ALU.mult)
                    V.tensor_tensor(out=fl4(self.E0T), in0=fl4(self.E0T), in1=fl(self.MU_sb[:, :, :]), op=ALU.mult)
                    V.tensor_tensor(out=fl4(self.E1), in0=fl4(self.E1), in1=fl(self.ML_sb[:, :, :]), op=ALU.mult)
                    for gi in range(G):
                        ktv = self.ktok[:, :].rearrange("p (g k d) -> p g k d", g=G, k=2)[:, gi, :, :]
                        vtv = self.vtok[:, gi, :].rearrange("p (h d) -> p h d", h=4)
                        for h in range(4):
                            kh = h // 2
                            V.scalar_tensor_tensor(out=self.X[:, gi, h, :], in0=ps[:, 5 + gi, hs(kh)], scalar=self.nbeta[:, gi, h:h + 1],
                                                   in1=self.E1[:, gi, h, :], op0=ALU.mult, op1=ALU.mult)
                            V.tensor_tensor(out=self.intraT[:, gi, h, :], in0=ps[:, 5 + gi, hs(2 + kh)], in1=self.E0T[:, gi, h, :], op=ALU.mult)
                            V.tensor_scalar(out=self.vb[:, gi, h, :], in0=vtv[:, h, :], scalar1=self.beta[:, gi, h:h + 1], scalar2=None, op0=ALU.mult)
                            V.tensor_scalar(out=self.kbg[:, gi, h, :], in0=ktv[:, kh, :], scalar1=self.bg[:, gi, h:h + 1], scalar2=None, op0=ALU.mult)
                            V.tensor_scalar(out=self.kdk[:, gi, h, :], in0=ktv[:, kh, :], scalar1=self.E0T[:, gi, h, C - 1:C], scalar2=None, op0=ALU.mult)
                            ins = V.tensor_tensor(out=self.qgT[:, gi, h, :], in0=self.qnT[:, kh, csl(gi)], in1=self.eR[:, gi, hs(h)], op=ALU.mult)
                    return ins
                step("V", f)

                def f():
                    for gi in range(G):
                        for h in range(4):
                            ins = PE.matmul(ps[:, 2 + gi, hs(h)], lhsT=self.X[:, gi, h, :], rhs=self.idb[:, :], start=True, stop=True)
                    return ins
                step("P", f)
                gb = lambda t: fl4(t).rearrange("p (b c) -> p b c", b=G)
                step("A", lambda: A.activation(out=gb(self.Y), in_=ps[:, 2:2 + G, :], func=AF.Copy))

                def f():
                    V.tensor_tensor(out=gb(self.Q), in0=ps[:, 2:2 + G, :], in1=fl(self.I4_sb[:, :, :]).rearrange("p (b c) -> p b c", b=G), op=ALU.add)
                    return V.tensor_copy(out=fl4(self.Tt), in_=fl4(self.Q))
                step("V", f)
                for lv in range(NL):
                    def f(lv=lv):
                        ins = None
                        for gi in range(G):
                            for h in range(4):
                                if lv <= NL - 2:
                                    ins = PE.matmul(ps[:, 0 + gi, hs(h)], lhsT=self.Y[:, gi, h, :], rhs=self.X[:, gi, h, :], start=True, stop=True)
                                if lv <= NL - 3:
                                    ins = PE.matmul(ps[:, 2 + gi, hs(h)], lhsT=self.X[:, gi, h, :], rhs=self.Y[:, gi, h, :], start=True, stop=True)
                                if lv >= 1:
                                    ins = PE.matmul(ps[:, 4 + gi, hs(h)], lhsT=self.X[:, gi, h, :], rhs=self.Tt[:, gi, h, :], start=True, stop=True)
                        return ins
                    step("P", f)
                    if lv <= NL - 3:
                        step("A", lambda: A.activation(out=gb(self.Y), in_=ps[:, 2:2 + G, :], func=AF.Copy))

                    def f(lv=lv):
                        ins = None
                        if lv >= 1:
                            V.tensor_tensor(out=gb(self.Q), in0=gb(self.Q), in1=ps[:, 4:4 + G, :], op=ALU.add)
                            ins = V.tensor_copy(out=fl4(self.Tt), in_=fl4(self.Q))
                        if lv <= NL - 2:
                            ins = V.tensor_copy(out=gb(self.X), in_=ps[:, 0:G, :])
                        return ins
                    step("V", f)

                def f():
                    for gi in range(G):
                        for h in range(4):
                            ins = PE.matmul(ps[:, 6 + gi, hs(h)], lhsT=self.kbg[:, gi, h, :], rhs=self.Tt[:, gi, h, :], start=True, stop=True)
                    return ins
                step("P", f)
                step("A", lambda: A.activation(out=gb(self.nwT), in_=ps[:, 6:6 + G, :], func=AF.Copy, scale=-1.0))

                for gi in range(G):
                    def f(gi=gi):
                        for h in range(4):
                            PE.matmul(ps[:, 0, h * 128:(h + 1) * 128], lhsT=self.Tt[:, gi, h, :], rhs=self.vb[:, gi, h, :], start=(h == 0), stop=False, skip_group_check=True)
                        for h in range(4):
                            ins = PE.matmul(ps[:, 0, h * 128:(h + 1) * 128], lhsT=self.nwT[:, gi, h, :], rhs=self.Stb[:, h, :], start=False, stop=(h == 3), skip_group_check=True)
                        return ins
                    step("P", f)
                    step("V", lambda: V.tensor_copy(out=fl(self.vnew[:, :, :]), in_=ps[:, 0, :]))

                    def f(gi=gi):
                        for h in range(4):
                            PE.matmul(ps[:, 1, hs(h)], lhsT=self.Stb[:, h, :], rhs=self.qgT[:, gi, h, :], start=(h == 0), stop=False, skip_group_check=True)
                        for h in range(4):
                            PE.matmul(ps[:, 1, hs(h)], lhsT=self.vnew[:, h, :], rhs=self.intraT[:, gi, h, :], start=False, stop=(h == 3), skip_group_check=True)
                        for h in range(4):
                            ins = PE.matmul(ps[:, 2, h * 128:(h + 1) * 128], lhsT=self.kdk[:, gi, h, :], rhs=self.vnew[:, h, :], start=True, stop=True)
                        return ins
                    step("P", f)
                    extra = [(s_o[sl].h, s_o[sl].n)] if (grp == 0 and gi == 0) else []
                    step("A", lambda gi=gi: A.activation(out=self.ob[:, sl, :, csl(gi)], in_=ps[:, 1, :].rearrange("p (h i) -> p h i", h=4), func=AF.Copy), extra=extra)

                    def f(gi=gi):
                        for h in range(4):
                            V.scalar_tensor_tensor(out=self.St[:, h, :], in0=self.St[:, h, :], scalar=self.egl[:, gi, h:h + 1], in1=ps[:, 2, h * 128:(h + 1) * 128],
                                                   op0=ALU.mult, op1=ALU.add)
                        return V.tensor_copy(out=self.Stb[:, :, :], in_=self.St[:, :, :])
                    step("V", f)
            sp_wait_all()
            s_o[sl].inc(SP.dma_start(out=self.og.rearrange("(h p) t -> p h t", p=128)[:, :, t0:t0 + BT], in_=self.ob[:, sl, :, :]), 16)
        for s in s_o:
            SP.wait_ge(s.h, s.n)
        return nc


def mixc_consts():
    U = (np.arange(C)[:, None] <= np.arange(C)[None, :]).astype(np.float32)
    p = np.arange(C)[:, None, None]; f = np.arange(C)[None, None, :]
    MU = np.broadcast_to((f >= p), (C, 4 * G, C)).astype(np.float32)
    ML = np.broadcast_to((p > f), (C, 4 * G, C)).astype(np.float32)
    I4 = np.broadcast_to((p == f), (C, 4 * G, C)).astype(np.float32)
    return U, np.ascontiguousarray(MU), np.ascontiguousarray(ML), np.ascontiguousarray(I4)


def mixc_inputs(inp, c, xT, consts):
    U, MU, ML, I4 = consts
    W = inp["gdn_w_in"][0]
    kh0 = 2 * c; vh0 = 4 * c
    qc = slice(kh0 * 128, kh0 * 128 + 256)
    kc = slice(2048 + kh0 * 128, 2048 + kh0 * 128 + 256)
    vc = slice(4096 + vh0 * 128, 4096 + vh0 * 128 + 512)
    bc = slice(12288 + vh0, 12288 + vh0 + 4)
    ac = slice(12320 + vh0, 12320 + vh0 + 4)
    wC = np.concatenate([W[:, qc], W[:, kc], W[:, vc], W[:, bc], W[:, ac]], axis=1)
    cw = inp["gdn_conv"][0]
    cwc = np.concatenate([cw[:, qc], cw[:, kc], cw[:, vc]], axis=1)
    convw = np.ascontiguousarray(cwc.reshape(4, 8, 128).transpose(2, 1, 0))
    hp = np.stack([np.broadcast_to(inp["gdn_a_log"][0][vh0:vh0 + 4], (128, G, 4)), np.broadcast_to(inp["gdn_dt_bias"][0][vh0:vh0 + 4], (128, G, 4))], axis=1)
    return {"xT": xT, "wC": np.ascontiguousarray(wC), "gain": np.ascontiguousarray(inp["norm_mix"][1].reshape(16, 128).T),
            "convw": convw.astype(np.float32), "hp": np.ascontiguousarray(hp.astype(np.float32)), "U": U, "MU": MU, "ML": ML, "I4": I4,
            "ident": np.eye(128, dtype=np.float32)}


def _run(nc, ins):
    res = run_bass_kernel_spmd(nc, ins, core_ids=list(range(8)))
    return res.results


def kernel(**inputs):
    inp = {k: np.asarray(v) for k, v in inputs.items()}
    S_ = 16384
    x = inp["x"][0]
    xT = np.ascontiguousarray(x.T)
    consts = mixa_consts()
    nc = MixA().build()
    ra = _run(nc, [mixa_inputs(inp, c, xT, consts) for c in range(8)])
    oT0 = np.empty((3072, S_), np.float32)
    for c in range(8):
        hr, vh = c // 2, c % 2
        oT0[hr * 512 + vh * 256: hr * 512 + (vh + 1) * 256] = ra[c]["oret"]
        oT0[2048 + c * 128: 2048 + (c + 1) * 128] = ra[c]["odil"]
    del ra
    nc = Post(0).build()
    rb = _run(nc, [post_inputs(0, inp, np.ascontiguousarray(xT[:, c * 2048:(c + 1) * 2048]), np.ascontiguousarray(oT0[:, c * 2048:(c + 1) * 2048]))
                   for c in range(8)])
    x1T = np.ascontiguousarray(np.concatenate([rb[c]["xo"] for c in range(8)], axis=1))
    del rb, oT0
    cc = mixc_consts()
    nc = MixC().build()
    rc = _run(nc, [mixc_inputs(inp, c, x1T, cc) for c in range(8)])
    oT1 = np.ascontiguousarray(np.concatenate([rc[c]["og"] for c in range(8)], axis=0))
    del rc
    nc = Post(1).build()
    rd = _run(nc, [post_inputs(1, inp, np.ascontiguousarray(x1T[:, c * 2048:(c + 1) * 2048]), np.ascontiguousarray(oT1[:, c * 2048:(c + 1) * 2048]))
                   for c in range(8)])
    outT = np.concatenate([rd[c]["xo"] for c in range(8)], axis=1)
    return np.ascontiguousarray(outT.T)[None].astype(np.float32)
```
